# Optimizing a Trainium2 kernel written in Bass

```python
import math
import jax, jax.numpy as jnp
from jax import lax
import numpy as np

D_MODEL = 1024
BATCH = 4
SEQ = 8192
DEPTH = 4

N_A = DEPTH // 2
N_B = DEPTH - N_A
HEAD_DIM = 64
N_HEADS = D_MODEL // HEAD_DIM
N_META = 16
BLOCK = 128
PAD = BLOCK - N_META
D_DECAY_LORA = max(32, int(round(1.8 * D_MODEL ** 0.5 / 32)) * 32)
D_AAA_LORA = max(32, int(round(1.8 * D_MODEL ** 0.5 / 32)) * 32)
D_MV_LORA = max(32, int(round(1.3 * D_MODEL ** 0.5 / 32)) * 32)
D_GATE_LORA = max(32, int(round(0.6 * D_MODEL ** 0.8 / 32)) * 32)
D_FF = int(math.ceil(8 * D_MODEL / 3 / 256)) * 256
CONV_WIDTH = 3
GN_EPS = 64e-5
RMS_EPS = 1e-6

kernel_name = "yoco_rwkv7_stickbreaking_hybrid"


def rms_norm(x, g):
    xf = x.astype(jnp.float32)
    y = xf * lax.rsqrt(jnp.mean(xf * xf, axis=-1, keepdims=True) + RMS_EPS) * g.astype(jnp.float32)
    return y.astype(x.dtype)


def token_shift(x):
    return jnp.pad(x, ((0, 0), (1, 0), (0, 0)))[:, :-1]


def conv_glu_ffn(x, w_up, conv_w, conv_b, w_down):
    u = x @ w_up
    c = u.shape[-1]
    u = lax.conv_general_dilated(
        u, conv_w.reshape(CONV_WIDTH, 1, c).astype(u.dtype),
        window_strides=(1,), padding=[(CONV_WIDTH - 1, 0)],
        dimension_numbers=('NWC', 'WIO', 'NWC'), feature_group_count=c) + conv_b
    gate, val = jnp.split(u, 2, axis=-1)
    return (jax.nn.silu(gate) * val) @ w_down


def rwkv7_time_mix(x, v_first, v_res, mix, w_r, w_k, w_v, w_o, w0, w1, w2,
                   a0, a1, a2, g1, g2, k_k, k_a, r_k, lnx_g, lnx_b):
    bsz, L, C = x.shape
    f32 = jnp.float32
    xx = token_shift(x) - x
    xr, xw, xk, xv, xa, xg = [x + xx * mix[i] for i in range(6)]
    r = xr @ w_r
    k = xk @ w_k
    v = xv @ w_v
    w_log = -jax.nn.softplus(-(w0 + jnp.tanh(xw @ w1) @ w2)) - 0.5
    a = jax.nn.sigmoid(a0 + (xa @ a1) @ a2)
    g = jax.nn.sigmoid(xg @ g1) @ g2

    def heads(t):
        return t.astype(f32).reshape(bsz, L, N_HEADS, HEAD_DIM)

    kk = heads(k * k_k)
    kk = kk / jnp.maximum(jnp.linalg.norm(kk, axis=-1, keepdims=True), 1e-12)
    k = k * (1 + (a - 1) * k_a)
    if v_res is None:
        v_first = v
    else:
        v0, v1, v2 = v_res
        v = v + (v_first - v) * jax.nn.sigmoid(v0 + (xv @ v1) @ v2)
    rh, kh, vh, ah = heads(r), heads(k), heads(v), heads(a)
    decay = jnp.exp(-jnp.exp(heads(w_log)))
    seq = tuple(jnp.moveaxis(t, 1, 0) for t in (rh, decay, kh, vh, -kk, kk * ah))

    def step(S, inp):
        r_t, w_t, k_t, v_t, a_t, b_t = inp
        sa = jnp.einsum('bhij,bhj->bhi', S, a_t)
        S = (S * w_t[:, :, None, :] + sa[..., None] * b_t[:, :, None, :]
             + v_t[..., None] * k_t[:, :, None, :])
        return S, jnp.einsum('bhij,bhj->bhi', S, r_t)

    S0 = jnp.zeros((bsz, N_HEADS, HEAD_DIM, HEAD_DIM), f32)
    _, y = lax.scan(step, S0, seq)
    y = jnp.moveaxis(y, 0, 1)
    mu = jnp.mean(y, axis=-1, keepdims=True)
    var = jnp.mean(jnp.square(y - mu), axis=-1, keepdims=True)
    y = ((y - mu) * lax.rsqrt(var + GN_EPS)).reshape(bsz, L, C) * lnx_g.astype(f32) + lnx_b.astype(f32)
    bonus = jnp.sum(rh * kh * r_k.astype(f32), axis=-1, keepdims=True) * vh
    y = (y + bonus.reshape(bsz, L, C)).astype(x.dtype)
    return (y * g) @ w_o, v_first


def to_padded_heads(t):
    bsz, L, _ = t.shape
    t = t.reshape(bsz, L, N_HEADS, HEAD_DIM)
    t = jnp.pad(t, ((0, 0), (PAD, 0), (0, 0), (0, 0)))
    return jnp.transpose(t, (0, 2, 1, 3))


def stick_breaking_attention(q, k, v):
    bsz, nh, lp, dh = q.shape
    nb = lp // BLOCK
    scale = dh ** -0.5
    key_pos = jnp.arange(lp)
    qb = jnp.moveaxis(q.reshape(bsz, nh, nb, BLOCK, dh), 2, 0)

    def one_block(args):
        q_blk, blk = args
        z = jnp.einsum('bhqd,bhkd->bhqk', q_blk, k).astype(jnp.float32) * scale
        q_pos = blk * BLOCK + jnp.arange(BLOCK)
        valid = (key_pos[None, :] < q_pos[:, None]) & (key_pos[None, :] >= PAD)
        log_1m = jnp.where(valid, jax.nn.log_sigmoid(-z), 0.0)
        log_after = lax.cumsum(log_1m, axis=3, reverse=True) - log_1m
        w = jnp.where(valid, jnp.exp(jax.nn.log_sigmoid(z) + log_after), 0.0)
        return jnp.einsum('bhqk,bhkd->bhqd', w.astype(v.dtype), v)

    out = lax.map(one_block, (qb, jnp.arange(nb)))
    out = jnp.moveaxis(out, 0, 2).reshape(bsz, nh, lp, dh)
    return jnp.transpose(out, (0, 2, 1, 3))[:, PAD:].reshape(bsz, lp - PAD, nh * dh)


def setup_inputs(seed: int = 0) -> dict:
    key = jax.random.key(seed)
    ks = jax.random.split(key, 40)
    counter = iter(range(40))

    def nk():
        return ks[next(counter)]

    def nrm(shape, s):
        return jax.random.normal(nk(), shape, jnp.float32) * s

    def uni(shape, lo, hi):
        return jax.random.uniform(nk(), shape, jnp.float32, lo, hi)

    D, F2 = D_MODEL, 2 * D_FF
    nv = max(N_A - 1, 0)
    return {
        "x": nrm((BATCH, SEQ, D), 1.0),
        "meta_tokens": nrm((N_META, D), 1.0),
        "norm_mix_g": 1.0 + nrm((DEPTH, D), 0.02),
        "norm_ffn_g": 1.0 + nrm((DEPTH, D), 0.02),
        "ffn_up": nrm((DEPTH, D, F2), D ** -0.5),
        "ffn_conv_w": nrm((DEPTH, CONV_WIDTH, F2), CONV_WIDTH ** -0.5),
        "ffn_conv_b": nrm((DEPTH, F2), 0.01),
        "ffn_down": nrm((DEPTH, D_FF, D), D_FF ** -0.5),
        "rw_mix": uni((N_A, 6, D), 0.0, 1.0),
        "rw_wr": nrm((N_A, D, D), D ** -0.5),
        "rw_wk": nrm((N_A, D, D), D ** -0.5),
        "rw_wv": nrm((N_A, D, D), D ** -0.5),
        "rw_wo": nrm((N_A, D, D), D ** -0.5),
        "rw_w0": uni((N_A, D), -6.5, -1.0),
        "rw_w1": nrm((N_A, D, D_DECAY_LORA), D ** -0.5),
        "rw_w2": nrm((N_A, D_DECAY_LORA, D), 0.1 * D_DECAY_LORA ** -0.5),
        "rw_a0": nrm((N_A, D), 0.1),
        "rw_a1": nrm((N_A, D, D_AAA_LORA), D ** -0.5),
        "rw_a2": nrm((N_A, D_AAA_LORA, D), 0.1 * D_AAA_LORA ** -0.5),
        "rw_g1": nrm((N_A, D, D_GATE_LORA), D ** -0.5),
        "rw_g2": nrm((N_A, D_GATE_LORA, D), D_GATE_LORA ** -0.5),
        "rw_kk": 0.85 + nrm((N_A, D), 0.05),
        "rw_ka": 1.0 + nrm((N_A, D), 0.05),
        "rw_rk": nrm((N_A, N_HEADS, HEAD_DIM), 0.1),
        "rw_lnx_g": 1.0 + nrm((N_A, D), 0.02),
        "rw_lnx_b": nrm((N_A, D), 0.01),
        "rw_v0": 1.0 + nrm((nv, D), 0.1),
        "rw_v1": nrm((nv, D, D_MV_LORA), D ** -0.5),
        "rw_v2": nrm((nv, D_MV_LORA, D), 0.1 * D_MV_LORA ** -0.5),
        "kv_norm_g": 1.0 + nrm((D,), 0.02),
        "sb_wk": nrm((D, D), D ** -0.5),
        "sb_wv": nrm((D, D), D ** -0.5),
        "sb_wq": nrm((N_B, D, D), D ** -0.5),
        "sb_wo": nrm((N_B, D, D), D ** -0.5),
        "final_norm_g": 1.0 + nrm((D,), 0.02),
    }


def reference(x, meta_tokens, norm_mix_g, norm_ffn_g, ffn_up, ffn_conv_w, ffn_conv_b, ffn_down,
              rw_mix, rw_wr, rw_wk, rw_wv, rw_wo, rw_w0, rw_w1, rw_w2, rw_a0, rw_a1, rw_a2,
              rw_g1, rw_g2, rw_kk, rw_ka, rw_rk, rw_lnx_g, rw_lnx_b, rw_v0, rw_v1, rw_v2,
              kv_norm_g, sb_wk, sb_wv, sb_wq, sb_wo, final_norm_g):
    bsz = x.shape[0]
    meta = jnp.broadcast_to(meta_tokens.astype(x.dtype)[None], (bsz, N_META, D_MODEL))
    h = jnp.concatenate([meta, x], axis=1)
    v_first = None
    k_sh = v_sh = None
    for layer in range(DEPTH):
        hn = rms_norm(h, norm_mix_g[layer])
        if layer < N_A:
            i = layer
            v_res = None if i == 0 else (rw_v0[i - 1], rw_v1[i - 1], rw_v2[i - 1])
            mix_out, v_first = rwkv7_time_mix(
                hn, v_first, v_res, rw_mix[i], rw_wr[i], rw_wk[i], rw_wv[i], rw_wo[i],
                rw_w0[i], rw_w1[i], rw_w2[i], rw_a0[i], rw_a1[i], rw_a2[i], rw_g1[i], rw_g2[i],
                rw_kk[i], rw_ka[i], rw_rk[i], rw_lnx_g[i], rw_lnx_b[i])
        else:
            j = layer - N_A
            if layer == N_A:
                kvn = rms_norm(h, kv_norm_g)
                k_sh = to_padded_heads(kvn @ sb_wk)
                v_sh = to_padded_heads(kvn @ sb_wv)
            q = to_padded_heads(hn @ sb_wq[j])
            mix_out = stick_breaking_attention(q, k_sh, v_sh) @ sb_wo[j]
        h = h + mix_out
        h = h + conv_glu_ffn(rms_norm(h, norm_ffn_g[layer]), ffn_up[layer], ffn_conv_w[layer],
                             ffn_conv_b[layer], ffn_down[layer])
    return rms_norm(h, final_norm_g)[:, N_META:, :]
```

```python
import numpy as np
from contextlib import ExitStack
import concourse.bass as bass
import concourse.mybir as mybir
from concourse.bass_utils import run_bass_kernel_spmd

F32 = mybir.dt.float32
BF16 = mybir.dt.bfloat16
ALU = mybir.AluOpType
AF = mybir.ActivationFunctionType
AX = mybir.AxisListType

D = 1024
KC = 8
NH = 16
HD = 64
NMETA = 16
PAD = 112
DFF = 2816
F2 = 5632
NJ = 22
FT = 384
DEPTH = 4
RMS_EPS = 1e-6
GN_EPS = 64e-5


class Res:
    __slots__ = ("name", "lw", "rd")

    def __init__(self, name):
        self.name = name
        self.lw = None
        self.rd = {}


class KB:
    SEM_LIMIT = 30000

    def __init__(self, nc):
        self.nc = nc
        self.q = {e: [] for e in ("pe", "act", "dve", "pool", "sp")}
        self.sems = {}
        self.cur = {}
        self.waited = {e: {} for e in self.q}
        self.nsem = 0
        self.dmasem = {}
        self.nops = 0

    def _newsem(self, tag):
        key = "%s_%d" % (tag, self.nsem)
        self.nsem += 1
        self.sems[key] = self.nc.alloc_semaphore(name=key)
        return key

    def _eng_event(self, eng):
        c = self.cur.get(eng)
        if c is None or c[1] >= self.SEM_LIMIT:
            c = [self._newsem(eng), 0]
            self.cur[eng] = c
        c[1] += 1
        return (c[0], c[1])

    def _dma_event(self, chain):
        c = self.dmasem.get(chain)
        if c is None or c[1] >= self.SEM_LIMIT:
            free = getattr(self, "free_dma", [])
            free.sort(key=lambda x: x[1])
            if free and free[0][1] < self.SEM_LIMIT // 2:
                c = list(free.pop(0))
            else:
                c = [self._newsem("d"), 0]
            self.dmasem[chain] = c
        c[1] += 16
        return (c[0], c[1])

    def recycle_dma(self):
        free = getattr(self, "free_dma", [])
        for ch, c in self.dmasem.items():
            free.append((c[0], c[1]))
        self.free_dma = free
        self.dmasem = {}

    def _deps(self, eng, reads, writes, is_dma):
        waits = {}

        def need(sk, v, src_eng, kind):
            if (not is_dma) and eng == "pe" and src_eng == "pe":
                return
            if (not is_dma) and src_eng == eng and kind == "war":
                return
            if self.waited[eng].get(sk, 0) >= v:
                return
            if waits.get(sk, 0) < v:
                waits[sk] = v

        for r in reads:
            if r.lw is not None:
                need(r.lw[0], r.lw[1], r.lw[2], "raw")
        for w in writes:
            if w.lw is not None:
                need(w.lw[0], w.lw[1], w.lw[2], "waw")
            for sk, (v, e) in w.rd.items():
                need(sk, v, e, "war")
        return waits

    def _record(self, ev, src, reads, writes):
        for r in reads:
            r.rd[ev[0]] = (ev[1], src)
        for w in writes:
            w.lw = (ev[0], ev[1], src)
            w.rd = {}

    def op(self, eng, fn, reads=(), writes=()):
        waits = self._deps(eng, reads, writes, False)
        for sk, v in waits.items():
            self.waited[eng][sk] = v
        ev = self._eng_event(eng)
        self._emit(eng, fn, waits, ev, 1)
        self._record(ev, eng, reads, writes)
        self.nops += 1
        return ev

    def dma(self, eng, fn, reads=(), writes=(), chain=None):
        waits = self._deps(eng, reads, writes, True)
        for sk, v in waits.items():
            self.waited[eng][sk] = v
        assert chain is not None
        ev = self._dma_event(chain)
        self._emit(eng, fn, waits, ev, 16)
        self._record(ev, "dma", reads, writes)
        self.nops += 1
        return ev

    def wait_all(self, eng, resources):
        waits = {}
        for r in resources:
            if r.lw is not None:
                sk, v = r.lw[0], r.lw[1]
                if self.waited[eng].get(sk, 0) < v and waits.get(sk, 0) < v:
                    waits[sk] = v
        for sk, v in waits.items():
            self.waited[eng][sk] = v
        self._emit(eng, None, waits, None, 0)

    ENG = {"pe": "tensor", "act": "scalar", "dve": "vector", "pool": "gpsimd", "sp": "sync"}

    def _emit(self, eng, fn, waits, ev, inc):
        engine = getattr(self.nc, self.ENG[eng])
        for sk, v in waits.items():
            engine.wait_ge(self.sems[sk], v)
        if fn is not None:
            ins = fn(engine)
            ins.then_inc(self.sems[ev[0]], inc)

    def emit(self):
        return

    def emit_old(self):
        nc = self.nc
        engs = {"pe": "tensor", "act": "scalar", "dve": "vector", "pool": "gpsimd", "sp": "sync"}
        with nc.Block() as block:
            for e, attr in engs.items():
                ops = self.q[e]
                if not ops:
                    continue

                def body(engine, ops=ops):
                    for fn, waits, ev, inc in ops:
                        for sk, v in waits:
                            engine.wait_ge(self.sems[sk], v)
                        if fn is not None:
                            ins = fn(engine)
                            ins.then_inc(self.sems[ev[0]], inc)
                getattr(block, attr)(body)


class T:
    def __init__(self, nc, name, shape, dtype, psum=False, stack=None):
        self.name = name
        if psum:
            cm = nc.psum_tensor(name, list(shape), dtype)
        else:
            cm = nc.sbuf_tensor(name, list(shape), dtype)
        self.t = stack.enter_context(cm)
        self.r = Res(name)

    def __getitem__(self, idx):
        return self.t[idx]


class Prog:
    def __init__(self, nb, dbg=None):
        self.nb = nb
        self.LP = nb * 128
        self.dbg = dbg or {}
        self.nc = bass.Bass("TRN2", target_bir_lowering=False)
        self.kb = KB(self.nc)
        self.dram = {}
        self.dres = {}
        self.tiles = {}
        self.rot = {}
        self.gstack = ExitStack()
        self.pstack = None
        self.pid = 0

    def phase_begin(self):
        self.pstack = ExitStack()
        self.pid += 1
        self.ptiles = []

    def phase_end(self):
        self.barrier()
        self.kb.recycle_dma()
        self.pstack.close()
        self.pstack = None
        for nm in self.ptiles:
            self.tiles.pop(nm, None)
        self.rot = {}

    def barrier(self):
        kb = self.kb
        targets = {}
        for e, c in kb.cur.items():
            targets[c[0]] = c[1]
        for ch, c in kb.dmasem.items():
            targets[c[0]] = c[1]
        for e in ("pe", "act", "dve", "pool", "sp"):
            waits = {}
            for sk, v in targets.items():
                if kb.cur.get(e) is not None and kb.cur[e][0] == sk:
                    continue
                if kb.waited[e].get(sk, 0) < v:
                    waits[sk] = v
                    kb.waited[e][sk] = v
            kb._emit(e, None, waits, None, 0)

    def din(self, name, shape, dtype=F32):
        self.dram[name] = self.nc.dram_tensor(name, list(shape), dtype, kind="ExternalInput").ap()
        return self.dram[name]

    def dout(self, name, shape, dtype=F32):
        self.dram[name] = self.nc.dram_tensor(name, list(shape), dtype, kind="ExternalOutput").ap()
        return self.dram[name]

    def dscratch(self, name, shape, dtype=F32):
        self.dram[name] = self.nc.dram_tensor(name, list(shape), dtype, kind="Internal").ap()
        return self.dram[name]

    def dr(self, key):
        r = self.dres.get(key)
        if r is None:
            r = Res(str(key))
            self.dres[key] = r
        return r

    def tile(self, name, shape, dtype=F32, psum=False):
        if self.pstack is not None:
            t = T(self.nc, "%s_p%d" % (name, self.pid), shape, dtype, psum, self.pstack)
            self.ptiles.append(name)
        else:
            t = T(self.nc, name, shape, dtype, psum, self.gstack)
        self.tiles[name] = t
        return t

    def rtile(self, name, n, shape, dtype=F32, psum=False):
        ent = self.rot.get(name)
        if ent is None:
            ent = [[self.tile("%s%d" % (name, i), shape, dtype, psum) for i in range(n)], 0]
            self.rot[name] = ent
        t = ent[0][ent[1] % n]
        ent[1] += 1
        return t

    def op(self, eng, fn, reads=(), writes=()):
        return self.kb.op(eng, fn, [x.r if isinstance(x, T) else x for x in reads],
                          [x.r if isinstance(x, T) else x for x in writes])

    def load(self, dst, dst_ap, src_ap, src_res=None, eng="sp"):
        self.kb.dma(eng, lambda e: e.dma_start(out=dst_ap, in_=src_ap),
                    reads=[], writes=[dst.r], chain="ld_" + dst.name)

    def store(self, dst_ap, dst_res, src, src_ap, eng="sp"):
        self.kb.dma(eng, lambda e: e.dma_start(out=dst_ap, in_=src_ap),
                    reads=[src.r], writes=[], chain="st_" + src.name)

    def rmsnorm(self, h, w, g_ap, out, out_off=0, tag="n"):
        rw_ = getattr(self, "rn_w", 512)
        sq = self.rtile("rn_sq", 1, [128, KC, rw_], BF16)
        ss = self.rtile("rn_ss", 1, [128, 512], F32, psum=True)
        rstd = self.rtile("rn_rstd", 2, [128, rw_], F32)
        ones = self.tiles["onesD"]
        self.op("dve", lambda e: e.tensor_tensor(out=sq[:, :, 0:w], in0=h[:, :, 0:w], in1=h[:, :, 0:w],
                                                  op=ALU.mult), [h], [sq])
        for c in range(KC):
            self.op("pe", lambda e, c=c: e.matmul(ss[:, 0:w], lhsT=ones[:], rhs=sq[:, c, 0:w],
                                                  start=(c == 0), stop=(c == KC - 1)), [ones, sq], [ss])
        self.op("act", lambda e: e.activation(out=rstd[:, 0:w], in_=ss[:, 0:w], func=AF.Sqrt, bias=RMS_EPS,
                                              scale=1.0), [ss], [rstd])
        self.op("dve", lambda e: e.reciprocal(out=rstd[:, 0:w], in_=rstd[:, 0:w]), [rstd], [rstd])
        for c in range(KC):
            self.op("dve", lambda e, c=c: e.scalar_tensor_tensor(
                out=out[:, c, out_off:out_off + w], in0=h[:, c, 0:w], scalar=g_ap(c), in1=rstd[:, 0:w],
                op0=ALU.mult, op1=ALU.mult), [h, rstd, self.tiles["consts"]], [out])

    def setup_consts(self):
        onesD = self.tile("onesD", [128, 128], BF16)
        self.op("pool", lambda e: e.memset(onesD[:], 1.0 / D), [], [onesD])
        zt = self.tile("zeros", [128, PAD], F32)
        self.op("pool", lambda e: e.memset(zt[:], 0.0), [], [zt])
        self.din("norm_ffn_g", [128, DEPTH * KC])
        self.din("conv_w", [128, DEPTH * 44 * 3])
        self.din("conv_b", [128, DEPTH * 44])
        self.din("final_g", [128, KC])
        self.din("norm_mix_g", [128, DEPTH * KC])
        self.din("kv_norm_g", [128, KC])
        for nm, n in (("rw_mix", 2 * 6 * KC), ("rw_w0", 2 * KC), ("rw_a0", 2 * KC), ("rw_kk", 2 * KC),
                      ("rw_ka", 2 * KC), ("rw_rk", 2 * KC)):
            self.din(nm, [128, n])
        ncol = DEPTH * KC + DEPTH * 44 * 3 + DEPTH * 44 + KC + DEPTH * KC + KC + 2 * 6 * KC + 5 * 2 * KC
        consts = self.tile("consts", [128, ncol], F32)
        self.coff = {}
        off = 0
        for nm, n in (("norm_ffn_g", DEPTH * KC), ("conv_w", DEPTH * 44 * 3), ("conv_b", DEPTH * 44),
                      ("final_g", KC), ("norm_mix_g", DEPTH * KC), ("kv_norm_g", KC),
                      ("rw_mix", 2 * 6 * KC), ("rw_w0", 2 * KC), ("rw_a0", 2 * KC), ("rw_kk", 2 * KC),
                      ("rw_ka", 2 * KC), ("rw_rk", 2 * KC)):
            self.coff[nm] = off
            self.load(consts, consts[:, off:off + n], self.dram[nm][:, :], self.dr(nm))
            off += n

    def cc(self, nm, idx):
        o = self.coff[nm] + idx
        return self.tiles["consts"][:, o:o + 1]

    def ffn_tiles(self):
        tiles = []
        o = PAD
        while o < self.LP:
            ow = min(FT - 2, self.LP - o)
            tiles.append((o - 2, ow))
            o += ow
        return tiles

    def ffn_weights(self, layer):
        wup = self.tiles.get("wup") or self.tile("wup", [128, KC, F2], BF16)
        wdn = self.tiles.get("wdn") or self.tile("wdn", [128, NJ, D], BF16)
        up = self.dram["ffn_up"]
        dn = self.dram["ffn_down"]
        for c in range(KC):
            for hf in range(2):
                self.load(wup, wup[:, c, hf * DFF:(hf + 1) * DFF], up[layer, :, c, hf * DFF:(hf + 1) * DFF],
                          self.dr("ffn_up"), eng="pool")
        for j0 in range(0, NJ, 2):
            self.load(wdn, wdn[:, j0:j0 + 2, :], dn[layer, :, j0:j0 + 2, :], self.dr("ffn_down"), eng="pool")
        return wup, wdn

    def ffn_phase(self, layer, src, dst):
        self.phase_begin()
        wup, wdn = self.ffn_weights(layer)
        hsrc = self.dram[src].rearrange("c p t -> p c t")
        hdst = self.dram[dst].rearrange("c p t -> p c t")
        for ti, (i0, ow) in enumerate(self.ffn_tiles()):
            iw = ow + 2
            h = self.rtile("f_h", 2, [128, KC, FT], F32)
            self.load(h, h[:, :, 0:iw], hsrc[:, :, i0:i0 + iw], self.dr((src, "all")))
            hn = self.rtile("f_hn", 1, [128, KC, FT], BF16)
            self.rmsnorm(h, iw, lambda c: self.cc("norm_ffn_g", layer * KC + c), hn)
            m = self.rtile("f_m", 1, [128, NJ, FT], BF16)
            for j in range(NJ):
                ys = []
                for half, ch in enumerate((j, NJ + j)):
                    ps = self.rtile("f_ps", 4, [128, 512], F32, psum=True)
                    for c in range(KC):
                        self.op("pe", lambda e, c=c, ps=ps, ch=ch: e.matmul(
                            ps[:, 0:iw], lhsT=wup[:, c, ch * 128:(ch + 1) * 128], rhs=hn[:, c, 0:iw],
                            start=(c == 0), stop=(c == KC - 1)), [wup, hn], [ps])
                    y = self.rtile("f_y", 4, [128, FT], F32)
                    cw = lambda tap, ch=ch: self.cc("conv_w", (layer * 44 + ch) * 3 + tap)
                    cb = self.cc("conv_b", layer * 44 + ch)
                    cst = self.tiles["consts"]
                    self.op("act", lambda e, y=y, ps=ps, cw=cw, cb=cb: e.activation(
                        out=y[:, 0:ow], in_=ps[:, 2:2 + ow], func=AF.Identity, bias=cb, scale=cw(2)),
                        [ps, cst], [y])
                    self.op("dve", lambda e, y=y, ps=ps, cw=cw: e.scalar_tensor_tensor(
                        out=y[:, 0:ow], in0=ps[:, 1:1 + ow], scalar=cw(1), in1=y[:, 0:ow],
                        op0=ALU.mult, op1=ALU.add), [ps, y, cst], [y])
                    self.op("dve", lambda e, y=y, ps=ps, cw=cw: e.scalar_tensor_tensor(
                        out=y[:, 0:ow], in0=ps[:, 0:ow], scalar=cw(0), in1=y[:, 0:ow],
                        op0=ALU.mult, op1=ALU.add), [ps, y, cst], [y])
                    ys.append(y)
                yg, yv = ys
                self.op("act", lambda e, yg=yg: e.activation(out=yg[:, 0:ow], in_=yg[:, 0:ow], func=AF.Silu),
                        [yg], [yg])
                self.op("dve", lambda e, yg=yg, yv=yv, j=j: e.tensor_tensor(
                    out=m[:, j, 0:ow], in0=yg[:, 0:ow], in1=yv[:, 0:ow], op=ALU.mult), [yg, yv], [m])
            for n in range(KC):
                ps = self.rtile("f_ps", 4, [128, 512], F32, psum=True)
                for j in range(NJ):
                    self.op("pe", lambda e, j=j, n=n, ps=ps: e.matmul(
                        ps[:, 0:ow], lhsT=wdn[:, j, n * 128:(n + 1) * 128], rhs=m[:, j, 0:ow],
                        start=(j == 0), stop=(j == NJ - 1)), [wdn, m], [ps])
                ho = self.rtile("f_ho", 3, [128, FT], F32)
                self.op("dve", lambda e, n=n, ps=ps, ho=ho: e.tensor_tensor(
                    out=ho[:, 0:ow], in0=ps[:, 0:ow], in1=h[:, n, 2:2 + ow], op=ALU.add), [ps, h], [ho])
                self.store(hdst[:, n, i0 + 2:i0 + 2 + ow], self.dr((dst, "all")), ho, ho[:, 0:ow], eng="sp")
        self.phase_end()


    def load_w1024(self, name, dram_ap):
        w = self.tile(name, [128, KC, D], BF16)
        for c0 in range(0, KC, 2):
            self.load(w, w[:, c0:c0 + 2, :], dram_ap[:, c0:c0 + 2, :], self.dr("wts"), eng="pool")
        return w

    def blk_groups(self):
        gs = []
        b = 0
        while b < self.nb:
            n = min(4, self.nb - b)
            gs.append((b, n))
            b += n
        return gs

    def qkv_phase(self, j, src, do_kv):
        self.phase_begin()
        layer = 2 + j
        wq = self.load_w1024("wq", self.dram["sb_wq"][j])
        if do_kv:
            wk = self.load_w1024("wk", self.dram["sb_wk"])
            wv = self.load_w1024("wv", self.dram["sb_wv"])
        hsrc = self.dram[src].rearrange("c p t -> p c t")
        qT = self.dram["QT"].rearrange("c p t -> p c t")
        kT = self.dram["KT"].rearrange("c p t -> p c t")
        vd = self.dram["VV"]
        for (b0, nblk) in self.blk_groups():
            t0, tw = b0 * 128, nblk * 128
            h = self.rtile("q_h", 2, [128, KC, 512], F32)
            self.load(h, h[:, :, 0:tw], hsrc[:, :, t0:t0 + tw], self.dr((src, "all")))
            hn = self.rtile("q_hn", 2, [128, KC, 512], BF16)
            self.rmsnorm(h, tw, lambda c: self.cc("norm_mix_g", layer * KC + c), hn)
            jobs = [(wq, qT, 0.125, "QT")]
            if do_kv:
                kn = self.rtile("q_kn", 2, [128, KC, 512], BF16)
                self.rmsnorm(h, tw, lambda c: self.cc("kv_norm_g", c), kn)
                jobs.append((wk, kT, 1.0, "KT"))
            for (w, dst, scale, dname) in jobs:
                xin = hn if dname == "QT" else kn
                for n in range(KC):
                    ps = self.rtile("q_ps", 4, [128, 512], F32, psum=True)
                    for c in range(KC):
                        self.op("pe", lambda e: e.matmul(ps[:, 0:tw], lhsT=w[:, c, n * 128:(n + 1) * 128],
                                                         rhs=xin[:, c, 0:tw], start=(c == 0), stop=(c == KC - 1)),
                                [w, xin], [ps])
                    o = self.rtile("q_o", 4, [128, 512], BF16)
                    self.op("act", lambda e: e.activation(out=o[:, 0:tw], in_=ps[:, 0:tw], func=AF.Copy,
                                                          scale=scale), [ps], [o])
                    self.store(dst[:, n, t0:t0 + tw], self.dr((dname, "all")), o, o[:, 0:tw])
            if do_kv:
                for bi in range(nblk):
                    for hf in range(2):
                        ps = self.rtile("q_ps", 4, [128, 512], F32, psum=True)
                        for c in range(KC):
                            self.op("pe", lambda e: e.matmul(ps[:, :], lhsT=kn[:, c, bi * 128:(bi + 1) * 128],
                                                             rhs=wv[:, c, hf * 512:(hf + 1) * 512],
                                                             start=(c == 0), stop=(c == KC - 1)), [wv, kn], [ps])
                        o = self.rtile("q_o", 4, [128, 512], BF16)
                        self.op("dve", lambda e: e.tensor_copy(out=o[:, :], in_=ps[:, :]), [ps], [o])
                        r0 = (b0 + bi) * 128
                        self.store(vd[r0:r0 + 128, hf * 512:(hf + 1) * 512], self.dr(("VV", "all")), o, o[:, :])
        self.phase_end()

    def att_phase(self):
        self.phase_begin()
        nb, LP = self.nb, self.LP
        msk = self.tile("amask", [128, 6, 512], BF16)
        self.load(msk, msk[:], self.dram["amask"][:, :, :], self.dr("amask"), eng="pool")
        ntri = self.tile("ntri", [128, 128], BF16)
        self.load(ntri, ntri[:], self.dram["ntri"][:, :], self.dr("ntri"), eng="pool")
        ones = self.tile("ones1", [128, 128], BF16)
        self.op("pool", lambda e: e.memset(ones[:], 1.0), [], [ones])
        qT = self.dram["QT"]
        kT = self.dram["KT"]
        vd = self.dram["VV"].rearrange("(b s) n -> s b n", s=128)
        aT = self.dram["AT"]
        groups = self.blk_groups()
        for c in range(KC):
            kt = self.rtile("a_k", 2, [128, LP], BF16)
            qt = self.rtile("a_q", 2, [128, LP], BF16)
            vt = self.rtile("a_v", 2, [128, nb, 128], BF16)
            at = self.rtile("a_o", 2, [128, LP], BF16)
            self.load(kt, kt[:], kT[c], self.dr(("KT", "all")))
            self.load(qt, qt[:], qT[c], self.dr(("QT", "all")))
            self.load(vt, vt[:], vd[:, :, c * 128:(c + 1) * 128], self.dr(("VV", "all")))
            for hh in range(2):
                pb = 64 * hh
                for (b0, ng) in groups:
                    W = ng * 128
                    q0 = b0 * 128
                    outp = self.rtile("a_out", 2, [128, 512], F32, psum=True)
                    carry = None
                    hi = b0 + ng - 1
                    for kb in range(hi, -1, -1):
                        z = self.rtile("a_z", 2, [128, 512], F32, psum=True)
                        self.op("pe", lambda e: e.matmul(z[:, 0:W], lhsT=kt[pb:pb + 64, kb * 128:(kb + 1) * 128],
                                                         rhs=qt[pb:pb + 64, q0:q0 + W], start=True, stop=False),
                                [kt, qt], [z])
                        ex = self.rtile("a_e", 2, [128, 512], F32)
                        self.op("act", lambda e: e.activation(out=ex[:, 0:W], in_=z[:, 0:W], func=AF.Exp), [z], [ex])
                        sp = self.rtile("a_sp", 2, [128, 512], BF16)
                        self.op("act", lambda e: e.activation(out=sp[:, 0:W], in_=ex[:, 0:W], func=AF.Ln, bias=1.0,
                                                              scale=1.0), [ex], [sp])
                        mi = None
                        if kb >= b0:
                            mi = kb - b0
                            if kb == 0:
                                mi = 4
                        elif kb == 0:
                            mi = 5
                        if mi is not None:
                            self.op("pool", lambda e: e.tensor_tensor(out=sp[:, 0:W], in0=sp[:, 0:W],
                                                                      in1=msk[:, mi, 0:W], op=ALU.mult), [sp, msk], [sp])
                        self.op("pe", lambda e: e.matmul(z[:, 0:W], lhsT=ntri[:], rhs=sp[:, 0:W], start=False,
                                                         stop=True), [ntri, sp], [z])
                        cs = self.rtile("a_cs", 2, [128, 512], F32, psum=True)
                        if kb > 0:
                            self.op("pe", lambda e: e.matmul(cs[:, 0:W], lhsT=ones[:], rhs=sp[:, 0:W], start=True,
                                                             stop=True), [ones, sp], [cs])
                        w = self.rtile("a_w", 2, [128, 512], BF16)
                        if carry is None:
                            self.op("act", lambda e: e.activation(out=w[:, 0:W], in_=z[:, 0:W], func=AF.Exp), [z], [w])
                        else:
                            arg = self.rtile("a_arg", 2, [128, 512], F32)
                            self.op("dve", lambda e: e.tensor_tensor(out=arg[:, 0:W], in0=z[:, 0:W],
                                                                     in1=carry[:, 0:W], op=ALU.subtract),
                                    [z, carry], [arg])
                            self.op("act", lambda e: e.activation(out=w[:, 0:W], in_=arg[:, 0:W], func=AF.Exp),
                                    [arg], [w])
                        if mi is not None:
                            self.op("pool", lambda e: e.tensor_tensor(out=w[:, 0:W], in0=w[:, 0:W],
                                                                      in1=msk[:, mi, 0:W], op=ALU.mult), [w, msk], [w])
                        self.op("pe", lambda e: e.matmul(outp[pb:pb + 64, 0:W], lhsT=vt[:, kb, pb:pb + 64],
                                                         rhs=w[:, 0:W], start=(kb == hi), stop=(kb == 0)),
                                [vt, w], [outp])
                        if kb > 0:
                            nc_ = self.rtile("a_carry", 3, [128, 512], F32)
                            if carry is None:
                                self.op("dve", lambda e: e.tensor_copy(out=nc_[:, 0:W], in_=cs[:, 0:W]), [cs], [nc_])
                            else:
                                self.op("dve", lambda e: e.tensor_tensor(out=nc_[:, 0:W], in0=cs[:, 0:W],
                                                                         in1=carry[:, 0:W], op=ALU.add),
                                        [cs, carry], [nc_])
                            carry = nc_
                    self.op("dve", lambda e: e.tensor_copy(out=at[pb:pb + 64, q0:q0 + W], in_=outp[pb:pb + 64, 0:W]),
                            [outp], [at])
            self.store(aT[c], self.dr(("AT", "all")), at, at[:])
        self.phase_end()

    def o_phase(self, j, src, dst):
        self.phase_begin()
        wo = self.load_w1024("wo", self.dram["sb_wo"][j])
        hsrc = self.dram[src].rearrange("c p t -> p c t")
        hdst = self.dram[dst].rearrange("c p t -> p c t")
        aT = self.dram["AT"].rearrange("c p t -> p c t")
        for (b0, nblk) in self.blk_groups():
            t0, tw = b0 * 128, nblk * 128
            v0 = PAD if b0 == 0 else 0
            h = self.rtile("o_h", 2, [128, KC, 512], F32)
            self.load(h, h[:, :, 0:tw], hsrc[:, :, t0:t0 + tw], self.dr((src, "all")))
            a = self.rtile("o_a", 2, [128, KC, 512], BF16)
            self.load(a, a[:, :, 0:tw], aT[:, :, t0:t0 + tw], self.dr(("AT", "all")))
            for n in range(KC):
                ps = self.rtile("o_ps", 4, [128, 512], F32, psum=True)
                for c in range(KC):
                    self.op("pe", lambda e: e.matmul(ps[:, 0:tw], lhsT=wo[:, c, n * 128:(n + 1) * 128],
                                                     rhs=a[:, c, 0:tw], start=(c == 0), stop=(c == KC - 1)),
                            [wo, a], [ps])
                ho = self.rtile("o_ho", 3, [128, 512], F32)
                self.op("dve", lambda e: e.tensor_tensor(out=ho[:, 0:tw], in0=ps[:, 0:tw], in1=h[:, n, 0:tw],
                                                         op=ALU.add), [ps, h], [ho])
                self.store(hdst[:, n, t0 + v0:t0 + tw], self.dr((dst, "all")), ho, ho[:, v0:tw])
        self.phase_end()


    def pool_init(self, n):
        self.tpool = [self.tile("tp%d" % i, [128, D], F32) for i in range(n)]

    def tget(self):
        return self.tpool.pop(0)

    def tfree(self, *ts):
        for t in ts:
            self.tpool.append(t)

    @staticmethod
    def fm(t, lo=0, hi=128):
        return t.t[:].rearrange("p (c t) -> p c t", c=KC)[:, :, lo:hi]

    def bc(self, nm, idx0):
        o = self.coff[nm] + idx0
        return self.tiles["consts"][:, o:o + KC].unsqueeze(2).broadcast_to([128, KC, 128])

    def rwkv_phase(self, i, src, dst):
        self.phase_begin()
        self.rn_w = 128
        nb, LP = self.nb, self.LP
        layer = i
        cst = self.tiles["consts"]
        dr = self.dram
        wr = self.load_w1024("wr", dr["rw_wr"][i])
        wk = self.load_w1024("wk", dr["rw_wk"][i])
        wv = self.load_w1024("wv", dr["rw_wv"][i])
        wo = self.load_w1024("wo", dr["rw_wo"][i])

        def ldw(name, shape, ap):
            t = self.tile(name, shape, BF16)
            self.load(t, t[:], ap, eng="pool")
            return t
        w1 = ldw("w1", [128, KC, 64], dr["rw_w1"][i])
        a1 = ldw("a1", [128, KC, 64], dr["rw_a1"][i])
        g1 = ldw("g1", [128, KC, 160], dr["rw_g1"][i])
        w2 = ldw("w2", [64, D], dr["rw_w2"][i])
        a2 = ldw("a2", [64, D], dr["rw_a2"][i])
        g2a = ldw("g2a", [128, D], dr["rw_g2"][i, 0:128, :])
        g2b = ldw("g2b", [32, D], dr["rw_g2"][i, 128:160, :])
        if i > 0:
            v1 = ldw("v1", [128, KC, 32], dr["rw_v1"][i - 1])
            v2 = ldw("v2", [32, D], dr["rw_v2"][i - 1])
        rc = self.tile("rwc", [128, 128 * 5 + 2 + 256], F32)
        self.load(rc, rc[:], dr["rw_const"][:, :])
        ident = rc[:, 0:128]
        bd = rc[:, 128:256]
        mstrict = rc[:, 256:384]
        mt2 = rc[:, 384:640]
        ind2 = rc[:, 640:642]
        onesf = rc[:, 642:770]
        zcol = rc[:, 770:771]
        tmb = self.tile("tmb", [128, 3 if i > 0 else 2, D], F32)
        self.load(tmb, tmb[:, 0, :], dr["rw_lnx_g"][i:i + 1, :].partition_broadcast(128))
        self.load(tmb, tmb[:, 1, :], dr["rw_lnx_b"][i:i + 1, :].partition_broadcast(128))
        if i > 0:
            self.load(tmb, tmb[:, 2, :], dr["rw_v0"][i - 1:i, :].partition_broadcast(128))
        self.pool_init(10)
        st2 = [self.tile("st2_%d" % c, [128, 128], F32) for c in range(KC)]
        for c in range(KC):
            self.op("pool", lambda e: e.memset(st2[c][:], 0.0), [], [st2[c]])
        hsrc = dr[src].rearrange("c p t -> p c t")
        hdst = dr[dst].rearrange("c p t -> p c t")
        vf = dr["VF"]
        hn_prev = None
        ART = self.tile("AR", [128, KC, 256], F32)
        btT = self.tile("bt", [128, KC, 128], F32)
        ktT = self.tile("kt", [128, KC, 128], F32)
        pcT = self.tile("pc", [128, KC], F32)
        ssb = self.tile("ssb", [128, NH], F32)
        mu = self.tile("mu", [128, NH], F32)
        var = self.tile("var", [128, NH], F32)
        midw = self.tile("midw", [64, 128], BF16)
        mida = self.tile("mida", [64, 128], BF16)
        midga = self.tile("midga", [128, 128], BF16)
        midgb = self.tile("midgb", [32, 128], BF16)
        midv = self.tile("midv", [32, 128], BF16)
        ygT = self.tile("yg", [128, KC, 128], BF16)
        C0 = 0.6065306597126334

        def pbig():
            return self.rtile("pbig", 2, [128, D], F32, psum=True)

        def psm():
            return self.rtile("psm", 3, [128, 512], F32, psum=True)

        for cb in range(nb):
            t0 = cb * 128
            h = self.rtile("r_h", 2, [128, KC, 128], F32)
            self.load(h, h[:], hsrc[:, :, t0:t0 + 128])
            hn = self.rtile("r_hn", 1, [128, KC, 129], F32)
            if hn_prev is None:
                self.op("pool", lambda e: e.memset(hn[:, :, 0:1], 0.0), [], [hn])
            else:
                self.op("pool", lambda e: e.tensor_copy(out=hn[:, :, 0:1], in_=hn[:, :, 128:129]), [hn], [hn])
            self.rmsnorm(h, 128, lambda c: self.cc("norm_mix_g", layer * KC + c), hn, out_off=1)
            hn_prev = hn
            xx = self.tget()
            self.op("dve", lambda e: e.tensor_tensor(out=self.fm(xx), in0=hn[:, :, 0:128], in1=hn[:, :, 1:129],
                                                     op=ALU.subtract), [hn], [xx])
            def mkx(q):
                x = self.rtile("xq", 3, [128, KC, 128], BF16)
                tm = self.tget()
                eng = "dve" if q % 2 == 0 else "pool"
                self.op(eng, lambda e: e.tensor_tensor(out=self.fm(tm), in0=self.fm(xx),
                                                       in1=self.bc("rw_mix", (i * 6 + q) * KC), op=ALU.mult),
                        [xx, cst], [tm])
                self.op(eng, lambda e: e.tensor_tensor(out=x[:], in0=self.fm(tm), in1=hn[:, :, 1:129],
                                                       op=ALU.add), [tm, hn], [x])
                self.tfree(tm)
                return x
            rT = self.tget()
            kT_ = self.tget()
            vS = self.tget()
            for (w, qi, dstt) in ((wr, 0, rT), (wk, 2, kT_)):
                x = mkx(qi)
                ps = pbig()
                for n in range(KC):
                    for c in range(KC):
                        self.op("pe", lambda e: e.matmul(ps[:, n * 128:(n + 1) * 128],
                                                         lhsT=w[:, c, n * 128:(n + 1) * 128], rhs=x[:, c, :],
                                                         start=(c == 0), stop=(c == KC - 1)), [w, x], [ps])
                self.op("act", lambda e: e.copy(out=dstt[:], in_=ps[:]), [ps], [dstt])
            xv = mkx(3)
            ps = pbig()
            for hf in range(2):
                for c in range(KC):
                    self.op("pe", lambda e: e.matmul(ps[:, hf * 512:(hf + 1) * 512], lhsT=xv[:, c, :],
                                                     rhs=wv[:, c, hf * 512:(hf + 1) * 512],
                                                     start=(c == 0), stop=(c == KC - 1)), [wv, xv], [ps])
            self.op("act", lambda e: e.copy(out=vS[:], in_=ps[:]), [ps], [vS])
            for (wt, qi, ncol, mid, fn) in (((v1, 3, 32, midv, AF.Copy),) if i > 0 else ()) + \
                    ((w1, 1, 64, midw, AF.Tanh), (a1, 4, 64, mida, AF.Copy),
                     (g1, 5, 128, midga, AF.Sigmoid), (g1, 5, 32, midgb, AF.Sigmoid)):
                x = xv if qi == 3 else (x if (mid is midgb) else mkx(qi))
                ps = psm()
                c0 = 128 if mid is midgb else 0
                for c in range(KC):
                    self.op("pe", lambda e: e.matmul(ps[0:ncol, 0:128], lhsT=wt[:, c, c0:c0 + ncol], rhs=x[:, c, :],
                                                     start=(c == 0), stop=(c == KC - 1)), [wt, x], [ps])
                self.op("act", lambda e: e.activation(out=mid[0:ncol, :], in_=ps[0:ncol, 0:128], func=fn), [ps], [mid])
            self.tfree(xx)
            sig = self.tget()
            ps = pbig()
            for n in range(KC):
                self.op("pe", lambda e: e.matmul(ps[:, n * 128:(n + 1) * 128], lhsT=w2[0:64, n * 128:(n + 1) * 128],
                                                 rhs=midw[0:64, :], start=True, stop=True), [w2, midw], [ps])
            self.op("dve", lambda e: e.tensor_tensor(out=self.fm(sig), in0=ps[:].rearrange("p (c t) -> p c t", c=KC),
                                                     in1=self.bc("rw_w0", i * KC), op=ALU.add), [ps, cst], [sig])
            self.op("act", lambda e: e.activation(out=sig[:], in_=sig[:], func=AF.Sigmoid), [sig], [sig])
            aT_ = self.tget()
            ps = pbig()
            for n in range(KC):
                self.op("pe", lambda e: e.matmul(ps[:, n * 128:(n + 1) * 128], lhsT=a2[0:64, n * 128:(n + 1) * 128],
                                                 rhs=mida[0:64, :], start=True, stop=True), [a2, mida], [ps])
            self.op("dve", lambda e: e.tensor_tensor(out=self.fm(aT_), in0=ps[:].rearrange("p (c t) -> p c t", c=KC),
                                                     in1=self.bc("rw_a0", i * KC), op=ALU.add), [ps, cst], [aT_])
            self.op("act", lambda e: e.activation(out=aT_[:], in_=aT_[:], func=AF.Sigmoid), [aT_], [aT_])
            if i > 0:
                vg = self.tget()
                vfT = self.tget()
                self.load(vfT, vfT[:], vf[t0:t0 + 128, :])
                ps = pbig()
                for hf in range(2):
                    self.op("pe", lambda e: e.matmul(ps[:, hf * 512:(hf + 1) * 512], lhsT=midv[0:32, :],
                                                     rhs=v2[0:32, hf * 512:(hf + 1) * 512], start=True, stop=True),
                            [v2, midv], [ps])
                self.op("dve", lambda e: e.tensor_tensor(out=vg[:], in0=ps[:], in1=tmb[:, 2, :], op=ALU.add),
                        [ps, tmb], [vg])
                self.op("act", lambda e: e.activation(out=vg[:], in_=vg[:], func=AF.Sigmoid), [vg], [vg])
                self.op("pool", lambda e: e.tensor_tensor(out=vfT[:], in0=vfT[:], in1=vS[:], op=ALU.subtract),
                        [vfT, vS], [vfT])
                self.op("dve", lambda e: e.tensor_tensor(out=vfT[:], in0=vfT[:], in1=vg[:], op=ALU.mult),
                        [vfT, vg], [vfT])
                self.op("dve", lambda e: e.tensor_tensor(out=vS[:], in0=vS[:], in1=vfT[:], op=ALU.add),
                        [vS, vfT], [vS])
                self.tfree(vg, vfT)
            else:
                self.store(vf[t0:t0 + 128, :], None, vS, vS[:])
            kk = self.tget()
            self.op("dve", lambda e: e.tensor_tensor(out=self.fm(kk), in0=self.fm(kT_), in1=self.bc("rw_kk", i * KC),
                                                     op=ALU.mult), [kT_, cst], [kk])
            ksq = self.tget()
            self.op("pool", lambda e: e.tensor_tensor(out=ksq[:], in0=kk[:], in1=kk[:], op=ALU.mult), [kk], [ksq])
            ps = pbig()
            for c in range(KC):
                self.op("pe", lambda e: e.matmul(ps[:, c * 128:(c + 1) * 128], lhsT=bd, rhs=ksq[:, c * 128:(c + 1) * 128],
                                                 start=True, stop=True), [rc, ksq], [ps])
            self.op("act", lambda e: e.activation(out=ksq[:], in_=ps[:], func=AF.Sqrt), [ps], [ksq])
            self.op("dve", lambda e: e.tensor_scalar(out=ksq[:], in0=ksq[:], scalar1=1e-12, scalar2=None, op0=ALU.max),
                    [ksq], [ksq])
            self.op("dve", lambda e: e.reciprocal(out=ksq[:], in_=ksq[:]), [ksq], [ksq])
            self.op("dve", lambda e: e.tensor_tensor(out=kk[:], in0=kk[:], in1=ksq[:], op=ALU.mult), [kk, ksq], [kk])
            self.tfree(ksq)
            km = self.tget()
            self.op("dve", lambda e: e.scalar_tensor_tensor(out=self.fm(km), in0=self.fm(aT_), scalar=-1.0,
                                                            in1=self.bc("rw_ka", i * KC), op0=ALU.add, op1=ALU.mult),
                    [aT_, cst], [km])
            self.op("dve", lambda e: e.scalar_tensor_tensor(out=km[:], in0=km[:], scalar=1.0, in1=kT_[:],
                                                            op0=ALU.add, op1=ALU.mult), [km, kT_], [km])
            self.tfree(kT_)
            cs = self.tget()
            for c in range(KC):
                self.op("dve", lambda e: e.tensor_tensor_scan(out=cs[:, c * 128:(c + 1) * 128], data0=onesf,
                                                              data1=sig[:, c * 128:(c + 1) * 128], initial=zcol,
                                                              op0=ALU.mult, op1=ALU.add), [sig, rc], [cs])
            ein = self.tget()
            eneg = self.tget()
            self.op("act", lambda e: e.activation(out=ein[:], in_=cs[:], func=AF.Exp, scale=-C0), [cs], [ein])
            self.op("act", lambda e: e.activation(out=eneg[:], in_=cs[:], func=AF.Exp, scale=C0), [cs], [eneg])
            self.op("pool", lambda e: e.tensor_tensor(out=cs[:], in0=cs[:], in1=sig[:], op=ALU.subtract), [cs, sig], [cs])
            self.op("act", lambda e: e.activation(out=cs[:], in_=cs[:], func=AF.Exp, scale=-C0), [cs], [cs])
            self.tfree(sig)
            self.op("dve", lambda e: e.scalar_tensor_tensor(out=ART[:, :, 0:128], in0=self.fm(kk), scalar=-1.0,
                                                            in1=self.fm(cs), op0=ALU.mult, op1=ALU.mult),
                    [kk, cs], [ART])
            self.op("pool", lambda e: e.tensor_tensor(out=ART[:, :, 128:256], in0=self.fm(rT), in1=self.fm(ein),
                                                      op=ALU.mult), [rT, ein], [ART])
            self.tfree(cs)
            self.op("dve", lambda e: e.tensor_tensor(out=btT[:], in0=self.fm(kk), in1=self.fm(aT_), op=ALU.mult),
                    [kk, aT_], [btT])
            self.op("dve", lambda e: e.tensor_tensor(out=btT[:], in0=btT[:], in1=self.fm(eneg), op=ALU.mult),
                    [btT, eneg], [btT])
            self.tfree(aT_, kk)
            self.op("pool", lambda e: e.tensor_tensor(out=ktT[:], in0=self.fm(km), in1=self.fm(eneg), op=ALU.mult),
                    [km, eneg], [ktT])
            self.tfree(eneg)
            self.op("act", lambda e: e.copy(out=pcT[:, :].unsqueeze(2), in_=self.fm(ein, 127, 128)), [ein], [pcT])
            self.tfree(ein)
            self.op("dve", lambda e: e.tensor_tensor(out=rT[:], in0=rT[:], in1=km[:], op=ALU.mult), [rT, km], [rT])
            self.op("dve", lambda e: e.tensor_tensor(out=self.fm(rT), in0=self.fm(rT), in1=self.bc("rw_rk", i * KC),
                                                     op=ALU.mult), [rT, cst], [rT])
            ps = psm()
            for c in range(KC):
                self.op("pe", lambda e: e.matmul(ps[:, 2 * c:2 * c + 2], lhsT=rT[:, c * 128:(c + 1) * 128], rhs=ind2,
                                                 start=True, stop=True), [rT, rc], [ps])
            self.op("act", lambda e: e.copy(out=ssb[:], in_=ps[:, 0:NH]), [ps], [ssb])
            self.tfree(rT, km)
            pcb = pcT[:, :].unsqueeze(2).broadcast_to([128, KC, 128])
            KH = self.tget()
            BH = self.tget()
            for (srcT, dstT) in ((ktT, KH), (btT, BH)):
                tmp = self.tget()
                self.op("dve", lambda e: e.tensor_tensor(out=self.fm(tmp), in0=srcT[:], in1=pcb, op=ALU.mult),
                        [srcT, pcT], [tmp])
                ps = pbig()
                for c in range(KC):
                    self.op("pe", lambda e: e.transpose(ps[:, c * 128:(c + 1) * 128], tmp[:, c * 128:(c + 1) * 128],
                                                        ident), [tmp, rc], [ps])
                self.op("act", lambda e: e.copy(out=dstT[:], in_=ps[:]), [ps], [dstT])
                self.tfree(tmp)
            yT = self.tget()
            for c in range(KC):
                sk = self.rtile("sk", 2, [128, 2, 256], F32)
                sbm = self.rtile("sbm", 2, [128, 2, 256], F32)
                sx = self.rtile("sx", 3, [128, 2, 192], F32)
                p1 = psm()
                p2 = psm()
                p3 = psm()
                for hh in range(2):
                    pb = 64 * hh
                    self.op("pe", lambda e: e.matmul(p1[:, hh * 256:(hh + 1) * 256], lhsT=ktT[pb:pb + 64, c, :],
                                                     rhs=ART[pb:pb + 64, c, :], start=True, stop=True), [ktT, ART], [p1])
                    self.op("pe", lambda e: e.matmul(p2[:, hh * 256:(hh + 1) * 256], lhsT=btT[pb:pb + 64, c, :],
                                                     rhs=ART[pb:pb + 64, c, :], start=True, stop=True), [btT, ART], [p2])
                    self.op("pe", lambda e: e.matmul(p3[:, hh * 128:(hh + 1) * 128], lhsT=ART[pb:pb + 64, c, 0:128],
                                                     rhs=btT[pb:pb + 64, c, :], start=True, stop=True), [btT, ART], [p3])
                mt2b = mt2.unsqueeze(1).broadcast_to([128, 2, 256])
                msb = mstrict.unsqueeze(1).broadcast_to([128, 2, 128])
                self.op("dve", lambda e: e.tensor_tensor(out=sk[:], in0=p1[:].rearrange("p (h x) -> p h x", h=2),
                                                         in1=mt2b, op=ALU.mult), [p1, rc], [sk])
                self.op("dve", lambda e: e.tensor_tensor(out=sbm[:], in0=p2[:].rearrange("p (h x) -> p h x", h=2),
                                                         in1=mt2b, op=ALU.mult), [p2, rc], [sbm])
                self.op("dve", lambda e: e.tensor_tensor(out=sx[:, :, 0:128],
                                                         in0=p3[:, 0:256].rearrange("p (h x) -> p h x", h=2),
                                                         in1=msb, op=ALU.mult), [p3, rc], [sx])
                px = psm()
                self.op("pe", lambda e: e.matmul(px[:, 0:128], lhsT=ART[:, c, 0:128], rhs=st2[c][:], start=True,
                                                 stop=False), [ART, st2[c]], [px])
                for hh in range(2):
                    hd = 2 * c + hh
                    self.op("pe", lambda e: e.matmul(px[:, hh * 64:(hh + 1) * 64], lhsT=sk[:, hh, 0:128],
                                                     rhs=vS[:, hd * 64:(hd + 1) * 64], start=False, stop=(hh == 1)),
                            [sk, vS], [px])
                self.op("act", lambda e: e.copy(out=sx[:, :, 128:192],
                                                in_=px[:, 0:128].rearrange("p (h x) -> p h x", h=2)), [px], [sx])
                tcur = None
                for k in range(7):
                    pd = psm()
                    for hh in range(2):
                        tk = sbm[:, hh, 0:128] if tcur is None else tcur[:, hh, :]
                        ncol = 192 if k < 6 else 64
                        c0 = 0 if k < 6 else 128
                        self.op("pe", lambda e: e.matmul(pd[:, hh * 192:hh * 192 + ncol], lhsT=tk,
                                                         rhs=sx[:, hh, c0:c0 + ncol], start=True, stop=True),
                                [sbm if tcur is None else tcur, sx], [pd])
                    if k < 6:
                        pt = psm()
                        for hh in range(2):
                            tk = sbm[:, hh, 0:128] if tcur is None else tcur[:, hh, :]
                            self.op("pe", lambda e: e.matmul(pt[:, hh * 128:(hh + 1) * 128], lhsT=sx[:, hh, 0:128],
                                                             rhs=tk, start=True, stop=True),
                                    [sbm if tcur is None else tcur, sx], [pt])
                        sxn = self.rtile("sx", 3, [128, 2, 192], F32)
                        pdv = pd[:, 0:384].rearrange("p (h x) -> p h x", h=2)
                        self.op("act", lambda e: e.copy(out=sxn[:, :, 0:128], in_=pdv[:, :, 0:128]), [pd], [sxn])
                        self.op("dve", lambda e: e.tensor_tensor(out=sxn[:, :, 128:192], in0=pdv[:, :, 128:192],
                                                                 in1=sx[:, :, 128:192], op=ALU.add), [pd, sx], [sxn])
                        tn = self.rtile("tt", 2, [128, 2, 128], F32)
                        self.op("act", lambda e: e.copy(out=tn[:], in_=pt[:, 0:256].rearrange("p (h x) -> p h x", h=2)),
                                [pt], [tn])
                        sx = sxn
                        tcur = tn
                    else:
                        sxn = self.rtile("sx", 3, [128, 2, 192], F32)
                        pdv = pd[:, 0:384].rearrange("p (h x) -> p h x", h=2)
                        self.op("dve", lambda e: e.tensor_tensor(out=sxn[:, :, 128:192], in0=pdv[:, :, 0:64],
                                                                 in1=sx[:, :, 128:192], op=ALU.add), [pd, sx], [sxn])
                        sx = sxn
                py = psm()
                self.op("pe", lambda e: e.matmul(py[:, 0:128], lhsT=ART[:, c, 128:256], rhs=st2[c][:], start=True,
                                                 stop=False), [ART, st2[c]], [py])
                for hh in range(2):
                    hd = 2 * c + hh
                    self.op("pe", lambda e: e.matmul(py[:, hh * 64:(hh + 1) * 64], lhsT=sbm[:, hh, 128:256],
                                                     rhs=sx[:, hh, 128:192], start=False, stop=False), [sbm, sx], [py])
                    self.op("pe", lambda e: e.matmul(py[:, hh * 64:(hh + 1) * 64], lhsT=sk[:, hh, 128:256],
                                                     rhs=vS[:, hd * 64:(hd + 1) * 64], start=False, stop=(hh == 1)),
                            [sk, vS], [py])
                self.op("act", lambda e: e.copy(out=yT[:, c * 128:(c + 1) * 128], in_=py[:, 0:128]), [py], [yT])
                pst = psm()
                for hh in range(2):
                    hd = 2 * c + hh
                    pb = 64 * hh
                    self.op("pe", lambda e: e.matmul(pst[pb:pb + 64, hh * 64:(hh + 1) * 64],
                                                     lhsT=BH[:, hd * 64:(hd + 1) * 64], rhs=sx[:, hh, 128:192],
                                                     start=True, stop=False), [BH, sx], [pst])
                    self.op("pe", lambda e: e.matmul(pst[pb:pb + 64, hh * 64:(hh + 1) * 64],
                                                     lhsT=KH[:, hd * 64:(hd + 1) * 64], rhs=vS[:, hd * 64:(hd + 1) * 64],
                                                     start=False, stop=True), [KH, vS], [pst])
                for hh in range(2):
                    pb = 64 * hh
                    self.op("dve", lambda e: e.scalar_tensor_tensor(
                        out=st2[c][pb:pb + 64, hh * 64:(hh + 1) * 64], in0=st2[c][pb:pb + 64, hh * 64:(hh + 1) * 64],
                        scalar=pcT[pb:pb + 64, c:c + 1], in1=pst[pb:pb + 64, hh * 64:(hh + 1) * 64],
                        op0=ALU.mult, op1=ALU.add), [st2[c], pcT, pst], [st2[c]])
            self.tfree(KH, BH)
            y3 = yT.t[:].rearrange("p (h x) -> p h x", h=NH)
            self.op("dve", lambda e: e.tensor_reduce(out=mu[:], in_=y3, axis=AX.X, op=ALU.add), [yT], [mu])
            mub = mu[:, :].unsqueeze(2).broadcast_to([128, NH, HD])
            self.op("dve", lambda e: e.scalar_tensor_tensor(out=y3, in0=mub, scalar=-1.0 / HD, in1=y3, op0=ALU.mult,
                                                            op1=ALU.add), [yT, mu], [yT])
            sq = self.tget()
            self.op("pool", lambda e: e.tensor_tensor(out=sq[:], in0=yT[:], in1=yT[:], op=ALU.mult), [yT], [sq])
            self.op("dve", lambda e: e.tensor_reduce(out=var[:], in_=sq.t[:].rearrange("p (h x) -> p h x", h=NH),
                                                     axis=AX.X, op=ALU.add), [sq], [var])
            self.op("act", lambda e: e.activation(out=var[:], in_=var[:], func=AF.Sqrt, bias=GN_EPS, scale=1.0 / HD),
                    [var], [var])
            self.op("dve", lambda e: e.reciprocal(out=var[:], in_=var[:]), [var], [var])
            varb = var[:, :].unsqueeze(2).broadcast_to([128, NH, HD])
            self.op("dve", lambda e: e.tensor_tensor(out=y3, in0=y3, in1=varb, op=ALU.mult), [yT, var], [yT])
            self.op("pool", lambda e: e.tensor_tensor(out=yT[:], in0=yT[:], in1=tmb[:, 0, :], op=ALU.mult), [yT, tmb], [yT])
            self.op("pool", lambda e: e.tensor_tensor(out=yT[:], in0=yT[:], in1=tmb[:, 1, :], op=ALU.add), [yT, tmb], [yT])
            ssbb = ssb[:, :].unsqueeze(2).broadcast_to([128, NH, HD])
            self.op("dve", lambda e: e.tensor_tensor(out=sq.t[:].rearrange("p (h x) -> p h x", h=NH),
                                                     in0=vS.t[:].rearrange("p (h x) -> p h x", h=NH), in1=ssbb,
                                                     op=ALU.mult), [vS, ssb], [sq])
            self.op("dve", lambda e: e.tensor_tensor(out=yT[:], in0=yT[:], in1=sq[:], op=ALU.add), [yT, sq], [yT])
            self.tfree(sq, vS)
            gS = self.tget()
            ps = pbig()
            for n in range(KC):
                self.op("pe", lambda e: e.matmul(ps[:, n * 128:(n + 1) * 128], lhsT=g2a[:, n * 128:(n + 1) * 128],
                                                 rhs=midga[:, :], start=True, stop=False), [g2a, midga], [ps])
                self.op("pe", lambda e: e.matmul(ps[:, n * 128:(n + 1) * 128], lhsT=g2b[0:32, n * 128:(n + 1) * 128],
                                                 rhs=midgb[0:32, :], start=False, stop=True), [g2b, midgb], [ps])
            self.op("act", lambda e: e.copy(out=gS[:], in_=ps[:]), [ps], [gS])
            ps = pbig()
            for c in range(KC):
                self.op("pe", lambda e: e.transpose(ps[:, c * 128:(c + 1) * 128], yT[:, c * 128:(c + 1) * 128], ident),
                        [yT, rc], [ps])
            self.op("dve", lambda e: e.tensor_tensor(out=ygT[:], in0=ps[:].rearrange("p (c t) -> p c t", c=KC),
                                                     in1=self.fm(gS), op=ALU.mult), [ps, gS], [ygT])
            self.tfree(yT, gS)
            ps = pbig()
            for n in range(KC):
                for c in range(KC):
                    self.op("pe", lambda e: e.matmul(ps[:, n * 128:(n + 1) * 128], lhsT=wo[:, c, n * 128:(n + 1) * 128],
                                                     rhs=ygT[:, c, :], start=(c == 0), stop=(c == KC - 1)),
                            [wo, ygT], [ps])
            ho = self.rtile("r_ho", 1, [128, KC, 128], F32)
            self.op("dve", lambda e: e.tensor_tensor(out=ho[:], in0=ps[:].rearrange("p (c t) -> p c t", c=KC),
                                                     in1=h[:], op=ALU.add), [ps, h], [ho])
            v0 = PAD if cb == 0 else 0
            self.store(hdst[:, :, t0 + v0:t0 + 128], None, ho, ho[:, :, v0:128])
        self.rn_w = 512
        self.phase_end()

    def zero_pads(self, names):
        zt = self.tiles["zeros"]
        for nm in names:
            for c in range(KC):
                self.store(self.dram[nm][c, :, 0:PAD], None, zt, zt[:], eng="sp")

    def finish(self, out_res=None):
        self.barrier()
        return self.nc


def build_ffn_test(nb):
    p = Prog(nb)
    p.din("hT0", [KC, 128, p.LP])
    p.din("ffn_up", [DEPTH, 128, KC, F2])
    p.din("ffn_down", [DEPTH, 128, NJ, D])
    p.dout("hT1", [KC, 128, p.LP])
    p.setup_consts()
    p.zero_pads(["hT1"])
    p.ffn_phase(0, "hT0", "hT1")
    return p.finish([p.dr(("hT1", "all"))])


def attn_masks():
    s = np.arange(128)[:, None]
    col = np.arange(512)[None, :]
    q, t = col // 128, col % 128
    m = np.zeros((128, 6, 512), np.float32)
    for r in range(4):
        m[:, r, :] = ((q > r) | ((q == r) & (t > s))).astype(np.float32)
    row = (s >= PAD).astype(np.float32)
    m[:, 4, :] = m[:, 0, :] * row
    m[:, 5, :] = row * np.ones((1, 512), np.float32)
    j = np.arange(128)[:, None]
    ss = np.arange(128)[None, :]
    ntri = -(j >= ss).astype(np.float32)
    return m, ntri


def build_att_test(nb):
    p = Prog(nb)
    p.din("hT0", [KC, 128, p.LP])
    p.din("sb_wq", [2, 128, KC, D]); p.din("sb_wk", [128, KC, D]); p.din("sb_wv", [128, KC, D])
    p.din("sb_wo", [2, 128, KC, D])
    p.din("amask", [128, 6, 512]); p.din("ntri", [128, 128])
    p.dscratch("QT", [KC, 128, p.LP], BF16); p.dscratch("KT", [KC, 128, p.LP], BF16)
    p.dscratch("VV", [p.LP, D], BF16); p.dscratch("AT", [KC, 128, p.LP], BF16)
    p.dout("hT1", [KC, 128, p.LP])
    p.setup_consts()
    p.zero_pads(["hT1"])
    p.qkv_phase(0, "hT0", True)
    p.att_phase()
    p.o_phase(0, "hT0", "hT1")
    return p.finish([p.dr(("hT1", "all"))])


def rw_consts():
    p = np.arange(128)[:, None]
    q = np.arange(128)[None, :]
    ident = (p == q).astype(np.float32)
    bd = ((p // 64) == (q // 64)).astype(np.float32)
    mstrict = (p > q).astype(np.float32)
    mt_strict = (q > p).astype(np.float32)
    mt_incl = (q >= p).astype(np.float32)
    ind2 = np.concatenate([(p // 64 == 0), (p // 64 == 1)], axis=1).astype(np.float32)
    onesf = np.ones((128, 128), np.float32)
    zcol = np.zeros((128, 1), np.float32)
    pad = np.zeros((128, 127), np.float32)
    return np.ascontiguousarray(np.concatenate([ident, bd, mstrict, mt_strict, mt_incl, ind2, onesf, zcol, pad],
                                               axis=1))


def declare_rw_inputs(p):
    p.din("rw_wr", [2, 128, KC, D]); p.din("rw_wk", [2, 128, KC, D]); p.din("rw_wv", [2, 128, KC, D])
    p.din("rw_wo", [2, 128, KC, D])
    p.din("rw_w1", [2, 128, KC, 64]); p.din("rw_a1", [2, 128, KC, 64]); p.din("rw_g1", [2, 128, KC, 160])
    p.din("rw_v1", [1, 128, KC, 32])
    p.din("rw_w2", [2, 64, D]); p.din("rw_a2", [2, 64, D]); p.din("rw_g2", [2, 160, D]); p.din("rw_v2", [1, 32, D])
    p.din("rw_lnx_g", [2, D]); p.din("rw_lnx_b", [2, D]); p.din("rw_v0", [1, D])
    p.din("rw_const", [128, 128 * 5 + 2 + 256])
    p.dscratch("VF", [p.LP, D], F32)


def build_rw_test(nb, nlayers=1):
    p = Prog(nb)
    p.din("hT0", [KC, 128, p.LP])
    declare_rw_inputs(p)
    p.dout("hT1", [KC, 128, p.LP])
    p.dscratch("hTa", [KC, 128, p.LP])
    p.setup_consts()
    p.zero_pads(["hT1", "hTa"])
    if nlayers == 1:
        p.rwkv_phase(0, "hT0", "hT1")
    else:
        p.rwkv_phase(0, "hT0", "hTa")
        p.rwkv_phase(1, "hTa", "hT1")
    return p.finish()


def final_phase(p, src):
    p.phase_begin()
    hsrc = p.dram[src].rearrange("c p t -> p c t")
    yo = p.dram["yT"].rearrange("c p t -> p c t")
    t = PAD + NMETA
    while t < p.LP:
        tw = min(512, p.LP - t)
        h = p.rtile("fn_h", 2, [128, KC, 512], F32)
        p.load(h, h[:, :, 0:tw], hsrc[:, :, t:t + tw])
        o = p.rtile("fn_o", 2, [128, KC, 512], F32)
        p.rmsnorm(h, tw, lambda c: p.cc("final_g", c), o)
        p.store(yo[:, :, t - PAD - NMETA:t - PAD - NMETA + tw], None, o, o[:, :, 0:tw])
        t += tw
    p.phase_end()


def build_full(nb):
    p = Prog(nb)
    p.din("hT0", [KC, 128, p.LP])
    p.din("ffn_up", [DEPTH, 128, KC, F2])
    p.din("ffn_down", [DEPTH, 128, NJ, D])
    declare_rw_inputs(p)
    p.din("sb_wq", [2, 128, KC, D]); p.din("sb_wk", [128, KC, D]); p.din("sb_wv", [128, KC, D])
    p.din("sb_wo", [2, 128, KC, D])
    p.din("amask", [128, 6, 512]); p.din("ntri", [128, 128])
    p.dscratch("QT", [KC, 128, p.LP], BF16); p.dscratch("KT", [KC, 128, p.LP], BF16)
    p.dscratch("VV", [p.LP, D], BF16); p.dscratch("AT", [KC, 128, p.LP], BF16)
    p.dscratch("hA", [KC, 128, p.LP]); p.dscratch("hB", [KC, 128, p.LP])
    p.dout("yT", [KC, 128, p.LP - PAD - NMETA])
    p.setup_consts()
    p.zero_pads(["hA", "hB"])
    p.barrier()
    p.rwkv_phase(0, "hT0", "hA")
    p.ffn_phase(0, "hA", "hB")
    p.rwkv_phase(1, "hB", "hA")
    p.ffn_phase(1, "hA", "hB")
    p.qkv_phase(0, "hB", True)
    p.att_phase()
    p.o_phase(0, "hB", "hA")
    p.ffn_phase(2, "hA", "hB")
    p.qkv_phase(1, "hB", False)
    p.att_phase()
    p.o_phase(1, "hB", "hA")
    p.ffn_phase(3, "hA", "hB")
    final_phase(p, "hB")
    return p.finish()


def _w1024(w):
    return np.ascontiguousarray(w.reshape(KC, 128, -1).transpose(1, 0, 2))


def _wst(w):
    return np.stack([_w1024(w[i]) for i in range(w.shape[0])])


def _cols(v):
    v = np.asarray(v, np.float32).reshape(-1, KC, 128)
    return np.ascontiguousarray(v.transpose(2, 0, 1).reshape(128, -1))


def host_inputs(inp, nb):
    f = {k: np.asarray(v, np.float32) for k, v in inp.items()}
    m, ntri = attn_masks()
    cw = f["ffn_conv_w"]
    shared = {
        "ffn_up": np.ascontiguousarray(f["ffn_up"].reshape(DEPTH, KC, 128, F2).transpose(0, 2, 1, 3)),
        "ffn_down": np.ascontiguousarray(f["ffn_down"].reshape(DEPTH, NJ, 128, D).transpose(0, 2, 1, 3)),
        "norm_ffn_g": _cols(f["norm_ffn_g"]), "norm_mix_g": _cols(f["norm_mix_g"]),
        "kv_norm_g": _cols(f["kv_norm_g"]), "final_g": _cols(f["final_norm_g"]),
        "conv_w": np.ascontiguousarray(cw.reshape(DEPTH, 3, 44, 128).transpose(3, 0, 2, 1).reshape(128, -1)),
        "conv_b": np.ascontiguousarray(f["ffn_conv_b"].reshape(DEPTH, 44, 128).transpose(2, 0, 1).reshape(128, -1)),
        "rw_mix": _cols(f["rw_mix"]), "rw_w0": _cols(f["rw_w0"]), "rw_a0": _cols(f["rw_a0"]),
        "rw_kk": _cols(f["rw_kk"]), "rw_ka": _cols(f["rw_ka"]), "rw_rk": _cols(f["rw_rk"].reshape(2, D)),
        "rw_wr": _wst(f["rw_wr"]), "rw_wk": _wst(f["rw_wk"]), "rw_wv": _wst(f["rw_wv"]), "rw_wo": _wst(f["rw_wo"]),
        "rw_w1": _wst(f["rw_w1"]), "rw_a1": _wst(f["rw_a1"]), "rw_g1": _wst(f["rw_g1"]), "rw_v1": _wst(f["rw_v1"]),
        "rw_w2": f["rw_w2"], "rw_a2": f["rw_a2"], "rw_g2": f["rw_g2"], "rw_v2": f["rw_v2"],
        "rw_lnx_g": f["rw_lnx_g"], "rw_lnx_b": f["rw_lnx_b"], "rw_v0": f["rw_v0"],
        "rw_const": rw_consts(),
        "sb_wq": _wst(f["sb_wq"]), "sb_wo": _wst(f["sb_wo"]), "sb_wk": _w1024(f["sb_wk"]), "sb_wv": _w1024(f["sb_wv"]),
        "amask": m, "ntri": ntri,
    }
    return shared


def host_h0(xb, meta, nb):
    LP = nb * 128
    hT = np.zeros((D, LP), np.float32)
    hT[:, PAD:PAD + NMETA] = meta.T
    hT[:, PAD + NMETA:] = xb.T
    return hT.reshape(KC, 128, LP)


_CACHE = {}


def kernel(**inputs):
    x = np.asarray(inputs["x"], np.float32)
    B, S, _ = x.shape
    nb = (PAD + NMETA + S) // 128
    if nb not in _CACHE:
        _CACHE[nb] = build_full(nb)
    nc = _CACHE[nb]
    shared = host_inputs({k: v for k, v in inputs.items() if k != "x"}, nb)
    meta = np.asarray(inputs["meta_tokens"], np.float32)
    ncores = 8
    in_maps = []
    for cidx in range(ncores):
        b = cidx % B
        m = dict(shared)
        m["hT0"] = host_h0(x[b], meta, nb)
        in_maps.append(m)
    res = run_bass_kernel_spmd(nc, in_maps, core_ids=list(range(ncores)))
    out = np.empty((B, S, D), np.float32)
    for b in range(B):
        out[b] = res.results[b]["yT"].reshape(D, S).T
    return out
```

```python
import numpy as np
from contextlib import ExitStack
import concourse.bass as bass
import concourse.mybir as mybir
from concourse.bass_utils import run_bass_kernel_spmd

F32 = mybir.dt.float32
BF16 = mybir.dt.bfloat16
ALU = mybir.AluOpType
AF = mybir.ActivationFunctionType
AX = mybir.AxisListType

D = 1024
KC = 8
NH = 16
HD = 64
NMETA = 16
PAD = 112
DFF = 2816
F2 = 5632
NJ = 22
FT = 384
DEPTH = 4
RMS_EPS = 1e-6
GN_EPS = 64e-5


class Res:
    __slots__ = ("name", "lw", "rd")

    def __init__(self, name):
        self.name = name
        self.lw = None
        self.rd = {}


class KB:
    SEM_LIMIT = 30000

    def __init__(self, nc):
        self.nc = nc
        self.q = {e: [] for e in ("pe", "act", "dve", "pool", "sp")}
        self.sems = {}
        self.cur = {}
        self.waited = {e: {} for e in self.q}
        self.nsem = 0
        self.dmasem = {}
        self.nops = 0

    def _newsem(self, tag):
        key = "%s_%d" % (tag, self.nsem)
        self.nsem += 1
        self.sems[key] = self.nc.alloc_semaphore(name=key)
        return key

    def _eng_event(self, eng):
        c = self.cur.get(eng)
        if c is None or c[1] >= self.SEM_LIMIT:
            c = [self._newsem(eng), 0]
            self.cur[eng] = c
        c[1] += 1
        return (c[0], c[1])

    def _dma_event(self, chain):
        c = self.dmasem.get(chain)
        if c is None or c[1] >= self.SEM_LIMIT:
            free = getattr(self, "free_dma", [])
            free.sort(key=lambda x: x[1])
            if free and free[0][1] < self.SEM_LIMIT // 2:
                c = list(free.pop(0))
            else:
                c = [self._newsem("d"), 0]
            self.dmasem[chain] = c
        c[1] += 16
        return (c[0], c[1])

    def recycle_dma(self):
        free = getattr(self, "free_dma", [])
        for ch, c in self.dmasem.items():
            free.append((c[0], c[1]))
        self.free_dma = free
        self.dmasem = {}

    def _deps(self, eng, reads, writes, is_dma):
        waits = {}

        def need(sk, v, src_eng, kind):
            if (not is_dma) and eng == "pe" and src_eng == "pe":
                return
            if (not is_dma) and src_eng == eng and kind == "war":
                return
            if self.waited[eng].get(sk, 0) >= v:
                return
            if waits.get(sk, 0) < v:
                waits[sk] = v

        for r in reads:
            if r.lw is not None:
                need(r.lw[0], r.lw[1], r.lw[2], "raw")
        for w in writes:
            if w.lw is not None:
                need(w.lw[0], w.lw[1], w.lw[2], "waw")
            for sk, (v, e) in w.rd.items():
                need(sk, v, e, "war")
        return waits

    def _record(self, ev, src, reads, writes):
        for r in reads:
            r.rd[ev[0]] = (ev[1], src)
        for w in writes:
            w.lw = (ev[0], ev[1], src)
            w.rd = {}

    def op(self, eng, fn, reads=(), writes=()):
        waits = self._deps(eng, reads, writes, False)
        for sk, v in waits.items():
            self.waited[eng][sk] = v
        ev = self._eng_event(eng)
        self._emit(eng, fn, waits, ev, 1)
        self._record(ev, eng, reads, writes)
        self.nops += 1
        return ev

    def dma(self, eng, fn, reads=(), writes=(), chain=None):
        waits = self._deps(eng, reads, writes, True)
        for sk, v in waits.items():
            self.waited[eng][sk] = v
        assert chain is not None
        ev = self._dma_event(chain)
        self._emit(eng, fn, waits, ev, 16)
        self._record(ev, "dma", reads, writes)
        self.nops += 1
        return ev

    def wait_all(self, eng, resources):
        waits = {}
        for r in resources:
            if r.lw is not None:
                sk, v = r.lw[0], r.lw[1]
                if self.waited[eng].get(sk, 0) < v and waits.get(sk, 0) < v:
                    waits[sk] = v
        for sk, v in waits.items():
            self.waited[eng][sk] = v
        self._emit(eng, None, waits, None, 0)

    ENG = {"pe": "tensor", "act": "scalar", "dve": "vector", "pool": "gpsimd", "sp": "sync"}

    def _emit(self, eng, fn, waits, ev, inc):
        engine = getattr(self.nc, self.ENG[eng])
        for sk, v in waits.items():
            engine.wait_ge(self.sems[sk], v)
        if fn is not None:
            ins = fn(engine)
            ins.then_inc(self.sems[ev[0]], inc)

    def emit(self):
        return

    def emit_old(self):
        nc = self.nc
        engs = {"pe": "tensor", "act": "scalar", "dve": "vector", "pool": "gpsimd", "sp": "sync"}
        with nc.Block() as block:
            for e, attr in engs.items():
                ops = self.q[e]
                if not ops:
                    continue

                def body(engine, ops=ops):
                    for fn, waits, ev, inc in ops:
                        for sk, v in waits:
                            engine.wait_ge(self.sems[sk], v)
                        if fn is not None:
                            ins = fn(engine)
                            ins.then_inc(self.sems[ev[0]], inc)
                getattr(block, attr)(body)


class T:
    def __init__(self, nc, name, shape, dtype, psum=False, stack=None):
        self.name = name
        if psum:
            cm = nc.psum_tensor(name, list(shape), dtype)
        else:
            cm = nc.sbuf_tensor(name, list(shape), dtype)
        self.t = stack.enter_context(cm)
        self.r = Res(name)

    def __getitem__(self, idx):
        return self.t[idx]


class Prog:
    def __init__(self, nb, dbg=None):
        self.nb = nb
        self.LP = nb * 128
        self.dbg = dbg or {}
        self.nc = bass.Bass("TRN2", target_bir_lowering=False)
        self.kb = KB(self.nc)
        self.dram = {}
        self.dres = {}
        self.tiles = {}
        self.rot = {}
        self.gstack = ExitStack()
        self.pstack = None
        self.pid = 0

    def phase_begin(self):
        self.pstack = ExitStack()
        self.pid += 1
        self.ptiles = []

    def phase_end(self):
        self.barrier()
        self.kb.recycle_dma()
        self.pstack.close()
        self.pstack = None
        for nm in self.ptiles:
            self.tiles.pop(nm, None)
        self.rot = {}

    def barrier(self):
        kb = self.kb
        targets = {}
        for e, c in kb.cur.items():
            targets[c[0]] = c[1]
        for ch, c in kb.dmasem.items():
            targets[c[0]] = c[1]
        for e in ("pe", "act", "dve", "pool", "sp"):
            waits = {}
            for sk, v in targets.items():
                if kb.cur.get(e) is not None and kb.cur[e][0] == sk:
                    continue
                if kb.waited[e].get(sk, 0) < v:
                    waits[sk] = v
                    kb.waited[e][sk] = v
            kb._emit(e, None, waits, None, 0)

    def din(self, name, shape, dtype=F32):
        self.dram[name] = self.nc.dram_tensor(name, list(shape), dtype, kind="ExternalInput").ap()
        return self.dram[name]

    def dout(self, name, shape, dtype=F32):
        self.dram[name] = self.nc.dram_tensor(name, list(shape), dtype, kind="ExternalOutput").ap()
        return self.dram[name]

    def dscratch(self, name, shape, dtype=F32):
        self.dram[name] = self.nc.dram_tensor(name, list(shape), dtype, kind="Internal").ap()
        return self.dram[name]

    def dr(self, key):
        r = self.dres.get(key)
        if r is None:
            r = Res(str(key))
            self.dres[key] = r
        return r

    def tile(self, name, shape, dtype=F32, psum=False):
        if self.pstack is not None:
            t = T(self.nc, "%s_p%d" % (name, self.pid), shape, dtype, psum, self.pstack)
            self.ptiles.append(name)
        else:
            t = T(self.nc, name, shape, dtype, psum, self.gstack)
        self.tiles[name] = t
        return t

    def rtile(self, name, n, shape, dtype=F32, psum=False):
        ent = self.rot.get(name)
        if ent is None:
            ent = [[self.tile("%s%d" % (name, i), shape, dtype, psum) for i in range(n)], 0]
            self.rot[name] = ent
        t = ent[0][ent[1] % n]
        ent[1] += 1
        return t

    def op(self, eng, fn, reads=(), writes=()):
        return self.kb.op(eng, fn, [x.r if isinstance(x, T) else x for x in reads],
                          [x.r if isinstance(x, T) else x for x in writes])

    def load(self, dst, dst_ap, src_ap, src_res=None, eng="sp"):
        self.kb.dma(eng, lambda e: e.dma_start(out=dst_ap, in_=src_ap),
                    reads=[], writes=[dst.r], chain="ld_" + dst.name)

    def store(self, dst_ap, dst_res, src, src_ap, eng="sp"):
        self.kb.dma(eng, lambda e: e.dma_start(out=dst_ap, in_=src_ap),
                    reads=[src.r], writes=[], chain="st_" + src.name)

    def rmsnorm(self, h, w, g_ap, out, out_off=0, tag="n"):
        rw_ = getattr(self, "rn_w", 512)
        sq = self.rtile("rn_sq", 1, [128, KC, rw_], BF16)
        ss = self.rtile("rn_ss", 1, [128, 512], F32, psum=True)
        rstd = self.rtile("rn_rstd", 2, [128, rw_], F32)
        ones = self.tiles["onesD"]
        self.op("dve", lambda e: e.tensor_tensor(out=sq[:, :, 0:w], in0=h[:, :, 0:w], in1=h[:, :, 0:w],
                                                  op=ALU.mult), [h], [sq])
        for c in range(KC):
            self.op("pe", lambda e, c=c: e.matmul(ss[:, 0:w], lhsT=ones[:], rhs=sq[:, c, 0:w],
                                                  start=(c == 0), stop=(c == KC - 1)), [ones, sq], [ss])
        self.op("act", lambda e: e.activation(out=rstd[:, 0:w], in_=ss[:, 0:w], func=AF.Sqrt, bias=RMS_EPS,
                                              scale=1.0), [ss], [rstd])
        self.op("dve", lambda e: e.reciprocal(out=rstd[:, 0:w], in_=rstd[:, 0:w]), [rstd], [rstd])
        for c in range(KC):
            self.op("dve", lambda e, c=c: e.scalar_tensor_tensor(
                out=out[:, c, out_off:out_off + w], in0=h[:, c, 0:w], scalar=g_ap(c), in1=rstd[:, 0:w],
                op0=ALU.mult, op1=ALU.mult), [h, rstd, self.tiles["consts"]], [out])

    def setup_consts(self):
        onesD = self.tile("onesD", [128, 128], BF16)
        self.op("pool", lambda e: e.memset(onesD[:], 1.0 / D), [], [onesD])
        zt = self.tile("zeros", [128, PAD], F32)
        self.op("pool", lambda e: e.memset(zt[:], 0.0), [], [zt])
        self.din("norm_ffn_g", [128, DEPTH * KC])
        self.din("conv_w", [128, DEPTH * 44 * 3])
        self.din("conv_b", [128, DEPTH * 44])
        self.din("final_g", [128, KC])
        self.din("norm_mix_g", [128, DEPTH * KC])
        self.din("kv_norm_g", [128, KC])
        for nm, n in (("rw_mix", 2 * 6 * KC), ("rw_w0", 2 * KC), ("rw_a0", 2 * KC), ("rw_kk", 2 * KC),
                      ("rw_ka", 2 * KC), ("rw_rk", 2 * KC)):
            self.din(nm, [128, n])
        ncol = DEPTH * KC + DEPTH * 44 * 3 + DEPTH * 44 + KC + DEPTH * KC + KC + 2 * 6 * KC + 5 * 2 * KC
        consts = self.tile("consts", [128, ncol], F32)
        self.coff = {}
        off = 0
        for nm, n in (("norm_ffn_g", DEPTH * KC), ("conv_w", DEPTH * 44 * 3), ("conv_b", DEPTH * 44),
                      ("final_g", KC), ("norm_mix_g", DEPTH * KC), ("kv_norm_g", KC),
                      ("rw_mix", 2 * 6 * KC), ("rw_w0", 2 * KC), ("rw_a0", 2 * KC), ("rw_kk", 2 * KC),
                      ("rw_ka", 2 * KC), ("rw_rk", 2 * KC)):
            self.coff[nm] = off
            self.load(consts, consts[:, off:off + n], self.dram[nm][:, :], self.dr(nm))
            off += n

    def cc(self, nm, idx):
        o = self.coff[nm] + idx
        return self.tiles["consts"][:, o:o + 1]

    def ffn_tiles(self):
        tiles = []
        o = PAD
        while o < self.LP:
            ow = min(FT - 2, self.LP - o)
            tiles.append((o - 2, ow))
            o += ow
        return tiles

    def ffn_weights(self, layer):
        wup = self.tiles.get("wup") or self.tile("wup", [128, KC, F2], BF16)
        wdn = self.tiles.get("wdn") or self.tile("wdn", [128, NJ, D], BF16)
        up = self.dram["ffn_up"]
        dn = self.dram["ffn_down"]
        for c in range(KC):
            for hf in range(2):
                self.load(wup, wup[:, c, hf * DFF:(hf + 1) * DFF], up[layer, :, c, hf * DFF:(hf + 1) * DFF],
                          self.dr("ffn_up"), eng="pool")
        for j0 in range(0, NJ, 2):
            self.load(wdn, wdn[:, j0:j0 + 2, :], dn[layer, :, j0:j0 + 2, :], self.dr("ffn_down"), eng="pool")
        return wup, wdn

    def ffn_phase(self, layer, src, dst):
        self.phase_begin()
        wup, wdn = self.ffn_weights(layer)
        hsrc = self.dram[src].rearrange("c p t -> p c t")
        hdst = self.dram[dst].rearrange("c p t -> p c t")
        for ti, (i0, ow) in enumerate(self.ffn_tiles()):
            iw = ow + 2
            h = self.rtile("f_h", 2, [128, KC, FT], F32)
            self.load(h, h[:, :, 0:iw], hsrc[:, :, i0:i0 + iw], self.dr((src, "all")))
            hn = self.rtile("f_hn", 1, [128, KC, FT], BF16)
            self.rmsnorm(h, iw, lambda c: self.cc("norm_ffn_g", layer * KC + c), hn)
            m = self.rtile("f_m", 1, [128, NJ, FT], BF16)
            for j in range(NJ):
                ys = []
                for half, ch in enumerate((j, NJ + j)):
                    ps = self.rtile("f_ps", 4, [128, 512], F32, psum=True)
                    for c in range(KC):
                        self.op("pe", lambda e, c=c, ps=ps, ch=ch: e.matmul(
                            ps[:, 0:iw], lhsT=wup[:, c, ch * 128:(ch + 1) * 128], rhs=hn[:, c, 0:iw],
                            start=(c == 0), stop=(c == KC - 1)), [wup, hn], [ps])
                    y = self.rtile("f_y", 4, [128, FT], F32)
                    cw = lambda tap, ch=ch: self.cc("conv_w", (layer * 44 + ch) * 3 + tap)
                    cb = self.cc("conv_b", layer * 44 + ch)
                    cst = self.tiles["consts"]
                    self.op("act", lambda e, y=y, ps=ps, cw=cw, cb=cb: e.activation(
                        out=y[:, 0:ow], in_=ps[:, 2:2 + ow], func=AF.Identity, bias=cb, scale=cw(2)),
                        [ps, cst], [y])
                    self.op("dve", lambda e, y=y, ps=ps, cw=cw: e.scalar_tensor_tensor(
                        out=y[:, 0:ow], in0=ps[:, 1:1 + ow], scalar=cw(1), in1=y[:, 0:ow],
                        op0=ALU.mult, op1=ALU.add), [ps, y, cst], [y])
                    self.op("dve", lambda e, y=y, ps=ps, cw=cw: e.scalar_tensor_tensor(
                        out=y[:, 0:ow], in0=ps[:, 0:ow], scalar=cw(0), in1=y[:, 0:ow],
                        op0=ALU.mult, op1=ALU.add), [ps, y, cst], [y])
                    ys.append(y)
                yg, yv = ys
                self.op("act", lambda e, yg=yg: e.activation(out=yg[:, 0:ow], in_=yg[:, 0:ow], func=AF.Silu),
                        [yg], [yg])
                self.op("dve", lambda e, yg=yg, yv=yv, j=j: e.tensor_tensor(
                    out=m[:, j, 0:ow], in0=yg[:, 0:ow], in1=yv[:, 0:ow], op=ALU.mult), [yg, yv], [m])
            for n in range(KC):
                ps = self.rtile("f_ps", 4, [128, 512], F32, psum=True)
                for j in range(NJ):
                    self.op("pe", lambda e, j=j, n=n, ps=ps: e.matmul(
                        ps[:, 0:ow], lhsT=wdn[:, j, n * 128:(n + 1) * 128], rhs=m[:, j, 0:ow],
                        start=(j == 0), stop=(j == NJ - 1)), [wdn, m], [ps])
                ho = self.rtile("f_ho", 3, [128, FT], F32)
                self.op("dve", lambda e, n=n, ps=ps, ho=ho: e.tensor_tensor(
                    out=ho[:, 0:ow], in0=ps[:, 0:ow], in1=h[:, n, 2:2 + ow], op=ALU.add), [ps, h], [ho])
                self.store(hdst[:, n, i0 + 2:i0 + 2 + ow], self.dr((dst, "all")), ho, ho[:, 0:ow], eng="sp")
        self.phase_end()


    def load_w1024(self, name, dram_ap):
        w = self.tile(name, [128, KC, D], BF16)
        for c0 in range(0, KC, 2):
            self.load(w, w[:, c0:c0 + 2, :], dram_ap[:, c0:c0 + 2, :], self.dr("wts"), eng="pool")
        return w

    def blk_groups(self):
        gs = []
        b = 0
        while b < self.nb:
            n = min(4, self.nb - b)
            gs.append((b, n))
            b += n
        return gs

    def qkv_phase(self, j, src, do_kv):
        self.phase_begin()
        layer = 2 + j
        wq = self.load_w1024("wq", self.dram["sb_wq"][j])
        if do_kv:
            wk = self.load_w1024("wk", self.dram["sb_wk"])
            wv = self.load_w1024("wv", self.dram["sb_wv"])
        hsrc = self.dram[src].rearrange("c p t -> p c t")
        qT = self.dram["QT"].rearrange("c p t -> p c t")
        kT = self.dram["KT"].rearrange("c p t -> p c t")
        vd = self.dram["VV"]
        for (b0, nblk) in self.blk_groups():
            t0, tw = b0 * 128, nblk * 128
            h = self.rtile("q_h", 2, [128, KC, 512], F32)
            self.load(h, h[:, :, 0:tw], hsrc[:, :, t0:t0 + tw], self.dr((src, "all")))
            hn = self.rtile("q_hn", 2, [128, KC, 512], BF16)
            self.rmsnorm(h, tw, lambda c: self.cc("norm_mix_g", layer * KC + c), hn)
            jobs = [(wq, qT, 0.125, "QT")]
            if do_kv:
                kn = self.rtile("q_kn", 2, [128, KC, 512], BF16)
                self.rmsnorm(h, tw, lambda c: self.cc("kv_norm_g", c), kn)
                jobs.append((wk, kT, 1.0, "KT"))
            for (w, dst, scale, dname) in jobs:
                xin = hn if dname == "QT" else kn
                for n in range(KC):
                    ps = self.rtile("q_ps", 4, [128, 512], F32, psum=True)
                    for c in range(KC):
                        self.op("pe", lambda e: e.matmul(ps[:, 0:tw], lhsT=w[:, c, n * 128:(n + 1) * 128],
                                                         rhs=xin[:, c, 0:tw], start=(c == 0), stop=(c == KC - 1)),
                                [w, xin], [ps])
                    o = self.rtile("q_o", 4, [128, 512], BF16)
                    self.op("act", lambda e: e.activation(out=o[:, 0:tw], in_=ps[:, 0:tw], func=AF.Copy,
                                                          scale=scale), [ps], [o])
                    self.store(dst[:, n, t0:t0 + tw], self.dr((dname, "all")), o, o[:, 0:tw])
            if do_kv:
                for bi in range(nblk):
                    for hf in range(2):
                        ps = self.rtile("q_ps", 4, [128, 512], F32, psum=True)
                        for c in range(KC):
                            self.op("pe", lambda e: e.matmul(ps[:, :], lhsT=kn[:, c, bi * 128:(bi + 1) * 128],
                                                             rhs=wv[:, c, hf * 512:(hf + 1) * 512],
                                                             start=(c == 0), stop=(c == KC - 1)), [wv, kn], [ps])
                        o = self.rtile("q_o", 4, [128, 512], BF16)
                        self.op("dve", lambda e: e.tensor_copy(out=o[:, :], in_=ps[:, :]), [ps], [o])
                        r0 = (b0 + bi) * 128
                        self.store(vd[r0:r0 + 128, hf * 512:(hf + 1) * 512], self.dr(("VV", "all")), o, o[:, :])
        self.phase_end()

    def att_sweep(self, slot, kt, qt, vt, at, msk, ntri, ones, hh, b0, ng):
        pb = 64 * hh
        W = ng * 128
        q0 = b0 * 128
        hi = b0 + ng - 1
        outp = slot["out"]
        z = slot["z"]
        carry = None
        for kb in range(hi, -1, -1):
            self.op("pe", lambda e: e.matmul(z[:, 0:W], lhsT=kt[pb:pb + 64, kb * 128:(kb + 1) * 128],
                                             rhs=qt[pb:pb + 64, q0:q0 + W], start=True, stop=False), [kt, qt], [z])
            yield
            ex = slot["ex"][kb % 2]
            sp = slot["sp"][kb % 2]
            self.op("act", lambda e: e.activation(out=ex[:, 0:W], in_=z[:, 0:W], func=AF.Exp), [z], [ex])
            self.op("act", lambda e: e.activation(out=sp[:, 0:W], in_=ex[:, 0:W], func=AF.Ln, bias=1.0, scale=1.0),
                    [ex], [sp])
            mi = None
            if kb >= b0:
                mi = 4 if kb == 0 else kb - b0
            elif kb == 0:
                mi = 5
            if mi is not None:
                self.op("pool", lambda e: e.tensor_tensor(out=sp[:, 0:W], in0=sp[:, 0:W], in1=msk[:, mi, 0:W],
                                                          op=ALU.mult), [sp, msk], [sp])
            yield
            self.op("pe", lambda e: e.matmul(z[:, 0:W], lhsT=ntri[:], rhs=sp[:, 0:W], start=False, stop=True),
                    [ntri, sp], [z])
            cs = None
            if kb > 0:
                cs = slot["cs"]
                self.op("pe", lambda e: e.matmul(cs[:, 0:W], lhsT=ones[:], rhs=sp[:, 0:W], start=True, stop=True),
                        [ones, sp], [cs])
            yield
            w = slot["w"][kb % 2]
            if carry is None:
                self.op("act", lambda e: e.activation(out=w[:, 0:W], in_=z[:, 0:W], func=AF.Exp), [z], [w])
            else:
                arg = slot["arg"][kb % 2]
                self.op("dve", lambda e: e.tensor_tensor(out=arg[:, 0:W], in0=z[:, 0:W], in1=carry[:, 0:W],
                                                         op=ALU.subtract), [z, carry], [arg])
                yield
                self.op("act", lambda e: e.activation(out=w[:, 0:W], in_=arg[:, 0:W], func=AF.Exp), [arg], [w])
            if mi is not None:
                self.op("pool", lambda e: e.tensor_tensor(out=w[:, 0:W], in0=w[:, 0:W], in1=msk[:, mi, 0:W],
                                                          op=ALU.mult), [w, msk], [w])
            if kb > 0:
                ncar = slot["carry"][kb % 2]
                if carry is None:
                    self.op("dve", lambda e: e.tensor_copy(out=ncar[:, 0:W], in_=cs[:, 0:W]), [cs], [ncar])
                else:
                    self.op("dve", lambda e: e.tensor_tensor(out=ncar[:, 0:W], in0=cs[:, 0:W], in1=carry[:, 0:W],
                                                             op=ALU.add), [cs, carry], [ncar])
                carry = ncar
            yield
            self.op("pe", lambda e: e.matmul(outp[pb:pb + 64, 0:W], lhsT=vt[:, kb, pb:pb + 64], rhs=w[:, 0:W],
                                             start=(kb == hi), stop=(kb == 0)), [vt, w], [outp])
        yield
        self.op("dve", lambda e: e.tensor_copy(out=at[pb:pb + 64, q0:q0 + W], in_=outp[pb:pb + 64, 0:W]), [outp], [at])

    def att_phase(self, nslots=3):
        self.phase_begin()
        nb, LP = self.nb, self.LP
        msk = self.tile("amask", [128, 6, 512], BF16)
        self.load(msk, msk[:], self.dram["amask"][:, :, :], eng="pool")
        ntri = self.tile("ntri", [128, 128], BF16)
        self.load(ntri, ntri[:], self.dram["ntri"][:, :], eng="pool")
        ones = self.tile("ones1", [128, 128], BF16)
        self.op("pool", lambda e: e.memset(ones[:], 1.0), [], [ones])
        qT = self.dram["QT"]
        kT = self.dram["KT"]
        vd = self.dram["VV"].rearrange("(b s) n -> s b n", s=128)
        aT = self.dram["AT"]
        slots = []
        outA = self.tile("a_outA", [128, 512], F32, psum=True)
        outB = self.tile("a_outB", [128, 512], F32, psum=True)
        for si in range(nslots):
            slots.append({
                "out": outA if si < 2 else outB,
                "outr": None,
                "cs": self.tile("a_cs%d" % si, [128, 512], F32, psum=True),
                "z": self.tile("a_z%d" % si, [128, 512], F32, psum=True),
                "ex": [self.tile("a_e%d_%d" % (si, k), [128, 512], F32) for k in range(2)],
                "sp": [self.tile("a_sp%d_%d" % (si, k), [128, 512], BF16) for k in range(2)],
                "w": [self.tile("a_w%d_%d" % (si, k), [128, 512], BF16) for k in range(2)],
                "arg": [self.tile("a_arg%d_%d" % (si, k), [128, 512], F32) for k in range(2)],
                "carry": [self.tile("a_car%d_%d" % (si, k), [128, 512], F32) for k in range(2)],
            })
        groups = self.blk_groups()
        for c in range(KC):
            kt = self.rtile("a_k", 2, [128, LP], BF16)
            qt = self.rtile("a_q", 2, [128, LP], BF16)
            vt = self.rtile("a_v", 2, [128, nb, 128], BF16)
            at = self.rtile("a_o", 2, [128, LP], BF16)
            self.load(kt, kt[:], kT[c])
            self.load(qt, qt[:], qT[c])
            self.load(vt, vt[:], vd[:, :, c * 128:(c + 1) * 128])
            todo = [[(hh, b0, ng) for (b0, ng) in reversed(groups)] for hh in range(2)]
            active = [None] * nslots
            while todo[0] or todo[1] or any(a is not None for a in active):
                for si in range(nslots):
                    if active[si] is None and (todo[0] or todo[1]):
                        if si < 2:
                            lst = todo[si]
                        else:
                            lst = todo[0] if len(todo[0]) >= len(todo[1]) else todo[1]
                        if lst:
                            hh, b0, ng = lst.pop(0)
                            active[si] = self.att_sweep(slots[si], kt, qt, vt, at, msk, ntri, ones, hh, b0, ng)
                    if active[si] is not None:
                        try:
                            next(active[si])
                        except StopIteration:
                            active[si] = None
            self.store(aT[c], None, at, at[:])
        self.phase_end()

    def o_phase(self, j, src, dst):
        self.phase_begin()
        wo = self.load_w1024("wo", self.dram["sb_wo"][j])
        hsrc = self.dram[src].rearrange("c p t -> p c t")
        hdst = self.dram[dst].rearrange("c p t -> p c t")
        aT = self.dram["AT"].rearrange("c p t -> p c t")
        for (b0, nblk) in self.blk_groups():
            t0, tw = b0 * 128, nblk * 128
            v0 = PAD if b0 == 0 else 0
            h = self.rtile("o_h", 2, [128, KC, 512], F32)
            self.load(h, h[:, :, 0:tw], hsrc[:, :, t0:t0 + tw], self.dr((src, "all")))
            a = self.rtile("o_a", 2, [128, KC, 512], BF16)
            self.load(a, a[:, :, 0:tw], aT[:, :, t0:t0 + tw], self.dr(("AT", "all")))
            for n in range(KC):
                ps = self.rtile("o_ps", 4, [128, 512], F32, psum=True)
                for c in range(KC):
                    self.op("pe", lambda e: e.matmul(ps[:, 0:tw], lhsT=wo[:, c, n * 128:(n + 1) * 128],
                                                     rhs=a[:, c, 0:tw], start=(c == 0), stop=(c == KC - 1)),
                            [wo, a], [ps])
                ho = self.rtile("o_ho", 3, [128, 512], F32)
                self.op("dve", lambda e: e.tensor_tensor(out=ho[:, 0:tw], in0=ps[:, 0:tw], in1=h[:, n, 0:tw],
                                                         op=ALU.add), [ps, h], [ho])
                self.store(hdst[:, n, t0 + v0:t0 + tw], self.dr((dst, "all")), ho, ho[:, v0:tw])
        self.phase_end()


    def pool_init(self, n):
        self.tpool = [self.tile("tp%d" % i, [128, D], F32) for i in range(n)]

    def tget(self):
        return self.tpool.pop(0)

    def tfree(self, *ts):
        for t in ts:
            self.tpool.append(t)

    @staticmethod
    def fm(t, lo=0, hi=128):
        return t.t[:].rearrange("p (c t) -> p c t", c=KC)[:, :, lo:hi]

    def bc(self, nm, idx0):
        o = self.coff[nm] + idx0
        return self.tiles["consts"][:, o:o + KC].unsqueeze(2).broadcast_to([128, KC, 128])

    def rwkv_phase(self, i, src, dst):
        self.phase_begin()
        self.rn_w = 128
        nb, LP = self.nb, self.LP
        layer = i
        cst = self.tiles["consts"]
        dr = self.dram
        wr = self.load_w1024("wr", dr["rw_wr"][i])
        wk = self.load_w1024("wk", dr["rw_wk"][i])
        wv = self.load_w1024("wv", dr["rw_wv"][i])
        wo = self.load_w1024("wo", dr["rw_wo"][i])

        def ldw(name, shape, ap):
            t = self.tile(name, shape, BF16)
            self.load(t, t[:], ap, eng="pool")
            return t
        w1 = ldw("w1", [128, KC, 64], dr["rw_w1"][i])
        a1 = ldw("a1", [128, KC, 64], dr["rw_a1"][i])
        g1 = ldw("g1", [128, KC, 160], dr["rw_g1"][i])
        w2 = ldw("w2", [64, D], dr["rw_w2"][i])
        a2 = ldw("a2", [64, D], dr["rw_a2"][i])
        g2a = ldw("g2a", [128, D], dr["rw_g2"][i, 0:128, :])
        g2b = ldw("g2b", [32, D], dr["rw_g2"][i, 128:160, :])
        if i > 0:
            v1 = ldw("v1", [128, KC, 32], dr["rw_v1"][i - 1])
            v2 = ldw("v2", [32, D], dr["rw_v2"][i - 1])
        rc = self.tile("rwc", [128, 128 * 5 + 2 + 256], F32)
        self.load(rc, rc[:], dr["rw_const"][:, :])
        ident = rc[:, 0:128]
        bd = rc[:, 128:256]
        mstrict = rc[:, 256:384]
        mt2 = rc[:, 384:640]
        ind2 = rc[:, 640:642]
        onesf = rc[:, 642:770]
        zcol = rc[:, 770:771]
        tmb = self.tile("tmb", [128, 3 if i > 0 else 2, D], F32)
        self.load(tmb, tmb[:, 0, :], dr["rw_lnx_g"][i:i + 1, :].partition_broadcast(128))
        self.load(tmb, tmb[:, 1, :], dr["rw_lnx_b"][i:i + 1, :].partition_broadcast(128))
        if i > 0:
            self.load(tmb, tmb[:, 2, :], dr["rw_v0"][i - 1:i, :].partition_broadcast(128))
        self.pool_init(10)
        st2 = [self.tile("st2_%d" % c, [128, 128], F32) for c in range(KC)]
        for c in range(KC):
            self.op("pool", lambda e: e.memset(st2[c][:], 0.0), [], [st2[c]])
        hsrc = dr[src].rearrange("c p t -> p c t")
        hdst = dr[dst].rearrange("c p t -> p c t")
        vf = dr["VF"]
        hn_prev = None
        ART = self.tile("AR", [128, KC, 256], F32)
        btT = self.tile("bt", [128, KC, 128], F32)
        ktT = self.tile("kt", [128, KC, 128], F32)
        pcT = self.tile("pc", [128, KC], F32)
        ssb = self.tile("ssb", [128, NH], F32)
        mu = self.tile("mu", [128, NH], F32)
        var = self.tile("var", [128, NH], F32)
        midw = self.tile("midw", [64, 128], BF16)
        mida = self.tile("mida", [64, 128], BF16)
        midga = self.tile("midga", [128, 128], BF16)
        midgb = self.tile("midgb", [32, 128], BF16)
        midv = self.tile("midv", [32, 128], BF16)
        ygT = self.tile("yg", [128, KC, 128], BF16)
        C0 = 0.6065306597126334

        def pbig():
            return self.rtile("pbig", 2, [128, D], F32, psum=True)

        def psm():
            return self.rtile("psm", 3, [128, 512], F32, psum=True)

        for cb in range(nb):
            t0 = cb * 128
            h = self.rtile("r_h", 2, [128, KC, 128], F32)
            self.load(h, h[:], hsrc[:, :, t0:t0 + 128])
            hn = self.rtile("r_hn", 1, [128, KC, 129], F32)
            if hn_prev is None:
                self.op("pool", lambda e: e.memset(hn[:, :, 0:1], 0.0), [], [hn])
            else:
                self.op("pool", lambda e: e.tensor_copy(out=hn[:, :, 0:1], in_=hn[:, :, 128:129]), [hn], [hn])
            self.rmsnorm(h, 128, lambda c: self.cc("norm_mix_g", layer * KC + c), hn, out_off=1)
            hn_prev = hn
            xx = self.tget()
            self.op("dve", lambda e: e.tensor_tensor(out=self.fm(xx), in0=hn[:, :, 0:128], in1=hn[:, :, 1:129],
                                                     op=ALU.subtract), [hn], [xx])
            def mkx(q):
                x = self.rtile("xq", 3, [128, KC, 128], BF16)
                tm = self.tget()
                eng = "dve" if q % 2 == 0 else "pool"
                self.op(eng, lambda e: e.tensor_tensor(out=self.fm(tm), in0=self.fm(xx),
                                                       in1=self.bc("rw_mix", (i * 6 + q) * KC), op=ALU.mult),
                        [xx, cst], [tm])
                self.op(eng, lambda e: e.tensor_tensor(out=x[:], in0=self.fm(tm), in1=hn[:, :, 1:129],
                                                       op=ALU.add), [tm, hn], [x])
                self.tfree(tm)
                return x
            rT = self.tget()
            kT_ = self.tget()
            vS = self.tget()
            for (w, qi, dstt) in ((wr, 0, rT), (wk, 2, kT_)):
                x = mkx(qi)
                ps = pbig()
                for n in range(KC):
                    for c in range(KC):
                        self.op("pe", lambda e: e.matmul(ps[:, n * 128:(n + 1) * 128],
                                                         lhsT=w[:, c, n * 128:(n + 1) * 128], rhs=x[:, c, :],
                                                         start=(c == 0), stop=(c == KC - 1)), [w, x], [ps])
                self.op("act", lambda e: e.copy(out=dstt[:], in_=ps[:]), [ps], [dstt])
            xv = mkx(3)
            ps = pbig()
            for hf in range(2):
                for c in range(KC):
                    self.op("pe", lambda e: e.matmul(ps[:, hf * 512:(hf + 1) * 512], lhsT=xv[:, c, :],
                                                     rhs=wv[:, c, hf * 512:(hf + 1) * 512],
                                                     start=(c == 0), stop=(c == KC - 1)), [wv, xv], [ps])
            self.op("act", lambda e: e.copy(out=vS[:], in_=ps[:]), [ps], [vS])
            for (wt, qi, ncol, mid, fn) in (((v1, 3, 32, midv, AF.Copy),) if i > 0 else ()) + \
                    ((w1, 1, 64, midw, AF.Tanh), (a1, 4, 64, mida, AF.Copy),
                     (g1, 5, 128, midga, AF.Sigmoid), (g1, 5, 32, midgb, AF.Sigmoid)):
                x = xv if qi == 3 else (x if (mid is midgb) else mkx(qi))
                ps = psm()
                c0 = 128 if mid is midgb else 0
                for c in range(KC):
                    self.op("pe", lambda e: e.matmul(ps[0:ncol, 0:128], lhsT=wt[:, c, c0:c0 + ncol], rhs=x[:, c, :],
                                                     start=(c == 0), stop=(c == KC - 1)), [wt, x], [ps])
                self.op("act", lambda e: e.activation(out=mid[0:ncol, :], in_=ps[0:ncol, 0:128], func=fn), [ps], [mid])
            self.tfree(xx)
            sig = self.tget()
            ps = pbig()
            for n in range(KC):
                self.op("pe", lambda e: e.matmul(ps[:, n * 128:(n + 1) * 128], lhsT=w2[0:64, n * 128:(n + 1) * 128],
                                                 rhs=midw[0:64, :], start=True, stop=True), [w2, midw], [ps])
            self.op("dve", lambda e: e.tensor_tensor(out=self.fm(sig), in0=ps[:].rearrange("p (c t) -> p c t", c=KC),
                                                     in1=self.bc("rw_w0", i * KC), op=ALU.add), [ps, cst], [sig])
            self.op("act", lambda e: e.activation(out=sig[:], in_=sig[:], func=AF.Sigmoid), [sig], [sig])
            aT_ = self.tget()
            ps = pbig()
            for n in range(KC):
                self.op("pe", lambda e: e.matmul(ps[:, n * 128:(n + 1) * 128], lhsT=a2[0:64, n * 128:(n + 1) * 128],
                                                 rhs=mida[0:64, :], start=True, stop=True), [a2, mida], [ps])
            self.op("dve", lambda e: e.tensor_tensor(out=self.fm(aT_), in0=ps[:].rearrange("p (c t) -> p c t", c=KC),
                                                     in1=self.bc("rw_a0", i * KC), op=ALU.add), [ps, cst], [aT_])
            self.op("act", lambda e: e.activation(out=aT_[:], in_=aT_[:], func=AF.Sigmoid), [aT_], [aT_])
            if i > 0:
                vg = self.tget()
                vfT = self.tget()
                self.load(vfT, vfT[:], vf[t0:t0 + 128, :])
                ps = pbig()
                for hf in range(2):
                    self.op("pe", lambda e: e.matmul(ps[:, hf * 512:(hf + 1) * 512], lhsT=midv[0:32, :],
                                                     rhs=v2[0:32, hf * 512:(hf + 1) * 512], start=True, stop=True),
                            [v2, midv], [ps])
                self.op("dve", lambda e: e.tensor_tensor(out=vg[:], in0=ps[:], in1=tmb[:, 2, :], op=ALU.add),
                        [ps, tmb], [vg])
                self.op("act", lambda e: e.activation(out=vg[:], in_=vg[:], func=AF.Sigmoid), [vg], [vg])
                self.op("pool", lambda e: e.tensor_tensor(out=vfT[:], in0=vfT[:], in1=vS[:], op=ALU.subtract),
                        [vfT, vS], [vfT])
                self.op("dve", lambda e: e.tensor_tensor(out=vfT[:], in0=vfT[:], in1=vg[:], op=ALU.mult),
                        [vfT, vg], [vfT])
                self.op("dve", lambda e: e.tensor_tensor(out=vS[:], in0=vS[:], in1=vfT[:], op=ALU.add),
                        [vS, vfT], [vS])
                self.tfree(vg, vfT)
            else:
                self.store(vf[t0:t0 + 128, :], None, vS, vS[:])
            kk = self.tget()
            self.op("dve", lambda e: e.tensor_tensor(out=self.fm(kk), in0=self.fm(kT_), in1=self.bc("rw_kk", i * KC),
                                                     op=ALU.mult), [kT_, cst], [kk])
            ksq = self.tget()
            self.op("pool", lambda e: e.tensor_tensor(out=ksq[:], in0=kk[:], in1=kk[:], op=ALU.mult), [kk], [ksq])
            ps = pbig()
            for c in range(KC):
                self.op("pe", lambda e: e.matmul(ps[:, c * 128:(c + 1) * 128], lhsT=bd, rhs=ksq[:, c * 128:(c + 1) * 128],
                                                 start=True, stop=True), [rc, ksq], [ps])
            self.op("act", lambda e: e.activation(out=ksq[:], in_=ps[:], func=AF.Sqrt), [ps], [ksq])
            self.op("dve", lambda e: e.tensor_scalar(out=ksq[:], in0=ksq[:], scalar1=1e-12, scalar2=None, op0=ALU.max),
                    [ksq], [ksq])
            self.op("dve", lambda e: e.reciprocal(out=ksq[:], in_=ksq[:]), [ksq], [ksq])
            self.op("dve", lambda e: e.tensor_tensor(out=kk[:], in0=kk[:], in1=ksq[:], op=ALU.mult), [kk, ksq], [kk])
            self.tfree(ksq)
            km = self.tget()
            self.op("dve", lambda e: e.scalar_tensor_tensor(out=self.fm(km), in0=self.fm(aT_), scalar=-1.0,
                                                            in1=self.bc("rw_ka", i * KC), op0=ALU.add, op1=ALU.mult),
                    [aT_, cst], [km])
            self.op("dve", lambda e: e.scalar_tensor_tensor(out=km[:], in0=km[:], scalar=1.0, in1=kT_[:],
                                                            op0=ALU.add, op1=ALU.mult), [km, kT_], [km])
            self.tfree(kT_)
            cs = self.tget()
            for c in range(KC):
                self.op("dve", lambda e: e.tensor_tensor_scan(out=cs[:, c * 128:(c + 1) * 128], data0=onesf,
                                                              data1=sig[:, c * 128:(c + 1) * 128], initial=zcol,
                                                              op0=ALU.mult, op1=ALU.add), [sig, rc], [cs])
            ein = self.tget()
            eneg = self.tget()
            self.op("act", lambda e: e.activation(out=ein[:], in_=cs[:], func=AF.Exp, scale=-C0), [cs], [ein])
            self.op("act", lambda e: e.activation(out=eneg[:], in_=cs[:], func=AF.Exp, scale=C0), [cs], [eneg])
            self.op("pool", lambda e: e.tensor_tensor(out=cs[:], in0=cs[:], in1=sig[:], op=ALU.subtract), [cs, sig], [cs])
            self.op("act", lambda e: e.activation(out=cs[:], in_=cs[:], func=AF.Exp, scale=-C0), [cs], [cs])
            self.tfree(sig)
            self.op("dve", lambda e: e.scalar_tensor_tensor(out=ART[:, :, 0:128], in0=self.fm(kk), scalar=-1.0,
                                                            in1=self.fm(cs), op0=ALU.mult, op1=ALU.mult),
                    [kk, cs], [ART])
            self.op("pool", lambda e: e.tensor_tensor(out=ART[:, :, 128:256], in0=self.fm(rT), in1=self.fm(ein),
                                                      op=ALU.mult), [rT, ein], [ART])
            self.tfree(cs)
            self.op("dve", lambda e: e.tensor_tensor(out=btT[:], in0=self.fm(kk), in1=self.fm(aT_), op=ALU.mult),
                    [kk, aT_], [btT])
            self.op("dve", lambda e: e.tensor_tensor(out=btT[:], in0=btT[:], in1=self.fm(eneg), op=ALU.mult),
                    [btT, eneg], [btT])
            self.tfree(aT_, kk)
            self.op("pool", lambda e: e.tensor_tensor(out=ktT[:], in0=self.fm(km), in1=self.fm(eneg), op=ALU.mult),
                    [km, eneg], [ktT])
            self.tfree(eneg)
            self.op("act", lambda e: e.copy(out=pcT[:, :].unsqueeze(2), in_=self.fm(ein, 127, 128)), [ein], [pcT])
            self.tfree(ein)
            self.op("dve", lambda e: e.tensor_tensor(out=rT[:], in0=rT[:], in1=km[:], op=ALU.mult), [rT, km], [rT])
            self.op("dve", lambda e: e.tensor_tensor(out=self.fm(rT), in0=self.fm(rT), in1=self.bc("rw_rk", i * KC),
                                                     op=ALU.mult), [rT, cst], [rT])
            ps = psm()
            for c in range(KC):
                self.op("pe", lambda e: e.matmul(ps[:, 2 * c:2 * c + 2], lhsT=rT[:, c * 128:(c + 1) * 128], rhs=ind2,
                                                 start=True, stop=True), [rT, rc], [ps])
            self.op("act", lambda e: e.copy(out=ssb[:], in_=ps[:, 0:NH]), [ps], [ssb])
            self.tfree(rT, km)
            pcb = pcT[:, :].unsqueeze(2).broadcast_to([128, KC, 128])
            KH = self.tget()
            BH = self.tget()
            for (srcT, dstT) in ((ktT, KH), (btT, BH)):
                tmp = self.tget()
                self.op("dve", lambda e: e.tensor_tensor(out=self.fm(tmp), in0=srcT[:], in1=pcb, op=ALU.mult),
                        [srcT, pcT], [tmp])
                ps = pbig()
                for c in range(KC):
                    self.op("pe", lambda e: e.transpose(ps[:, c * 128:(c + 1) * 128], tmp[:, c * 128:(c + 1) * 128],
                                                        ident), [tmp, rc], [ps])
                self.op("act", lambda e: e.copy(out=dstT[:], in_=ps[:]), [ps], [dstT])
                self.tfree(tmp)
            yT = self.tget()
            for c in range(KC):
                sk = self.rtile("sk", 2, [128, 2, 256], F32)
                sbm = self.rtile("sbm", 2, [128, 2, 256], F32)
                sx = self.rtile("sx", 3, [128, 2, 192], F32)
                p1 = psm()
                p2 = psm()
                p3 = psm()
                for hh in range(2):
                    pb = 64 * hh
                    self.op("pe", lambda e: e.matmul(p1[:, hh * 256:(hh + 1) * 256], lhsT=ktT[pb:pb + 64, c, :],
                                                     rhs=ART[pb:pb + 64, c, :], start=True, stop=True), [ktT, ART], [p1])
                    self.op("pe", lambda e: e.matmul(p2[:, hh * 256:(hh + 1) * 256], lhsT=btT[pb:pb + 64, c, :],
                                                     rhs=ART[pb:pb + 64, c, :], start=True, stop=True), [btT, ART], [p2])
                    self.op("pe", lambda e: e.matmul(p3[:, hh * 128:(hh + 1) * 128], lhsT=ART[pb:pb + 64, c, 0:128],
                                                     rhs=btT[pb:pb + 64, c, :], start=True, stop=True), [btT, ART], [p3])
                mt2b = mt2.unsqueeze(1).broadcast_to([128, 2, 256])
                msb = mstrict.unsqueeze(1).broadcast_to([128, 2, 128])
                self.op("dve", lambda e: e.tensor_tensor(out=sk[:], in0=p1[:].rearrange("p (h x) -> p h x", h=2),
                                                         in1=mt2b, op=ALU.mult), [p1, rc], [sk])
                self.op("dve", lambda e: e.tensor_tensor(out=sbm[:], in0=p2[:].rearrange("p (h x) -> p h x", h=2),
                                                         in1=mt2b, op=ALU.mult), [p2, rc], [sbm])
                self.op("dve", lambda e: e.tensor_tensor(out=sx[:, :, 0:128],
                                                         in0=p3[:, 0:256].rearrange("p (h x) -> p h x", h=2),
                                                         in1=msb, op=ALU.mult), [p3, rc], [sx])
                px = psm()
                self.op("pe", lambda e: e.matmul(px[:, 0:128], lhsT=ART[:, c, 0:128], rhs=st2[c][:], start=True,
                                                 stop=False), [ART, st2[c]], [px])
                for hh in range(2):
                    hd = 2 * c + hh
                    self.op("pe", lambda e: e.matmul(px[:, hh * 64:(hh + 1) * 64], lhsT=sk[:, hh, 0:128],
                                                     rhs=vS[:, hd * 64:(hd + 1) * 64], start=False, stop=(hh == 1)),
                            [sk, vS], [px])
                self.op("act", lambda e: e.copy(out=sx[:, :, 128:192],
                                                in_=px[:, 0:128].rearrange("p (h x) -> p h x", h=2)), [px], [sx])
                tcur = None
                for k in range(7):
                    pd = psm()
                    for hh in range(2):
                        tk = sbm[:, hh, 0:128] if tcur is None else tcur[:, hh, :]
                        ncol = 192 if k < 6 else 64
                        c0 = 0 if k < 6 else 128
                        self.op("pe", lambda e: e.matmul(pd[:, hh * 192:hh * 192 + ncol], lhsT=tk,
                                                         rhs=sx[:, hh, c0:c0 + ncol], start=True, stop=True),
                                [sbm if tcur is None else tcur, sx], [pd])
                    if k < 6:
                        pt = psm()
                        for hh in range(2):
                            tk = sbm[:, hh, 0:128] if tcur is None else tcur[:, hh, :]
                            self.op("pe", lambda e: e.matmul(pt[:, hh * 128:(hh + 1) * 128], lhsT=sx[:, hh, 0:128],
                                                             rhs=tk, start=True, stop=True),
                                    [sbm if tcur is None else tcur, sx], [pt])
                        sxn = self.rtile("sx", 3, [128, 2, 192], F32)
                        pdv = pd[:, 0:384].rearrange("p (h x) -> p h x", h=2)
                        self.op("act", lambda e: e.copy(out=sxn[:, :, 0:128], in_=pdv[:, :, 0:128]), [pd], [sxn])
                        self.op("dve", lambda e: e.tensor_tensor(out=sxn[:, :, 128:192], in0=pdv[:, :, 128:192],
                                                                 in1=sx[:, :, 128:192], op=ALU.add), [pd, sx], [sxn])
                        tn = self.rtile("tt", 2, [128, 2, 128], F32)
                        self.op("act", lambda e: e.copy(out=tn[:], in_=pt[:, 0:256].rearrange("p (h x) -> p h x", h=2)),
                                [pt], [tn])
                        sx = sxn
                        tcur = tn
                    else:
                        sxn = self.rtile("sx", 3, [128, 2, 192], F32)
                        pdv = pd[:, 0:384].rearrange("p (h x) -> p h x", h=2)
                        self.op("dve", lambda e: e.tensor_tensor(out=sxn[:, :, 128:192], in0=pdv[:, :, 0:64],
                                                                 in1=sx[:, :, 128:192], op=ALU.add), [pd, sx], [sxn])
                        sx = sxn
                py = psm()
                self.op("pe", lambda e: e.matmul(py[:, 0:128], lhsT=ART[:, c, 128:256], rhs=st2[c][:], start=True,
                                                 stop=False), [ART, st2[c]], [py])
                for hh in range(2):
                    hd = 2 * c + hh
                    self.op("pe", lambda e: e.matmul(py[:, hh * 64:(hh + 1) * 64], lhsT=sbm[:, hh, 128:256],
                                                     rhs=sx[:, hh, 128:192], start=False, stop=False), [sbm, sx], [py])
                    self.op("pe", lambda e: e.matmul(py[:, hh * 64:(hh + 1) * 64], lhsT=sk[:, hh, 128:256],
                                                     rhs=vS[:, hd * 64:(hd + 1) * 64], start=False, stop=(hh == 1)),
                            [sk, vS], [py])
                self.op("act", lambda e: e.copy(out=yT[:, c * 128:(c + 1) * 128], in_=py[:, 0:128]), [py], [yT])
                pst = psm()
                for hh in range(2):
                    hd = 2 * c + hh
                    pb = 64 * hh
                    self.op("pe", lambda e: e.matmul(pst[pb:pb + 64, hh * 64:(hh + 1) * 64],
                                                     lhsT=BH[:, hd * 64:(hd + 1) * 64], rhs=sx[:, hh, 128:192],
                                                     start=True, stop=False), [BH, sx], [pst])
                    self.op("pe", lambda e: e.matmul(pst[pb:pb + 64, hh * 64:(hh + 1) * 64],
                                                     lhsT=KH[:, hd * 64:(hd + 1) * 64], rhs=vS[:, hd * 64:(hd + 1) * 64],
                                                     start=False, stop=True), [KH, vS], [pst])
                for hh in range(2):
                    pb = 64 * hh
                    self.op("dve", lambda e: e.scalar_tensor_tensor(
                        out=st2[c][pb:pb + 64, hh * 64:(hh + 1) * 64], in0=st2[c][pb:pb + 64, hh * 64:(hh + 1) * 64],
                        scalar=pcT[pb:pb + 64, c:c + 1], in1=pst[pb:pb + 64, hh * 64:(hh + 1) * 64],
                        op0=ALU.mult, op1=ALU.add), [st2[c], pcT, pst], [st2[c]])
            self.tfree(KH, BH)
            y3 = yT.t[:].rearrange("p (h x) -> p h x", h=NH)
            self.op("dve", lambda e: e.tensor_reduce(out=mu[:], in_=y3, axis=AX.X, op=ALU.add), [yT], [mu])
            mub = mu[:, :].unsqueeze(2).broadcast_to([128, NH, HD])
            self.op("dve", lambda e: e.scalar_tensor_tensor(out=y3, in0=mub, scalar=-1.0 / HD, in1=y3, op0=ALU.mult,
                                                            op1=ALU.add), [yT, mu], [yT])
            sq = self.tget()
            self.op("pool", lambda e: e.tensor_tensor(out=sq[:], in0=yT[:], in1=yT[:], op=ALU.mult), [yT], [sq])
            self.op("dve", lambda e: e.tensor_reduce(out=var[:], in_=sq.t[:].rearrange("p (h x) -> p h x", h=NH),
                                                     axis=AX.X, op=ALU.add), [sq], [var])
            self.op("act", lambda e: e.activation(out=var[:], in_=var[:], func=AF.Sqrt, bias=GN_EPS, scale=1.0 / HD),
                    [var], [var])
            self.op("dve", lambda e: e.reciprocal(out=var[:], in_=var[:]), [var], [var])
            varb = var[:, :].unsqueeze(2).broadcast_to([128, NH, HD])
            self.op("dve", lambda e: e.tensor_tensor(out=y3, in0=y3, in1=varb, op=ALU.mult), [yT, var], [yT])
            self.op("pool", lambda e: e.tensor_tensor(out=yT[:], in0=yT[:], in1=tmb[:, 0, :], op=ALU.mult), [yT, tmb], [yT])
            self.op("pool", lambda e: e.tensor_tensor(out=yT[:], in0=yT[:], in1=tmb[:, 1, :], op=ALU.add), [yT, tmb], [yT])
            ssbb = ssb[:, :].unsqueeze(2).broadcast_to([128, NH, HD])
            self.op("dve", lambda e: e.tensor_tensor(out=sq.t[:].rearrange("p (h x) -> p h x", h=NH),
                                                     in0=vS.t[:].rearrange("p (h x) -> p h x", h=NH), in1=ssbb,
                                                     op=ALU.mult), [vS, ssb], [sq])
            self.op("dve", lambda e: e.tensor_tensor(out=yT[:], in0=yT[:], in1=sq[:], op=ALU.add), [yT, sq], [yT])
            self.tfree(sq, vS)
            gS = self.tget()
            ps = pbig()
            for n in range(KC):
                self.op("pe", lambda e: e.matmul(ps[:, n * 128:(n + 1) * 128], lhsT=g2a[:, n * 128:(n + 1) * 128],
                                                 rhs=midga[:, :], start=True, stop=False), [g2a, midga], [ps])
                self.op("pe", lambda e: e.matmul(ps[:, n * 128:(n + 1) * 128], lhsT=g2b[0:32, n * 128:(n + 1) * 128],
                                                 rhs=midgb[0:32, :], start=False, stop=True), [g2b, midgb], [ps])
            self.op("act", lambda e: e.copy(out=gS[:], in_=ps[:]), [ps], [gS])
            ps = pbig()
            for c in range(KC):
                self.op("pe", lambda e: e.transpose(ps[:, c * 128:(c + 1) * 128], yT[:, c * 128:(c + 1) * 128], ident),
                        [yT, rc], [ps])
            self.op("dve", lambda e: e.tensor_tensor(out=ygT[:], in0=ps[:].rearrange("p (c t) -> p c t", c=KC),
                                                     in1=self.fm(gS), op=ALU.mult), [ps, gS], [ygT])
            self.tfree(yT, gS)
            ps = pbig()
            for n in range(KC):
                for c in range(KC):
                    self.op("pe", lambda e: e.matmul(ps[:, n * 128:(n + 1) * 128], lhsT=wo[:, c, n * 128:(n + 1) * 128],
                                                     rhs=ygT[:, c, :], start=(c == 0), stop=(c == KC - 1)),
                            [wo, ygT], [ps])
            ho = self.rtile("r_ho", 1, [128, KC, 128], F32)
            self.op("dve", lambda e: e.tensor_tensor(out=ho[:], in0=ps[:].rearrange("p (c t) -> p c t", c=KC),
                                                     in1=h[:], op=ALU.add), [ps, h], [ho])
            v0 = PAD if cb == 0 else 0
            self.store(hdst[:, :, t0 + v0:t0 + 128], None, ho, ho[:, :, v0:128])
        self.rn_w = 512
        self.phase_end()

    def zero_pads(self, names):
        zt = self.tiles["zeros"]
        for nm in names:
            for c in range(KC):
                self.store(self.dram[nm][c, :, 0:PAD], None, zt, zt[:], eng="sp")

    def finish(self, out_res=None):
        self.barrier()
        return self.nc


def build_ffn_test(nb):
    p = Prog(nb)
    p.din("hT0", [KC, 128, p.LP])
    p.din("ffn_up", [DEPTH, 128, KC, F2])
    p.din("ffn_down", [DEPTH, 128, NJ, D])
    p.dout("hT1", [KC, 128, p.LP])
    p.setup_consts()
    p.zero_pads(["hT1"])
    p.ffn_phase(0, "hT0", "hT1")
    return p.finish([p.dr(("hT1", "all"))])


def attn_masks():
    s = np.arange(128)[:, None]
    col = np.arange(512)[None, :]
    q, t = col // 128, col % 128
    m = np.zeros((128, 6, 512), np.float32)
    for r in range(4):
        m[:, r, :] = ((q > r) | ((q == r) & (t > s))).astype(np.float32)
    row = (s >= PAD).astype(np.float32)
    m[:, 4, :] = m[:, 0, :] * row
    m[:, 5, :] = row * np.ones((1, 512), np.float32)
    j = np.arange(128)[:, None]
    ss = np.arange(128)[None, :]
    ntri = -(j >= ss).astype(np.float32)
    return m, ntri


def build_att_test(nb):
    p = Prog(nb)
    p.din("hT0", [KC, 128, p.LP])
    p.din("sb_wq", [2, 128, KC, D]); p.din("sb_wk", [128, KC, D]); p.din("sb_wv", [128, KC, D])
    p.din("sb_wo", [2, 128, KC, D])
    p.din("amask", [128, 6, 512]); p.din("ntri", [128, 128])
    p.dscratch("QT", [KC, 128, p.LP], BF16); p.dscratch("KT", [KC, 128, p.LP], BF16)
    p.dscratch("VV", [p.LP, D], BF16); p.dscratch("AT", [KC, 128, p.LP], BF16)
    p.dout("hT1", [KC, 128, p.LP])
    p.setup_consts()
    p.zero_pads(["hT1"])
    p.qkv_phase(0, "hT0", True)
    p.att_phase()
    p.o_phase(0, "hT0", "hT1")
    return p.finish([p.dr(("hT1", "all"))])


def rw_consts():
    p = np.arange(128)[:, None]
    q = np.arange(128)[None, :]
    ident = (p == q).astype(np.float32)
    bd = ((p // 64) == (q // 64)).astype(np.float32)
    mstrict = (p > q).astype(np.float32)
    mt_strict = (q > p).astype(np.float32)
    mt_incl = (q >= p).astype(np.float32)
    ind2 = np.concatenate([(p // 64 == 0), (p // 64 == 1)], axis=1).astype(np.float32)
    onesf = np.ones((128, 128), np.float32)
    zcol = np.zeros((128, 1), np.float32)
    pad = np.zeros((128, 127), np.float32)
    return np.ascontiguousarray(np.concatenate([ident, bd, mstrict, mt_strict, mt_incl, ind2, onesf, zcol, pad],
                                               axis=1))


def declare_rw_inputs(p):
    p.din("rw_wr", [2, 128, KC, D]); p.din("rw_wk", [2, 128, KC, D]); p.din("rw_wv", [2, 128, KC, D])
    p.din("rw_wo", [2, 128, KC, D])
    p.din("rw_w1", [2, 128, KC, 64]); p.din("rw_a1", [2, 128, KC, 64]); p.din("rw_g1", [2, 128, KC, 160])
    p.din("rw_v1", [1, 128, KC, 32])
    p.din("rw_w2", [2, 64, D]); p.din("rw_a2", [2, 64, D]); p.din("rw_g2", [2, 160, D]); p.din("rw_v2", [1, 32, D])
    p.din("rw_lnx_g", [2, D]); p.din("rw_lnx_b", [2, D]); p.din("rw_v0", [1, D])
    p.din("rw_const", [128, 128 * 5 + 2 + 256])
    p.dscratch("VF", [p.LP, D], F32)


def build_rw_test(nb, nlayers=1):
    p = Prog(nb)
    p.din("hT0", [KC, 128, p.LP])
    declare_rw_inputs(p)
    p.dout("hT1", [KC, 128, p.LP])
    p.dscratch("hTa", [KC, 128, p.LP])
    p.setup_consts()
    p.zero_pads(["hT1", "hTa"])
    if nlayers == 1:
        p.rwkv_phase(0, "hT0", "hT1")
    else:
        p.rwkv_phase(0, "hT0", "hTa")
        p.rwkv_phase(1, "hTa", "hT1")
    return p.finish()


def final_phase(p, src):
    p.phase_begin()
    hsrc = p.dram[src].rearrange("c p t -> p c t")
    yo = p.dram["yT"].rearrange("c p t -> p c t")
    t = PAD + NMETA
    while t < p.LP:
        tw = min(512, p.LP - t)
        h = p.rtile("fn_h", 2, [128, KC, 512], F32)
        p.load(h, h[:, :, 0:tw], hsrc[:, :, t:t + tw])
        o = p.rtile("fn_o", 2, [128, KC, 512], F32)
        p.rmsnorm(h, tw, lambda c: p.cc("final_g", c), o)
        p.store(yo[:, :, t - PAD - NMETA:t - PAD - NMETA + tw], None, o, o[:, :, 0:tw])
        t += tw
    p.phase_end()


def build_full(nb):
    p = Prog(nb)
    p.din("hT0", [KC, 128, p.LP])
    p.din("ffn_up", [DEPTH, 128, KC, F2])
    p.din("ffn_down", [DEPTH, 128, NJ, D])
    declare_rw_inputs(p)
    p.din("sb_wq", [2, 128, KC, D]); p.din("sb_wk", [128, KC, D]); p.din("sb_wv", [128, KC, D])
    p.din("sb_wo", [2, 128, KC, D])
    p.din("amask", [128, 6, 512]); p.din("ntri", [128, 128])
    p.dscratch("QT", [KC, 128, p.LP], BF16); p.dscratch("KT", [KC, 128, p.LP], BF16)
    p.dscratch("VV", [p.LP, D], BF16); p.dscratch("AT", [KC, 128, p.LP], BF16)
    p.dscratch("hA", [KC, 128, p.LP]); p.dscratch("hB", [KC, 128, p.LP])
    p.dout("yT", [KC, 128, p.LP - PAD - NMETA])
    p.setup_consts()
    p.zero_pads(["hA", "hB"])
    p.barrier()
    p.rwkv_phase(0, "hT0", "hA")
    p.ffn_phase(0, "hA", "hB")
    p.rwkv_phase(1, "hB", "hA")
    p.ffn_phase(1, "hA", "hB")
    p.qkv_phase(0, "hB", True)
    p.att_phase()
    p.o_phase(0, "hB", "hA")
    p.ffn_phase(2, "hA", "hB")
    p.qkv_phase(1, "hB", False)
    p.att_phase()
    p.o_phase(1, "hB", "hA")
    p.ffn_phase(3, "hA", "hB")
    final_phase(p, "hB")
    return p.finish()


def _w1024(w):
    return np.ascontiguousarray(w.reshape(KC, 128, -1).transpose(1, 0, 2))


def _wst(w):
    return np.stack([_w1024(w[i]) for i in range(w.shape[0])])


def _cols(v):
    v = np.asarray(v, np.float32).reshape(-1, KC, 128)
    return np.ascontiguousarray(v.transpose(2, 0, 1).reshape(128, -1))


def host_inputs(inp, nb):
    f = {k: np.asarray(v, np.float32) for k, v in inp.items()}
    m, ntri = attn_masks()
    cw = f["ffn_conv_w"]
    shared = {
        "ffn_up": np.ascontiguousarray(f["ffn_up"].reshape(DEPTH, KC, 128, F2).transpose(0, 2, 1, 3)),
        "ffn_down": np.ascontiguousarray(f["ffn_down"].reshape(DEPTH, NJ, 128, D).transpose(0, 2, 1, 3)),
        "norm_ffn_g": _cols(f["norm_ffn_g"]), "norm_mix_g": _cols(f["norm_mix_g"]),
        "kv_norm_g": _cols(f["kv_norm_g"]), "final_g": _cols(f["final_norm_g"]),
        "conv_w": np.ascontiguousarray(cw.reshape(DEPTH, 3, 44, 128).transpose(3, 0, 2, 1).reshape(128, -1)),
        "conv_b": np.ascontiguousarray(f["ffn_conv_b"].reshape(DEPTH, 44, 128).transpose(2, 0, 1).reshape(128, -1)),
        "rw_mix": _cols(f["rw_mix"]), "rw_w0": _cols(f["rw_w0"]), "rw_a0": _cols(f["rw_a0"]),
        "rw_kk": _cols(f["rw_kk"]), "rw_ka": _cols(f["rw_ka"]), "rw_rk": _cols(f["rw_rk"].reshape(2, D)),
        "rw_wr": _wst(f["rw_wr"]), "rw_wk": _wst(f["rw_wk"]), "rw_wv": _wst(f["rw_wv"]), "rw_wo": _wst(f["rw_wo"]),
        "rw_w1": _wst(f["rw_w1"]), "rw_a1": _wst(f["rw_a1"]), "rw_g1": _wst(f["rw_g1"]), "rw_v1": _wst(f["rw_v1"]),
        "rw_w2": f["rw_w2"], "rw_a2": f["rw_a2"], "rw_g2": f["rw_g2"], "rw_v2": f["rw_v2"],
        "rw_lnx_g": f["rw_lnx_g"], "rw_lnx_b": f["rw_lnx_b"], "rw_v0": f["rw_v0"],
        "rw_const": rw_consts(),
        "sb_wq": _wst(f["sb_wq"]), "sb_wo": _wst(f["sb_wo"]), "sb_wk": _w1024(f["sb_wk"]), "sb_wv": _w1024(f["sb_wv"]),
        "amask": m, "ntri": ntri,
    }
    return shared


def host_h0(xb, meta, nb):
    LP = nb * 128
    hT = np.zeros((D, LP), np.float32)
    hT[:, PAD:PAD + NMETA] = meta.T
    hT[:, PAD + NMETA:] = xb.T
    return hT.reshape(KC, 128, LP)


_CACHE = {}


def kernel(**inputs):
    x = np.asarray(inputs["x"], np.float32)
    B, S, _ = x.shape
    nb = (PAD + NMETA + S) // 128
    if nb not in _CACHE:
        _CACHE[nb] = build_full(nb)
    nc = _CACHE[nb]
    shared = host_inputs({k: v for k, v in inputs.items() if k != "x"}, nb)
    meta = np.asarray(inputs["meta_tokens"], np.float32)
    ncores = 8
    in_maps = []
    for cidx in range(ncores):
        b = cidx % B
        m = dict(shared)
        m["hT0"] = host_h0(x[b], meta, nb)
        in_maps.append(m)
    res = run_bass_kernel_spmd(nc, in_maps, core_ids=list(range(ncores)))
    out = np.empty((B, S, D), np.float32)
    for b in range(B):
        out[b] = res.results[b]["yT"].reshape(D, S).T
    return out
```

```python
import numpy as np
from contextlib import ExitStack
import concourse.bass as bass
import concourse.mybir as mybir
from concourse.bass_utils import run_bass_kernel_spmd

F32 = mybir.dt.float32
BF16 = mybir.dt.bfloat16
ALU = mybir.AluOpType
AF = mybir.ActivationFunctionType
AX = mybir.AxisListType

D = 1024
KC = 8
NH = 16
HD = 64
NMETA = 16
PAD = 112
DFF = 2816
F2 = 5632
NJ = 22
FT = 384
DEPTH = 4
RMS_EPS = 1e-6
GN_EPS = 64e-5


class Res:
    __slots__ = ("name", "lw", "rd")

    def __init__(self, name):
        self.name = name
        self.lw = None
        self.rd = {}


class KB:
    SEM_LIMIT = 30000

    def __init__(self, nc):
        self.nc = nc
        self.q = {e: [] for e in ("pe", "act", "dve", "pool", "sp")}
        self.sems = {}
        self.cur = {}
        self.waited = {e: {} for e in self.q}
        self.nsem = 0
        self.dmasem = {}
        self.nops = 0

    def _newsem(self, tag):
        key = "%s_%d" % (tag, self.nsem)
        self.nsem += 1
        self.sems[key] = self.nc.alloc_semaphore(name=key)
        return key

    def _eng_event(self, eng):
        c = self.cur.get(eng)
        if c is None or c[1] >= self.SEM_LIMIT:
            c = [self._newsem(eng), 0]
            self.cur[eng] = c
        c[1] += 1
        return (c[0], c[1])

    def _dma_event(self, chain, eng="sp"):
        cls = "sw" if eng == "pool" else "hw"
        chain = (cls, chain)
        c = self.dmasem.get(chain)
        if c is None or c[1] >= self.SEM_LIMIT:
            free = getattr(self, "free_dma", {}).get(cls, [])
            free.sort(key=lambda x: x[1])
            if free and free[0][1] < self.SEM_LIMIT // 2:
                c = list(free.pop(0))
            else:
                c = [self._newsem("d" + cls), 0]
            self.dmasem[chain] = c
        c[1] += 16
        return (c[0], c[1])

    def recycle_dma(self):
        free = getattr(self, "free_dma", {"sw": [], "hw": []})
        for ch, c in self.dmasem.items():
            free[ch[0]].append((c[0], c[1]))
        self.free_dma = free
        self.dmasem = {}

    def _deps(self, eng, reads, writes, is_dma):
        waits = {}

        def need(sk, v, src_eng, kind):
            if (not is_dma) and eng == "pe" and src_eng == "pe":
                return
            if (not is_dma) and src_eng == eng and kind == "war":
                return
            if self.waited[eng].get(sk, 0) >= v:
                return
            if waits.get(sk, 0) < v:
                waits[sk] = v

        for r in reads:
            if r.lw is not None:
                need(r.lw[0], r.lw[1], r.lw[2], "raw")
        for w in writes:
            if w.lw is not None:
                need(w.lw[0], w.lw[1], w.lw[2], "waw")
            for sk, (v, e) in w.rd.items():
                need(sk, v, e, "war")
        return waits

    def _record(self, ev, src, reads, writes):
        for r in reads:
            r.rd[ev[0]] = (ev[1], src)
        for w in writes:
            w.lw = (ev[0], ev[1], src)
            w.rd = {}

    def op(self, eng, fn, reads=(), writes=()):
        waits = self._deps(eng, reads, writes, False)
        for sk, v in waits.items():
            self.waited[eng][sk] = v
        ev = self._eng_event(eng)
        self._emit(eng, fn, waits, ev, 1)
        self._record(ev, eng, reads, writes)
        self.nops += 1
        return ev

    def dma(self, eng, fn, reads=(), writes=(), chain=None):
        waits = self._deps(eng, reads, writes, True)
        for sk, v in waits.items():
            self.waited[eng][sk] = v
        assert chain is not None
        ev = self._dma_event(chain, eng)
        self._emit(eng, fn, waits, ev, 16)
        self._record(ev, "dma", reads, writes)
        self.nops += 1
        return ev

    def wait_all(self, eng, resources):
        waits = {}
        for r in resources:
            if r.lw is not None:
                sk, v = r.lw[0], r.lw[1]
                if self.waited[eng].get(sk, 0) < v and waits.get(sk, 0) < v:
                    waits[sk] = v
        for sk, v in waits.items():
            self.waited[eng][sk] = v
        self._emit(eng, None, waits, None, 0)

    ENG = {"pe": "tensor", "act": "scalar", "dve": "vector", "pool": "gpsimd", "sp": "sync"}

    def _emit(self, eng, fn, waits, ev, inc):
        engine = getattr(self.nc, self.ENG[eng])
        for sk, v in waits.items():
            engine.wait_ge(self.sems[sk], v)
        if fn is not None:
            ins = fn(engine)
            ins.then_inc(self.sems[ev[0]], inc)

    def emit(self):
        return

    def emit_old(self):
        nc = self.nc
        engs = {"pe": "tensor", "act": "scalar", "dve": "vector", "pool": "gpsimd", "sp": "sync"}
        with nc.Block() as block:
            for e, attr in engs.items():
                ops = self.q[e]
                if not ops:
                    continue

                def body(engine, ops=ops):
                    for fn, waits, ev, inc in ops:
                        for sk, v in waits:
                            engine.wait_ge(self.sems[sk], v)
                        if fn is not None:
                            ins = fn(engine)
                            ins.then_inc(self.sems[ev[0]], inc)
                getattr(block, attr)(body)


class T:
    def __init__(self, nc, name, shape, dtype, psum=False, stack=None):
        self.name = name
        if psum:
            cm = nc.psum_tensor(name, list(shape), dtype)
        else:
            cm = nc.sbuf_tensor(name, list(shape), dtype)
        self.t = stack.enter_context(cm)
        self.r = Res(name)

    def __getitem__(self, idx):
        return self.t[idx]


class Prog:
    def __init__(self, nb, dbg=None):
        self.nb = nb
        self.LP = nb * 128
        self.dbg = dbg or {}
        self.nc = bass.Bass("TRN2", target_bir_lowering=False)
        self.kb = KB(self.nc)
        self.dram = {}
        self.dres = {}
        self.tiles = {}
        self.rot = {}
        self.gstack = ExitStack()
        self.pstack = None
        self.pid = 0

    def phase_begin(self):
        self.pstack = ExitStack()
        self.pid += 1
        self.ptiles = []

    def phase_end(self):
        self.barrier()
        self.kb.recycle_dma()
        self.pstack.close()
        self.pstack = None
        for nm in self.ptiles:
            self.tiles.pop(nm, None)
        self.rot = {}

    def barrier(self):
        kb = self.kb
        targets = {}
        for e, c in kb.cur.items():
            targets[c[0]] = c[1]
        for ch, c in kb.dmasem.items():
            targets[c[0]] = c[1]
        for e in ("pe", "act", "dve", "pool", "sp"):
            waits = {}
            for sk, v in targets.items():
                if kb.cur.get(e) is not None and kb.cur[e][0] == sk:
                    continue
                if kb.waited[e].get(sk, 0) < v:
                    waits[sk] = v
                    kb.waited[e][sk] = v
            kb._emit(e, None, waits, None, 0)

    def din(self, name, shape, dtype=F32):
        self.dram[name] = self.nc.dram_tensor(name, list(shape), dtype, kind="ExternalInput").ap()
        return self.dram[name]

    def dout(self, name, shape, dtype=F32):
        self.dram[name] = self.nc.dram_tensor(name, list(shape), dtype, kind="ExternalOutput").ap()
        return self.dram[name]

    def dscratch(self, name, shape, dtype=F32):
        self.dram[name] = self.nc.dram_tensor(name, list(shape), dtype, kind="Internal").ap()
        return self.dram[name]

    def dr(self, key):
        r = self.dres.get(key)
        if r is None:
            r = Res(str(key))
            self.dres[key] = r
        return r

    def tile(self, name, shape, dtype=F32, psum=False):
        if self.pstack is not None:
            t = T(self.nc, "%s_p%d" % (name, self.pid), shape, dtype, psum, self.pstack)
            self.ptiles.append(name)
        else:
            t = T(self.nc, name, shape, dtype, psum, self.gstack)
        self.tiles[name] = t
        return t

    def rtile(self, name, n, shape, dtype=F32, psum=False):
        ent = self.rot.get(name)
        if ent is None:
            ent = [[self.tile("%s%d" % (name, i), shape, dtype, psum) for i in range(n)], 0]
            self.rot[name] = ent
        t = ent[0][ent[1] % n]
        ent[1] += 1
        return t

    def op(self, eng, fn, reads=(), writes=()):
        return self.kb.op(eng, fn, [x.r if isinstance(x, T) else x for x in reads],
                          [x.r if isinstance(x, T) else x for x in writes])

    def load(self, dst, dst_ap, src_ap, src_res=None, eng="sp"):
        self.kb.dma(eng, lambda e: e.dma_start(out=dst_ap, in_=src_ap),
                    reads=[], writes=[dst.r], chain="ld_" + dst.name)

    def store(self, dst_ap, dst_res, src, src_ap, eng="sp"):
        self.kb.dma(eng, lambda e: e.dma_start(out=dst_ap, in_=src_ap),
                    reads=[src.r], writes=[], chain="st_" + src.name)

    def rmsnorm(self, h, w, g_ap, out, out_off=0, tag="n"):
        rw_ = getattr(self, "rn_w", 512)
        sq = self.rtile("rn_sq", 1, [128, KC, rw_], BF16)
        ss = self.rtile("rn_ss", 1, [128, 512], F32, psum=True)
        rstd = self.rtile("rn_rstd", 2, [128, rw_], F32)
        ones = self.tiles["onesD"]
        self.op("dve", lambda e: e.tensor_tensor(out=sq[:, :, 0:w], in0=h[:, :, 0:w], in1=h[:, :, 0:w],
                                                  op=ALU.mult), [h], [sq])
        for c in range(KC):
            self.op("pe", lambda e, c=c: e.matmul(ss[:, 0:w], lhsT=ones[:], rhs=sq[:, c, 0:w],
                                                  start=(c == 0), stop=(c == KC - 1)), [ones, sq], [ss])
        self.op("act", lambda e: e.activation(out=rstd[:, 0:w], in_=ss[:, 0:w], func=AF.Sqrt, bias=RMS_EPS,
                                              scale=1.0), [ss], [rstd])
        self.op("dve", lambda e: e.reciprocal(out=rstd[:, 0:w], in_=rstd[:, 0:w]), [rstd], [rstd])
        for c in range(KC):
            self.op("dve", lambda e, c=c: e.scalar_tensor_tensor(
                out=out[:, c, out_off:out_off + w], in0=h[:, c, 0:w], scalar=g_ap(c), in1=rstd[:, 0:w],
                op0=ALU.mult, op1=ALU.mult), [h, rstd, self.tiles["consts"]], [out])

    def setup_consts(self):
        onesD = self.tile("onesD", [128, 128], BF16)
        self.op("pool", lambda e: e.memset(onesD[:], 1.0 / D), [], [onesD])
        zt = self.tile("zeros", [128, PAD], F32)
        self.op("pool", lambda e: e.memset(zt[:], 0.0), [], [zt])
        self.din("norm_ffn_g", [128, DEPTH * KC])
        self.din("conv_w", [128, DEPTH * 44 * 3])
        self.din("conv_b", [128, DEPTH * 44])
        self.din("final_g", [128, KC])
        self.din("norm_mix_g", [128, DEPTH * KC])
        self.din("kv_norm_g", [128, KC])
        for nm, n in (("rw_mix", 2 * 6 * KC), ("rw_w0", 2 * KC), ("rw_a0", 2 * KC), ("rw_kk", 2 * KC),
                      ("rw_ka", 2 * KC), ("rw_rk", 2 * KC)):
            self.din(nm, [128, n])
        ncol = DEPTH * KC + DEPTH * 44 * 3 + DEPTH * 44 + KC + DEPTH * KC + KC + 2 * 6 * KC + 5 * 2 * KC
        consts = self.tile("consts", [128, ncol], F32)
        self.coff = {}
        off = 0
        for nm, n in (("norm_ffn_g", DEPTH * KC), ("conv_w", DEPTH * 44 * 3), ("conv_b", DEPTH * 44),
                      ("final_g", KC), ("norm_mix_g", DEPTH * KC), ("kv_norm_g", KC),
                      ("rw_mix", 2 * 6 * KC), ("rw_w0", 2 * KC), ("rw_a0", 2 * KC), ("rw_kk", 2 * KC),
                      ("rw_ka", 2 * KC), ("rw_rk", 2 * KC)):
            self.coff[nm] = off
            self.load(consts, consts[:, off:off + n], self.dram[nm][:, :], self.dr(nm))
            off += n

    def cc(self, nm, idx):
        o = self.coff[nm] + idx
        return self.tiles["consts"][:, o:o + 1]

    def ffn_tiles(self):
        tiles = []
        o = PAD
        while o < self.LP:
            ow = min(FT - 2, self.LP - o)
            tiles.append((o - 2, ow))
            o += ow
        return tiles

    def ffn_weights(self, layer):
        wup = self.tiles.get("wup") or self.tile("wup", [128, KC, F2], BF16)
        wdn = self.tiles.get("wdn") or self.tile("wdn", [128, NJ, D], BF16)
        up = self.dram["ffn_up"]
        dn = self.dram["ffn_down"]
        for c in range(KC):
            for hf in range(2):
                self.load(wup, wup[:, c, hf * DFF:(hf + 1) * DFF], up[layer, :, c, hf * DFF:(hf + 1) * DFF],
                          self.dr("ffn_up"), eng="pool")
        for j0 in range(0, NJ, 2):
            self.load(wdn, wdn[:, j0:j0 + 2, :], dn[layer, :, j0:j0 + 2, :], self.dr("ffn_down"), eng="pool")
        return wup, wdn

    def ffn_phase(self, layer, src, dst):
        self.phase_begin()
        wup, wdn = self.ffn_weights(layer)
        hsrc = self.dram[src].rearrange("c p t -> p c t")
        hdst = self.dram[dst].rearrange("c p t -> p c t")
        for ti, (i0, ow) in enumerate(self.ffn_tiles()):
            iw = ow + 2
            h = self.rtile("f_h", 2, [128, KC, FT], F32)
            self.load(h, h[:, :, 0:iw], hsrc[:, :, i0:i0 + iw], self.dr((src, "all")))
            hn = self.rtile("f_hn", 1, [128, KC, FT], BF16)
            self.rmsnorm(h, iw, lambda c: self.cc("norm_ffn_g", layer * KC + c), hn)
            m = self.rtile("f_m", 1, [128, NJ, FT], BF16)
            for j in range(NJ):
                ys = []
                for half, ch in enumerate((j, NJ + j)):
                    ps = self.rtile("f_ps", 4, [128, 512], F32, psum=True)
                    for c in range(KC):
                        self.op("pe", lambda e, c=c, ps=ps, ch=ch: e.matmul(
                            ps[:, 0:iw], lhsT=wup[:, c, ch * 128:(ch + 1) * 128], rhs=hn[:, c, 0:iw],
                            start=(c == 0), stop=(c == KC - 1)), [wup, hn], [ps])
                    y = self.rtile("f_y", 4, [128, FT], F32)
                    cw = lambda tap, ch=ch: self.cc("conv_w", (layer * 44 + ch) * 3 + tap)
                    cb = self.cc("conv_b", layer * 44 + ch)
                    cst = self.tiles["consts"]
                    self.op("act", lambda e, y=y, ps=ps, cw=cw, cb=cb: e.activation(
                        out=y[:, 0:ow], in_=ps[:, 2:2 + ow], func=AF.Identity, bias=cb, scale=cw(2)),
                        [ps, cst], [y])
                    self.op("dve", lambda e, y=y, ps=ps, cw=cw: e.scalar_tensor_tensor(
                        out=y[:, 0:ow], in0=ps[:, 1:1 + ow], scalar=cw(1), in1=y[:, 0:ow],
                        op0=ALU.mult, op1=ALU.add), [ps, y, cst], [y])
                    self.op("dve", lambda e, y=y, ps=ps, cw=cw: e.scalar_tensor_tensor(
                        out=y[:, 0:ow], in0=ps[:, 0:ow], scalar=cw(0), in1=y[:, 0:ow],
                        op0=ALU.mult, op1=ALU.add), [ps, y, cst], [y])
                    ys.append(y)
                yg, yv = ys
                self.op("act", lambda e, yg=yg: e.activation(out=yg[:, 0:ow], in_=yg[:, 0:ow], func=AF.Silu),
                        [yg], [yg])
                self.op("dve", lambda e, yg=yg, yv=yv, j=j: e.tensor_tensor(
                    out=m[:, j, 0:ow], in0=yg[:, 0:ow], in1=yv[:, 0:ow], op=ALU.mult), [yg, yv], [m])
            for n in range(KC):
                ps = self.rtile("f_ps", 4, [128, 512], F32, psum=True)
                for j in range(NJ):
                    self.op("pe", lambda e, j=j, n=n, ps=ps: e.matmul(
                        ps[:, 0:ow], lhsT=wdn[:, j, n * 128:(n + 1) * 128], rhs=m[:, j, 0:ow],
                        start=(j == 0), stop=(j == NJ - 1)), [wdn, m], [ps])
                ho = self.rtile("f_ho", 3, [128, FT], F32)
                self.op("dve", lambda e, n=n, ps=ps, ho=ho: e.tensor_tensor(
                    out=ho[:, 0:ow], in0=ps[:, 0:ow], in1=h[:, n, 2:2 + ow], op=ALU.add), [ps, h], [ho])
                self.store(hdst[:, n, i0 + 2:i0 + 2 + ow], self.dr((dst, "all")), ho, ho[:, 0:ow], eng="sp")
        self.phase_end()


    def load_w1024(self, name, dram_ap):
        w = self.tile(name, [128, KC, D], BF16)
        for c0 in range(0, KC, 2):
            self.load(w, w[:, c0:c0 + 2, :], dram_ap[:, c0:c0 + 2, :], self.dr("wts"), eng="pool")
        return w

    def blk_groups(self):
        gs = []
        b = 0
        while b < self.nb:
            n = min(4, self.nb - b)
            gs.append((b, n))
            b += n
        return gs

    def qkv_phase(self, j, src, do_kv):
        self.phase_begin()
        layer = 2 + j
        wq = self.load_w1024("wq", self.dram["sb_wq"][j])
        if do_kv:
            wk = self.load_w1024("wk", self.dram["sb_wk"])
            wv = self.load_w1024("wv", self.dram["sb_wv"])
        hsrc = self.dram[src].rearrange("c p t -> p c t")
        qT = self.dram["QT"].rearrange("c p t -> p c t")
        kT = self.dram["KT"].rearrange("c p t -> p c t")
        vd = self.dram["VV"]
        for (b0, nblk) in self.blk_groups():
            t0, tw = b0 * 128, nblk * 128
            h = self.rtile("q_h", 2, [128, KC, 512], F32)
            self.load(h, h[:, :, 0:tw], hsrc[:, :, t0:t0 + tw], self.dr((src, "all")))
            hn = self.rtile("q_hn", 2, [128, KC, 512], BF16)
            self.rmsnorm(h, tw, lambda c: self.cc("norm_mix_g", layer * KC + c), hn)
            jobs = [(wq, qT, 0.125, "QT")]
            if do_kv:
                kn = self.rtile("q_kn", 2, [128, KC, 512], BF16)
                self.rmsnorm(h, tw, lambda c: self.cc("kv_norm_g", c), kn)
                jobs.append((wk, kT, 1.0, "KT"))
            for (w, dst, scale, dname) in jobs:
                xin = hn if dname == "QT" else kn
                for n in range(KC):
                    ps = self.rtile("q_ps", 4, [128, 512], F32, psum=True)
                    for c in range(KC):
                        self.op("pe", lambda e: e.matmul(ps[:, 0:tw], lhsT=w[:, c, n * 128:(n + 1) * 128],
                                                         rhs=xin[:, c, 0:tw], start=(c == 0), stop=(c == KC - 1)),
                                [w, xin], [ps])
                    o = self.rtile("q_o", 4, [128, 512], BF16)
                    self.op("act", lambda e: e.activation(out=o[:, 0:tw], in_=ps[:, 0:tw], func=AF.Copy,
                                                          scale=scale), [ps], [o])
                    self.store(dst[:, n, t0:t0 + tw], self.dr((dname, "all")), o, o[:, 0:tw])
            if do_kv:
                for bi in range(nblk):
                    for hf in range(2):
                        ps = self.rtile("q_ps", 4, [128, 512], F32, psum=True)
                        for c in range(KC):
                            self.op("pe", lambda e: e.matmul(ps[:, :], lhsT=kn[:, c, bi * 128:(bi + 1) * 128],
                                                             rhs=wv[:, c, hf * 512:(hf + 1) * 512],
                                                             start=(c == 0), stop=(c == KC - 1)), [wv, kn], [ps])
                        o = self.rtile("q_o", 4, [128, 512], BF16)
                        self.op("dve", lambda e: e.tensor_copy(out=o[:, :], in_=ps[:, :]), [ps], [o])
                        r0 = (b0 + bi) * 128
                        self.store(vd[r0:r0 + 128, hf * 512:(hf + 1) * 512], self.dr(("VV", "all")), o, o[:, :])
        self.phase_end()

    def att_sweep(self, slot, kt, qt, vt, at, msk, ntri, ones, hh, b0, ng):
        pb = 64 * hh
        W = ng * 128
        q0 = b0 * 128
        hi = b0 + ng - 1
        outp = slot["out"]
        z = slot["z"]
        carry = None
        for kb in range(hi, -1, -1):
            self.op("pe", lambda e: e.matmul(z[:, 0:W], lhsT=kt[pb:pb + 64, kb * 128:(kb + 1) * 128],
                                             rhs=qt[pb:pb + 64, q0:q0 + W], start=True, stop=False), [kt, qt], [z])
            yield
            ex = slot["ex"][kb % 2]
            sp = slot["sp"][kb % 2]
            self.op("act", lambda e: e.activation(out=ex[:, 0:W], in_=z[:, 0:W], func=AF.Exp), [z], [ex])
            self.op("act", lambda e: e.activation(out=sp[:, 0:W], in_=ex[:, 0:W], func=AF.Ln, bias=1.0, scale=1.0),
                    [ex], [sp])
            mi = None
            if kb >= b0:
                mi = 4 if kb == 0 else kb - b0
            elif kb == 0:
                mi = 5
            if mi is not None:
                self.op("pool", lambda e: e.tensor_tensor(out=sp[:, 0:W], in0=sp[:, 0:W], in1=msk[:, mi, 0:W],
                                                          op=ALU.mult), [sp, msk], [sp])
            yield
            self.op("pe", lambda e: e.matmul(z[:, 0:W], lhsT=ntri[:], rhs=sp[:, 0:W], start=False, stop=True),
                    [ntri, sp], [z])
            cs = None
            if kb > 0:
                cs = slot["cs"]
                self.op("pe", lambda e: e.matmul(cs[:, 0:W], lhsT=ones[:], rhs=sp[:, 0:W], start=True, stop=True),
                        [ones, sp], [cs])
            yield
            w = slot["w"][kb % 2]
            if carry is None:
                self.op("act", lambda e: e.activation(out=w[:, 0:W], in_=z[:, 0:W], func=AF.Exp), [z], [w])
            else:
                arg = slot["arg"][kb % 2]
                self.op("dve", lambda e: e.tensor_tensor(out=arg[:, 0:W], in0=z[:, 0:W], in1=carry[:, 0:W],
                                                         op=ALU.subtract), [z, carry], [arg])
                yield
                self.op("act", lambda e: e.activation(out=w[:, 0:W], in_=arg[:, 0:W], func=AF.Exp), [arg], [w])
            if mi is not None:
                self.op("pool", lambda e: e.tensor_tensor(out=w[:, 0:W], in0=w[:, 0:W], in1=msk[:, mi, 0:W],
                                                          op=ALU.mult), [w, msk], [w])
            if kb > 0:
                ncar = slot["carry"][kb % 2]
                if carry is None:
                    self.op("dve", lambda e: e.tensor_copy(out=ncar[:, 0:W], in_=cs[:, 0:W]), [cs], [ncar])
                else:
                    self.op("dve", lambda e: e.tensor_tensor(out=ncar[:, 0:W], in0=cs[:, 0:W], in1=carry[:, 0:W],
                                                             op=ALU.add), [cs, carry], [ncar])
                carry = ncar
            yield
            self.op("pe", lambda e: e.matmul(outp[pb:pb + 64, 0:W], lhsT=vt[:, kb, pb:pb + 64], rhs=w[:, 0:W],
                                             start=(kb == hi), stop=(kb == 0)), [vt, w], [outp])
        yield
        self.op("dve", lambda e: e.tensor_copy(out=at[pb:pb + 64, q0:q0 + W], in_=outp[pb:pb + 64, 0:W]), [outp], [at])

    def att_phase(self, nslots=3):
        self.phase_begin()
        nb, LP = self.nb, self.LP
        msk = self.tile("amask", [128, 6, 512], BF16)
        self.load(msk, msk[:], self.dram["amask"][:, :, :], eng="pool")
        ntri = self.tile("ntri", [128, 128], BF16)
        self.load(ntri, ntri[:], self.dram["ntri"][:, :], eng="pool")
        ones = self.tile("ones1", [128, 128], BF16)
        self.op("pool", lambda e: e.memset(ones[:], 1.0), [], [ones])
        qT = self.dram["QT"]
        kT = self.dram["KT"]
        vd = self.dram["VV"].rearrange("(b s) n -> s b n", s=128)
        aT = self.dram["AT"]
        slots = []
        outA = self.tile("a_outA", [128, 512], F32, psum=True)
        outB = self.tile("a_outB", [128, 512], F32, psum=True)
        for si in range(nslots):
            slots.append({
                "out": outA if si < 2 else outB,
                "outr": None,
                "cs": self.tile("a_cs%d" % si, [128, 512], F32, psum=True),
                "z": self.tile("a_z%d" % si, [128, 512], F32, psum=True),
                "ex": [self.tile("a_e%d_%d" % (si, k), [128, 512], F32) for k in range(2)],
                "sp": [self.tile("a_sp%d_%d" % (si, k), [128, 512], BF16) for k in range(2)],
                "w": [self.tile("a_w%d_%d" % (si, k), [128, 512], BF16) for k in range(2)],
                "arg": [self.tile("a_arg%d_%d" % (si, k), [128, 512], F32) for k in range(2)],
                "carry": [self.tile("a_car%d_%d" % (si, k), [128, 512], F32) for k in range(2)],
            })
        groups = self.blk_groups()
        for c in range(KC):
            kt = self.rtile("a_k", 2, [128, LP], BF16)
            qt = self.rtile("a_q", 2, [128, LP], BF16)
            vt = self.rtile("a_v", 2, [128, nb, 128], BF16)
            at = self.rtile("a_o", 2, [128, LP], BF16)
            self.load(kt, kt[:], kT[c])
            self.load(qt, qt[:], qT[c])
            self.load(vt, vt[:], vd[:, :, c * 128:(c + 1) * 128])
            todo = [[(hh, b0, ng) for (b0, ng) in reversed(groups)] for hh in range(2)]
            active = [None] * nslots
            while todo[0] or todo[1] or any(a is not None for a in active):
                for si in range(nslots):
                    if active[si] is None and (todo[0] or todo[1]):
                        if si < 2:
                            lst = todo[si]
                        else:
                            lst = todo[0] if len(todo[0]) >= len(todo[1]) else todo[1]
                        if lst:
                            hh, b0, ng = lst.pop(0)
                            active[si] = self.att_sweep(slots[si], kt, qt, vt, at, msk, ntri, ones, hh, b0, ng)
                    if active[si] is not None:
                        try:
                            next(active[si])
                        except StopIteration:
                            active[si] = None
            self.store(aT[c], None, at, at[:])
        self.phase_end()

    def o_phase(self, j, src, dst):
        self.phase_begin()
        wo = self.load_w1024("wo", self.dram["sb_wo"][j])
        hsrc = self.dram[src].rearrange("c p t -> p c t")
        hdst = self.dram[dst].rearrange("c p t -> p c t")
        aT = self.dram["AT"].rearrange("c p t -> p c t")
        for (b0, nblk) in self.blk_groups():
            t0, tw = b0 * 128, nblk * 128
            v0 = PAD if b0 == 0 else 0
            h = self.rtile("o_h", 2, [128, KC, 512], F32)
            self.load(h, h[:, :, 0:tw], hsrc[:, :, t0:t0 + tw], self.dr((src, "all")))
            a = self.rtile("o_a", 2, [128, KC, 512], BF16)
            self.load(a, a[:, :, 0:tw], aT[:, :, t0:t0 + tw], self.dr(("AT", "all")))
            for n in range(KC):
                ps = self.rtile("o_ps", 4, [128, 512], F32, psum=True)
                for c in range(KC):
                    self.op("pe", lambda e: e.matmul(ps[:, 0:tw], lhsT=wo[:, c, n * 128:(n + 1) * 128],
                                                     rhs=a[:, c, 0:tw], start=(c == 0), stop=(c == KC - 1)),
                            [wo, a], [ps])
                ho = self.rtile("o_ho", 3, [128, 512], F32)
                self.op("dve", lambda e: e.tensor_tensor(out=ho[:, 0:tw], in0=ps[:, 0:tw], in1=h[:, n, 0:tw],
                                                         op=ALU.add), [ps, h], [ho])
                self.store(hdst[:, n, t0 + v0:t0 + tw], self.dr((dst, "all")), ho, ho[:, v0:tw])
        self.phase_end()


    def pool_init(self, n):
        self.tpool = [self.tile("tp%d" % i, [128, D], F32) for i in range(n)]

    def tget(self):
        return self.tpool.pop(0)

    def tfree(self, *ts):
        for t in ts:
            self.tpool.append(t)

    @staticmethod
    def fm(t, lo=0, hi=128):
        return t.t[:].rearrange("p (c t) -> p c t", c=KC)[:, :, lo:hi]

    def bc(self, nm, idx0):
        o = self.coff[nm] + idx0
        return self.tiles["consts"][:, o:o + KC].unsqueeze(2).broadcast_to([128, KC, 128])

    def rwkv_phase(self, i, src, dst):
        self.phase_begin()
        self.rn_w = 128
        nb, LP = self.nb, self.LP
        layer = i
        cst = self.tiles["consts"]
        dr = self.dram
        wr = self.load_w1024("wr", dr["rw_wr"][i])
        wk = self.load_w1024("wk", dr["rw_wk"][i])
        wv = self.load_w1024("wv", dr["rw_wv"][i])
        wo = self.load_w1024("wo", dr["rw_wo"][i])

        def ldw(name, shape, ap):
            t = self.tile(name, shape, BF16)
            self.load(t, t[:], ap, eng="pool")
            return t
        w1 = ldw("w1", [128, KC, 64], dr["rw_w1"][i])
        a1 = ldw("a1", [128, KC, 64], dr["rw_a1"][i])
        g1 = ldw("g1", [128, KC, 160], dr["rw_g1"][i])
        w2 = ldw("w2", [64, D], dr["rw_w2"][i])
        a2 = ldw("a2", [64, D], dr["rw_a2"][i])
        g2a = ldw("g2a", [128, D], dr["rw_g2"][i, 0:128, :])
        g2b = ldw("g2b", [32, D], dr["rw_g2"][i, 128:160, :])
        if i > 0:
            v1 = ldw("v1", [128, KC, 32], dr["rw_v1"][i - 1])
            v2 = ldw("v2", [32, D], dr["rw_v2"][i - 1])
        rc = self.tile("rwc", [128, 128 * 5 + 2 + 256], F32)
        self.load(rc, rc[:], dr["rw_const"][:, :])
        ident = rc[:, 0:128]
        bd = rc[:, 128:256]
        mstrict = rc[:, 256:384]
        mt2 = rc[:, 384:640]
        ind2 = rc[:, 640:642]
        onesf = rc[:, 642:770]
        zcol = rc[:, 770:771]
        tmb = self.tile("tmb", [128, 3 if i > 0 else 2, D], F32)
        self.load(tmb, tmb[:, 0, :], dr["rw_lnx_g"][i:i + 1, :].partition_broadcast(128))
        self.load(tmb, tmb[:, 1, :], dr["rw_lnx_b"][i:i + 1, :].partition_broadcast(128))
        if i > 0:
            self.load(tmb, tmb[:, 2, :], dr["rw_v0"][i - 1:i, :].partition_broadcast(128))
        self.pool_init(10)
        st2 = [self.tile("st2_%d" % c, [128, 128], F32) for c in range(KC)]
        for c in range(KC):
            self.op("pool", lambda e: e.memset(st2[c][:], 0.0), [], [st2[c]])
        hsrc = dr[src].rearrange("c p t -> p c t")
        hdst = dr[dst].rearrange("c p t -> p c t")
        vf = dr["VF"]
        hn_prev = None
        ART = self.tile("AR", [128, KC, 256], F32)
        btT = self.tile("bt", [128, KC, 128], F32)
        ktT = self.tile("kt", [128, KC, 128], F32)
        pcT = self.tile("pc", [128, KC], F32)
        ssb = self.tile("ssb", [128, NH], F32)
        mu = self.tile("mu", [128, NH], F32)
        var = self.tile("var", [128, NH], F32)
        midw = self.tile("midw", [64, 128], BF16)
        mida = self.tile("mida", [64, 128], BF16)
        midga = self.tile("midga", [128, 128], BF16)
        midgb = self.tile("midgb", [32, 128], BF16)
        midv = self.tile("midv", [32, 128], BF16)
        ygT = self.tile("yg", [128, KC, 128], BF16)
        C0 = 0.6065306597126334

        def pbig():
            return self.rtile("pbig", 2, [128, D], F32, psum=True)

        def psm():
            return self.rtile("psm", 3, [128, 512], F32, psum=True)

        for cb in range(nb):
            t0 = cb * 128
            h = self.rtile("r_h", 2, [128, KC, 128], F32)
            self.load(h, h[:], hsrc[:, :, t0:t0 + 128])
            hn = self.rtile("r_hn", 1, [128, KC, 129], F32)
            if hn_prev is None:
                self.op("pool", lambda e: e.memset(hn[:, :, 0:1], 0.0), [], [hn])
            else:
                self.op("pool", lambda e: e.tensor_copy(out=hn[:, :, 0:1], in_=hn[:, :, 128:129]), [hn], [hn])
            self.rmsnorm(h, 128, lambda c: self.cc("norm_mix_g", layer * KC + c), hn, out_off=1)
            hn_prev = hn
            xx = self.tget()
            self.op("dve", lambda e: e.tensor_tensor(out=self.fm(xx), in0=hn[:, :, 0:128], in1=hn[:, :, 1:129],
                                                     op=ALU.subtract), [hn], [xx])
            def mkx(q):
                x = self.rtile("xq", 3, [128, KC, 128], BF16)
                tm = self.tget()
                eng = "dve" if q % 2 == 0 else "pool"
                self.op(eng, lambda e: e.tensor_tensor(out=self.fm(tm), in0=self.fm(xx),
                                                       in1=self.bc("rw_mix", (i * 6 + q) * KC), op=ALU.mult),
                        [xx, cst], [tm])
                self.op(eng, lambda e: e.tensor_tensor(out=x[:], in0=self.fm(tm), in1=hn[:, :, 1:129],
                                                       op=ALU.add), [tm, hn], [x])
                self.tfree(tm)
                return x
            rT = self.tget()
            kT_ = self.tget()
            vS = self.tget()
            for (w, qi, dstt) in ((wr, 0, rT), (wk, 2, kT_)):
                x = mkx(qi)
                ps = pbig()
                for n in range(KC):
                    for c in range(KC):
                        self.op("pe", lambda e: e.matmul(ps[:, n * 128:(n + 1) * 128],
                                                         lhsT=w[:, c, n * 128:(n + 1) * 128], rhs=x[:, c, :],
                                                         start=(c == 0), stop=(c == KC - 1)), [w, x], [ps])
                self.op("act", lambda e: e.copy(out=dstt[:], in_=ps[:]), [ps], [dstt])
            xv = mkx(3)
            ps = pbig()
            for hf in range(2):
                for c in range(KC):
                    self.op("pe", lambda e: e.matmul(ps[:, hf * 512:(hf + 1) * 512], lhsT=xv[:, c, :],
                                                     rhs=wv[:, c, hf * 512:(hf + 1) * 512],
                                                     start=(c == 0), stop=(c == KC - 1)), [wv, xv], [ps])
            self.op("act", lambda e: e.copy(out=vS[:], in_=ps[:]), [ps], [vS])
            for (wt, qi, ncol, mid, fn) in (((v1, 3, 32, midv, AF.Copy),) if i > 0 else ()) + \
                    ((w1, 1, 64, midw, AF.Tanh), (a1, 4, 64, mida, AF.Copy),
                     (g1, 5, 128, midga, AF.Sigmoid), (g1, 5, 32, midgb, AF.Sigmoid)):
                x = xv if qi == 3 else (x if (mid is midgb) else mkx(qi))
                ps = psm()
                c0 = 128 if mid is midgb else 0
                for c in range(KC):
                    self.op("pe", lambda e: e.matmul(ps[0:ncol, 0:128], lhsT=wt[:, c, c0:c0 + ncol], rhs=x[:, c, :],
                                                     start=(c == 0), stop=(c == KC - 1)), [wt, x], [ps])
                self.op("act", lambda e: e.activation(out=mid[0:ncol, :], in_=ps[0:ncol, 0:128], func=fn), [ps], [mid])
            self.tfree(xx)
            sig = self.tget()
            ps = pbig()
            for n in range(KC):
                self.op("pe", lambda e: e.matmul(ps[:, n * 128:(n + 1) * 128], lhsT=w2[0:64, n * 128:(n + 1) * 128],
                                                 rhs=midw[0:64, :], start=True, stop=True), [w2, midw], [ps])
            self.op("dve", lambda e: e.tensor_tensor(out=self.fm(sig), in0=ps[:].rearrange("p (c t) -> p c t", c=KC),
                                                     in1=self.bc("rw_w0", i * KC), op=ALU.add), [ps, cst], [sig])
            self.op("act", lambda e: e.activation(out=sig[:], in_=sig[:], func=AF.Sigmoid), [sig], [sig])
            aT_ = self.tget()
            ps = pbig()
            for n in range(KC):
                self.op("pe", lambda e: e.matmul(ps[:, n * 128:(n + 1) * 128], lhsT=a2[0:64, n * 128:(n + 1) * 128],
                                                 rhs=mida[0:64, :], start=True, stop=True), [a2, mida], [ps])
            self.op("dve", lambda e: e.tensor_tensor(out=self.fm(aT_), in0=ps[:].rearrange("p (c t) -> p c t", c=KC),
                                                     in1=self.bc("rw_a0", i * KC), op=ALU.add), [ps, cst], [aT_])
            self.op("act", lambda e: e.activation(out=aT_[:], in_=aT_[:], func=AF.Sigmoid), [aT_], [aT_])
            if i > 0:
                vg = self.tget()
                vfT = self.tget()
                self.load(vfT, vfT[:], vf[t0:t0 + 128, :])
                ps = pbig()
                for hf in range(2):
                    self.op("pe", lambda e: e.matmul(ps[:, hf * 512:(hf + 1) * 512], lhsT=midv[0:32, :],
                                                     rhs=v2[0:32, hf * 512:(hf + 1) * 512], start=True, stop=True),
                            [v2, midv], [ps])
                self.op("dve", lambda e: e.tensor_tensor(out=vg[:], in0=ps[:], in1=tmb[:, 2, :], op=ALU.add),
                        [ps, tmb], [vg])
                self.op("act", lambda e: e.activation(out=vg[:], in_=vg[:], func=AF.Sigmoid), [vg], [vg])
                self.op("pool", lambda e: e.tensor_tensor(out=vfT[:], in0=vfT[:], in1=vS[:], op=ALU.subtract),
                        [vfT, vS], [vfT])
                self.op("dve", lambda e: e.tensor_tensor(out=vfT[:], in0=vfT[:], in1=vg[:], op=ALU.mult),
                        [vfT, vg], [vfT])
                self.op("dve", lambda e: e.tensor_tensor(out=vS[:], in0=vS[:], in1=vfT[:], op=ALU.add),
                        [vS, vfT], [vS])
                self.tfree(vg, vfT)
            else:
                self.store(vf[t0:t0 + 128, :], None, vS, vS[:])
            kk = self.tget()
            self.op("dve", lambda e: e.tensor_tensor(out=self.fm(kk), in0=self.fm(kT_), in1=self.bc("rw_kk", i * KC),
                                                     op=ALU.mult), [kT_, cst], [kk])
            ksq = self.tget()
            self.op("pool", lambda e: e.tensor_tensor(out=ksq[:], in0=kk[:], in1=kk[:], op=ALU.mult), [kk], [ksq])
            ps = pbig()
            for c in range(KC):
                self.op("pe", lambda e: e.matmul(ps[:, c * 128:(c + 1) * 128], lhsT=bd, rhs=ksq[:, c * 128:(c + 1) * 128],
                                                 start=True, stop=True), [rc, ksq], [ps])
            self.op("act", lambda e: e.activation(out=ksq[:], in_=ps[:], func=AF.Sqrt), [ps], [ksq])
            self.op("dve", lambda e: e.tensor_scalar(out=ksq[:], in0=ksq[:], scalar1=1e-12, scalar2=None, op0=ALU.max),
                    [ksq], [ksq])
            self.op("dve", lambda e: e.reciprocal(out=ksq[:], in_=ksq[:]), [ksq], [ksq])
            self.op("dve", lambda e: e.tensor_tensor(out=kk[:], in0=kk[:], in1=ksq[:], op=ALU.mult), [kk, ksq], [kk])
            self.tfree(ksq)
            km = self.tget()
            self.op("dve", lambda e: e.scalar_tensor_tensor(out=self.fm(km), in0=self.fm(aT_), scalar=-1.0,
                                                            in1=self.bc("rw_ka", i * KC), op0=ALU.add, op1=ALU.mult),
                    [aT_, cst], [km])
            self.op("dve", lambda e: e.scalar_tensor_tensor(out=km[:], in0=km[:], scalar=1.0, in1=kT_[:],
                                                            op0=ALU.add, op1=ALU.mult), [km, kT_], [km])
            self.tfree(kT_)
            cs = self.tget()
            for c in range(KC):
                self.op("dve", lambda e: e.tensor_tensor_scan(out=cs[:, c * 128:(c + 1) * 128], data0=onesf,
                                                              data1=sig[:, c * 128:(c + 1) * 128], initial=zcol,
                                                              op0=ALU.mult, op1=ALU.add), [sig, rc], [cs])
            ein = self.tget()
            eneg = self.tget()
            self.op("act", lambda e: e.activation(out=ein[:], in_=cs[:], func=AF.Exp, scale=-C0), [cs], [ein])
            self.op("act", lambda e: e.activation(out=eneg[:], in_=cs[:], func=AF.Exp, scale=C0), [cs], [eneg])
            self.op("pool", lambda e: e.tensor_tensor(out=cs[:], in0=cs[:], in1=sig[:], op=ALU.subtract), [cs, sig], [cs])
            self.op("act", lambda e: e.activation(out=cs[:], in_=cs[:], func=AF.Exp, scale=-C0), [cs], [cs])
            self.tfree(sig)
            self.op("dve", lambda e: e.scalar_tensor_tensor(out=ART[:, :, 0:128], in0=self.fm(kk), scalar=-1.0,
                                                            in1=self.fm(cs), op0=ALU.mult, op1=ALU.mult),
                    [kk, cs], [ART])
            self.op("pool", lambda e: e.tensor_tensor(out=ART[:, :, 128:256], in0=self.fm(rT), in1=self.fm(ein),
                                                      op=ALU.mult), [rT, ein], [ART])
            self.tfree(cs)
            self.op("dve", lambda e: e.tensor_tensor(out=btT[:], in0=self.fm(kk), in1=self.fm(aT_), op=ALU.mult),
                    [kk, aT_], [btT])
            self.op("dve", lambda e: e.tensor_tensor(out=btT[:], in0=btT[:], in1=self.fm(eneg), op=ALU.mult),
                    [btT, eneg], [btT])
            self.tfree(aT_, kk)
            self.op("pool", lambda e: e.tensor_tensor(out=ktT[:], in0=self.fm(km), in1=self.fm(eneg), op=ALU.mult),
                    [km, eneg], [ktT])
            self.tfree(eneg)
            self.op("act", lambda e: e.copy(out=pcT[:, :].unsqueeze(2), in_=self.fm(ein, 127, 128)), [ein], [pcT])
            self.tfree(ein)
            self.op("dve", lambda e: e.tensor_tensor(out=rT[:], in0=rT[:], in1=km[:], op=ALU.mult), [rT, km], [rT])
            self.op("dve", lambda e: e.tensor_tensor(out=self.fm(rT), in0=self.fm(rT), in1=self.bc("rw_rk", i * KC),
                                                     op=ALU.mult), [rT, cst], [rT])
            ps = psm()
            for c in range(KC):
                self.op("pe", lambda e: e.matmul(ps[:, 2 * c:2 * c + 2], lhsT=rT[:, c * 128:(c + 1) * 128], rhs=ind2,
                                                 start=True, stop=True), [rT, rc], [ps])
            self.op("act", lambda e: e.copy(out=ssb[:], in_=ps[:, 0:NH]), [ps], [ssb])
            self.tfree(rT, km)
            pcb = pcT[:, :].unsqueeze(2).broadcast_to([128, KC, 128])
            KH = self.tget()
            BH = self.tget()
            for (srcT, dstT) in ((ktT, KH), (btT, BH)):
                tmp = self.tget()
                self.op("dve", lambda e: e.tensor_tensor(out=self.fm(tmp), in0=srcT[:], in1=pcb, op=ALU.mult),
                        [srcT, pcT], [tmp])
                ps = pbig()
                for c in range(KC):
                    self.op("pe", lambda e: e.transpose(ps[:, c * 128:(c + 1) * 128], tmp[:, c * 128:(c + 1) * 128],
                                                        ident), [tmp, rc], [ps])
                self.op("act", lambda e: e.copy(out=dstT[:], in_=ps[:]), [ps], [dstT])
                self.tfree(tmp)
            yT = self.tget()
            for c in range(KC):
                sk = self.rtile("sk", 2, [128, 2, 256], F32)
                sbm = self.rtile("sbm", 2, [128, 2, 256], F32)
                sx = self.rtile("sx", 3, [128, 2, 192], F32)
                p1 = psm()
                p2 = psm()
                p3 = psm()
                for hh in range(2):
                    pb = 64 * hh
                    self.op("pe", lambda e: e.matmul(p1[:, hh * 256:(hh + 1) * 256], lhsT=ktT[pb:pb + 64, c, :],
                                                     rhs=ART[pb:pb + 64, c, :], start=True, stop=True), [ktT, ART], [p1])
                    self.op("pe", lambda e: e.matmul(p2[:, hh * 256:(hh + 1) * 256], lhsT=btT[pb:pb + 64, c, :],
                                                     rhs=ART[pb:pb + 64, c, :], start=True, stop=True), [btT, ART], [p2])
                    self.op("pe", lambda e: e.matmul(p3[:, hh * 128:(hh + 1) * 128], lhsT=ART[pb:pb + 64, c, 0:128],
                                                     rhs=btT[pb:pb + 64, c, :], start=True, stop=True), [btT, ART], [p3])
                mt2b = mt2.unsqueeze(1).broadcast_to([128, 2, 256])
                msb = mstrict.unsqueeze(1).broadcast_to([128, 2, 128])
                self.op("dve", lambda e: e.tensor_tensor(out=sk[:], in0=p1[:].rearrange("p (h x) -> p h x", h=2),
                                                         in1=mt2b, op=ALU.mult), [p1, rc], [sk])
                self.op("dve", lambda e: e.tensor_tensor(out=sbm[:], in0=p2[:].rearrange("p (h x) -> p h x", h=2),
                                                         in1=mt2b, op=ALU.mult), [p2, rc], [sbm])
                self.op("dve", lambda e: e.tensor_tensor(out=sx[:, :, 0:128],
                                                         in0=p3[:, 0:256].rearrange("p (h x) -> p h x", h=2),
                                                         in1=msb, op=ALU.mult), [p3, rc], [sx])
                px = psm()
                self.op("pe", lambda e: e.matmul(px[:, 0:128], lhsT=ART[:, c, 0:128], rhs=st2[c][:], start=True,
                                                 stop=False), [ART, st2[c]], [px])
                for hh in range(2):
                    hd = 2 * c + hh
                    self.op("pe", lambda e: e.matmul(px[:, hh * 64:(hh + 1) * 64], lhsT=sk[:, hh, 0:128],
                                                     rhs=vS[:, hd * 64:(hd + 1) * 64], start=False, stop=(hh == 1)),
                            [sk, vS], [px])
                self.op("act", lambda e: e.copy(out=sx[:, :, 128:192],
                                                in_=px[:, 0:128].rearrange("p (h x) -> p h x", h=2)), [px], [sx])
                tcur = None
                for k in range(7):
                    pd = psm()
                    for hh in range(2):
                        tk = sbm[:, hh, 0:128] if tcur is None else tcur[:, hh, :]
                        ncol = 192 if k < 6 else 64
                        c0 = 0 if k < 6 else 128
                        self.op("pe", lambda e: e.matmul(pd[:, hh * 192:hh * 192 + ncol], lhsT=tk,
                                                         rhs=sx[:, hh, c0:c0 + ncol], start=True, stop=True),
                                [sbm if tcur is None else tcur, sx], [pd])
                    if k < 6:
                        pt = psm()
                        for hh in range(2):
                            tk = sbm[:, hh, 0:128] if tcur is None else tcur[:, hh, :]
                            self.op("pe", lambda e: e.matmul(pt[:, hh * 128:(hh + 1) * 128], lhsT=sx[:, hh, 0:128],
                                                             rhs=tk, start=True, stop=True),
                                    [sbm if tcur is None else tcur, sx], [pt])
                        sxn = self.rtile("sx", 3, [128, 2, 192], F32)
                        pdv = pd[:, 0:384].rearrange("p (h x) -> p h x", h=2)
                        self.op("act", lambda e: e.copy(out=sxn[:, :, 0:128], in_=pdv[:, :, 0:128]), [pd], [sxn])
                        self.op("dve", lambda e: e.tensor_tensor(out=sxn[:, :, 128:192], in0=pdv[:, :, 128:192],
                                                                 in1=sx[:, :, 128:192], op=ALU.add), [pd, sx], [sxn])
                        tn = self.rtile("tt", 2, [128, 2, 128], F32)
                        self.op("act", lambda e: e.copy(out=tn[:], in_=pt[:, 0:256].rearrange("p (h x) -> p h x", h=2)),
                                [pt], [tn])
                        sx = sxn
                        tcur = tn
                    else:
                        sxn = self.rtile("sx", 3, [128, 2, 192], F32)
                        pdv = pd[:, 0:384].rearrange("p (h x) -> p h x", h=2)
                        self.op("dve", lambda e: e.tensor_tensor(out=sxn[:, :, 128:192], in0=pdv[:, :, 0:64],
                                                                 in1=sx[:, :, 128:192], op=ALU.add), [pd, sx], [sxn])
                        sx = sxn
                py = psm()
                self.op("pe", lambda e: e.matmul(py[:, 0:128], lhsT=ART[:, c, 128:256], rhs=st2[c][:], start=True,
                                                 stop=False), [ART, st2[c]], [py])
                for hh in range(2):
                    hd = 2 * c + hh
                    self.op("pe", lambda e: e.matmul(py[:, hh * 64:(hh + 1) * 64], lhsT=sbm[:, hh, 128:256],
                                                     rhs=sx[:, hh, 128:192], start=False, stop=False), [sbm, sx], [py])
                    self.op("pe", lambda e: e.matmul(py[:, hh * 64:(hh + 1) * 64], lhsT=sk[:, hh, 128:256],
                                                     rhs=vS[:, hd * 64:(hd + 1) * 64], start=False, stop=(hh == 1)),
                            [sk, vS], [py])
                self.op("act", lambda e: e.copy(out=yT[:, c * 128:(c + 1) * 128], in_=py[:, 0:128]), [py], [yT])
                pst = psm()
                for hh in range(2):
                    hd = 2 * c + hh
                    pb = 64 * hh
                    self.op("pe", lambda e: e.matmul(pst[pb:pb + 64, hh * 64:(hh + 1) * 64],
                                                     lhsT=BH[:, hd * 64:(hd + 1) * 64], rhs=sx[:, hh, 128:192],
                                                     start=True, stop=False), [BH, sx], [pst])
                    self.op("pe", lambda e: e.matmul(pst[pb:pb + 64, hh * 64:(hh + 1) * 64],
                                                     lhsT=KH[:, hd * 64:(hd + 1) * 64], rhs=vS[:, hd * 64:(hd + 1) * 64],
                                                     start=False, stop=True), [KH, vS], [pst])
                for hh in range(2):
                    pb = 64 * hh
                    self.op("dve", lambda e: e.scalar_tensor_tensor(
                        out=st2[c][pb:pb + 64, hh * 64:(hh + 1) * 64], in0=st2[c][pb:pb + 64, hh * 64:(hh + 1) * 64],
                        scalar=pcT[pb:pb + 64, c:c + 1], in1=pst[pb:pb + 64, hh * 64:(hh + 1) * 64],
                        op0=ALU.mult, op1=ALU.add), [st2[c], pcT, pst], [st2[c]])
            self.tfree(KH, BH)
            y3 = yT.t[:].rearrange("p (h x) -> p h x", h=NH)
            self.op("dve", lambda e: e.tensor_reduce(out=mu[:], in_=y3, axis=AX.X, op=ALU.add), [yT], [mu])
            mub = mu[:, :].unsqueeze(2).broadcast_to([128, NH, HD])
            self.op("dve", lambda e: e.scalar_tensor_tensor(out=y3, in0=mub, scalar=-1.0 / HD, in1=y3, op0=ALU.mult,
                                                            op1=ALU.add), [yT, mu], [yT])
            sq = self.tget()
            self.op("pool", lambda e: e.tensor_tensor(out=sq[:], in0=yT[:], in1=yT[:], op=ALU.mult), [yT], [sq])
            self.op("dve", lambda e: e.tensor_reduce(out=var[:], in_=sq.t[:].rearrange("p (h x) -> p h x", h=NH),
                                                     axis=AX.X, op=ALU.add), [sq], [var])
            self.op("act", lambda e: e.activation(out=var[:], in_=var[:], func=AF.Sqrt, bias=GN_EPS, scale=1.0 / HD),
                    [var], [var])
            self.op("dve", lambda e: e.reciprocal(out=var[:], in_=var[:]), [var], [var])
            varb = var[:, :].unsqueeze(2).broadcast_to([128, NH, HD])
            self.op("dve", lambda e: e.tensor_tensor(out=y3, in0=y3, in1=varb, op=ALU.mult), [yT, var], [yT])
            self.op("pool", lambda e: e.tensor_tensor(out=yT[:], in0=yT[:], in1=tmb[:, 0, :], op=ALU.mult), [yT, tmb], [yT])
            self.op("pool", lambda e: e.tensor_tensor(out=yT[:], in0=yT[:], in1=tmb[:, 1, :], op=ALU.add), [yT, tmb], [yT])
            ssbb = ssb[:, :].unsqueeze(2).broadcast_to([128, NH, HD])
            self.op("dve", lambda e: e.tensor_tensor(out=sq.t[:].rearrange("p (h x) -> p h x", h=NH),
                                                     in0=vS.t[:].rearrange("p (h x) -> p h x", h=NH), in1=ssbb,
                                                     op=ALU.mult), [vS, ssb], [sq])
            self.op("dve", lambda e: e.tensor_tensor(out=yT[:], in0=yT[:], in1=sq[:], op=ALU.add), [yT, sq], [yT])
            self.tfree(sq, vS)
            gS = self.tget()
            ps = pbig()
            for n in range(KC):
                self.op("pe", lambda e: e.matmul(ps[:, n * 128:(n + 1) * 128], lhsT=g2a[:, n * 128:(n + 1) * 128],
                                                 rhs=midga[:, :], start=True, stop=False), [g2a, midga], [ps])
                self.op("pe", lambda e: e.matmul(ps[:, n * 128:(n + 1) * 128], lhsT=g2b[0:32, n * 128:(n + 1) * 128],
                                                 rhs=midgb[0:32, :], start=False, stop=True), [g2b, midgb], [ps])
            self.op("act", lambda e: e.copy(out=gS[:], in_=ps[:]), [ps], [gS])
            ps = pbig()
            for c in range(KC):
                self.op("pe", lambda e: e.transpose(ps[:, c * 128:(c + 1) * 128], yT[:, c * 128:(c + 1) * 128], ident),
                        [yT, rc], [ps])
            self.op("dve", lambda e: e.tensor_tensor(out=ygT[:], in0=ps[:].rearrange("p (c t) -> p c t", c=KC),
                                                     in1=self.fm(gS), op=ALU.mult), [ps, gS], [ygT])
            self.tfree(yT, gS)
            ps = pbig()
            for n in range(KC):
                for c in range(KC):
                    self.op("pe", lambda e: e.matmul(ps[:, n * 128:(n + 1) * 128], lhsT=wo[:, c, n * 128:(n + 1) * 128],
                                                     rhs=ygT[:, c, :], start=(c == 0), stop=(c == KC - 1)),
                            [wo, ygT], [ps])
            ho = self.rtile("r_ho", 1, [128, KC, 128], F32)
            self.op("dve", lambda e: e.tensor_tensor(out=ho[:], in0=ps[:].rearrange("p (c t) -> p c t", c=KC),
                                                     in1=h[:], op=ALU.add), [ps, h], [ho])
            v0 = PAD if cb == 0 else 0
            self.store(hdst[:, :, t0 + v0:t0 + 128], None, ho, ho[:, :, v0:128])
        self.rn_w = 512
        self.phase_end()

    def zero_pads(self, names):
        zt = self.tiles["zeros"]
        for nm in names:
            for c in range(KC):
                self.store(self.dram[nm][c, :, 0:PAD], None, zt, zt[:], eng="sp")

    def finish(self, out_res=None):
        self.barrier()
        return self.nc


def build_ffn_test(nb):
    p = Prog(nb)
    p.din("hT0", [KC, 128, p.LP])
    p.din("ffn_up", [DEPTH, 128, KC, F2])
    p.din("ffn_down", [DEPTH, 128, NJ, D])
    p.dout("hT1", [KC, 128, p.LP])
    p.setup_consts()
    p.zero_pads(["hT1"])
    p.ffn_phase(0, "hT0", "hT1")
    return p.finish([p.dr(("hT1", "all"))])


def attn_masks():
    s = np.arange(128)[:, None]
    col = np.arange(512)[None, :]
    q, t = col // 128, col % 128
    m = np.zeros((128, 6, 512), np.float32)
    for r in range(4):
        m[:, r, :] = ((q > r) | ((q == r) & (t > s))).astype(np.float32)
    row = (s >= PAD).astype(np.float32)
    m[:, 4, :] = m[:, 0, :] * row
    m[:, 5, :] = row * np.ones((1, 512), np.float32)
    j = np.arange(128)[:, None]
    ss = np.arange(128)[None, :]
    ntri = -(j >= ss).astype(np.float32)
    return m, ntri


def build_att_test(nb):
    p = Prog(nb)
    p.din("hT0", [KC, 128, p.LP])
    p.din("sb_wq", [2, 128, KC, D]); p.din("sb_wk", [128, KC, D]); p.din("sb_wv", [128, KC, D])
    p.din("sb_wo", [2, 128, KC, D])
    p.din("amask", [128, 6, 512]); p.din("ntri", [128, 128])
    p.dscratch("QT", [KC, 128, p.LP], BF16); p.dscratch("KT", [KC, 128, p.LP], BF16)
    p.dscratch("VV", [p.LP, D], BF16); p.dscratch("AT", [KC, 128, p.LP], BF16)
    p.dout("hT1", [KC, 128, p.LP])
    p.setup_consts()
    p.zero_pads(["hT1"])
    p.qkv_phase(0, "hT0", True)
    p.att_phase()
    p.o_phase(0, "hT0", "hT1")
    return p.finish([p.dr(("hT1", "all"))])


def rw_consts():
    p = np.arange(128)[:, None]
    q = np.arange(128)[None, :]
    ident = (p == q).astype(np.float32)
    bd = ((p // 64) == (q // 64)).astype(np.float32)
    mstrict = (p > q).astype(np.float32)
    mt_strict = (q > p).astype(np.float32)
    mt_incl = (q >= p).astype(np.float32)
    ind2 = np.concatenate([(p // 64 == 0), (p // 64 == 1)], axis=1).astype(np.float32)
    onesf = np.ones((128, 128), np.float32)
    zcol = np.zeros((128, 1), np.float32)
    pad = np.zeros((128, 127), np.float32)
    return np.ascontiguousarray(np.concatenate([ident, bd, mstrict, mt_strict, mt_incl, ind2, onesf, zcol, pad],
                                               axis=1))


def declare_rw_inputs(p):
    p.din("rw_wr", [2, 128, KC, D]); p.din("rw_wk", [2, 128, KC, D]); p.din("rw_wv", [2, 128, KC, D])
    p.din("rw_wo", [2, 128, KC, D])
    p.din("rw_w1", [2, 128, KC, 64]); p.din("rw_a1", [2, 128, KC, 64]); p.din("rw_g1", [2, 128, KC, 160])
    p.din("rw_v1", [1, 128, KC, 32])
    p.din("rw_w2", [2, 64, D]); p.din("rw_a2", [2, 64, D]); p.din("rw_g2", [2, 160, D]); p.din("rw_v2", [1, 32, D])
    p.din("rw_lnx_g", [2, D]); p.din("rw_lnx_b", [2, D]); p.din("rw_v0", [1, D])
    p.din("rw_const", [128, 128 * 5 + 2 + 256])
    p.dscratch("VF", [p.LP, D], F32)


def build_rw_test(nb, nlayers=1):
    p = Prog(nb)
    p.din("hT0", [KC, 128, p.LP])
    declare_rw_inputs(p)
    p.dout("hT1", [KC, 128, p.LP])
    p.dscratch("hTa", [KC, 128, p.LP])
    p.setup_consts()
    p.zero_pads(["hT1", "hTa"])
    if nlayers == 1:
        p.rwkv_phase(0, "hT0", "hT1")
    else:
        p.rwkv_phase(0, "hT0", "hTa")
        p.rwkv_phase(1, "hTa", "hT1")
    return p.finish()


def final_phase(p, src):
    p.phase_begin()
    hsrc = p.dram[src].rearrange("c p t -> p c t")
    yo = p.dram["yT"].rearrange("c p t -> p c t")
    t = PAD + NMETA
    while t < p.LP:
        tw = min(512, p.LP - t)
        h = p.rtile("fn_h", 2, [128, KC, 512], F32)
        p.load(h, h[:, :, 0:tw], hsrc[:, :, t:t + tw])
        o = p.rtile("fn_o", 2, [128, KC, 512], F32)
        p.rmsnorm(h, tw, lambda c: p.cc("final_g", c), o)
        p.store(yo[:, :, t - PAD - NMETA:t - PAD - NMETA + tw], None, o, o[:, :, 0:tw])
        t += tw
    p.phase_end()


def build_full(nb):
    p = Prog(nb)
    p.din("hT0", [KC, 128, p.LP])
    p.din("ffn_up", [DEPTH, 128, KC, F2])
    p.din("ffn_down", [DEPTH, 128, NJ, D])
    declare_rw_inputs(p)
    p.din("sb_wq", [2, 128, KC, D]); p.din("sb_wk", [128, KC, D]); p.din("sb_wv", [128, KC, D])
    p.din("sb_wo", [2, 128, KC, D])
    p.din("amask", [128, 6, 512]); p.din("ntri", [128, 128])
    p.dscratch("QT", [KC, 128, p.LP], BF16); p.dscratch("KT", [KC, 128, p.LP], BF16)
    p.dscratch("VV", [p.LP, D], BF16); p.dscratch("AT", [KC, 128, p.LP], BF16)
    p.dscratch("hA", [KC, 128, p.LP]); p.dscratch("hB", [KC, 128, p.LP])
    p.dout("yT", [KC, 128, p.LP - PAD - NMETA])
    p.setup_consts()
    p.zero_pads(["hA", "hB"])
    p.barrier()
    p.rwkv_phase(0, "hT0", "hA")
    p.ffn_phase(0, "hA", "hB")
    p.rwkv_phase(1, "hB", "hA")
    p.ffn_phase(1, "hA", "hB")
    p.qkv_phase(0, "hB", True)
    p.att_phase()
    p.o_phase(0, "hB", "hA")
    p.ffn_phase(2, "hA", "hB")
    p.qkv_phase(1, "hB", False)
    p.att_phase()
    p.o_phase(1, "hB", "hA")
    p.ffn_phase(3, "hA", "hB")
    final_phase(p, "hB")
    return p.finish()


def _w1024(w):
    return np.ascontiguousarray(w.reshape(KC, 128, -1).transpose(1, 0, 2))


def _wst(w):
    return np.stack([_w1024(w[i]) for i in range(w.shape[0])])


def _cols(v):
    v = np.asarray(v, np.float32).reshape(-1, KC, 128)
    return np.ascontiguousarray(v.transpose(2, 0, 1).reshape(128, -1))


def host_inputs(inp, nb):
    f = {k: np.asarray(v, np.float32) for k, v in inp.items()}
    m, ntri = attn_masks()
    cw = f["ffn_conv_w"]
    shared = {
        "ffn_up": np.ascontiguousarray(f["ffn_up"].reshape(DEPTH, KC, 128, F2).transpose(0, 2, 1, 3)),
        "ffn_down": np.ascontiguousarray(f["ffn_down"].reshape(DEPTH, NJ, 128, D).transpose(0, 2, 1, 3)),
        "norm_ffn_g": _cols(f["norm_ffn_g"]), "norm_mix_g": _cols(f["norm_mix_g"]),
        "kv_norm_g": _cols(f["kv_norm_g"]), "final_g": _cols(f["final_norm_g"]),
        "conv_w": np.ascontiguousarray(cw.reshape(DEPTH, 3, 44, 128).transpose(3, 0, 2, 1).reshape(128, -1)),
        "conv_b": np.ascontiguousarray(f["ffn_conv_b"].reshape(DEPTH, 44, 128).transpose(2, 0, 1).reshape(128, -1)),
        "rw_mix": _cols(f["rw_mix"]), "rw_w0": _cols(f["rw_w0"]), "rw_a0": _cols(f["rw_a0"]),
        "rw_kk": _cols(f["rw_kk"]), "rw_ka": _cols(f["rw_ka"]), "rw_rk": _cols(f["rw_rk"].reshape(2, D)),
        "rw_wr": _wst(f["rw_wr"]), "rw_wk": _wst(f["rw_wk"]), "rw_wv": _wst(f["rw_wv"]), "rw_wo": _wst(f["rw_wo"]),
        "rw_w1": _wst(f["rw_w1"]), "rw_a1": _wst(f["rw_a1"]), "rw_g1": _wst(f["rw_g1"]), "rw_v1": _wst(f["rw_v1"]),
        "rw_w2": f["rw_w2"], "rw_a2": f["rw_a2"], "rw_g2": f["rw_g2"], "rw_v2": f["rw_v2"],
        "rw_lnx_g": f["rw_lnx_g"], "rw_lnx_b": f["rw_lnx_b"], "rw_v0": f["rw_v0"],
        "rw_const": rw_consts(),
        "sb_wq": _wst(f["sb_wq"]), "sb_wo": _wst(f["sb_wo"]), "sb_wk": _w1024(f["sb_wk"]), "sb_wv": _w1024(f["sb_wv"]),
        "amask": m, "ntri": ntri,
    }
    return shared


def host_h0(xb, meta, nb):
    LP = nb * 128
    hT = np.zeros((D, LP), np.float32)
    hT[:, PAD:PAD + NMETA] = meta.T
    hT[:, PAD + NMETA:] = xb.T
    return hT.reshape(KC, 128, LP)


_CACHE = {}


def kernel(**inputs):
    x = np.asarray(inputs["x"], np.float32)
    B, S, _ = x.shape
    nb = (PAD + NMETA + S) // 128
    if nb not in _CACHE:
        _CACHE[nb] = build_full(nb)
    nc = _CACHE[nb]
    shared = host_inputs({k: v for k, v in inputs.items() if k != "x"}, nb)
    meta = np.asarray(inputs["meta_tokens"], np.float32)
    ncores = 8
    in_maps = []
    for cidx in range(ncores):
        b = cidx % B
        m = dict(shared)
        m["hT0"] = host_h0(x[b], meta, nb)
        in_maps.append(m)
    res = run_bass_kernel_spmd(nc, in_maps, core_ids=list(range(ncores)))
    out = np.empty((B, S, D), np.float32)
    for b in range(B):
        out[b] = res.results[b]["yT"].reshape(D, S).T
    return out
```

```python
import numpy as np
from contextlib import ExitStack
import concourse.bass as bass
import concourse.mybir as mybir
from concourse.bass_utils import run_bass_kernel_spmd

F32 = mybir.dt.float32
BF16 = mybir.dt.bfloat16
ALU = mybir.AluOpType
AF = mybir.ActivationFunctionType
AX = mybir.AxisListType

D = 1024
KC = 8
NH = 16
HD = 64
NMETA = 16
PAD = 112
DFF = 2816
F2 = 5632
NJ = 22
FT = 384
DEPTH = 4
RMS_EPS = 1e-6
GN_EPS = 64e-5


class Res:
    __slots__ = ("name", "lw", "rd")

    def __init__(self, name):
        self.name = name
        self.lw = None
        self.rd = {}


class KB:
    SEM_LIMIT = 30000

    def __init__(self, nc):
        self.nc = nc
        self.q = {e: [] for e in ("pe", "act", "dve", "pool", "sp")}
        self.sems = {}
        self.cur = {}
        self.waited = {e: {} for e in self.q}
        self.nsem = 0
        self.dmasem = {}
        self.nops = 0

    def _newsem(self, tag):
        key = "%s_%d" % (tag, self.nsem)
        self.nsem += 1
        self.sems[key] = self.nc.alloc_semaphore(name=key)
        return key

    def _eng_event(self, eng):
        c = self.cur.get(eng)
        if c is None or c[1] >= self.SEM_LIMIT:
            c = [self._newsem(eng), 0]
            self.cur[eng] = c
        c[1] += 1
        return (c[0], c[1])

    def _dma_event(self, chain, eng="sp"):
        cls = "sw" if eng == "pool" else "hw"
        chain = (cls, chain)
        c = self.dmasem.get(chain)
        if c is None or c[1] >= self.SEM_LIMIT:
            free = getattr(self, "free_dma", {}).get(cls, [])
            free.sort(key=lambda x: x[1])
            if free and free[0][1] < self.SEM_LIMIT // 2:
                c = list(free.pop(0))
            else:
                c = [self._newsem("d" + cls), 0]
            self.dmasem[chain] = c
        c[1] += 16
        return (c[0], c[1])

    def recycle_dma(self):
        free = getattr(self, "free_dma", {"sw": [], "hw": []})
        for ch, c in self.dmasem.items():
            free[ch[0]].append((c[0], c[1]))
        self.free_dma = free
        self.dmasem = {}

    def _deps(self, eng, reads, writes, is_dma):
        waits = {}

        def need(sk, v, src_eng, kind):
            if (not is_dma) and eng == "pe" and src_eng == "pe":
                return
            if (not is_dma) and src_eng == eng and kind == "war":
                return
            if self.waited[eng].get(sk, 0) >= v:
                return
            if waits.get(sk, 0) < v:
                waits[sk] = v

        for r in reads:
            if r.lw is not None:
                need(r.lw[0], r.lw[1], r.lw[2], "raw")
        for w in writes:
            if w.lw is not None:
                need(w.lw[0], w.lw[1], w.lw[2], "waw")
            for sk, (v, e) in w.rd.items():
                need(sk, v, e, "war")
        return waits

    def _record(self, ev, src, reads, writes):
        for r in reads:
            r.rd[ev[0]] = (ev[1], src)
        for w in writes:
            w.lw = (ev[0], ev[1], src)
            w.rd = {}

    def op(self, eng, fn, reads=(), writes=()):
        waits = self._deps(eng, reads, writes, False)
        for sk, v in waits.items():
            self.waited[eng][sk] = v
        ev = self._eng_event(eng)
        self._emit(eng, fn, waits, ev, 1)
        self._record(ev, eng, reads, writes)
        self.nops += 1
        return ev

    def dma(self, eng, fn, reads=(), writes=(), chain=None):
        waits = self._deps(eng, reads, writes, True)
        for sk, v in waits.items():
            self.waited[eng][sk] = v
        assert chain is not None
        ev = self._dma_event(chain, eng)
        self._emit(eng, fn, waits, ev, 16)
        self._record(ev, "dma", reads, writes)
        self.nops += 1
        return ev

    def wait_all(self, eng, resources):
        waits = {}
        for r in resources:
            if r.lw is not None:
                sk, v = r.lw[0], r.lw[1]
                if self.waited[eng].get(sk, 0) < v and waits.get(sk, 0) < v:
                    waits[sk] = v
        for sk, v in waits.items():
            self.waited[eng][sk] = v
        self._emit(eng, None, waits, None, 0)

    ENG = {"pe": "tensor", "act": "scalar", "dve": "vector", "pool": "gpsimd", "sp": "sync"}

    def _emit(self, eng, fn, waits, ev, inc):
        engine = getattr(self.nc, self.ENG[eng])
        for sk, v in waits.items():
            engine.wait_ge(self.sems[sk], v)
        if fn is not None:
            ins = fn(engine)
            ins.then_inc(self.sems[ev[0]], inc)

    def emit(self):
        return

    def emit_old(self):
        nc = self.nc
        engs = {"pe": "tensor", "act": "scalar", "dve": "vector", "pool": "gpsimd", "sp": "sync"}
        with nc.Block() as block:
            for e, attr in engs.items():
                ops = self.q[e]
                if not ops:
                    continue

                def body(engine, ops=ops):
                    for fn, waits, ev, inc in ops:
                        for sk, v in waits:
                            engine.wait_ge(self.sems[sk], v)
                        if fn is not None:
                            ins = fn(engine)
                            ins.then_inc(self.sems[ev[0]], inc)
                getattr(block, attr)(body)


class T:
    def __init__(self, nc, name, shape, dtype, psum=False, stack=None):
        self.name = name
        if psum:
            cm = nc.psum_tensor(name, list(shape), dtype)
        else:
            cm = nc.sbuf_tensor(name, list(shape), dtype)
        self.t = stack.enter_context(cm)
        self.r = Res(name)

    def __getitem__(self, idx):
        return self.t[idx]


class Prog:
    def __init__(self, nb, dbg=None):
        self.nb = nb
        self.LP = nb * 128
        self.dbg = dbg or {}
        self.nc = bass.Bass("TRN2", target_bir_lowering=False)
        self.kb = KB(self.nc)
        self.dram = {}
        self.dres = {}
        self.tiles = {}
        self.rot = {}
        self.gstack = ExitStack()
        self.pstack = None
        self.pid = 0

    def phase_begin(self):
        self.pstack = ExitStack()
        self.pid += 1
        self.ptiles = []

    def phase_end(self):
        self.barrier()
        self.kb.recycle_dma()
        self.pstack.close()
        self.pstack = None
        for nm in self.ptiles:
            self.tiles.pop(nm, None)
        self.rot = {}

    def barrier(self):
        kb = self.kb
        targets = {}
        for e, c in kb.cur.items():
            targets[c[0]] = c[1]
        for ch, c in kb.dmasem.items():
            targets[c[0]] = c[1]
        for e in ("pe", "act", "dve", "pool", "sp"):
            waits = {}
            for sk, v in targets.items():
                if kb.cur.get(e) is not None and kb.cur[e][0] == sk:
                    continue
                if kb.waited[e].get(sk, 0) < v:
                    waits[sk] = v
                    kb.waited[e][sk] = v
            kb._emit(e, None, waits, None, 0)

    def din(self, name, shape, dtype=F32):
        self.dram[name] = self.nc.dram_tensor(name, list(shape), dtype, kind="ExternalInput").ap()
        return self.dram[name]

    def dout(self, name, shape, dtype=F32):
        self.dram[name] = self.nc.dram_tensor(name, list(shape), dtype, kind="ExternalOutput").ap()
        return self.dram[name]

    def dscratch(self, name, shape, dtype=F32):
        self.dram[name] = self.nc.dram_tensor(name, list(shape), dtype, kind="Internal").ap()
        return self.dram[name]

    def dr(self, key):
        r = self.dres.get(key)
        if r is None:
            r = Res(str(key))
            self.dres[key] = r
        return r

    def tile(self, name, shape, dtype=F32, psum=False):
        if self.pstack is not None:
            t = T(self.nc, "%s_p%d" % (name, self.pid), shape, dtype, psum, self.pstack)
            self.ptiles.append(name)
        else:
            t = T(self.nc, name, shape, dtype, psum, self.gstack)
        self.tiles[name] = t
        return t

    def rtile(self, name, n, shape, dtype=F32, psum=False):
        ent = self.rot.get(name)
        if ent is None:
            ent = [[self.tile("%s%d" % (name, i), shape, dtype, psum) for i in range(n)], 0]
            self.rot[name] = ent
        t = ent[0][ent[1] % n]
        ent[1] += 1
        return t

    def op(self, eng, fn, reads=(), writes=()):
        return self.kb.op(eng, fn, [x.r if isinstance(x, T) else x for x in reads],
                          [x.r if isinstance(x, T) else x for x in writes])

    def load(self, dst, dst_ap, src_ap, src_res=None, eng="sp"):
        self.kb.dma(eng, lambda e: e.dma_start(out=dst_ap, in_=src_ap),
                    reads=[], writes=[dst.r], chain="ld_" + dst.name)

    def store(self, dst_ap, dst_res, src, src_ap, eng="sp"):
        self.kb.dma(eng, lambda e: e.dma_start(out=dst_ap, in_=src_ap),
                    reads=[src.r], writes=[], chain="st_" + src.name)

    def rmsnorm(self, h, w, g_ap, out, out_off=0, tag="n"):
        rw_ = getattr(self, "rn_w", 512)
        sq = self.rtile("rn_sq", 1, [128, KC, rw_], BF16)
        ss = self.rtile("rn_ss", 1, [128, 512], F32, psum=True)
        rstd = self.rtile("rn_rstd", 2, [128, rw_], F32)
        ones = self.tiles["onesD"]
        self.op("dve", lambda e: e.tensor_tensor(out=sq[:, :, 0:w], in0=h[:, :, 0:w], in1=h[:, :, 0:w],
                                                  op=ALU.mult), [h], [sq])
        for c in range(KC):
            self.op("pe", lambda e, c=c: e.matmul(ss[:, 0:w], lhsT=ones[:], rhs=sq[:, c, 0:w],
                                                  start=(c == 0), stop=(c == KC - 1)), [ones, sq], [ss])
        self.op("act", lambda e: e.activation(out=rstd[:, 0:w], in_=ss[:, 0:w], func=AF.Sqrt, bias=RMS_EPS,
                                              scale=1.0), [ss], [rstd])
        self.op("dve", lambda e: e.reciprocal(out=rstd[:, 0:w], in_=rstd[:, 0:w]), [rstd], [rstd])
        for c in range(KC):
            self.op("dve", lambda e, c=c: e.scalar_tensor_tensor(
                out=out[:, c, out_off:out_off + w], in0=h[:, c, 0:w], scalar=g_ap(c), in1=rstd[:, 0:w],
                op0=ALU.mult, op1=ALU.mult), [h, rstd, self.tiles["consts"]], [out])

    def setup_consts(self):
        onesD = self.tile("onesD", [128, 128], BF16)
        self.op("pool", lambda e: e.memset(onesD[:], 1.0 / D), [], [onesD])
        zt = self.tile("zeros", [128, PAD], F32)
        self.op("pool", lambda e: e.memset(zt[:], 0.0), [], [zt])
        self.din("norm_ffn_g", [128, DEPTH * KC])
        self.din("conv_w", [128, DEPTH * 44 * 3])
        self.din("conv_b", [128, DEPTH * 44])
        self.din("final_g", [128, KC])
        self.din("norm_mix_g", [128, DEPTH * KC])
        self.din("kv_norm_g", [128, KC])
        for nm, n in (("rw_mix", 2 * 6 * KC), ("rw_w0", 2 * KC), ("rw_a0", 2 * KC), ("rw_kk", 2 * KC),
                      ("rw_ka", 2 * KC), ("rw_rk", 2 * KC)):
            self.din(nm, [128, n])
        ncol = DEPTH * KC + DEPTH * 44 * 3 + DEPTH * 44 + KC + DEPTH * KC + KC + 2 * 6 * KC + 5 * 2 * KC
        consts = self.tile("consts", [128, ncol], F32)
        self.coff = {}
        off = 0
        for nm, n in (("norm_ffn_g", DEPTH * KC), ("conv_w", DEPTH * 44 * 3), ("conv_b", DEPTH * 44),
                      ("final_g", KC), ("norm_mix_g", DEPTH * KC), ("kv_norm_g", KC),
                      ("rw_mix", 2 * 6 * KC), ("rw_w0", 2 * KC), ("rw_a0", 2 * KC), ("rw_kk", 2 * KC),
                      ("rw_ka", 2 * KC), ("rw_rk", 2 * KC)):
            self.coff[nm] = off
            self.load(consts, consts[:, off:off + n], self.dram[nm][:, :], self.dr(nm))
            off += n

    def cc(self, nm, idx):
        o = self.coff[nm] + idx
        return self.tiles["consts"][:, o:o + 1]

    def ffn_tiles(self):
        tiles = []
        o = PAD
        while o < self.LP:
            ow = min(FT - 2, self.LP - o)
            tiles.append((o - 2, ow))
            o += ow
        return tiles

    def ffn_weights(self, layer):
        wup = self.tiles.get("wup") or self.tile("wup", [128, KC, F2], BF16)
        wdn = self.tiles.get("wdn") or self.tile("wdn", [128, NJ, D], BF16)
        up = self.dram["ffn_up"]
        dn = self.dram["ffn_down"]
        for c in range(KC):
            for hf in range(2):
                self.load(wup, wup[:, c, hf * DFF:(hf + 1) * DFF], up[layer, :, c, hf * DFF:(hf + 1) * DFF],
                          self.dr("ffn_up"), eng="pool")
        for j0 in range(0, NJ, 2):
            self.load(wdn, wdn[:, j0:j0 + 2, :], dn[layer, :, j0:j0 + 2, :], self.dr("ffn_down"), eng="pool")
        return wup, wdn

    def ffn_phase(self, layer, src, dst):
        self.phase_begin()
        wup, wdn = self.ffn_weights(layer)
        hsrc = self.dram[src].rearrange("c p t -> p c t")
        hdst = self.dram[dst].rearrange("c p t -> p c t")
        for ti, (i0, ow) in enumerate(self.ffn_tiles()):
            iw = ow + 2
            h = self.rtile("f_h", 2, [128, KC, FT], F32)
            self.load(h, h[:, :, 0:iw], hsrc[:, :, i0:i0 + iw], self.dr((src, "all")))
            hn = self.rtile("f_hn", 1, [128, KC, FT], BF16)
            self.rmsnorm(h, iw, lambda c: self.cc("norm_ffn_g", layer * KC + c), hn)
            m = self.rtile("f_m", 1, [128, NJ, FT], BF16)
            for j in range(NJ):
                ys = []
                for half, ch in enumerate((j, NJ + j)):
                    ps = self.rtile("f_ps", 4, [128, 512], F32, psum=True)
                    for c in range(KC):
                        self.op("pe", lambda e, c=c, ps=ps, ch=ch: e.matmul(
                            ps[:, 0:iw], lhsT=wup[:, c, ch * 128:(ch + 1) * 128], rhs=hn[:, c, 0:iw],
                            start=(c == 0), stop=(c == KC - 1)), [wup, hn], [ps])
                    y = self.rtile("f_y", 4, [128, FT], F32)
                    cw = lambda tap, ch=ch: self.cc("conv_w", (layer * 44 + ch) * 3 + tap)
                    cb = self.cc("conv_b", layer * 44 + ch)
                    cst = self.tiles["consts"]
                    self.op("act", lambda e, y=y, ps=ps, cw=cw, cb=cb: e.activation(
                        out=y[:, 0:ow], in_=ps[:, 2:2 + ow], func=AF.Identity, bias=cb, scale=cw(2)),
                        [ps, cst], [y])
                    self.op("dve", lambda e, y=y, ps=ps, cw=cw: e.scalar_tensor_tensor(
                        out=y[:, 0:ow], in0=ps[:, 1:1 + ow], scalar=cw(1), in1=y[:, 0:ow],
                        op0=ALU.mult, op1=ALU.add), [ps, y, cst], [y])
                    self.op("dve", lambda e, y=y, ps=ps, cw=cw: e.scalar_tensor_tensor(
                        out=y[:, 0:ow], in0=ps[:, 0:ow], scalar=cw(0), in1=y[:, 0:ow],
                        op0=ALU.mult, op1=ALU.add), [ps, y, cst], [y])
                    ys.append(y)
                yg, yv = ys
                self.op("act", lambda e, yg=yg: e.activation(out=yg[:, 0:ow], in_=yg[:, 0:ow], func=AF.Silu),
                        [yg], [yg])
                self.op("dve", lambda e, yg=yg, yv=yv, j=j: e.tensor_tensor(
                    out=m[:, j, 0:ow], in0=yg[:, 0:ow], in1=yv[:, 0:ow], op=ALU.mult), [yg, yv], [m])
            for n in range(KC):
                ps = self.rtile("f_ps", 4, [128, 512], F32, psum=True)
                for j in range(NJ):
                    self.op("pe", lambda e, j=j, n=n, ps=ps: e.matmul(
                        ps[:, 0:ow], lhsT=wdn[:, j, n * 128:(n + 1) * 128], rhs=m[:, j, 0:ow],
                        start=(j == 0), stop=(j == NJ - 1)), [wdn, m], [ps])
                ho = self.rtile("f_ho", 3, [128, FT], F32)
                self.op("dve", lambda e, n=n, ps=ps, ho=ho: e.tensor_tensor(
                    out=ho[:, 0:ow], in0=ps[:, 0:ow], in1=h[:, n, 2:2 + ow], op=ALU.add), [ps, h], [ho])
                self.store(hdst[:, n, i0 + 2:i0 + 2 + ow], self.dr((dst, "all")), ho, ho[:, 0:ow], eng="sp")
        self.phase_end()


    def load_w1024(self, name, dram_ap):
        w = self.tile(name, [128, KC, D], BF16)
        for c0 in range(0, KC, 2):
            self.load(w, w[:, c0:c0 + 2, :], dram_ap[:, c0:c0 + 2, :], self.dr("wts"), eng="pool")
        return w

    def blk_groups(self):
        gs = []
        b = 0
        while b < self.nb:
            n = min(4, self.nb - b)
            gs.append((b, n))
            b += n
        return gs

    def qkv_phase(self, j, src, do_kv):
        self.phase_begin()
        layer = 2 + j
        wq = self.load_w1024("wq", self.dram["sb_wq"][j])
        if do_kv:
            wk = self.load_w1024("wk", self.dram["sb_wk"])
            wv = self.load_w1024("wv", self.dram["sb_wv"])
        hsrc = self.dram[src].rearrange("c p t -> p c t")
        qT = self.dram["QT"].rearrange("c p t -> p c t")
        kT = self.dram["KT"].rearrange("c p t -> p c t")
        vd = self.dram["VV"]
        for (b0, nblk) in self.blk_groups():
            t0, tw = b0 * 128, nblk * 128
            h = self.rtile("q_h", 2, [128, KC, 512], F32)
            self.load(h, h[:, :, 0:tw], hsrc[:, :, t0:t0 + tw], self.dr((src, "all")))
            hn = self.rtile("q_hn", 2, [128, KC, 512], BF16)
            self.rmsnorm(h, tw, lambda c: self.cc("norm_mix_g", layer * KC + c), hn)
            jobs = [(wq, qT, 0.125, "QT")]
            if do_kv:
                kn = self.rtile("q_kn", 2, [128, KC, 512], BF16)
                self.rmsnorm(h, tw, lambda c: self.cc("kv_norm_g", c), kn)
                jobs.append((wk, kT, 1.0, "KT"))
            for (w, dst, scale, dname) in jobs:
                xin = hn if dname == "QT" else kn
                for n in range(KC):
                    ps = self.rtile("q_ps", 4, [128, 512], F32, psum=True)
                    for c in range(KC):
                        self.op("pe", lambda e: e.matmul(ps[:, 0:tw], lhsT=w[:, c, n * 128:(n + 1) * 128],
                                                         rhs=xin[:, c, 0:tw], start=(c == 0), stop=(c == KC - 1)),
                                [w, xin], [ps])
                    o = self.rtile("q_o", 4, [128, 512], BF16)
                    self.op("act", lambda e: e.activation(out=o[:, 0:tw], in_=ps[:, 0:tw], func=AF.Copy,
                                                          scale=scale), [ps], [o])
                    self.store(dst[:, n, t0:t0 + tw], self.dr((dname, "all")), o, o[:, 0:tw])
            if do_kv:
                for bi in range(nblk):
                    for hf in range(2):
                        ps = self.rtile("q_ps", 4, [128, 512], F32, psum=True)
                        for c in range(KC):
                            self.op("pe", lambda e: e.matmul(ps[:, :], lhsT=kn[:, c, bi * 128:(bi + 1) * 128],
                                                             rhs=wv[:, c, hf * 512:(hf + 1) * 512],
                                                             start=(c == 0), stop=(c == KC - 1)), [wv, kn], [ps])
                        o = self.rtile("q_o", 4, [128, 512], BF16)
                        self.op("dve", lambda e: e.tensor_copy(out=o[:, :], in_=ps[:, :]), [ps], [o])
                        r0 = (b0 + bi) * 128
                        self.store(vd[r0:r0 + 128, hf * 512:(hf + 1) * 512], self.dr(("VV", "all")), o, o[:, :])
        self.phase_end()

    def att_sweep(self, slot, kt, qt, vt, at, msk, ntri, ones, hh, b0, ng):
        pb = 64 * hh
        W = ng * 128
        q0 = b0 * 128
        hi = b0 + ng - 1
        outp = slot["out"]
        z = slot["z"]
        carry = None
        for kb in range(hi, -1, -1):
            self.op("pe", lambda e: e.matmul(z[:, 0:W], lhsT=kt[pb:pb + 64, kb * 128:(kb + 1) * 128],
                                             rhs=qt[pb:pb + 64, q0:q0 + W], start=True, stop=False), [kt, qt], [z])
            yield
            ex = slot["ex"][kb % 2]
            sp = slot["sp"][kb % 2]
            self.op("act", lambda e: e.activation(out=ex[:, 0:W], in_=z[:, 0:W], func=AF.Exp), [z], [ex])
            self.op("act", lambda e: e.activation(out=sp[:, 0:W], in_=ex[:, 0:W], func=AF.Ln, bias=1.0, scale=1.0),
                    [ex], [sp])
            mi = None
            if kb >= b0:
                mi = 4 if kb == 0 else kb - b0
            elif kb == 0:
                mi = 5
            if mi is not None:
                self.op("pool", lambda e: e.tensor_tensor(out=sp[:, 0:W], in0=sp[:, 0:W], in1=msk[:, mi, 0:W],
                                                          op=ALU.mult), [sp, msk], [sp])
            yield
            self.op("pe", lambda e: e.matmul(z[:, 0:W], lhsT=ntri[:], rhs=sp[:, 0:W], start=False, stop=True),
                    [ntri, sp], [z])
            cs = None
            if kb > 0:
                cs = slot["cs"]
                self.op("pe", lambda e: e.matmul(cs[:, 0:W], lhsT=ones[:], rhs=sp[:, 0:W], start=True, stop=True),
                        [ones, sp], [cs])
            yield
            w = slot["w"][kb % 2]
            if carry is None:
                self.op("act", lambda e: e.activation(out=w[:, 0:W], in_=z[:, 0:W], func=AF.Exp), [z], [w])
            else:
                arg = slot["arg"][kb % 2]
                self.op("dve", lambda e: e.tensor_tensor(out=arg[:, 0:W], in0=z[:, 0:W], in1=carry[:, 0:W],
                                                         op=ALU.subtract), [z, carry], [arg])
                yield
                self.op("act", lambda e: e.activation(out=w[:, 0:W], in_=arg[:, 0:W], func=AF.Exp), [arg], [w])
            if mi is not None:
                self.op("pool", lambda e: e.tensor_tensor(out=w[:, 0:W], in0=w[:, 0:W], in1=msk[:, mi, 0:W],
                                                          op=ALU.mult), [w, msk], [w])
            if kb > 0:
                ncar = slot["carry"][kb % 2]
                if carry is None:
                    self.op("dve", lambda e: e.tensor_copy(out=ncar[:, 0:W], in_=cs[:, 0:W]), [cs], [ncar])
                else:
                    self.op("dve", lambda e: e.tensor_tensor(out=ncar[:, 0:W], in0=cs[:, 0:W], in1=carry[:, 0:W],
                                                             op=ALU.add), [cs, carry], [ncar])
                carry = ncar
            yield
            self.op("pe", lambda e: e.matmul(outp[pb:pb + 64, 0:W], lhsT=vt[:, kb, pb:pb + 64], rhs=w[:, 0:W],
                                             start=(kb == hi), stop=(kb == 0)), [vt, w], [outp])
        yield
        self.op("dve", lambda e: e.tensor_copy(out=at[pb:pb + 64, q0:q0 + W], in_=outp[pb:pb + 64, 0:W]), [outp], [at])

    def att_phase(self, nslots=3):
        self.phase_begin()
        nb, LP = self.nb, self.LP
        msk = self.tile("amask", [128, 6, 512], BF16)
        self.load(msk, msk[:], self.dram["amask"][:, :, :], eng="pool")
        ntri = self.tile("ntri", [128, 128], BF16)
        self.load(ntri, ntri[:], self.dram["ntri"][:, :], eng="pool")
        ones = self.tile("ones1", [128, 128], BF16)
        self.op("pool", lambda e: e.memset(ones[:], 1.0), [], [ones])
        qT = self.dram["QT"]
        kT = self.dram["KT"]
        vd = self.dram["VV"].rearrange("(b s) n -> s b n", s=128)
        aT = self.dram["AT"]
        slots = []
        outA = self.tile("a_outA", [128, 512], F32, psum=True)
        outB = self.tile("a_outB", [128, 512], F32, psum=True)
        for si in range(nslots):
            slots.append({
                "out": outA if si < 2 else outB,
                "outr": None,
                "cs": self.tile("a_cs%d" % si, [128, 512], F32, psum=True),
                "z": self.tile("a_z%d" % si, [128, 512], F32, psum=True),
                "ex": [self.tile("a_e%d_%d" % (si, k), [128, 512], F32) for k in range(2)],
                "sp": [self.tile("a_sp%d_%d" % (si, k), [128, 512], BF16) for k in range(2)],
                "w": [self.tile("a_w%d_%d" % (si, k), [128, 512], BF16) for k in range(2)],
                "arg": [self.tile("a_arg%d_%d" % (si, k), [128, 512], F32) for k in range(2)],
                "carry": [self.tile("a_car%d_%d" % (si, k), [128, 512], F32) for k in range(2)],
            })
        groups = self.blk_groups()
        for c in range(KC):
            kt = self.rtile("a_k", 2, [128, LP], BF16)
            qt = self.rtile("a_q", 2, [128, LP], BF16)
            vt = self.rtile("a_v", 2, [128, nb, 128], BF16)
            at = self.rtile("a_o", 2, [128, LP], BF16)
            self.load(kt, kt[:], kT[c])
            self.load(qt, qt[:], qT[c])
            self.load(vt, vt[:], vd[:, :, c * 128:(c + 1) * 128])
            todo = [[(hh, b0, ng) for (b0, ng) in reversed(groups)] for hh in range(2)]
            active = [None] * nslots
            while todo[0] or todo[1] or any(a is not None for a in active):
                for si in range(nslots):
                    if active[si] is None and (todo[0] or todo[1]):
                        if si < 2:
                            lst = todo[si]
                        else:
                            lst = todo[0] if len(todo[0]) >= len(todo[1]) else todo[1]
                        if lst:
                            hh, b0, ng = lst.pop(0)
                            active[si] = self.att_sweep(slots[si], kt, qt, vt, at, msk, ntri, ones, hh, b0, ng)
                    if active[si] is not None:
                        try:
                            next(active[si])
                        except StopIteration:
                            active[si] = None
            self.store(aT[c], None, at, at[:])
        self.phase_end()

    def o_phase(self, j, src, dst):
        self.phase_begin()
        wo = self.load_w1024("wo", self.dram["sb_wo"][j])
        hsrc = self.dram[src].rearrange("c p t -> p c t")
        hdst = self.dram[dst].rearrange("c p t -> p c t")
        aT = self.dram["AT"].rearrange("c p t -> p c t")
        for (b0, nblk) in self.blk_groups():
            t0, tw = b0 * 128, nblk * 128
            v0 = PAD if b0 == 0 else 0
            h = self.rtile("o_h", 2, [128, KC, 512], F32)
            self.load(h, h[:, :, 0:tw], hsrc[:, :, t0:t0 + tw], self.dr((src, "all")))
            a = self.rtile("o_a", 2, [128, KC, 512], BF16)
            self.load(a, a[:, :, 0:tw], aT[:, :, t0:t0 + tw], self.dr(("AT", "all")))
            for n in range(KC):
                ps = self.rtile("o_ps", 4, [128, 512], F32, psum=True)
                for c in range(KC):
                    self.op("pe", lambda e: e.matmul(ps[:, 0:tw], lhsT=wo[:, c, n * 128:(n + 1) * 128],
                                                     rhs=a[:, c, 0:tw], start=(c == 0), stop=(c == KC - 1)),
                            [wo, a], [ps])
                ho = self.rtile("o_ho", 3, [128, 512], F32)
                self.op("dve", lambda e: e.tensor_tensor(out=ho[:, 0:tw], in0=ps[:, 0:tw], in1=h[:, n, 0:tw],
                                                         op=ALU.add), [ps, h], [ho])
                self.store(hdst[:, n, t0 + v0:t0 + tw], self.dr((dst, "all")), ho, ho[:, v0:tw])
        self.phase_end()


    def pool_init(self, n):
        self.tpool = [self.tile("tp%d" % i, [128, D], F32) for i in range(n)]

    def tget(self):
        return self.tpool.pop(0)

    def tfree(self, *ts):
        for t in ts:
            self.tpool.append(t)

    @staticmethod
    def fm(t, lo=0, hi=128):
        return t.t[:].rearrange("p (c t) -> p c t", c=KC)[:, :, lo:hi]

    def bc(self, nm, idx0):
        o = self.coff[nm] + idx0
        return self.tiles["consts"][:, o:o + KC].unsqueeze(2).broadcast_to([128, KC, 128])

    def rwkv_phase(self, i, src, dst):
        self.phase_begin()
        self.rn_w = 128
        nb, LP = self.nb, self.LP
        layer = i
        cst = self.tiles["consts"]
        dr = self.dram
        wr = self.load_w1024("wr", dr["rw_wr"][i])
        wk = self.load_w1024("wk", dr["rw_wk"][i])
        wv = self.load_w1024("wv", dr["rw_wv"][i])
        wo = self.load_w1024("wo", dr["rw_wo"][i])

        def ldw(name, shape, ap):
            t = self.tile(name, shape, BF16)
            self.load(t, t[:], ap, eng="pool")
            return t
        w1 = ldw("w1", [128, KC, 64], dr["rw_w1"][i])
        a1 = ldw("a1", [128, KC, 64], dr["rw_a1"][i])
        g1 = ldw("g1", [128, KC, 160], dr["rw_g1"][i])
        w2 = ldw("w2", [64, D], dr["rw_w2"][i])
        a2 = ldw("a2", [64, D], dr["rw_a2"][i])
        g2a = ldw("g2a", [128, D], dr["rw_g2"][i, 0:128, :])
        g2b = ldw("g2b", [32, D], dr["rw_g2"][i, 128:160, :])
        if i > 0:
            v1 = ldw("v1", [128, KC, 32], dr["rw_v1"][i - 1])
            v2 = ldw("v2", [32, D], dr["rw_v2"][i - 1])
        rc = self.tile("rwc", [128, 128 * 5 + 2 + 256], F32)
        self.load(rc, rc[:], dr["rw_const"][:, :])
        ident = rc[:, 0:128]
        bd = rc[:, 128:256]
        mstrict = rc[:, 256:384]
        mt2 = rc[:, 384:640]
        ind2 = rc[:, 640:642]
        onesf = rc[:, 642:770]
        zcol = rc[:, 770:771]
        tmb = self.tile("tmb", [128, 3 if i > 0 else 2, D], F32)
        self.load(tmb, tmb[:, 0, :], dr["rw_lnx_g"][i:i + 1, :].partition_broadcast(128))
        self.load(tmb, tmb[:, 1, :], dr["rw_lnx_b"][i:i + 1, :].partition_broadcast(128))
        if i > 0:
            self.load(tmb, tmb[:, 2, :], dr["rw_v0"][i - 1:i, :].partition_broadcast(128))
        self.pool_init(10)
        st2 = [self.tile("st2_%d" % c, [128, 128], F32) for c in range(KC)]
        for c in range(KC):
            self.op("pool", lambda e: e.memset(st2[c][:], 0.0), [], [st2[c]])
        hsrc = dr[src].rearrange("c p t -> p c t")
        hdst = dr[dst].rearrange("c p t -> p c t")
        vf = dr["VF"]
        hn_prev = None
        ART = self.tile("AR", [128, KC, 256], F32)
        btT = self.tile("bt", [128, KC, 128], F32)
        ktT = self.tile("kt", [128, KC, 128], F32)
        pcT = self.tile("pc", [128, KC], F32)
        ssb = self.tile("ssb", [128, NH], F32)
        mu = self.tile("mu", [128, NH], F32)
        var = self.tile("var", [128, NH], F32)
        midw = self.tile("midw", [64, 128], BF16)
        mida = self.tile("mida", [64, 128], BF16)
        midga = self.tile("midga", [128, 128], BF16)
        midgb = self.tile("midgb", [32, 128], BF16)
        midv = self.tile("midv", [32, 128], BF16)
        ygT = self.tile("yg", [128, KC, 128], BF16)
        C0 = 0.6065306597126334

        def pbig():
            return self.rtile("pbig", 2, [128, D], F32, psum=True)

        def psm():
            return self.rtile("psm", 3, [128, 512], F32, psum=True)

        for cb in range(nb):
            t0 = cb * 128
            h = self.rtile("r_h", 2, [128, KC, 128], F32)
            self.load(h, h[:], hsrc[:, :, t0:t0 + 128])
            hn = self.rtile("r_hn", 1, [128, KC, 129], F32)
            if hn_prev is None:
                self.op("pool", lambda e: e.memset(hn[:, :, 0:1], 0.0), [], [hn])
            else:
                self.op("pool", lambda e: e.tensor_copy(out=hn[:, :, 0:1], in_=hn[:, :, 128:129]), [hn], [hn])
            self.rmsnorm(h, 128, lambda c: self.cc("norm_mix_g", layer * KC + c), hn, out_off=1)
            hn_prev = hn
            xx = self.tget()
            self.op("dve", lambda e: e.tensor_tensor(out=self.fm(xx), in0=hn[:, :, 0:128], in1=hn[:, :, 1:129],
                                                     op=ALU.subtract), [hn], [xx])
            def mkx(q):
                x = self.rtile("xq", 3, [128, KC, 128], BF16)
                tm = self.tget()
                eng = "dve" if q % 2 == 0 else "pool"
                self.op(eng, lambda e: e.tensor_tensor(out=self.fm(tm), in0=self.fm(xx),
                                                       in1=self.bc("rw_mix", (i * 6 + q) * KC), op=ALU.mult),
                        [xx, cst], [tm])
                self.op(eng, lambda e: e.tensor_tensor(out=x[:], in0=self.fm(tm), in1=hn[:, :, 1:129],
                                                       op=ALU.add), [tm, hn], [x])
                self.tfree(tm)
                return x
            rT = self.tget()
            kT_ = self.tget()
            vS = self.tget()
            for (w, qi, dstt) in ((wr, 0, rT), (wk, 2, kT_)):
                x = mkx(qi)
                ps = pbig()
                for n in range(KC):
                    for c in range(KC):
                        self.op("pe", lambda e: e.matmul(ps[:, n * 128:(n + 1) * 128],
                                                         lhsT=w[:, c, n * 128:(n + 1) * 128], rhs=x[:, c, :],
                                                         start=(c == 0), stop=(c == KC - 1)), [w, x], [ps])
                self.op("act", lambda e: e.copy(out=dstt[:], in_=ps[:]), [ps], [dstt])
            xv = mkx(3)
            ps = pbig()
            for hf in range(2):
                for c in range(KC):
                    self.op("pe", lambda e: e.matmul(ps[:, hf * 512:(hf + 1) * 512], lhsT=xv[:, c, :],
                                                     rhs=wv[:, c, hf * 512:(hf + 1) * 512],
                                                     start=(c == 0), stop=(c == KC - 1)), [wv, xv], [ps])
            self.op("act", lambda e: e.copy(out=vS[:], in_=ps[:]), [ps], [vS])
            for (wt, qi, ncol, mid, fn) in (((v1, 3, 32, midv, AF.Copy),) if i > 0 else ()) + \
                    ((w1, 1, 64, midw, AF.Tanh), (a1, 4, 64, mida, AF.Copy),
                     (g1, 5, 128, midga, AF.Sigmoid), (g1, 5, 32, midgb, AF.Sigmoid)):
                x = xv if qi == 3 else (x if (mid is midgb) else mkx(qi))
                ps = psm()
                c0 = 128 if mid is midgb else 0
                for c in range(KC):
                    self.op("pe", lambda e: e.matmul(ps[0:ncol, 0:128], lhsT=wt[:, c, c0:c0 + ncol], rhs=x[:, c, :],
                                                     start=(c == 0), stop=(c == KC - 1)), [wt, x], [ps])
                self.op("act", lambda e: e.activation(out=mid[0:ncol, :], in_=ps[0:ncol, 0:128], func=fn), [ps], [mid])
            self.tfree(xx)
            sig = self.tget()
            ps = pbig()
            for n in range(KC):
                self.op("pe", lambda e: e.matmul(ps[:, n * 128:(n + 1) * 128], lhsT=w2[0:64, n * 128:(n + 1) * 128],
                                                 rhs=midw[0:64, :], start=True, stop=True), [w2, midw], [ps])
            self.op("dve", lambda e: e.tensor_tensor(out=self.fm(sig), in0=ps[:].rearrange("p (c t) -> p c t", c=KC),
                                                     in1=self.bc("rw_w0", i * KC), op=ALU.add), [ps, cst], [sig])
            self.op("act", lambda e: e.activation(out=sig[:], in_=sig[:], func=AF.Sigmoid), [sig], [sig])
            aT_ = self.tget()
            ps = pbig()
            for n in range(KC):
                self.op("pe", lambda e: e.matmul(ps[:, n * 128:(n + 1) * 128], lhsT=a2[0:64, n * 128:(n + 1) * 128],
                                                 rhs=mida[0:64, :], start=True, stop=True), [a2, mida], [ps])
            self.op("dve", lambda e: e.tensor_tensor(out=self.fm(aT_), in0=ps[:].rearrange("p (c t) -> p c t", c=KC),
                                                     in1=self.bc("rw_a0", i * KC), op=ALU.add), [ps, cst], [aT_])
            self.op("act", lambda e: e.activation(out=aT_[:], in_=aT_[:], func=AF.Sigmoid), [aT_], [aT_])
            if i > 0:
                vg = self.tget()
                vfT = self.tget()
                self.load(vfT, vfT[:], vf[t0:t0 + 128, :])
                ps = pbig()
                for hf in range(2):
                    self.op("pe", lambda e: e.matmul(ps[:, hf * 512:(hf + 1) * 512], lhsT=midv[0:32, :],
                                                     rhs=v2[0:32, hf * 512:(hf + 1) * 512], start=True, stop=True),
                            [v2, midv], [ps])
                self.op("dve", lambda e: e.tensor_tensor(out=vg[:], in0=ps[:], in1=tmb[:, 2, :], op=ALU.add),
                        [ps, tmb], [vg])
                self.op("act", lambda e: e.activation(out=vg[:], in_=vg[:], func=AF.Sigmoid), [vg], [vg])
                self.op("pool", lambda e: e.tensor_tensor(out=vfT[:], in0=vfT[:], in1=vS[:], op=ALU.subtract),
                        [vfT, vS], [vfT])
                self.op("dve", lambda e: e.tensor_tensor(out=vfT[:], in0=vfT[:], in1=vg[:], op=ALU.mult),
                        [vfT, vg], [vfT])
                self.op("dve", lambda e: e.tensor_tensor(out=vS[:], in0=vS[:], in1=vfT[:], op=ALU.add),
                        [vS, vfT], [vS])
                self.tfree(vg, vfT)
            else:
                self.store(vf[t0:t0 + 128, :], None, vS, vS[:])
            kk = self.tget()
            self.op("dve", lambda e: e.tensor_tensor(out=self.fm(kk), in0=self.fm(kT_), in1=self.bc("rw_kk", i * KC),
                                                     op=ALU.mult), [kT_, cst], [kk])
            ksq = self.tget()
            self.op("pool", lambda e: e.tensor_tensor(out=ksq[:], in0=kk[:], in1=kk[:], op=ALU.mult), [kk], [ksq])
            ps = pbig()
            for c in range(KC):
                self.op("pe", lambda e: e.matmul(ps[:, c * 128:(c + 1) * 128], lhsT=bd, rhs=ksq[:, c * 128:(c + 1) * 128],
                                                 start=True, stop=True), [rc, ksq], [ps])
            self.op("act", lambda e: e.activation(out=ksq[:], in_=ps[:], func=AF.Sqrt), [ps], [ksq])
            self.op("dve", lambda e: e.tensor_scalar(out=ksq[:], in0=ksq[:], scalar1=1e-12, scalar2=None, op0=ALU.max),
                    [ksq], [ksq])
            self.op("dve", lambda e: e.reciprocal(out=ksq[:], in_=ksq[:]), [ksq], [ksq])
            self.op("dve", lambda e: e.tensor_tensor(out=kk[:], in0=kk[:], in1=ksq[:], op=ALU.mult), [kk, ksq], [kk])
            self.tfree(ksq)
            km = self.tget()
            self.op("dve", lambda e: e.scalar_tensor_tensor(out=self.fm(km), in0=self.fm(aT_), scalar=-1.0,
                                                            in1=self.bc("rw_ka", i * KC), op0=ALU.add, op1=ALU.mult),
                    [aT_, cst], [km])
            self.op("dve", lambda e: e.scalar_tensor_tensor(out=km[:], in0=km[:], scalar=1.0, in1=kT_[:],
                                                            op0=ALU.add, op1=ALU.mult), [km, kT_], [km])
            self.tfree(kT_)
            cs = self.tget()
            for c in range(KC):
                self.op("dve", lambda e: e.tensor_tensor_scan(out=cs[:, c * 128:(c + 1) * 128], data0=onesf,
                                                              data1=sig[:, c * 128:(c + 1) * 128], initial=zcol,
                                                              op0=ALU.mult, op1=ALU.add), [sig, rc], [cs])
            ein = self.tget()
            eneg = self.tget()
            self.op("act", lambda e: e.activation(out=ein[:], in_=cs[:], func=AF.Exp, scale=-C0), [cs], [ein])
            self.op("act", lambda e: e.activation(out=eneg[:], in_=cs[:], func=AF.Exp, scale=C0), [cs], [eneg])
            self.op("pool", lambda e: e.tensor_tensor(out=cs[:], in0=cs[:], in1=sig[:], op=ALU.subtract), [cs, sig], [cs])
            self.op("act", lambda e: e.activation(out=cs[:], in_=cs[:], func=AF.Exp, scale=-C0), [cs], [cs])
            self.tfree(sig)
            self.op("dve", lambda e: e.scalar_tensor_tensor(out=ART[:, :, 0:128], in0=self.fm(kk), scalar=-1.0,
                                                            in1=self.fm(cs), op0=ALU.mult, op1=ALU.mult),
                    [kk, cs], [ART])
            self.op("pool", lambda e: e.tensor_tensor(out=ART[:, :, 128:256], in0=self.fm(rT), in1=self.fm(ein),
                                                      op=ALU.mult), [rT, ein], [ART])
            self.tfree(cs)
            self.op("dve", lambda e: e.tensor_tensor(out=btT[:], in0=self.fm(kk), in1=self.fm(aT_), op=ALU.mult),
                    [kk, aT_], [btT])
            self.op("dve", lambda e: e.tensor_tensor(out=btT[:], in0=btT[:], in1=self.fm(eneg), op=ALU.mult),
                    [btT, eneg], [btT])
            self.tfree(aT_, kk)
            self.op("pool", lambda e: e.tensor_tensor(out=ktT[:], in0=self.fm(km), in1=self.fm(eneg), op=ALU.mult),
                    [km, eneg], [ktT])
            self.tfree(eneg)
            self.op("act", lambda e: e.copy(out=pcT[:, :].unsqueeze(2), in_=self.fm(ein, 127, 128)), [ein], [pcT])
            self.tfree(ein)
            self.op("dve", lambda e: e.tensor_tensor(out=rT[:], in0=rT[:], in1=km[:], op=ALU.mult), [rT, km], [rT])
            self.op("dve", lambda e: e.tensor_tensor(out=self.fm(rT), in0=self.fm(rT), in1=self.bc("rw_rk", i * KC),
                                                     op=ALU.mult), [rT, cst], [rT])
            ps = psm()
            for c in range(KC):
                self.op("pe", lambda e: e.matmul(ps[:, 2 * c:2 * c + 2], lhsT=rT[:, c * 128:(c + 1) * 128], rhs=ind2,
                                                 start=True, stop=True), [rT, rc], [ps])
            self.op("act", lambda e: e.copy(out=ssb[:], in_=ps[:, 0:NH]), [ps], [ssb])
            self.tfree(rT, km)
            pcb = pcT[:, :].unsqueeze(2).broadcast_to([128, KC, 128])
            KH = self.tget()
            BH = self.tget()
            for (srcT, dstT) in ((ktT, KH), (btT, BH)):
                tmp = self.tget()
                self.op("dve", lambda e: e.tensor_tensor(out=self.fm(tmp), in0=srcT[:], in1=pcb, op=ALU.mult),
                        [srcT, pcT], [tmp])
                ps = pbig()
                for c in range(KC):
                    self.op("pe", lambda e: e.transpose(ps[:, c * 128:(c + 1) * 128], tmp[:, c * 128:(c + 1) * 128],
                                                        ident), [tmp, rc], [ps])
                self.op("act", lambda e: e.copy(out=dstT[:], in_=ps[:]), [ps], [dstT])
                self.tfree(tmp)
            yT = self.tget()
            for c in range(KC):
                sk = self.rtile("sk", 2, [128, 2, 256], F32)
                sbm = self.rtile("sbm", 2, [128, 2, 256], F32)
                tb0 = self.rtile("tb0", 2, [128, 2, 128], BF16)
                sx = self.rtile("sx", 3, [128, 2, 192], BF16)
                p1 = psm()
                p2 = psm()
                p3 = psm()
                for hh in range(2):
                    pb = 64 * hh
                    self.op("pe", lambda e: e.matmul(p1[:, hh * 256:(hh + 1) * 256], lhsT=ktT[pb:pb + 64, c, :],
                                                     rhs=ART[pb:pb + 64, c, :], start=True, stop=True), [ktT, ART], [p1])
                    self.op("pe", lambda e: e.matmul(p2[:, hh * 256:(hh + 1) * 256], lhsT=btT[pb:pb + 64, c, :],
                                                     rhs=ART[pb:pb + 64, c, :], start=True, stop=True), [btT, ART], [p2])
                    self.op("pe", lambda e: e.matmul(p3[:, hh * 128:(hh + 1) * 128], lhsT=ART[pb:pb + 64, c, 0:128],
                                                     rhs=btT[pb:pb + 64, c, :], start=True, stop=True), [btT, ART], [p3])
                mt2b = mt2.unsqueeze(1).broadcast_to([128, 2, 256])
                msb = mstrict.unsqueeze(1).broadcast_to([128, 2, 128])
                self.op("dve", lambda e: e.tensor_tensor(out=sk[:], in0=p1[:].rearrange("p (h x) -> p h x", h=2),
                                                         in1=mt2b, op=ALU.mult), [p1, rc], [sk])
                self.op("dve", lambda e: e.tensor_tensor(out=sbm[:], in0=p2[:].rearrange("p (h x) -> p h x", h=2),
                                                         in1=mt2b, op=ALU.mult), [p2, rc], [sbm])
                self.op("act", lambda e: e.copy(out=tb0[:], in_=sbm[:, :, 0:128]), [sbm], [tb0])
                self.op("dve", lambda e: e.tensor_tensor(out=sx[:, :, 0:128],
                                                         in0=p3[:, 0:256].rearrange("p (h x) -> p h x", h=2),
                                                         in1=msb, op=ALU.mult), [p3, rc], [sx])
                px = psm()
                self.op("pe", lambda e: e.matmul(px[:, 0:128], lhsT=ART[:, c, 0:128], rhs=st2[c][:], start=True,
                                                 stop=False), [ART, st2[c]], [px])
                for hh in range(2):
                    hd = 2 * c + hh
                    self.op("pe", lambda e: e.matmul(px[:, hh * 64:(hh + 1) * 64], lhsT=sk[:, hh, 0:128],
                                                     rhs=vS[:, hd * 64:(hd + 1) * 64], start=False, stop=(hh == 1)),
                            [sk, vS], [px])
                self.op("act", lambda e: e.copy(out=sx[:, :, 128:192],
                                                in_=px[:, 0:128].rearrange("p (h x) -> p h x", h=2)), [px], [sx])
                tcur = None
                for k in range(7):
                    pd = psm()
                    for hh in range(2):
                        tk = tb0[:, hh, :] if tcur is None else tcur[:, hh, :]
                        ncol = 192 if k < 6 else 64
                        c0 = 0 if k < 6 else 128
                        self.op("pe", lambda e: e.matmul(pd[:, hh * 192:hh * 192 + ncol], lhsT=tk,
                                                         rhs=sx[:, hh, c0:c0 + ncol], start=True, stop=True),
                                [tb0 if tcur is None else tcur, sx], [pd])
                    if k < 6:
                        pt = psm()
                        for hh in range(2):
                            tk = tb0[:, hh, :] if tcur is None else tcur[:, hh, :]
                            self.op("pe", lambda e: e.matmul(pt[:, hh * 128:(hh + 1) * 128], lhsT=sx[:, hh, 0:128],
                                                             rhs=tk, start=True, stop=True),
                                    [tb0 if tcur is None else tcur, sx], [pt])
                        sxn = self.rtile("sx", 3, [128, 2, 192], BF16)
                        pdv = pd[:, 0:384].rearrange("p (h x) -> p h x", h=2)
                        self.op("act", lambda e: e.copy(out=sxn[:, :, 0:128], in_=pdv[:, :, 0:128]), [pd], [sxn])
                        self.op("dve", lambda e: e.tensor_tensor(out=sxn[:, :, 128:192], in0=pdv[:, :, 128:192],
                                                                 in1=sx[:, :, 128:192], op=ALU.add), [pd, sx], [sxn])
                        tn = self.rtile("tt", 2, [128, 2, 128], BF16)
                        self.op("act", lambda e: e.copy(out=tn[:], in_=pt[:, 0:256].rearrange("p (h x) -> p h x", h=2)),
                                [pt], [tn])
                        sx = sxn
                        tcur = tn
                    else:
                        uf = self.rtile("uf", 2, [128, 2, 64], F32)
                        pdv = pd[:, 0:384].rearrange("p (h x) -> p h x", h=2)
                        self.op("dve", lambda e: e.tensor_tensor(out=uf[:], in0=pdv[:, :, 0:64],
                                                                 in1=sx[:, :, 128:192], op=ALU.add), [pd, sx], [uf])
                py = psm()
                self.op("pe", lambda e: e.matmul(py[:, 0:128], lhsT=ART[:, c, 128:256], rhs=st2[c][:], start=True,
                                                 stop=False), [ART, st2[c]], [py])
                for hh in range(2):
                    hd = 2 * c + hh
                    self.op("pe", lambda e: e.matmul(py[:, hh * 64:(hh + 1) * 64], lhsT=sbm[:, hh, 128:256],
                                                     rhs=uf[:, hh, :], start=False, stop=False), [sbm, uf], [py])
                    self.op("pe", lambda e: e.matmul(py[:, hh * 64:(hh + 1) * 64], lhsT=sk[:, hh, 128:256],
                                                     rhs=vS[:, hd * 64:(hd + 1) * 64], start=False, stop=(hh == 1)),
                            [sk, vS], [py])
                self.op("act", lambda e: e.copy(out=yT[:, c * 128:(c + 1) * 128], in_=py[:, 0:128]), [py], [yT])
                pst = psm()
                for hh in range(2):
                    hd = 2 * c + hh
                    pb = 64 * hh
                    self.op("pe", lambda e: e.matmul(pst[pb:pb + 64, hh * 64:(hh + 1) * 64],
                                                     lhsT=BH[:, hd * 64:(hd + 1) * 64], rhs=uf[:, hh, :],
                                                     start=True, stop=False), [BH, uf], [pst])
                    self.op("pe", lambda e: e.matmul(pst[pb:pb + 64, hh * 64:(hh + 1) * 64],
                                                     lhsT=KH[:, hd * 64:(hd + 1) * 64], rhs=vS[:, hd * 64:(hd + 1) * 64],
                                                     start=False, stop=True), [KH, vS], [pst])
                for hh in range(2):
                    pb = 64 * hh
                    self.op("dve", lambda e: e.scalar_tensor_tensor(
                        out=st2[c][pb:pb + 64, hh * 64:(hh + 1) * 64], in0=st2[c][pb:pb + 64, hh * 64:(hh + 1) * 64],
                        scalar=pcT[pb:pb + 64, c:c + 1], in1=pst[pb:pb + 64, hh * 64:(hh + 1) * 64],
                        op0=ALU.mult, op1=ALU.add), [st2[c], pcT, pst], [st2[c]])
            self.tfree(KH, BH)
            y3 = yT.t[:].rearrange("p (h x) -> p h x", h=NH)
            self.op("dve", lambda e: e.tensor_reduce(out=mu[:], in_=y3, axis=AX.X, op=ALU.add), [yT], [mu])
            mub = mu[:, :].unsqueeze(2).broadcast_to([128, NH, HD])
            self.op("dve", lambda e: e.scalar_tensor_tensor(out=y3, in0=mub, scalar=-1.0 / HD, in1=y3, op0=ALU.mult,
                                                            op1=ALU.add), [yT, mu], [yT])
            sq = self.tget()
            self.op("pool", lambda e: e.tensor_tensor(out=sq[:], in0=yT[:], in1=yT[:], op=ALU.mult), [yT], [sq])
            self.op("dve", lambda e: e.tensor_reduce(out=var[:], in_=sq.t[:].rearrange("p (h x) -> p h x", h=NH),
                                                     axis=AX.X, op=ALU.add), [sq], [var])
            self.op("act", lambda e: e.activation(out=var[:], in_=var[:], func=AF.Sqrt, bias=GN_EPS, scale=1.0 / HD),
                    [var], [var])
            self.op("dve", lambda e: e.reciprocal(out=var[:], in_=var[:]), [var], [var])
            varb = var[:, :].unsqueeze(2).broadcast_to([128, NH, HD])
            self.op("dve", lambda e: e.tensor_tensor(out=y3, in0=y3, in1=varb, op=ALU.mult), [yT, var], [yT])
            self.op("pool", lambda e: e.tensor_tensor(out=yT[:], in0=yT[:], in1=tmb[:, 0, :], op=ALU.mult), [yT, tmb], [yT])
            self.op("pool", lambda e: e.tensor_tensor(out=yT[:], in0=yT[:], in1=tmb[:, 1, :], op=ALU.add), [yT, tmb], [yT])
            ssbb = ssb[:, :].unsqueeze(2).broadcast_to([128, NH, HD])
            self.op("dve", lambda e: e.tensor_tensor(out=sq.t[:].rearrange("p (h x) -> p h x", h=NH),
                                                     in0=vS.t[:].rearrange("p (h x) -> p h x", h=NH), in1=ssbb,
                                                     op=ALU.mult), [vS, ssb], [sq])
            self.op("dve", lambda e: e.tensor_tensor(out=yT[:], in0=yT[:], in1=sq[:], op=ALU.add), [yT, sq], [yT])
            self.tfree(sq, vS)
            gS = self.tget()
            ps = pbig()
            for n in range(KC):
                self.op("pe", lambda e: e.matmul(ps[:, n * 128:(n + 1) * 128], lhsT=g2a[:, n * 128:(n + 1) * 128],
                                                 rhs=midga[:, :], start=True, stop=False), [g2a, midga], [ps])
                self.op("pe", lambda e: e.matmul(ps[:, n * 128:(n + 1) * 128], lhsT=g2b[0:32, n * 128:(n + 1) * 128],
                                                 rhs=midgb[0:32, :], start=False, stop=True), [g2b, midgb], [ps])
            self.op("act", lambda e: e.copy(out=gS[:], in_=ps[:]), [ps], [gS])
            ps = pbig()
            for c in range(KC):
                self.op("pe", lambda e: e.transpose(ps[:, c * 128:(c + 1) * 128], yT[:, c * 128:(c + 1) * 128], ident),
                        [yT, rc], [ps])
            self.op("dve", lambda e: e.tensor_tensor(out=ygT[:], in0=ps[:].rearrange("p (c t) -> p c t", c=KC),
                                                     in1=self.fm(gS), op=ALU.mult), [ps, gS], [ygT])
            self.tfree(yT, gS)
            ps = pbig()
            for n in range(KC):
                for c in range(KC):
                    self.op("pe", lambda e: e.matmul(ps[:, n * 128:(n + 1) * 128], lhsT=wo[:, c, n * 128:(n + 1) * 128],
                                                     rhs=ygT[:, c, :], start=(c == 0), stop=(c == KC - 1)),
                            [wo, ygT], [ps])
            ho = self.rtile("r_ho", 1, [128, KC, 128], F32)
            self.op("dve", lambda e: e.tensor_tensor(out=ho[:], in0=ps[:].rearrange("p (c t) -> p c t", c=KC),
                                                     in1=h[:], op=ALU.add), [ps, h], [ho])
            v0 = PAD if cb == 0 else 0
            self.store(hdst[:, :, t0 + v0:t0 + 128], None, ho, ho[:, :, v0:128])
        self.rn_w = 512
        self.phase_end()

    def zero_pads(self, names):
        zt = self.tiles["zeros"]
        for nm in names:
            for c in range(KC):
                self.store(self.dram[nm][c, :, 0:PAD], None, zt, zt[:], eng="sp")

    def finish(self, out_res=None):
        self.barrier()
        return self.nc


def build_ffn_test(nb):
    p = Prog(nb)
    p.din("hT0", [KC, 128, p.LP])
    p.din("ffn_up", [DEPTH, 128, KC, F2])
    p.din("ffn_down", [DEPTH, 128, NJ, D])
    p.dout("hT1", [KC, 128, p.LP])
    p.setup_consts()
    p.zero_pads(["hT1"])
    p.ffn_phase(0, "hT0", "hT1")
    return p.finish([p.dr(("hT1", "all"))])


def attn_masks():
    s = np.arange(128)[:, None]
    col = np.arange(512)[None, :]
    q, t = col // 128, col % 128
    m = np.zeros((128, 6, 512), np.float32)
    for r in range(4):
        m[:, r, :] = ((q > r) | ((q == r) & (t > s))).astype(np.float32)
    row = (s >= PAD).astype(np.float32)
    m[:, 4, :] = m[:, 0, :] * row
    m[:, 5, :] = row * np.ones((1, 512), np.float32)
    j = np.arange(128)[:, None]
    ss = np.arange(128)[None, :]
    ntri = -(j >= ss).astype(np.float32)
    return m, ntri


def build_att_test(nb):
    p = Prog(nb)
    p.din("hT0", [KC, 128, p.LP])
    p.din("sb_wq", [2, 128, KC, D]); p.din("sb_wk", [128, KC, D]); p.din("sb_wv", [128, KC, D])
    p.din("sb_wo", [2, 128, KC, D])
    p.din("amask", [128, 6, 512]); p.din("ntri", [128, 128])
    p.dscratch("QT", [KC, 128, p.LP], BF16); p.dscratch("KT", [KC, 128, p.LP], BF16)
    p.dscratch("VV", [p.LP, D], BF16); p.dscratch("AT", [KC, 128, p.LP], BF16)
    p.dout("hT1", [KC, 128, p.LP])
    p.setup_consts()
    p.zero_pads(["hT1"])
    p.qkv_phase(0, "hT0", True)
    p.att_phase()
    p.o_phase(0, "hT0", "hT1")
    return p.finish([p.dr(("hT1", "all"))])


def rw_consts():
    p = np.arange(128)[:, None]
    q = np.arange(128)[None, :]
    ident = (p == q).astype(np.float32)
    bd = ((p // 64) == (q // 64)).astype(np.float32)
    mstrict = (p > q).astype(np.float32)
    mt_strict = (q > p).astype(np.float32)
    mt_incl = (q >= p).astype(np.float32)
    ind2 = np.concatenate([(p // 64 == 0), (p // 64 == 1)], axis=1).astype(np.float32)
    onesf = np.ones((128, 128), np.float32)
    zcol = np.zeros((128, 1), np.float32)
    pad = np.zeros((128, 127), np.float32)
    return np.ascontiguousarray(np.concatenate([ident, bd, mstrict, mt_strict, mt_incl, ind2, onesf, zcol, pad],
                                               axis=1))


def declare_rw_inputs(p):
    p.din("rw_wr", [2, 128, KC, D]); p.din("rw_wk", [2, 128, KC, D]); p.din("rw_wv", [2, 128, KC, D])
    p.din("rw_wo", [2, 128, KC, D])
    p.din("rw_w1", [2, 128, KC, 64]); p.din("rw_a1", [2, 128, KC, 64]); p.din("rw_g1", [2, 128, KC, 160])
    p.din("rw_v1", [1, 128, KC, 32])
    p.din("rw_w2", [2, 64, D]); p.din("rw_a2", [2, 64, D]); p.din("rw_g2", [2, 160, D]); p.din("rw_v2", [1, 32, D])
    p.din("rw_lnx_g", [2, D]); p.din("rw_lnx_b", [2, D]); p.din("rw_v0", [1, D])
    p.din("rw_const", [128, 128 * 5 + 2 + 256])
    p.dscratch("VF", [p.LP, D], F32)


def build_rw_test(nb, nlayers=1):
    p = Prog(nb)
    p.din("hT0", [KC, 128, p.LP])
    declare_rw_inputs(p)
    p.dout("hT1", [KC, 128, p.LP])
    p.dscratch("hTa", [KC, 128, p.LP])
    p.setup_consts()
    p.zero_pads(["hT1", "hTa"])
    if nlayers == 1:
        p.rwkv_phase(0, "hT0", "hT1")
    else:
        p.rwkv_phase(0, "hT0", "hTa")
        p.rwkv_phase(1, "hTa", "hT1")
    return p.finish()


def final_phase(p, src):
    p.phase_begin()
    hsrc = p.dram[src].rearrange("c p t -> p c t")
    yo = p.dram["yT"].rearrange("c p t -> p c t")
    t = PAD + NMETA
    while t < p.LP:
        tw = min(512, p.LP - t)
        h = p.rtile("fn_h", 2, [128, KC, 512], F32)
        p.load(h, h[:, :, 0:tw], hsrc[:, :, t:t + tw])
        o = p.rtile("fn_o", 2, [128, KC, 512], F32)
        p.rmsnorm(h, tw, lambda c: p.cc("final_g", c), o)
        p.store(yo[:, :, t - PAD - NMETA:t - PAD - NMETA + tw], None, o, o[:, :, 0:tw])
        t += tw
    p.phase_end()


def build_full(nb):
    p = Prog(nb)
    p.din("hT0", [KC, 128, p.LP])
    p.din("ffn_up", [DEPTH, 128, KC, F2])
    p.din("ffn_down", [DEPTH, 128, NJ, D])
    declare_rw_inputs(p)
    p.din("sb_wq", [2, 128, KC, D]); p.din("sb_wk", [128, KC, D]); p.din("sb_wv", [128, KC, D])
    p.din("sb_wo", [2, 128, KC, D])
    p.din("amask", [128, 6, 512]); p.din("ntri", [128, 128])
    p.dscratch("QT", [KC, 128, p.LP], BF16); p.dscratch("KT", [KC, 128, p.LP], BF16)
    p.dscratch("VV", [p.LP, D], BF16); p.dscratch("AT", [KC, 128, p.LP], BF16)
    p.dscratch("hA", [KC, 128, p.LP]); p.dscratch("hB", [KC, 128, p.LP])
    p.dout("yT", [KC, 128, p.LP - PAD - NMETA])
    p.setup_consts()
    p.zero_pads(["hA", "hB"])
    p.barrier()
    p.rwkv_phase(0, "hT0", "hA")
    p.ffn_phase(0, "hA", "hB")
    p.rwkv_phase(1, "hB", "hA")
    p.ffn_phase(1, "hA", "hB")
    p.qkv_phase(0, "hB", True)
    p.att_phase()
    p.o_phase(0, "hB", "hA")
    p.ffn_phase(2, "hA", "hB")
    p.qkv_phase(1, "hB", False)
    p.att_phase()
    p.o_phase(1, "hB", "hA")
    p.ffn_phase(3, "hA", "hB")
    final_phase(p, "hB")
    return p.finish()


def _w1024(w):
    return np.ascontiguousarray(w.reshape(KC, 128, -1).transpose(1, 0, 2))


def _wst(w):
    return np.stack([_w1024(w[i]) for i in range(w.shape[0])])


def _cols(v):
    v = np.asarray(v, np.float32).reshape(-1, KC, 128)
    return np.ascontiguousarray(v.transpose(2, 0, 1).reshape(128, -1))


def host_inputs(inp, nb):
    f = {k: np.asarray(v, np.float32) for k, v in inp.items()}
    m, ntri = attn_masks()
    cw = f["ffn_conv_w"]
    shared = {
        "ffn_up": np.ascontiguousarray(f["ffn_up"].reshape(DEPTH, KC, 128, F2).transpose(0, 2, 1, 3)),
        "ffn_down": np.ascontiguousarray(f["ffn_down"].reshape(DEPTH, NJ, 128, D).transpose(0, 2, 1, 3)),
        "norm_ffn_g": _cols(f["norm_ffn_g"]), "norm_mix_g": _cols(f["norm_mix_g"]),
        "kv_norm_g": _cols(f["kv_norm_g"]), "final_g": _cols(f["final_norm_g"]),
        "conv_w": np.ascontiguousarray(cw.reshape(DEPTH, 3, 44, 128).transpose(3, 0, 2, 1).reshape(128, -1)),
        "conv_b": np.ascontiguousarray(f["ffn_conv_b"].reshape(DEPTH, 44, 128).transpose(2, 0, 1).reshape(128, -1)),
        "rw_mix": _cols(f["rw_mix"]), "rw_w0": _cols(f["rw_w0"]), "rw_a0": _cols(f["rw_a0"]),
        "rw_kk": _cols(f["rw_kk"]), "rw_ka": _cols(f["rw_ka"]), "rw_rk": _cols(f["rw_rk"].reshape(2, D)),
        "rw_wr": _wst(f["rw_wr"]), "rw_wk": _wst(f["rw_wk"]), "rw_wv": _wst(f["rw_wv"]), "rw_wo": _wst(f["rw_wo"]),
        "rw_w1": _wst(f["rw_w1"]), "rw_a1": _wst(f["rw_a1"]), "rw_g1": _wst(f["rw_g1"]), "rw_v1": _wst(f["rw_v1"]),
        "rw_w2": f["rw_w2"], "rw_a2": f["rw_a2"], "rw_g2": f["rw_g2"], "rw_v2": f["rw_v2"],
        "rw_lnx_g": f["rw_lnx_g"], "rw_lnx_b": f["rw_lnx_b"], "rw_v0": f["rw_v0"],
        "rw_const": rw_consts(),
        "sb_wq": _wst(f["sb_wq"]), "sb_wo": _wst(f["sb_wo"]), "sb_wk": _w1024(f["sb_wk"]), "sb_wv": _w1024(f["sb_wv"]),
        "amask": m, "ntri": ntri,
    }
    return shared


def host_h0(xb, meta, nb):
    LP = nb * 128
    hT = np.zeros((D, LP), np.float32)
    hT[:, PAD:PAD + NMETA] = meta.T
    hT[:, PAD + NMETA:] = xb.T
    return hT.reshape(KC, 128, LP)


_CACHE = {}


def kernel(**inputs):
    x = np.asarray(inputs["x"], np.float32)
    B, S, _ = x.shape
    nb = (PAD + NMETA + S) // 128
    if nb not in _CACHE:
        _CACHE[nb] = build_full(nb)
    nc = _CACHE[nb]
    shared = host_inputs({k: v for k, v in inputs.items() if k != "x"}, nb)
    meta = np.asarray(inputs["meta_tokens"], np.float32)
    ncores = 8
    in_maps = []
    for cidx in range(ncores):
        b = cidx % B
        m = dict(shared)
        m["hT0"] = host_h0(x[b], meta, nb)
        in_maps.append(m)
    res = run_bass_kernel_spmd(nc, in_maps, core_ids=list(range(ncores)))
    out = np.empty((B, S, D), np.float32)
    for b in range(B):
        out[b] = res.results[b]["yT"].reshape(D, S).T
    return out
```

```python
import numpy as np
from contextlib import ExitStack
import concourse.bass as bass
import concourse.mybir as mybir
from concourse.bass_utils import run_bass_kernel_spmd

F32 = mybir.dt.float32
BF16 = mybir.dt.bfloat16
ALU = mybir.AluOpType
AF = mybir.ActivationFunctionType
AX = mybir.AxisListType

D = 1024
KC = 8
NH = 16
HD = 64
NMETA = 16
PAD = 112
DFF = 2816
F2 = 5632
NJ = 22
FT = 384
DEPTH = 4
RMS_EPS = 1e-6
GN_EPS = 64e-5


class Res:
    __slots__ = ("name", "lw", "rd")

    def __init__(self, name):
        self.name = name
        self.lw = None
        self.rd = {}


class KB:
    SEM_LIMIT = 30000

    def __init__(self, nc):
        self.nc = nc
        self.q = {e: [] for e in ("pe", "act", "dve", "pool", "sp")}
        self.sems = {}
        self.cur = {}
        self.waited = {e: {} for e in self.q}
        self.nsem = 0
        self.dmasem = {}
        self.nops = 0

    def _newsem(self, tag):
        key = "%s_%d" % (tag, self.nsem)
        self.nsem += 1
        self.sems[key] = self.nc.alloc_semaphore(name=key)
        return key

    def _eng_event(self, eng):
        c = self.cur.get(eng)
        if c is None or c[1] >= self.SEM_LIMIT:
            c = [self._newsem(eng), 0]
            self.cur[eng] = c
        c[1] += 1
        return (c[0], c[1])

    def _dma_event(self, chain, eng="sp"):
        cls = "sw" if eng == "pool" else "hw"
        chain = (cls, chain)
        c = self.dmasem.get(chain)
        if c is None or c[1] >= self.SEM_LIMIT:
            free = getattr(self, "free_dma", {}).get(cls, [])
            free.sort(key=lambda x: x[1])
            if free and free[0][1] < self.SEM_LIMIT // 2:
                c = list(free.pop(0))
            else:
                c = [self._newsem("d" + cls), 0]
            self.dmasem[chain] = c
        c[1] += 16
        return (c[0], c[1])

    def recycle_dma(self):
        free = getattr(self, "free_dma", {"sw": [], "hw": []})
        for ch, c in self.dmasem.items():
            free[ch[0]].append((c[0], c[1]))
        self.free_dma = free
        self.dmasem = {}

    def _deps(self, eng, reads, writes, is_dma):
        waits = {}

        def need(sk, v, src_eng, kind):
            if (not is_dma) and eng == "pe" and src_eng == "pe":
                return
            if (not is_dma) and src_eng == eng and kind == "war":
                return
            if self.waited[eng].get(sk, 0) >= v:
                return
            if waits.get(sk, 0) < v:
                waits[sk] = v

        for r in reads:
            if r.lw is not None:
                need(r.lw[0], r.lw[1], r.lw[2], "raw")
        for w in writes:
            if w.lw is not None:
                need(w.lw[0], w.lw[1], w.lw[2], "waw")
            for sk, (v, e) in w.rd.items():
                need(sk, v, e, "war")
        return waits

    def _record(self, ev, src, reads, writes):
        for r in reads:
            r.rd[ev[0]] = (ev[1], src)
        for w in writes:
            w.lw = (ev[0], ev[1], src)
            w.rd = {}

    def op(self, eng, fn, reads=(), writes=()):
        waits = self._deps(eng, reads, writes, False)
        for sk, v in waits.items():
            self.waited[eng][sk] = v
        ev = self._eng_event(eng)
        self._emit(eng, fn, waits, ev, 1)
        self._record(ev, eng, reads, writes)
        self.nops += 1
        return ev

    def dma(self, eng, fn, reads=(), writes=(), chain=None):
        waits = self._deps(eng, reads, writes, True)
        for sk, v in waits.items():
            self.waited[eng][sk] = v
        assert chain is not None
        ev = self._dma_event(chain, eng)
        self._emit(eng, fn, waits, ev, 16)
        self._record(ev, "dma", reads, writes)
        self.nops += 1
        return ev

    def wait_all(self, eng, resources):
        waits = {}
        for r in resources:
            if r.lw is not None:
                sk, v = r.lw[0], r.lw[1]
                if self.waited[eng].get(sk, 0) < v and waits.get(sk, 0) < v:
                    waits[sk] = v
        for sk, v in waits.items():
            self.waited[eng][sk] = v
        self._emit(eng, None, waits, None, 0)

    ENG = {"pe": "tensor", "act": "scalar", "dve": "vector", "pool": "gpsimd", "sp": "sync"}

    def _emit(self, eng, fn, waits, ev, inc):
        engine = getattr(self.nc, self.ENG[eng])
        for sk, v in waits.items():
            engine.wait_ge(self.sems[sk], v)
        if fn is not None:
            ins = fn(engine)
            ins.then_inc(self.sems[ev[0]], inc)

    def emit(self):
        return

    def emit_old(self):
        nc = self.nc
        engs = {"pe": "tensor", "act": "scalar", "dve": "vector", "pool": "gpsimd", "sp": "sync"}
        with nc.Block() as block:
            for e, attr in engs.items():
                ops = self.q[e]
                if not ops:
                    continue

                def body(engine, ops=ops):
                    for fn, waits, ev, inc in ops:
                        for sk, v in waits:
                            engine.wait_ge(self.sems[sk], v)
                        if fn is not None:
                            ins = fn(engine)
                            ins.then_inc(self.sems[ev[0]], inc)
                getattr(block, attr)(body)


class T:
    def __init__(self, nc, name, shape, dtype, psum=False, stack=None):
        self.name = name
        if psum:
            cm = nc.psum_tensor(name, list(shape), dtype)
        else:
            cm = nc.sbuf_tensor(name, list(shape), dtype)
        self.t = stack.enter_context(cm)
        self.r = Res(name)

    def __getitem__(self, idx):
        return self.t[idx]


class Prog:
    def __init__(self, nb, dbg=None):
        self.nb = nb
        self.LP = nb * 128
        self.dbg = dbg or {}
        self.nc = bass.Bass("TRN2", target_bir_lowering=False)
        self.kb = KB(self.nc)
        self.dram = {}
        self.dres = {}
        self.tiles = {}
        self.rot = {}
        self.gstack = ExitStack()
        self.pstack = None
        self.pid = 0

    def phase_begin(self):
        self.pstack = ExitStack()
        self.pid += 1
        self.ptiles = []

    def phase_end(self):
        self.barrier()
        self.kb.recycle_dma()
        self.pstack.close()
        self.pstack = None
        for nm in self.ptiles:
            self.tiles.pop(nm, None)
        self.rot = {}

    def barrier(self):
        kb = self.kb
        targets = {}
        for e, c in kb.cur.items():
            targets[c[0]] = c[1]
        for ch, c in kb.dmasem.items():
            targets[c[0]] = c[1]
        for e in ("pe", "act", "dve", "pool", "sp"):
            waits = {}
            for sk, v in targets.items():
                if kb.cur.get(e) is not None and kb.cur[e][0] == sk:
                    continue
                if kb.waited[e].get(sk, 0) < v:
                    waits[sk] = v
                    kb.waited[e][sk] = v
            kb._emit(e, None, waits, None, 0)

    def din(self, name, shape, dtype=F32):
        self.dram[name] = self.nc.dram_tensor(name, list(shape), dtype, kind="ExternalInput").ap()
        return self.dram[name]

    def dout(self, name, shape, dtype=F32):
        self.dram[name] = self.nc.dram_tensor(name, list(shape), dtype, kind="ExternalOutput").ap()
        return self.dram[name]

    def dscratch(self, name, shape, dtype=F32):
        self.dram[name] = self.nc.dram_tensor(name, list(shape), dtype, kind="Internal").ap()
        return self.dram[name]

    def dr(self, key):
        r = self.dres.get(key)
        if r is None:
            r = Res(str(key))
            self.dres[key] = r
        return r

    def tile(self, name, shape, dtype=F32, psum=False):
        if self.pstack is not None:
            t = T(self.nc, "%s_p%d" % (name, self.pid), shape, dtype, psum, self.pstack)
            self.ptiles.append(name)
        else:
            t = T(self.nc, name, shape, dtype, psum, self.gstack)
        self.tiles[name] = t
        return t

    def rtile(self, name, n, shape, dtype=F32, psum=False):
        ent = self.rot.get(name)
        if ent is None:
            ent = [[self.tile("%s%d" % (name, i), shape, dtype, psum) for i in range(n)], 0]
            self.rot[name] = ent
        t = ent[0][ent[1] % n]
        ent[1] += 1
        return t

    def op(self, eng, fn, reads=(), writes=()):
        return self.kb.op(eng, fn, [x.r if isinstance(x, T) else x for x in reads],
                          [x.r if isinstance(x, T) else x for x in writes])

    def load(self, dst, dst_ap, src_ap, src_res=None, eng="sp"):
        self.kb.dma(eng, lambda e: e.dma_start(out=dst_ap, in_=src_ap),
                    reads=[], writes=[dst.r], chain="ld_" + dst.name)

    def store(self, dst_ap, dst_res, src, src_ap, eng="sp"):
        self.kb.dma(eng, lambda e: e.dma_start(out=dst_ap, in_=src_ap),
                    reads=[src.r], writes=[], chain="st_" + src.name)

    def rmsnorm(self, h, w, g_ap, out, out_off=0, tag="n"):
        rw_ = getattr(self, "rn_w", 512)
        sq = self.rtile("rn_sq", 1, [128, KC, rw_], BF16)
        ss = self.rtile("rn_ss", 1, [128, 512], F32, psum=True)
        rstd = self.rtile("rn_rstd", 2, [128, rw_], F32)
        ones = self.tiles["onesD"]
        self.op("dve", lambda e: e.tensor_tensor(out=sq[:, :, 0:w], in0=h[:, :, 0:w], in1=h[:, :, 0:w],
                                                  op=ALU.mult), [h], [sq])
        for c in range(KC):
            self.op("pe", lambda e, c=c: e.matmul(ss[:, 0:w], lhsT=ones[:], rhs=sq[:, c, 0:w],
                                                  start=(c == 0), stop=(c == KC - 1)), [ones, sq], [ss])
        self.op("act", lambda e: e.activation(out=rstd[:, 0:w], in_=ss[:, 0:w], func=AF.Sqrt, bias=RMS_EPS,
                                              scale=1.0), [ss], [rstd])
        self.op("dve", lambda e: e.reciprocal(out=rstd[:, 0:w], in_=rstd[:, 0:w]), [rstd], [rstd])
        for c in range(KC):
            self.op("dve", lambda e, c=c: e.scalar_tensor_tensor(
                out=out[:, c, out_off:out_off + w], in0=h[:, c, 0:w], scalar=g_ap(c), in1=rstd[:, 0:w],
                op0=ALU.mult, op1=ALU.mult), [h, rstd, self.tiles["consts"]], [out])

    def setup_consts(self):
        onesD = self.tile("onesD", [128, 128], BF16)
        self.op("pool", lambda e: e.memset(onesD[:], 1.0 / D), [], [onesD])
        zt = self.tile("zeros", [128, PAD], F32)
        self.op("pool", lambda e: e.memset(zt[:], 0.0), [], [zt])
        self.din("norm_ffn_g", [128, DEPTH * KC])
        self.din("conv_w", [128, DEPTH * 44 * 3])
        self.din("conv_b", [128, DEPTH * 44])
        self.din("final_g", [128, KC])
        self.din("norm_mix_g", [128, DEPTH * KC])
        self.din("kv_norm_g", [128, KC])
        for nm, n in (("rw_mix", 2 * 6 * KC), ("rw_w0", 2 * KC), ("rw_a0", 2 * KC), ("rw_kk", 2 * KC),
                      ("rw_ka", 2 * KC), ("rw_rk", 2 * KC)):
            self.din(nm, [128, n])
        ncol = DEPTH * KC + DEPTH * 44 * 3 + DEPTH * 44 + KC + DEPTH * KC + KC + 2 * 6 * KC + 5 * 2 * KC
        consts = self.tile("consts", [128, ncol], F32)
        self.coff = {}
        off = 0
        for nm, n in (("norm_ffn_g", DEPTH * KC), ("conv_w", DEPTH * 44 * 3), ("conv_b", DEPTH * 44),
                      ("final_g", KC), ("norm_mix_g", DEPTH * KC), ("kv_norm_g", KC),
                      ("rw_mix", 2 * 6 * KC), ("rw_w0", 2 * KC), ("rw_a0", 2 * KC), ("rw_kk", 2 * KC),
                      ("rw_ka", 2 * KC), ("rw_rk", 2 * KC)):
            self.coff[nm] = off
            self.load(consts, consts[:, off:off + n], self.dram[nm][:, :], self.dr(nm))
            off += n

    def cc(self, nm, idx):
        o = self.coff[nm] + idx
        return self.tiles["consts"][:, o:o + 1]

    def ffn_tiles(self):
        tiles = []
        o = PAD
        while o < self.LP:
            ow = min(FT - 2, self.LP - o)
            tiles.append((o - 2, ow))
            o += ow
        return tiles

    def ffn_weights(self, layer):
        wup = self.tiles.get("wup") or self.tile("wup", [128, KC, F2], BF16)
        wdn = self.tiles.get("wdn") or self.tile("wdn", [128, NJ, D], BF16)
        up = self.dram["ffn_up"]
        dn = self.dram["ffn_down"]
        for c in range(KC):
            for hf in range(2):
                self.load(wup, wup[:, c, hf * DFF:(hf + 1) * DFF], up[layer, :, c, hf * DFF:(hf + 1) * DFF],
                          self.dr("ffn_up"), eng="pool")
        for j0 in range(0, NJ, 2):
            self.load(wdn, wdn[:, j0:j0 + 2, :], dn[layer, :, j0:j0 + 2, :], self.dr("ffn_down"), eng="pool")
        return wup, wdn

    def ffn_phase(self, layer, src, dst):
        self.phase_begin()
        wup, wdn = self.ffn_weights(layer)
        hsrc = self.dram[src].rearrange("c p t -> p c t")
        hdst = self.dram[dst].rearrange("c p t -> p c t")
        for ti, (i0, ow) in enumerate(self.ffn_tiles()):
            iw = ow + 2
            h = self.rtile("f_h", 2, [128, KC, FT], F32)
            self.load(h, h[:, :, 0:iw], hsrc[:, :, i0:i0 + iw], self.dr((src, "all")))
            hn = self.rtile("f_hn", 1, [128, KC, FT], BF16)
            self.rmsnorm(h, iw, lambda c: self.cc("norm_ffn_g", layer * KC + c), hn)
            m = self.rtile("f_m", 1, [128, NJ, FT], BF16)
            for j in range(NJ):
                ys = []
                for half, ch in enumerate((j, NJ + j)):
                    ps = self.rtile("f_ps", 4, [128, 512], F32, psum=True)
                    for c in range(KC):
                        self.op("pe", lambda e, c=c, ps=ps, ch=ch: e.matmul(
                            ps[:, 0:iw], lhsT=wup[:, c, ch * 128:(ch + 1) * 128], rhs=hn[:, c, 0:iw],
                            start=(c == 0), stop=(c == KC - 1)), [wup, hn], [ps])
                    y = self.rtile("f_y", 4, [128, FT], F32)
                    cw = lambda tap, ch=ch: self.cc("conv_w", (layer * 44 + ch) * 3 + tap)
                    cb = self.cc("conv_b", layer * 44 + ch)
                    cst = self.tiles["consts"]
                    self.op("act", lambda e, y=y, ps=ps, cw=cw, cb=cb: e.activation(
                        out=y[:, 0:ow], in_=ps[:, 2:2 + ow], func=AF.Identity, bias=cb, scale=cw(2)),
                        [ps, cst], [y])
                    self.op("dve", lambda e, y=y, ps=ps, cw=cw: e.scalar_tensor_tensor(
                        out=y[:, 0:ow], in0=ps[:, 1:1 + ow], scalar=cw(1), in1=y[:, 0:ow],
                        op0=ALU.mult, op1=ALU.add), [ps, y, cst], [y])
                    self.op("dve", lambda e, y=y, ps=ps, cw=cw: e.scalar_tensor_tensor(
                        out=y[:, 0:ow], in0=ps[:, 0:ow], scalar=cw(0), in1=y[:, 0:ow],
                        op0=ALU.mult, op1=ALU.add), [ps, y, cst], [y])
                    ys.append(y)
                yg, yv = ys
                self.op("act", lambda e, yg=yg: e.activation(out=yg[:, 0:ow], in_=yg[:, 0:ow], func=AF.Silu),
                        [yg], [yg])
                self.op("dve", lambda e, yg=yg, yv=yv, j=j: e.tensor_tensor(
                    out=m[:, j, 0:ow], in0=yg[:, 0:ow], in1=yv[:, 0:ow], op=ALU.mult), [yg, yv], [m])
            for n in range(KC):
                ps = self.rtile("f_ps", 4, [128, 512], F32, psum=True)
                for j in range(NJ):
                    self.op("pe", lambda e, j=j, n=n, ps=ps: e.matmul(
                        ps[:, 0:ow], lhsT=wdn[:, j, n * 128:(n + 1) * 128], rhs=m[:, j, 0:ow],
                        start=(j == 0), stop=(j == NJ - 1)), [wdn, m], [ps])
                ho = self.rtile("f_ho", 3, [128, FT], F32)
                self.op("dve", lambda e, n=n, ps=ps, ho=ho: e.tensor_tensor(
                    out=ho[:, 0:ow], in0=ps[:, 0:ow], in1=h[:, n, 2:2 + ow], op=ALU.add), [ps, h], [ho])
                self.store(hdst[:, n, i0 + 2:i0 + 2 + ow], self.dr((dst, "all")), ho, ho[:, 0:ow], eng="sp")
        self.phase_end()


    def load_w1024(self, name, dram_ap):
        w = self.tile(name, [128, KC, D], BF16)
        for c0 in range(0, KC, 2):
            self.load(w, w[:, c0:c0 + 2, :], dram_ap[:, c0:c0 + 2, :], self.dr("wts"), eng="pool")
        return w

    def blk_groups(self):
        gs = []
        b = 0
        while b < self.nb:
            n = min(4, self.nb - b)
            gs.append((b, n))
            b += n
        return gs

    def qkv_phase(self, j, src, do_kv):
        self.phase_begin()
        layer = 2 + j
        wq = self.load_w1024("wq", self.dram["sb_wq"][j])
        if do_kv:
            wk = self.load_w1024("wk", self.dram["sb_wk"])
            wv = self.load_w1024("wv", self.dram["sb_wv"])
        hsrc = self.dram[src].rearrange("c p t -> p c t")
        qT = self.dram["QT"].rearrange("c p t -> p c t")
        kT = self.dram["KT"].rearrange("c p t -> p c t")
        vd = self.dram["VV"]
        for (b0, nblk) in self.blk_groups():
            t0, tw = b0 * 128, nblk * 128
            h = self.rtile("q_h", 2, [128, KC, 512], F32)
            self.load(h, h[:, :, 0:tw], hsrc[:, :, t0:t0 + tw], self.dr((src, "all")))
            hn = self.rtile("q_hn", 2, [128, KC, 512], BF16)
            self.rmsnorm(h, tw, lambda c: self.cc("norm_mix_g", layer * KC + c), hn)
            jobs = [(wq, qT, 0.125, "QT")]
            if do_kv:
                kn = self.rtile("q_kn", 2, [128, KC, 512], BF16)
                self.rmsnorm(h, tw, lambda c: self.cc("kv_norm_g", c), kn)
                jobs.append((wk, kT, 1.0, "KT"))
            for (w, dst, scale, dname) in jobs:
                xin = hn if dname == "QT" else kn
                for n in range(KC):
                    ps = self.rtile("q_ps", 4, [128, 512], F32, psum=True)
                    for c in range(KC):
                        self.op("pe", lambda e: e.matmul(ps[:, 0:tw], lhsT=w[:, c, n * 128:(n + 1) * 128],
                                                         rhs=xin[:, c, 0:tw], start=(c == 0), stop=(c == KC - 1)),
                                [w, xin], [ps])
                    o = self.rtile("q_o", 4, [128, 512], BF16)
                    self.op("act", lambda e: e.activation(out=o[:, 0:tw], in_=ps[:, 0:tw], func=AF.Copy,
                                                          scale=scale), [ps], [o])
                    self.store(dst[:, n, t0:t0 + tw], self.dr((dname, "all")), o, o[:, 0:tw])
            if do_kv:
                for bi in range(nblk):
                    for hf in range(2):
                        ps = self.rtile("q_ps", 4, [128, 512], F32, psum=True)
                        for c in range(KC):
                            self.op("pe", lambda e: e.matmul(ps[:, :], lhsT=kn[:, c, bi * 128:(bi + 1) * 128],
                                                             rhs=wv[:, c, hf * 512:(hf + 1) * 512],
                                                             start=(c == 0), stop=(c == KC - 1)), [wv, kn], [ps])
                        o = self.rtile("q_o", 4, [128, 512], BF16)
                        self.op("dve", lambda e: e.tensor_copy(out=o[:, :], in_=ps[:, :]), [ps], [o])
                        r0 = (b0 + bi) * 128
                        self.store(vd[r0:r0 + 128, hf * 512:(hf + 1) * 512], self.dr(("VV", "all")), o, o[:, :])
        self.phase_end()

    def att_sweep(self, slot, kt, qt, vt, at, msk, ntri, nones, hh, b0, ng):
        pb = 64 * hh
        W = ng * 128
        q0 = b0 * 128
        hi = b0 + ng - 1
        outp = slot["out"]
        z = slot["z"]
        spsum = None
        for kb in range(hi, -1, -1):
            self.op("pe", lambda e: e.matmul(z[:, 0:W], lhsT=kt[pb:pb + 64, kb * 128:(kb + 1) * 128],
                                             rhs=qt[pb:pb + 64, q0:q0 + W], start=True, stop=False), [kt, qt], [z])
            yield
            ex = slot["ex"][kb % 2]
            sp = slot["sp"][kb % 2]
            self.op("act", lambda e: e.activation(out=ex[:, 0:W], in_=z[:, 0:W], func=AF.Exp), [z], [ex])
            self.op("act", lambda e: e.activation(out=sp[:, 0:W], in_=ex[:, 0:W], func=AF.Ln, bias=1.0, scale=1.0),
                    [ex], [sp])
            mi = None
            if kb >= b0:
                mi = 4 if kb == 0 else kb - b0
            elif kb == 0:
                mi = 5
            if mi is not None:
                self.op("pool", lambda e: e.tensor_tensor(out=sp[:, 0:W], in0=sp[:, 0:W], in1=msk[:, mi, 0:W],
                                                          op=ALU.mult), [sp, msk], [sp])
            yield
            self.op("pe", lambda e: e.matmul(z[:, 0:W], lhsT=ntri[:], rhs=sp[:, 0:W], start=False,
                                             stop=(spsum is None)), [ntri, sp], [z])
            if spsum is not None:
                self.op("pe", lambda e: e.matmul(z[:, 0:W], lhsT=nones[:], rhs=spsum[:, 0:W], start=False, stop=True),
                        [nones, spsum], [z])
            if kb > 0:
                nsum = slot["spsum"][kb % 2]
                if spsum is None:
                    self.op("dve", lambda e: e.tensor_copy(out=nsum[:, 0:W], in_=sp[:, 0:W]), [sp], [nsum])
                else:
                    self.op("dve", lambda e: e.tensor_tensor(out=nsum[:, 0:W], in0=spsum[:, 0:W], in1=sp[:, 0:W],
                                                              op=ALU.add), [spsum, sp], [nsum])
                spsum = nsum
            yield
            w = slot["w"][kb % 2]
            self.op("act", lambda e: e.activation(out=w[:, 0:W], in_=z[:, 0:W], func=AF.Exp), [z], [w])
            if mi is not None:
                self.op("pool", lambda e: e.tensor_tensor(out=w[:, 0:W], in0=w[:, 0:W], in1=msk[:, mi, 0:W],
                                                          op=ALU.mult), [w, msk], [w])
            yield
            self.op("pe", lambda e: e.matmul(outp[pb:pb + 64, 0:W], lhsT=vt[:, kb, pb:pb + 64], rhs=w[:, 0:W],
                                             start=(kb == hi), stop=(kb == 0)), [vt, w], [outp])
        yield
        self.op("dve", lambda e: e.tensor_copy(out=at[pb:pb + 64, q0:q0 + W], in_=outp[pb:pb + 64, 0:W]), [outp], [at])

    def att_phase(self, nslots=4):
        self.phase_begin()
        nb, LP = self.nb, self.LP
        msk = self.tile("amask", [128, 6, 512], BF16)
        self.load(msk, msk[:], self.dram["amask"][:, :, :], eng="pool")
        ntri = self.tile("ntri", [128, 128], BF16)
        self.load(ntri, ntri[:], self.dram["ntri"][:, :], eng="pool")
        nones = self.tile("nones1", [128, 128], BF16)
        self.op("pool", lambda e: e.memset(nones[:], -1.0), [], [nones])
        qT = self.dram["QT"]
        kT = self.dram["KT"]
        vd = self.dram["VV"].rearrange("(b s) n -> s b n", s=128)
        aT = self.dram["AT"]
        slots = []
        outs = [self.tile("a_out%d" % k, [128, 512], F32, psum=True) for k in range((nslots + 1) // 2)]
        for si in range(nslots):
            slots.append({
                "out": outs[si // 2],
                "z": self.tile("a_z%d" % si, [128, 512], F32, psum=True),
                "ex": [self.tile("a_e%d_%d" % (si, k), [128, 512], F32) for k in range(2)],
                "sp": [self.tile("a_sp%d_%d" % (si, k), [128, 512], BF16) for k in range(2)],
                "w": [self.tile("a_w%d_%d" % (si, k), [128, 512], BF16) for k in range(2)],
                "spsum": [self.tile("a_ss%d_%d" % (si, k), [128, 512], BF16) for k in range(2)],
            })
        groups = self.blk_groups()
        for c in range(KC):
            kt = self.rtile("a_k", 2, [128, LP], BF16)
            qt = self.rtile("a_q", 2, [128, LP], BF16)
            vt = self.rtile("a_v", 2, [128, nb, 128], BF16)
            at = self.rtile("a_o", 2, [128, LP], BF16)
            self.load(kt, kt[:], kT[c])
            self.load(qt, qt[:], qT[c])
            self.load(vt, vt[:], vd[:, :, c * 128:(c + 1) * 128])
            todo = [[(hh, b0, ng) for (b0, ng) in reversed(groups)] for hh in range(2)]
            active = [None] * nslots
            while todo[0] or todo[1] or any(a is not None for a in active):
                for si in range(nslots):
                    if active[si] is None:
                        lst = todo[si % 2]
                        if lst:
                            hh, b0, ng = lst.pop(0)
                            active[si] = self.att_sweep(slots[si], kt, qt, vt, at, msk, ntri, nones, hh, b0, ng)
                    if active[si] is not None:
                        try:
                            next(active[si])
                        except StopIteration:
                            active[si] = None
            self.store(aT[c], None, at, at[:])
        self.phase_end()

    def o_phase(self, j, src, dst):
        self.phase_begin()
        wo = self.load_w1024("wo", self.dram["sb_wo"][j])
        hsrc = self.dram[src].rearrange("c p t -> p c t")
        hdst = self.dram[dst].rearrange("c p t -> p c t")
        aT = self.dram["AT"].rearrange("c p t -> p c t")
        for (b0, nblk) in self.blk_groups():
            t0, tw = b0 * 128, nblk * 128
            v0 = PAD if b0 == 0 else 0
            h = self.rtile("o_h", 2, [128, KC, 512], F32)
            self.load(h, h[:, :, 0:tw], hsrc[:, :, t0:t0 + tw], self.dr((src, "all")))
            a = self.rtile("o_a", 2, [128, KC, 512], BF16)
            self.load(a, a[:, :, 0:tw], aT[:, :, t0:t0 + tw], self.dr(("AT", "all")))
            for n in range(KC):
                ps = self.rtile("o_ps", 4, [128, 512], F32, psum=True)
                for c in range(KC):
                    self.op("pe", lambda e: e.matmul(ps[:, 0:tw], lhsT=wo[:, c, n * 128:(n + 1) * 128],
                                                     rhs=a[:, c, 0:tw], start=(c == 0), stop=(c == KC - 1)),
                            [wo, a], [ps])
                ho = self.rtile("o_ho", 3, [128, 512], F32)
                self.op("dve", lambda e: e.tensor_tensor(out=ho[:, 0:tw], in0=ps[:, 0:tw], in1=h[:, n, 0:tw],
                                                         op=ALU.add), [ps, h], [ho])
                self.store(hdst[:, n, t0 + v0:t0 + tw], self.dr((dst, "all")), ho, ho[:, v0:tw])
        self.phase_end()


    def pool_init(self, n):
        self.tpool = [self.tile("tp%d" % i, [128, D], F32) for i in range(n)]

    def tget(self):
        return self.tpool.pop(0)

    def tfree(self, *ts):
        for t in ts:
            self.tpool.append(t)

    @staticmethod
    def fm(t, lo=0, hi=128):
        return t.t[:].rearrange("p (c t) -> p c t", c=KC)[:, :, lo:hi]

    def bc(self, nm, idx0):
        o = self.coff[nm] + idx0
        return self.tiles["consts"][:, o:o + KC].unsqueeze(2).broadcast_to([128, KC, 128])

    def rwkv_phase(self, i, src, dst):
        self.phase_begin()
        self.rn_w = 128
        nb, LP = self.nb, self.LP
        layer = i
        cst = self.tiles["consts"]
        dr = self.dram
        wr = self.load_w1024("wr", dr["rw_wr"][i])
        wk = self.load_w1024("wk", dr["rw_wk"][i])
        wv = self.load_w1024("wv", dr["rw_wv"][i])
        wo = self.load_w1024("wo", dr["rw_wo"][i])

        def ldw(name, shape, ap):
            t = self.tile(name, shape, BF16)
            self.load(t, t[:], ap, eng="pool")
            return t
        w1 = ldw("w1", [128, KC, 64], dr["rw_w1"][i])
        a1 = ldw("a1", [128, KC, 64], dr["rw_a1"][i])
        g1 = ldw("g1", [128, KC, 160], dr["rw_g1"][i])
        w2 = ldw("w2", [64, D], dr["rw_w2"][i])
        a2 = ldw("a2", [64, D], dr["rw_a2"][i])
        g2a = ldw("g2a", [128, D], dr["rw_g2"][i, 0:128, :])
        g2b = ldw("g2b", [32, D], dr["rw_g2"][i, 128:160, :])
        if i > 0:
            v1 = ldw("v1", [128, KC, 32], dr["rw_v1"][i - 1])
            v2 = ldw("v2", [32, D], dr["rw_v2"][i - 1])
        rc = self.tile("rwc", [128, 128 * 5 + 2 + 256], F32)
        self.load(rc, rc[:], dr["rw_const"][:, :])
        ident = rc[:, 0:128]
        bd = rc[:, 128:256]
        mstrict = rc[:, 256:384]
        mt2 = rc[:, 384:640]
        ind2 = rc[:, 640:642]
        onesf = rc[:, 642:770]
        zcol = rc[:, 770:771]
        tmb = self.tile("tmb", [128, 3 if i > 0 else 2, D], F32)
        self.load(tmb, tmb[:, 0, :], dr["rw_lnx_g"][i:i + 1, :].partition_broadcast(128))
        self.load(tmb, tmb[:, 1, :], dr["rw_lnx_b"][i:i + 1, :].partition_broadcast(128))
        if i > 0:
            self.load(tmb, tmb[:, 2, :], dr["rw_v0"][i - 1:i, :].partition_broadcast(128))
        self.pool_init(10)
        st2 = [self.tile("st2_%d" % c, [128, 128], F32) for c in range(KC)]
        for c in range(KC):
            self.op("pool", lambda e: e.memset(st2[c][:], 0.0), [], [st2[c]])
        hsrc = dr[src].rearrange("c p t -> p c t")
        hdst = dr[dst].rearrange("c p t -> p c t")
        vf = dr["VF"]
        hn_prev = None
        ART = self.tile("AR", [128, KC, 256], F32)
        btT = self.tile("bt", [128, KC, 128], F32)
        ktT = self.tile("kt", [128, KC, 128], F32)
        pcT = self.tile("pc", [128, KC], F32)
        ssb = self.tile("ssb", [128, NH], F32)
        mu = self.tile("mu", [128, NH], F32)
        var = self.tile("var", [128, NH], F32)
        midw = self.tile("midw", [64, 128], BF16)
        mida = self.tile("mida", [64, 128], BF16)
        midga = self.tile("midga", [128, 128], BF16)
        midgb = self.tile("midgb", [32, 128], BF16)
        midv = self.tile("midv", [32, 128], BF16)
        ygT = self.tile("yg", [128, KC, 128], BF16)
        C0 = 0.6065306597126334

        def pbig():
            return self.rtile("pbig", 2, [128, D], F32, psum=True)

        def psm():
            return self.rtile("psm", 3, [128, 512], F32, psum=True)

        for cb in range(nb):
            t0 = cb * 128
            h = self.rtile("r_h", 2, [128, KC, 128], F32)
            self.load(h, h[:], hsrc[:, :, t0:t0 + 128])
            hn = self.rtile("r_hn", 1, [128, KC, 129], F32)
            if hn_prev is None:
                self.op("pool", lambda e: e.memset(hn[:, :, 0:1], 0.0), [], [hn])
            else:
                self.op("pool", lambda e: e.tensor_copy(out=hn[:, :, 0:1], in_=hn[:, :, 128:129]), [hn], [hn])
            self.rmsnorm(h, 128, lambda c: self.cc("norm_mix_g", layer * KC + c), hn, out_off=1)
            hn_prev = hn
            xx = self.tget()
            self.op("dve", lambda e: e.tensor_tensor(out=self.fm(xx), in0=hn[:, :, 0:128], in1=hn[:, :, 1:129],
                                                     op=ALU.subtract), [hn], [xx])
            def mkx(q):
                x = self.rtile("xq", 3, [128, KC, 128], BF16)
                tm = self.tget()
                eng = "dve" if q % 2 == 0 else "pool"
                self.op(eng, lambda e: e.tensor_tensor(out=self.fm(tm), in0=self.fm(xx),
                                                       in1=self.bc("rw_mix", (i * 6 + q) * KC), op=ALU.mult),
                        [xx, cst], [tm])
                self.op(eng, lambda e: e.tensor_tensor(out=x[:], in0=self.fm(tm), in1=hn[:, :, 1:129],
                                                       op=ALU.add), [tm, hn], [x])
                self.tfree(tm)
                return x
            rT = self.tget()
            kT_ = self.tget()
            vS = self.tget()
            for (w, qi, dstt) in ((wr, 0, rT), (wk, 2, kT_)):
                x = mkx(qi)
                ps = pbig()
                for n in range(KC):
                    for c in range(KC):
                        self.op("pe", lambda e: e.matmul(ps[:, n * 128:(n + 1) * 128],
                                                         lhsT=w[:, c, n * 128:(n + 1) * 128], rhs=x[:, c, :],
                                                         start=(c == 0), stop=(c == KC - 1)), [w, x], [ps])
                self.op("act", lambda e: e.copy(out=dstt[:], in_=ps[:]), [ps], [dstt])
            xv = mkx(3)
            ps = pbig()
            for hf in range(2):
                for c in range(KC):
                    self.op("pe", lambda e: e.matmul(ps[:, hf * 512:(hf + 1) * 512], lhsT=xv[:, c, :],
                                                     rhs=wv[:, c, hf * 512:(hf + 1) * 512],
                                                     start=(c == 0), stop=(c == KC - 1)), [wv, xv], [ps])
            self.op("act", lambda e: e.copy(out=vS[:], in_=ps[:]), [ps], [vS])
            for (wt, qi, ncol, mid, fn) in (((v1, 3, 32, midv, AF.Copy),) if i > 0 else ()) + \
                    ((w1, 1, 64, midw, AF.Tanh), (a1, 4, 64, mida, AF.Copy),
                     (g1, 5, 128, midga, AF.Sigmoid), (g1, 5, 32, midgb, AF.Sigmoid)):
                x = xv if qi == 3 else (x if (mid is midgb) else mkx(qi))
                ps = psm()
                c0 = 128 if mid is midgb else 0
                for c in range(KC):
                    self.op("pe", lambda e: e.matmul(ps[0:ncol, 0:128], lhsT=wt[:, c, c0:c0 + ncol], rhs=x[:, c, :],
                                                     start=(c == 0), stop=(c == KC - 1)), [wt, x], [ps])
                self.op("act", lambda e: e.activation(out=mid[0:ncol, :], in_=ps[0:ncol, 0:128], func=fn), [ps], [mid])
            self.tfree(xx)
            sig = self.tget()
            ps = pbig()
            for n in range(KC):
                self.op("pe", lambda e: e.matmul(ps[:, n * 128:(n + 1) * 128], lhsT=w2[0:64, n * 128:(n + 1) * 128],
                                                 rhs=midw[0:64, :], start=True, stop=True), [w2, midw], [ps])
            self.op("dve", lambda e: e.tensor_tensor(out=self.fm(sig), in0=ps[:].rearrange("p (c t) -> p c t", c=KC),
                                                     in1=self.bc("rw_w0", i * KC), op=ALU.add), [ps, cst], [sig])
            self.op("act", lambda e: e.activation(out=sig[:], in_=sig[:], func=AF.Sigmoid), [sig], [sig])
            aT_ = self.tget()
            ps = pbig()
            for n in range(KC):
                self.op("pe", lambda e: e.matmul(ps[:, n * 128:(n + 1) * 128], lhsT=a2[0:64, n * 128:(n + 1) * 128],
                                                 rhs=mida[0:64, :], start=True, stop=True), [a2, mida], [ps])
            self.op("dve", lambda e: e.tensor_tensor(out=self.fm(aT_), in0=ps[:].rearrange("p (c t) -> p c t", c=KC),
                                                     in1=self.bc("rw_a0", i * KC), op=ALU.add), [ps, cst], [aT_])
            self.op("act", lambda e: e.activation(out=aT_[:], in_=aT_[:], func=AF.Sigmoid), [aT_], [aT_])
            if i > 0:
                vg = self.tget()
                vfT = self.tget()
                self.load(vfT, vfT[:], vf[t0:t0 + 128, :])
                ps = pbig()
                for hf in range(2):
                    self.op("pe", lambda e: e.matmul(ps[:, hf * 512:(hf + 1) * 512], lhsT=midv[0:32, :],
                                                     rhs=v2[0:32, hf * 512:(hf + 1) * 512], start=True, stop=True),
                            [v2, midv], [ps])
                self.op("dve", lambda e: e.tensor_tensor(out=vg[:], in0=ps[:], in1=tmb[:, 2, :], op=ALU.add),
                        [ps, tmb], [vg])
                self.op("act", lambda e: e.activation(out=vg[:], in_=vg[:], func=AF.Sigmoid), [vg], [vg])
                self.op("pool", lambda e: e.tensor_tensor(out=vfT[:], in0=vfT[:], in1=vS[:], op=ALU.subtract),
                        [vfT, vS], [vfT])
                self.op("dve", lambda e: e.tensor_tensor(out=vfT[:], in0=vfT[:], in1=vg[:], op=ALU.mult),
                        [vfT, vg], [vfT])
                self.op("dve", lambda e: e.tensor_tensor(out=vS[:], in0=vS[:], in1=vfT[:], op=ALU.add),
                        [vS, vfT], [vS])
                self.tfree(vg, vfT)
            else:
                self.store(vf[t0:t0 + 128, :], None, vS, vS[:])
            kk = self.tget()
            self.op("dve", lambda e: e.tensor_tensor(out=self.fm(kk), in0=self.fm(kT_), in1=self.bc("rw_kk", i * KC),
                                                     op=ALU.mult), [kT_, cst], [kk])
            ksq = self.tget()
            self.op("pool", lambda e: e.tensor_tensor(out=ksq[:], in0=kk[:], in1=kk[:], op=ALU.mult), [kk], [ksq])
            ps = pbig()
            for c in range(KC):
                self.op("pe", lambda e: e.matmul(ps[:, c * 128:(c + 1) * 128], lhsT=bd, rhs=ksq[:, c * 128:(c + 1) * 128],
                                                 start=True, stop=True), [rc, ksq], [ps])
            self.op("act", lambda e: e.activation(out=ksq[:], in_=ps[:], func=AF.Sqrt), [ps], [ksq])
            self.op("dve", lambda e: e.tensor_scalar(out=ksq[:], in0=ksq[:], scalar1=1e-12, scalar2=None, op0=ALU.max),
                    [ksq], [ksq])
            self.op("dve", lambda e: e.reciprocal(out=ksq[:], in_=ksq[:]), [ksq], [ksq])
            self.op("dve", lambda e: e.tensor_tensor(out=kk[:], in0=kk[:], in1=ksq[:], op=ALU.mult), [kk, ksq], [kk])
            self.tfree(ksq)
            km = self.tget()
            self.op("dve", lambda e: e.scalar_tensor_tensor(out=self.fm(km), in0=self.fm(aT_), scalar=-1.0,
                                                            in1=self.bc("rw_ka", i * KC), op0=ALU.add, op1=ALU.mult),
                    [aT_, cst], [km])
            self.op("dve", lambda e: e.scalar_tensor_tensor(out=km[:], in0=km[:], scalar=1.0, in1=kT_[:],
                                                            op0=ALU.add, op1=ALU.mult), [km, kT_], [km])
            self.tfree(kT_)
            cs = self.tget()
            for c in range(KC):
                self.op("dve", lambda e: e.tensor_tensor_scan(out=cs[:, c * 128:(c + 1) * 128], data0=onesf,
                                                              data1=sig[:, c * 128:(c + 1) * 128], initial=zcol,
                                                              op0=ALU.mult, op1=ALU.add), [sig, rc], [cs])
            ein = self.tget()
            eneg = self.tget()
            self.op("act", lambda e: e.activation(out=ein[:], in_=cs[:], func=AF.Exp, scale=-C0), [cs], [ein])
            self.op("act", lambda e: e.activation(out=eneg[:], in_=cs[:], func=AF.Exp, scale=C0), [cs], [eneg])
            self.op("pool", lambda e: e.tensor_tensor(out=cs[:], in0=cs[:], in1=sig[:], op=ALU.subtract), [cs, sig], [cs])
            self.op("act", lambda e: e.activation(out=cs[:], in_=cs[:], func=AF.Exp, scale=-C0), [cs], [cs])
            self.tfree(sig)
            self.op("dve", lambda e: e.scalar_tensor_tensor(out=ART[:, :, 0:128], in0=self.fm(kk), scalar=-1.0,
                                                            in1=self.fm(cs), op0=ALU.mult, op1=ALU.mult),
                    [kk, cs], [ART])
            self.op("pool", lambda e: e.tensor_tensor(out=ART[:, :, 128:256], in0=self.fm(rT), in1=self.fm(ein),
                                                      op=ALU.mult), [rT, ein], [ART])
            self.tfree(cs)
            self.op("dve", lambda e: e.tensor_tensor(out=btT[:], in0=self.fm(kk), in1=self.fm(aT_), op=ALU.mult),
                    [kk, aT_], [btT])
            self.op("dve", lambda e: e.tensor_tensor(out=btT[:], in0=btT[:], in1=self.fm(eneg), op=ALU.mult),
                    [btT, eneg], [btT])
            self.tfree(aT_, kk)
            self.op("pool", lambda e: e.tensor_tensor(out=ktT[:], in0=self.fm(km), in1=self.fm(eneg), op=ALU.mult),
                    [km, eneg], [ktT])
            self.tfree(eneg)
            self.op("act", lambda e: e.copy(out=pcT[:, :].unsqueeze(2), in_=self.fm(ein, 127, 128)), [ein], [pcT])
            self.tfree(ein)
            self.op("dve", lambda e: e.tensor_tensor(out=rT[:], in0=rT[:], in1=km[:], op=ALU.mult), [rT, km], [rT])
            self.op("dve", lambda e: e.tensor_tensor(out=self.fm(rT), in0=self.fm(rT), in1=self.bc("rw_rk", i * KC),
                                                     op=ALU.mult), [rT, cst], [rT])
            ps = psm()
            for c in range(KC):
                self.op("pe", lambda e: e.matmul(ps[:, 2 * c:2 * c + 2], lhsT=rT[:, c * 128:(c + 1) * 128], rhs=ind2,
                                                 start=True, stop=True), [rT, rc], [ps])
            self.op("act", lambda e: e.copy(out=ssb[:], in_=ps[:, 0:NH]), [ps], [ssb])
            self.tfree(rT, km)
            pcb = pcT[:, :].unsqueeze(2).broadcast_to([128, KC, 128])
            KH = self.tget()
            BH = self.tget()
            for (srcT, dstT) in ((ktT, KH), (btT, BH)):
                tmp = self.tget()
                self.op("dve", lambda e: e.tensor_tensor(out=self.fm(tmp), in0=srcT[:], in1=pcb, op=ALU.mult),
                        [srcT, pcT], [tmp])
                ps = pbig()
                for c in range(KC):
                    self.op("pe", lambda e: e.transpose(ps[:, c * 128:(c + 1) * 128], tmp[:, c * 128:(c + 1) * 128],
                                                        ident), [tmp, rc], [ps])
                self.op("act", lambda e: e.copy(out=dstT[:], in_=ps[:]), [ps], [dstT])
                self.tfree(tmp)
            yT = self.tget()
            for c in range(KC):
                sk = self.rtile("sk", 2, [128, 2, 256], F32)
                sbm = self.rtile("sbm", 2, [128, 2, 256], F32)
                tb0 = self.rtile("tb0", 2, [128, 2, 128], BF16)
                sx = self.rtile("sx", 3, [128, 2, 192], BF16)
                p1 = psm()
                p2 = psm()
                p3 = psm()
                for hh in range(2):
                    pb = 64 * hh
                    self.op("pe", lambda e: e.matmul(p1[:, hh * 256:(hh + 1) * 256], lhsT=ktT[pb:pb + 64, c, :],
                                                     rhs=ART[pb:pb + 64, c, :], start=True, stop=True), [ktT, ART], [p1])
                    self.op("pe", lambda e: e.matmul(p2[:, hh * 256:(hh + 1) * 256], lhsT=btT[pb:pb + 64, c, :],
                                                     rhs=ART[pb:pb + 64, c, :], start=True, stop=True), [btT, ART], [p2])
                    self.op("pe", lambda e: e.matmul(p3[:, hh * 128:(hh + 1) * 128], lhsT=ART[pb:pb + 64, c, 0:128],
                                                     rhs=btT[pb:pb + 64, c, :], start=True, stop=True), [btT, ART], [p3])
                mt2b = mt2.unsqueeze(1).broadcast_to([128, 2, 256])
                msb = mstrict.unsqueeze(1).broadcast_to([128, 2, 128])
                self.op("dve", lambda e: e.tensor_tensor(out=sk[:], in0=p1[:].rearrange("p (h x) -> p h x", h=2),
                                                         in1=mt2b, op=ALU.mult), [p1, rc], [sk])
                self.op("dve", lambda e: e.tensor_tensor(out=sbm[:], in0=p2[:].rearrange("p (h x) -> p h x", h=2),
                                                         in1=mt2b, op=ALU.mult), [p2, rc], [sbm])
                self.op("act", lambda e: e.copy(out=tb0[:], in_=sbm[:, :, 0:128]), [sbm], [tb0])
                self.op("dve", lambda e: e.tensor_tensor(out=sx[:, :, 0:128],
                                                         in0=p3[:, 0:256].rearrange("p (h x) -> p h x", h=2),
                                                         in1=msb, op=ALU.mult), [p3, rc], [sx])
                px = psm()
                self.op("pe", lambda e: e.matmul(px[:, 0:128], lhsT=ART[:, c, 0:128], rhs=st2[c][:], start=True,
                                                 stop=False), [ART, st2[c]], [px])
                for hh in range(2):
                    hd = 2 * c + hh
                    self.op("pe", lambda e: e.matmul(px[:, hh * 64:(hh + 1) * 64], lhsT=sk[:, hh, 0:128],
                                                     rhs=vS[:, hd * 64:(hd + 1) * 64], start=False, stop=(hh == 1)),
                            [sk, vS], [px])
                self.op("act", lambda e: e.copy(out=sx[:, :, 128:192],
                                                in_=px[:, 0:128].rearrange("p (h x) -> p h x", h=2)), [px], [sx])
                tcur = None
                for k in range(7):
                    pd = psm()
                    for hh in range(2):
                        tk = tb0[:, hh, :] if tcur is None else tcur[:, hh, :]
                        ncol = 192 if k < 6 else 64
                        c0 = 0 if k < 6 else 128
                        self.op("pe", lambda e: e.matmul(pd[:, hh * 192:hh * 192 + ncol], lhsT=tk,
                                                         rhs=sx[:, hh, c0:c0 + ncol], start=True, stop=True),
                                [tb0 if tcur is None else tcur, sx], [pd])
                    if k < 6:
                        pt = psm()
                        for hh in range(2):
                            tk = tb0[:, hh, :] if tcur is None else tcur[:, hh, :]
                            self.op("pe", lambda e: e.matmul(pt[:, hh * 128:(hh + 1) * 128], lhsT=sx[:, hh, 0:128],
                                                             rhs=tk, start=True, stop=True),
                                    [tb0 if tcur is None else tcur, sx], [pt])
                        sxn = self.rtile("sx", 3, [128, 2, 192], BF16)
                        pdv = pd[:, 0:384].rearrange("p (h x) -> p h x", h=2)
                        self.op("act", lambda e: e.copy(out=sxn[:, :, 0:128], in_=pdv[:, :, 0:128]), [pd], [sxn])
                        self.op("dve", lambda e: e.tensor_tensor(out=sxn[:, :, 128:192], in0=pdv[:, :, 128:192],
                                                                 in1=sx[:, :, 128:192], op=ALU.add), [pd, sx], [sxn])
                        tn = self.rtile("tt", 2, [128, 2, 128], BF16)
                        self.op("act", lambda e: e.copy(out=tn[:], in_=pt[:, 0:256].rearrange("p (h x) -> p h x", h=2)),
                                [pt], [tn])
                        sx = sxn
                        tcur = tn
                    else:
                        uf = self.rtile("uf", 2, [128, 2, 64], F32)
                        pdv = pd[:, 0:384].rearrange("p (h x) -> p h x", h=2)
                        self.op("dve", lambda e: e.tensor_tensor(out=uf[:], in0=pdv[:, :, 0:64],
                                                                 in1=sx[:, :, 128:192], op=ALU.add), [pd, sx], [uf])
                py = psm()
                self.op("pe", lambda e: e.matmul(py[:, 0:128], lhsT=ART[:, c, 128:256], rhs=st2[c][:], start=True,
                                                 stop=False), [ART, st2[c]], [py])
                for hh in range(2):
                    hd = 2 * c + hh
                    self.op("pe", lambda e: e.matmul(py[:, hh * 64:(hh + 1) * 64], lhsT=sbm[:, hh, 128:256],
                                                     rhs=uf[:, hh, :], start=False, stop=False), [sbm, uf], [py])
                    self.op("pe", lambda e: e.matmul(py[:, hh * 64:(hh + 1) * 64], lhsT=sk[:, hh, 128:256],
                                                     rhs=vS[:, hd * 64:(hd + 1) * 64], start=False, stop=(hh == 1)),
                            [sk, vS], [py])
                self.op("act", lambda e: e.copy(out=yT[:, c * 128:(c + 1) * 128], in_=py[:, 0:128]), [py], [yT])
                pst = psm()
                for hh in range(2):
                    hd = 2 * c + hh
                    pb = 64 * hh
                    self.op("pe", lambda e: e.matmul(pst[pb:pb + 64, hh * 64:(hh + 1) * 64],
                                                     lhsT=BH[:, hd * 64:(hd + 1) * 64], rhs=uf[:, hh, :],
                                                     start=True, stop=False), [BH, uf], [pst])
                    self.op("pe", lambda e: e.matmul(pst[pb:pb + 64, hh * 64:(hh + 1) * 64],
                                                     lhsT=KH[:, hd * 64:(hd + 1) * 64], rhs=vS[:, hd * 64:(hd + 1) * 64],
                                                     start=False, stop=True), [KH, vS], [pst])
                for hh in range(2):
                    pb = 64 * hh
                    self.op("dve", lambda e: e.scalar_tensor_tensor(
                        out=st2[c][pb:pb + 64, hh * 64:(hh + 1) * 64], in0=st2[c][pb:pb + 64, hh * 64:(hh + 1) * 64],
                        scalar=pcT[pb:pb + 64, c:c + 1], in1=pst[pb:pb + 64, hh * 64:(hh + 1) * 64],
                        op0=ALU.mult, op1=ALU.add), [st2[c], pcT, pst], [st2[c]])
            self.tfree(KH, BH)
            y3 = yT.t[:].rearrange("p (h x) -> p h x", h=NH)
            self.op("dve", lambda e: e.tensor_reduce(out=mu[:], in_=y3, axis=AX.X, op=ALU.add), [yT], [mu])
            mub = mu[:, :].unsqueeze(2).broadcast_to([128, NH, HD])
            self.op("dve", lambda e: e.scalar_tensor_tensor(out=y3, in0=mub, scalar=-1.0 / HD, in1=y3, op0=ALU.mult,
                                                            op1=ALU.add), [yT, mu], [yT])
            sq = self.tget()
            self.op("pool", lambda e: e.tensor_tensor(out=sq[:], in0=yT[:], in1=yT[:], op=ALU.mult), [yT], [sq])
            self.op("dve", lambda e: e.tensor_reduce(out=var[:], in_=sq.t[:].rearrange("p (h x) -> p h x", h=NH),
                                                     axis=AX.X, op=ALU.add), [sq], [var])
            self.op("act", lambda e: e.activation(out=var[:], in_=var[:], func=AF.Sqrt, bias=GN_EPS, scale=1.0 / HD),
                    [var], [var])
            self.op("dve", lambda e: e.reciprocal(out=var[:], in_=var[:]), [var], [var])
            varb = var[:, :].unsqueeze(2).broadcast_to([128, NH, HD])
            self.op("dve", lambda e: e.tensor_tensor(out=y3, in0=y3, in1=varb, op=ALU.mult), [yT, var], [yT])
            self.op("pool", lambda e: e.tensor_tensor(out=yT[:], in0=yT[:], in1=tmb[:, 0, :], op=ALU.mult), [yT, tmb], [yT])
            self.op("pool", lambda e: e.tensor_tensor(out=yT[:], in0=yT[:], in1=tmb[:, 1, :], op=ALU.add), [yT, tmb], [yT])
            ssbb = ssb[:, :].unsqueeze(2).broadcast_to([128, NH, HD])
            self.op("dve", lambda e: e.tensor_tensor(out=sq.t[:].rearrange("p (h x) -> p h x", h=NH),
                                                     in0=vS.t[:].rearrange("p (h x) -> p h x", h=NH), in1=ssbb,
                                                     op=ALU.mult), [vS, ssb], [sq])
            self.op("dve", lambda e: e.tensor_tensor(out=yT[:], in0=yT[:], in1=sq[:], op=ALU.add), [yT, sq], [yT])
            self.tfree(sq, vS)
            gS = self.tget()
            ps = pbig()
            for n in range(KC):
                self.op("pe", lambda e: e.matmul(ps[:, n * 128:(n + 1) * 128], lhsT=g2a[:, n * 128:(n + 1) * 128],
                                                 rhs=midga[:, :], start=True, stop=False), [g2a, midga], [ps])
                self.op("pe", lambda e: e.matmul(ps[:, n * 128:(n + 1) * 128], lhsT=g2b[0:32, n * 128:(n + 1) * 128],
                                                 rhs=midgb[0:32, :], start=False, stop=True), [g2b, midgb], [ps])
            self.op("act", lambda e: e.copy(out=gS[:], in_=ps[:]), [ps], [gS])
            ps = pbig()
            for c in range(KC):
                self.op("pe", lambda e: e.transpose(ps[:, c * 128:(c + 1) * 128], yT[:, c * 128:(c + 1) * 128], ident),
                        [yT, rc], [ps])
            self.op("dve", lambda e: e.tensor_tensor(out=ygT[:], in0=ps[:].rearrange("p (c t) -> p c t", c=KC),
                                                     in1=self.fm(gS), op=ALU.mult), [ps, gS], [ygT])
            self.tfree(yT, gS)
            ps = pbig()
            for n in range(KC):
                for c in range(KC):
                    self.op("pe", lambda e: e.matmul(ps[:, n * 128:(n + 1) * 128], lhsT=wo[:, c, n * 128:(n + 1) * 128],
                                                     rhs=ygT[:, c, :], start=(c == 0), stop=(c == KC - 1)),
                            [wo, ygT], [ps])
            ho = self.rtile("r_ho", 1, [128, KC, 128], F32)
            self.op("dve", lambda e: e.tensor_tensor(out=ho[:], in0=ps[:].rearrange("p (c t) -> p c t", c=KC),
                                                     in1=h[:], op=ALU.add), [ps, h], [ho])
            v0 = PAD if cb == 0 else 0
            self.store(hdst[:, :, t0 + v0:t0 + 128], None, ho, ho[:, :, v0:128])
        self.rn_w = 512
        self.phase_end()

    def zero_pads(self, names):
        zt = self.tiles["zeros"]
        for nm in names:
            for c in range(KC):
                self.store(self.dram[nm][c, :, 0:PAD], None, zt, zt[:], eng="sp")

    def finish(self, out_res=None):
        self.barrier()
        return self.nc


def build_ffn_test(nb):
    p = Prog(nb)
    p.din("hT0", [KC, 128, p.LP])
    p.din("ffn_up", [DEPTH, 128, KC, F2])
    p.din("ffn_down", [DEPTH, 128, NJ, D])
    p.dout("hT1", [KC, 128, p.LP])
    p.setup_consts()
    p.zero_pads(["hT1"])
    p.ffn_phase(0, "hT0", "hT1")
    return p.finish([p.dr(("hT1", "all"))])


def attn_masks():
    s = np.arange(128)[:, None]
    col = np.arange(512)[None, :]
    q, t = col // 128, col % 128
    m = np.zeros((128, 6, 512), np.float32)
    for r in range(4):
        m[:, r, :] = ((q > r) | ((q == r) & (t > s))).astype(np.float32)
    row = (s >= PAD).astype(np.float32)
    m[:, 4, :] = m[:, 0, :] * row
    m[:, 5, :] = row * np.ones((1, 512), np.float32)
    j = np.arange(128)[:, None]
    ss = np.arange(128)[None, :]
    ntri = -(j >= ss).astype(np.float32)
    return m, ntri


def build_att_test(nb):
    p = Prog(nb)
    p.din("hT0", [KC, 128, p.LP])
    p.din("sb_wq", [2, 128, KC, D]); p.din("sb_wk", [128, KC, D]); p.din("sb_wv", [128, KC, D])
    p.din("sb_wo", [2, 128, KC, D])
    p.din("amask", [128, 6, 512]); p.din("ntri", [128, 128])
    p.dscratch("QT", [KC, 128, p.LP], BF16); p.dscratch("KT", [KC, 128, p.LP], BF16)
    p.dscratch("VV", [p.LP, D], BF16); p.dscratch("AT", [KC, 128, p.LP], BF16)
    p.dout("hT1", [KC, 128, p.LP])
    p.setup_consts()
    p.zero_pads(["hT1"])
    p.qkv_phase(0, "hT0", True)
    p.att_phase()
    p.o_phase(0, "hT0", "hT1")
    return p.finish([p.dr(("hT1", "all"))])


def rw_consts():
    p = np.arange(128)[:, None]
    q = np.arange(128)[None, :]
    ident = (p == q).astype(np.float32)
    bd = ((p // 64) == (q // 64)).astype(np.float32)
    mstrict = (p > q).astype(np.float32)
    mt_strict = (q > p).astype(np.float32)
    mt_incl = (q >= p).astype(np.float32)
    ind2 = np.concatenate([(p // 64 == 0), (p // 64 == 1)], axis=1).astype(np.float32)
    onesf = np.ones((128, 128), np.float32)
    zcol = np.zeros((128, 1), np.float32)
    pad = np.zeros((128, 127), np.float32)
    return np.ascontiguousarray(np.concatenate([ident, bd, mstrict, mt_strict, mt_incl, ind2, onesf, zcol, pad],
                                               axis=1))


def declare_rw_inputs(p):
    p.din("rw_wr", [2, 128, KC, D]); p.din("rw_wk", [2, 128, KC, D]); p.din("rw_wv", [2, 128, KC, D])
    p.din("rw_wo", [2, 128, KC, D])
    p.din("rw_w1", [2, 128, KC, 64]); p.din("rw_a1", [2, 128, KC, 64]); p.din("rw_g1", [2, 128, KC, 160])
    p.din("rw_v1", [1, 128, KC, 32])
    p.din("rw_w2", [2, 64, D]); p.din("rw_a2", [2, 64, D]); p.din("rw_g2", [2, 160, D]); p.din("rw_v2", [1, 32, D])
    p.din("rw_lnx_g", [2, D]); p.din("rw_lnx_b", [2, D]); p.din("rw_v0", [1, D])
    p.din("rw_const", [128, 128 * 5 + 2 + 256])
    p.dscratch("VF", [p.LP, D], F32)


def build_rw_test(nb, nlayers=1):
    p = Prog(nb)
    p.din("hT0", [KC, 128, p.LP])
    declare_rw_inputs(p)
    p.dout("hT1", [KC, 128, p.LP])
    p.dscratch("hTa", [KC, 128, p.LP])
    p.setup_consts()
    p.zero_pads(["hT1", "hTa"])
    if nlayers == 1:
        p.rwkv_phase(0, "hT0", "hT1")
    else:
        p.rwkv_phase(0, "hT0", "hTa")
        p.rwkv_phase(1, "hTa", "hT1")
    return p.finish()


def final_phase(p, src):
    p.phase_begin()
    hsrc = p.dram[src].rearrange("c p t -> p c t")
    yo = p.dram["yT"].rearrange("c p t -> p c t")
    t = PAD + NMETA
    while t < p.LP:
        tw = min(512, p.LP - t)
        h = p.rtile("fn_h", 2, [128, KC, 512], F32)
        p.load(h, h[:, :, 0:tw], hsrc[:, :, t:t + tw])
        o = p.rtile("fn_o", 2, [128, KC, 512], F32)
        p.rmsnorm(h, tw, lambda c: p.cc("final_g", c), o)
        p.store(yo[:, :, t - PAD - NMETA:t - PAD - NMETA + tw], None, o, o[:, :, 0:tw])
        t += tw
    p.phase_end()


def build_full(nb):
    p = Prog(nb)
    p.din("hT0", [KC, 128, p.LP])
    p.din("ffn_up", [DEPTH, 128, KC, F2])
    p.din("ffn_down", [DEPTH, 128, NJ, D])
    declare_rw_inputs(p)
    p.din("sb_wq", [2, 128, KC, D]); p.din("sb_wk", [128, KC, D]); p.din("sb_wv", [128, KC, D])
    p.din("sb_wo", [2, 128, KC, D])
    p.din("amask", [128, 6, 512]); p.din("ntri", [128, 128])
    p.dscratch("QT", [KC, 128, p.LP], BF16); p.dscratch("KT", [KC, 128, p.LP], BF16)
    p.dscratch("VV", [p.LP, D], BF16); p.dscratch("AT", [KC, 128, p.LP], BF16)
    p.dscratch("hA", [KC, 128, p.LP]); p.dscratch("hB", [KC, 128, p.LP])
    p.dout("yT", [KC, 128, p.LP - PAD - NMETA])
    p.setup_consts()
    p.zero_pads(["hA", "hB"])
    p.barrier()
    p.rwkv_phase(0, "hT0", "hA")
    p.ffn_phase(0, "hA", "hB")
    p.rwkv_phase(1, "hB", "hA")
    p.ffn_phase(1, "hA", "hB")
    p.qkv_phase(0, "hB", True)
    p.att_phase()
    p.o_phase(0, "hB", "hA")
    p.ffn_phase(2, "hA", "hB")
    p.qkv_phase(1, "hB", False)
    p.att_phase()
    p.o_phase(1, "hB", "hA")
    p.ffn_phase(3, "hA", "hB")
    final_phase(p, "hB")
    return p.finish()


def _w1024(w):
    return np.ascontiguousarray(w.reshape(KC, 128, -1).transpose(1, 0, 2))


def _wst(w):
    return np.stack([_w1024(w[i]) for i in range(w.shape[0])])


def _cols(v):
    v = np.asarray(v, np.float32).reshape(-1, KC, 128)
    return np.ascontiguousarray(v.transpose(2, 0, 1).reshape(128, -1))


def host_inputs(inp, nb):
    f = {k: np.asarray(v, np.float32) for k, v in inp.items()}
    m, ntri = attn_masks()
    cw = f["ffn_conv_w"]
    shared = {
        "ffn_up": np.ascontiguousarray(f["ffn_up"].reshape(DEPTH, KC, 128, F2).transpose(0, 2, 1, 3)),
        "ffn_down": np.ascontiguousarray(f["ffn_down"].reshape(DEPTH, NJ, 128, D).transpose(0, 2, 1, 3)),
        "norm_ffn_g": _cols(f["norm_ffn_g"]), "norm_mix_g": _cols(f["norm_mix_g"]),
        "kv_norm_g": _cols(f["kv_norm_g"]), "final_g": _cols(f["final_norm_g"]),
        "conv_w": np.ascontiguousarray(cw.reshape(DEPTH, 3, 44, 128).transpose(3, 0, 2, 1).reshape(128, -1)),
        "conv_b": np.ascontiguousarray(f["ffn_conv_b"].reshape(DEPTH, 44, 128).transpose(2, 0, 1).reshape(128, -1)),
        "rw_mix": _cols(f["rw_mix"]), "rw_w0": _cols(f["rw_w0"]), "rw_a0": _cols(f["rw_a0"]),
        "rw_kk": _cols(f["rw_kk"]), "rw_ka": _cols(f["rw_ka"]), "rw_rk": _cols(f["rw_rk"].reshape(2, D)),
        "rw_wr": _wst(f["rw_wr"]), "rw_wk": _wst(f["rw_wk"]), "rw_wv": _wst(f["rw_wv"]), "rw_wo": _wst(f["rw_wo"]),
        "rw_w1": _wst(f["rw_w1"]), "rw_a1": _wst(f["rw_a1"]), "rw_g1": _wst(f["rw_g1"]), "rw_v1": _wst(f["rw_v1"]),
        "rw_w2": f["rw_w2"], "rw_a2": f["rw_a2"], "rw_g2": f["rw_g2"], "rw_v2": f["rw_v2"],
        "rw_lnx_g": f["rw_lnx_g"], "rw_lnx_b": f["rw_lnx_b"], "rw_v0": f["rw_v0"],
        "rw_const": rw_consts(),
        "sb_wq": _wst(f["sb_wq"]), "sb_wo": _wst(f["sb_wo"]), "sb_wk": _w1024(f["sb_wk"]), "sb_wv": _w1024(f["sb_wv"]),
        "amask": m, "ntri": ntri,
    }
    return shared


def host_h0(xb, meta, nb):
    LP = nb * 128
    hT = np.zeros((D, LP), np.float32)
    hT[:, PAD:PAD + NMETA] = meta.T
    hT[:, PAD + NMETA:] = xb.T
    return hT.reshape(KC, 128, LP)


_CACHE = {}


def kernel(**inputs):
    x = np.asarray(inputs["x"], np.float32)
    B, S, _ = x.shape
    nb = (PAD + NMETA + S) // 128
    if nb not in _CACHE:
        _CACHE[nb] = build_full(nb)
    nc = _CACHE[nb]
    shared = host_inputs({k: v for k, v in inputs.items() if k != "x"}, nb)
    meta = np.asarray(inputs["meta_tokens"], np.float32)
    ncores = 8
    in_maps = []
    for cidx in range(ncores):
        b = cidx % B
        m = dict(shared)
        m["hT0"] = host_h0(x[b], meta, nb)
        in_maps.append(m)
    res = run_bass_kernel_spmd(nc, in_maps, core_ids=list(range(ncores)))
    out = np.empty((B, S, D), np.float32)
    for b in range(B):
        out[b] = res.results[b]["yT"].reshape(D, S).T
    return out
```

```python
import numpy as np
from contextlib import ExitStack
import concourse.bass as bass
import concourse.mybir as mybir
from concourse.bass_utils import run_bass_kernel_spmd

F32 = mybir.dt.float32
BF16 = mybir.dt.bfloat16
ALU = mybir.AluOpType
AF = mybir.ActivationFunctionType
AX = mybir.AxisListType

D = 1024
KC = 8
NH = 16
HD = 64
NMETA = 16
PAD = 112
DFF = 2816
F2 = 5632
NJ = 22
FT = 384
DEPTH = 4
RMS_EPS = 1e-6
GN_EPS = 64e-5


class Res:
    __slots__ = ("name", "lw", "rd")

    def __init__(self, name):
        self.name = name
        self.lw = None
        self.rd = {}


class KB:
    SEM_LIMIT = 30000

    def __init__(self, nc):
        self.nc = nc
        self.q = {e: [] for e in ("pe", "act", "dve", "pool", "sp")}
        self.sems = {}
        self.cur = {}
        self.waited = {e: {} for e in self.q}
        self.nsem = 0
        self.dmasem = {}
        self.nops = 0

    def _newsem(self, tag):
        key = "%s_%d" % (tag, self.nsem)
        self.nsem += 1
        self.sems[key] = self.nc.alloc_semaphore(name=key)
        return key

    def _eng_event(self, eng):
        c = self.cur.get(eng)
        if c is None or c[1] >= self.SEM_LIMIT:
            c = [self._newsem(eng), 0]
            self.cur[eng] = c
        c[1] += 1
        return (c[0], c[1])

    def _dma_event(self, chain, eng="sp"):
        cls = "sw" if eng == "pool" else "hw"
        chain = (cls, chain)
        c = self.dmasem.get(chain)
        if c is None or c[1] >= self.SEM_LIMIT:
            free = getattr(self, "free_dma", {}).get(cls, [])
            free.sort(key=lambda x: x[1])
            if free and free[0][1] < self.SEM_LIMIT // 2:
                c = list(free.pop(0))
            else:
                c = [self._newsem("d" + cls), 0]
            self.dmasem[chain] = c
        c[1] += 16
        return (c[0], c[1])

    def recycle_dma(self):
        free = getattr(self, "free_dma", {"sw": [], "hw": []})
        for ch, c in self.dmasem.items():
            free[ch[0]].append((c[0], c[1]))
        self.free_dma = free
        self.dmasem = {}

    def _deps(self, eng, reads, writes, is_dma):
        waits = {}

        def need(sk, v, src_eng, kind):
            if (not is_dma) and eng == "pe" and src_eng == "pe":
                return
            if (not is_dma) and src_eng == eng and kind == "war":
                return
            if self.waited[eng].get(sk, 0) >= v:
                return
            if waits.get(sk, 0) < v:
                waits[sk] = v

        for r in reads:
            if r.lw is not None:
                need(r.lw[0], r.lw[1], r.lw[2], "raw")
        for w in writes:
            if w.lw is not None:
                need(w.lw[0], w.lw[1], w.lw[2], "waw")
            for sk, (v, e) in w.rd.items():
                need(sk, v, e, "war")
        return waits

    def _record(self, ev, src, reads, writes):
        for r in reads:
            r.rd[ev[0]] = (ev[1], src)
        for w in writes:
            w.lw = (ev[0], ev[1], src)
            w.rd = {}

    def op(self, eng, fn, reads=(), writes=()):
        waits = self._deps(eng, reads, writes, False)
        for sk, v in waits.items():
            self.waited[eng][sk] = v
        ev = self._eng_event(eng)
        self._emit(eng, fn, waits, ev, 1)
        self._record(ev, eng, reads, writes)
        self.nops += 1
        return ev

    def dma(self, eng, fn, reads=(), writes=(), chain=None):
        waits = self._deps(eng, reads, writes, True)
        for sk, v in waits.items():
            self.waited[eng][sk] = v
        assert chain is not None
        ev = self._dma_event(chain, eng)
        self._emit(eng, fn, waits, ev, 16)
        self._record(ev, "dma", reads, writes)
        self.nops += 1
        return ev

    def wait_all(self, eng, resources):
        waits = {}
        for r in resources:
            if r.lw is not None:
                sk, v = r.lw[0], r.lw[1]
                if self.waited[eng].get(sk, 0) < v and waits.get(sk, 0) < v:
                    waits[sk] = v
        for sk, v in waits.items():
            self.waited[eng][sk] = v
        self._emit(eng, None, waits, None, 0)

    ENG = {"pe": "tensor", "act": "scalar", "dve": "vector", "pool": "gpsimd", "sp": "sync"}

    def _emit(self, eng, fn, waits, ev, inc):
        engine = getattr(self.nc, self.ENG[eng])
        for sk, v in waits.items():
            engine.wait_ge(self.sems[sk], v)
        if fn is not None:
            ins = fn(engine)
            ins.then_inc(self.sems[ev[0]], inc)

    def emit(self):
        return

    def emit_old(self):
        nc = self.nc
        engs = {"pe": "tensor", "act": "scalar", "dve": "vector", "pool": "gpsimd", "sp": "sync"}
        with nc.Block() as block:
            for e, attr in engs.items():
                ops = self.q[e]
                if not ops:
                    continue

                def body(engine, ops=ops):
                    for fn, waits, ev, inc in ops:
                        for sk, v in waits:
                            engine.wait_ge(self.sems[sk], v)
                        if fn is not None:
                            ins = fn(engine)
                            ins.then_inc(self.sems[ev[0]], inc)
                getattr(block, attr)(body)


class T:
    def __init__(self, nc, name, shape, dtype, psum=False, stack=None):
        self.name = name
        if psum:
            cm = nc.psum_tensor(name, list(shape), dtype)
        else:
            cm = nc.sbuf_tensor(name, list(shape), dtype)
        self.t = stack.enter_context(cm)
        self.r = Res(name)

    def __getitem__(self, idx):
        return self.t[idx]


class Prog:
    def __init__(self, nb, dbg=None):
        self.nb = nb
        self.LP = nb * 128
        self.dbg = dbg or {}
        self.nc = bass.Bass("TRN2", target_bir_lowering=False)
        self.kb = KB(self.nc)
        self.dram = {}
        self.dres = {}
        self.tiles = {}
        self.rot = {}
        self.gstack = ExitStack()
        self.pstack = None
        self.pid = 0

    def phase_begin(self):
        self.pstack = ExitStack()
        self.pid += 1
        self.ptiles = []

    def phase_end(self):
        self.barrier()
        self.kb.recycle_dma()
        self.pstack.close()
        self.pstack = None
        for nm in self.ptiles:
            self.tiles.pop(nm, None)
        self.rot = {}

    def barrier(self):
        kb = self.kb
        targets = {}
        for e, c in kb.cur.items():
            targets[c[0]] = c[1]
        for ch, c in kb.dmasem.items():
            targets[c[0]] = c[1]
        for e in ("pe", "act", "dve", "pool", "sp"):
            waits = {}
            for sk, v in targets.items():
                if kb.cur.get(e) is not None and kb.cur[e][0] == sk:
                    continue
                if kb.waited[e].get(sk, 0) < v:
                    waits[sk] = v
                    kb.waited[e][sk] = v
            kb._emit(e, None, waits, None, 0)

    def din(self, name, shape, dtype=F32):
        self.dram[name] = self.nc.dram_tensor(name, list(shape), dtype, kind="ExternalInput").ap()
        return self.dram[name]

    def dout(self, name, shape, dtype=F32):
        self.dram[name] = self.nc.dram_tensor(name, list(shape), dtype, kind="ExternalOutput").ap()
        return self.dram[name]

    def dscratch(self, name, shape, dtype=F32):
        self.dram[name] = self.nc.dram_tensor(name, list(shape), dtype, kind="Internal").ap()
        return self.dram[name]

    def dr(self, key):
        r = self.dres.get(key)
        if r is None:
            r = Res(str(key))
            self.dres[key] = r
        return r

    def tile(self, name, shape, dtype=F32, psum=False):
        if self.pstack is not None:
            t = T(self.nc, "%s_p%d" % (name, self.pid), shape, dtype, psum, self.pstack)
            self.ptiles.append(name)
        else:
            t = T(self.nc, name, shape, dtype, psum, self.gstack)
        self.tiles[name] = t
        return t

    def rtile(self, name, n, shape, dtype=F32, psum=False):
        ent = self.rot.get(name)
        if ent is None:
            ent = [[self.tile("%s%d" % (name, i), shape, dtype, psum) for i in range(n)], 0]
            self.rot[name] = ent
        t = ent[0][ent[1] % n]
        ent[1] += 1
        return t

    def op(self, eng, fn, reads=(), writes=()):
        return self.kb.op(eng, fn, [x.r if isinstance(x, T) else x for x in reads],
                          [x.r if isinstance(x, T) else x for x in writes])

    def load(self, dst, dst_ap, src_ap, src_res=None, eng="sp", nowaw=False):
        if nowaw:
            dst.r.lw = None
        self.kb.dma(eng, lambda e: e.dma_start(out=dst_ap, in_=src_ap),
                    reads=[], writes=[dst.r], chain="ld_" + dst.name)

    def store(self, dst_ap, dst_res, src, src_ap, eng="sp"):
        self.kb.dma(eng, lambda e: e.dma_start(out=dst_ap, in_=src_ap),
                    reads=[src.r], writes=[], chain="st_" + src.name)

    def rmsnorm(self, h, w, g_ap, out, out_off=0, tag="n"):
        rw_ = getattr(self, "rn_w", 512)
        sq = self.rtile("rn_sq", 1, [128, KC, rw_], BF16)
        ss = self.rtile("rn_ss", 1, [128, 512], F32, psum=True)
        rstd = self.rtile("rn_rstd", 2, [128, rw_], F32)
        ones = self.tiles["onesD"]
        self.op("dve", lambda e: e.tensor_tensor(out=sq[:, :, 0:w], in0=h[:, :, 0:w], in1=h[:, :, 0:w],
                                                  op=ALU.mult), [h], [sq])
        for c in range(KC):
            self.op("pe", lambda e, c=c: e.matmul(ss[:, 0:w], lhsT=ones[:], rhs=sq[:, c, 0:w],
                                                  start=(c == 0), stop=(c == KC - 1)), [ones, sq], [ss])
        self.op("act", lambda e: e.activation(out=rstd[:, 0:w], in_=ss[:, 0:w], func=AF.Sqrt, bias=RMS_EPS,
                                              scale=1.0), [ss], [rstd])
        self.op("dve", lambda e: e.reciprocal(out=rstd[:, 0:w], in_=rstd[:, 0:w]), [rstd], [rstd])
        for c in range(KC):
            self.op("dve", lambda e, c=c: e.scalar_tensor_tensor(
                out=out[:, c, out_off:out_off + w], in0=h[:, c, 0:w], scalar=g_ap(c), in1=rstd[:, 0:w],
                op0=ALU.mult, op1=ALU.mult), [h, rstd, self.tiles["consts"]], [out])

    def setup_consts(self):
        onesD = self.tile("onesD", [128, 128], BF16)
        self.op("pool", lambda e: e.memset(onesD[:], 1.0 / D), [], [onesD])
        zt = self.tile("zeros", [128, PAD], F32)
        self.op("pool", lambda e: e.memset(zt[:], 0.0), [], [zt])
        self.din("norm_ffn_g", [128, DEPTH * KC])
        self.din("conv_w", [128, DEPTH * 44 * 3])
        self.din("conv_b", [128, DEPTH * 44])
        self.din("final_g", [128, KC])
        self.din("norm_mix_g", [128, DEPTH * KC])
        self.din("kv_norm_g", [128, KC])
        for nm, n in (("rw_mix", 2 * 6 * KC), ("rw_w0", 2 * KC), ("rw_a0", 2 * KC), ("rw_kk", 2 * KC),
                      ("rw_ka", 2 * KC), ("rw_rk", 2 * KC)):
            self.din(nm, [128, n])
        ncol = DEPTH * KC + DEPTH * 44 * 3 + DEPTH * 44 + KC + DEPTH * KC + KC + 2 * 6 * KC + 5 * 2 * KC
        consts = self.tile("consts", [128, ncol], F32)
        self.coff = {}
        off = 0
        for nm, n in (("norm_ffn_g", DEPTH * KC), ("conv_w", DEPTH * 44 * 3), ("conv_b", DEPTH * 44),
                      ("final_g", KC), ("norm_mix_g", DEPTH * KC), ("kv_norm_g", KC),
                      ("rw_mix", 2 * 6 * KC), ("rw_w0", 2 * KC), ("rw_a0", 2 * KC), ("rw_kk", 2 * KC),
                      ("rw_ka", 2 * KC), ("rw_rk", 2 * KC)):
            self.coff[nm] = off
            self.load(consts, consts[:, off:off + n], self.dram[nm][:, :], self.dr(nm))
            off += n

    def cc(self, nm, idx):
        o = self.coff[nm] + idx
        return self.tiles["consts"][:, o:o + 1]

    def ffn_tiles(self):
        tiles = []
        o = PAD
        while o < self.LP:
            ow = min(FT - 2, self.LP - o)
            tiles.append((o - 2, ow))
            o += ow
        return tiles

    def ffn_weights(self, layer):
        wup = self.tiles.get("wup") or self.tile("wup", [128, KC, F2], BF16)
        wdn = self.tiles.get("wdn") or self.tile("wdn", [128, NJ, D], BF16)
        up = self.dram["ffn_up"]
        dn = self.dram["ffn_down"]
        for c in range(KC):
            for hf in range(2):
                self.load(wup, wup[:, c, hf * DFF:(hf + 1) * DFF], up[layer, :, c, hf * DFF:(hf + 1) * DFF],
                          self.dr("ffn_up"), eng="pool", nowaw=True)
        for j0 in range(0, NJ, 2):
            self.load(wdn, wdn[:, j0:j0 + 2, :], dn[layer, :, j0:j0 + 2, :], self.dr("ffn_down"), eng="pool",
                      nowaw=True)
        return wup, wdn

    def ffn_phase(self, layer, src, dst):
        self.phase_begin()
        wup, wdn = self.ffn_weights(layer)
        hsrc = self.dram[src].rearrange("c p t -> p c t")
        hdst = self.dram[dst].rearrange("c p t -> p c t")
        for ti, (i0, ow) in enumerate(self.ffn_tiles()):
            iw = ow + 2
            h = self.rtile("f_h", 2, [128, KC, FT], F32)
            self.load(h, h[:, :, 0:iw], hsrc[:, :, i0:i0 + iw], self.dr((src, "all")))
            hn = self.rtile("f_hn", 1, [128, KC, FT], BF16)
            self.rmsnorm(h, iw, lambda c: self.cc("norm_ffn_g", layer * KC + c), hn)
            m = self.rtile("f_m", 1, [128, NJ, FT], BF16)
            for j in range(NJ):
                ys = []
                for half, ch in enumerate((j, NJ + j)):
                    ps = self.rtile("f_ps", 4, [128, 512], F32, psum=True)
                    for c in range(KC):
                        self.op("pe", lambda e, c=c, ps=ps, ch=ch: e.matmul(
                            ps[:, 0:iw], lhsT=wup[:, c, ch * 128:(ch + 1) * 128], rhs=hn[:, c, 0:iw],
                            start=(c == 0), stop=(c == KC - 1)), [wup, hn], [ps])
                    y = self.rtile("f_y", 4, [128, FT], F32)
                    cw = lambda tap, ch=ch: self.cc("conv_w", (layer * 44 + ch) * 3 + tap)
                    cb = self.cc("conv_b", layer * 44 + ch)
                    cst = self.tiles["consts"]
                    self.op("act", lambda e, y=y, ps=ps, cw=cw, cb=cb: e.activation(
                        out=y[:, 0:ow], in_=ps[:, 2:2 + ow], func=AF.Identity, bias=cb, scale=cw(2)),
                        [ps, cst], [y])
                    self.op("dve", lambda e, y=y, ps=ps, cw=cw: e.scalar_tensor_tensor(
                        out=y[:, 0:ow], in0=ps[:, 1:1 + ow], scalar=cw(1), in1=y[:, 0:ow],
                        op0=ALU.mult, op1=ALU.add), [ps, y, cst], [y])
                    self.op("dve", lambda e, y=y, ps=ps, cw=cw: e.scalar_tensor_tensor(
                        out=y[:, 0:ow], in0=ps[:, 0:ow], scalar=cw(0), in1=y[:, 0:ow],
                        op0=ALU.mult, op1=ALU.add), [ps, y, cst], [y])
                    ys.append(y)
                yg, yv = ys
                self.op("act", lambda e, yg=yg: e.activation(out=yg[:, 0:ow], in_=yg[:, 0:ow], func=AF.Silu),
                        [yg], [yg])
                self.op("dve", lambda e, yg=yg, yv=yv, j=j: e.tensor_tensor(
                    out=m[:, j, 0:ow], in0=yg[:, 0:ow], in1=yv[:, 0:ow], op=ALU.mult), [yg, yv], [m])
            for n in range(KC):
                ps = self.rtile("f_ps", 4, [128, 512], F32, psum=True)
                for j in range(NJ):
                    self.op("pe", lambda e, j=j, n=n, ps=ps: e.matmul(
                        ps[:, 0:ow], lhsT=wdn[:, j, n * 128:(n + 1) * 128], rhs=m[:, j, 0:ow],
                        start=(j == 0), stop=(j == NJ - 1)), [wdn, m], [ps])
                ho = self.rtile("f_ho", 3, [128, FT], F32)
                self.op("dve", lambda e, n=n, ps=ps, ho=ho: e.tensor_tensor(
                    out=ho[:, 0:ow], in0=ps[:, 0:ow], in1=h[:, n, 2:2 + ow], op=ALU.add), [ps, h], [ho])
                self.store(hdst[:, n, i0 + 2:i0 + 2 + ow], self.dr((dst, "all")), ho, ho[:, 0:ow], eng="sp")
        self.phase_end()


    def load_w1024(self, name, dram_ap):
        w = self.tile(name, [128, KC, D], BF16)
        for c0 in range(0, KC, 2):
            self.load(w, w[:, c0:c0 + 2, :], dram_ap[:, c0:c0 + 2, :], self.dr("wts"), eng="pool", nowaw=True)
        return w

    def blk_groups(self):
        gs = []
        b = 0
        while b < self.nb:
            n = min(4, self.nb - b)
            gs.append((b, n))
            b += n
        return gs

    def qkv_phase(self, j, src, do_kv):
        self.phase_begin()
        layer = 2 + j
        wq = self.load_w1024("wq", self.dram["sb_wq"][j])
        if do_kv:
            wk = self.load_w1024("wk", self.dram["sb_wk"])
            wv = self.load_w1024("wv", self.dram["sb_wv"])
        hsrc = self.dram[src].rearrange("c p t -> p c t")
        qT = self.dram["QT"].rearrange("c p t -> p c t")
        kT = self.dram["KT"].rearrange("c p t -> p c t")
        vd = self.dram["VV"]
        for (b0, nblk) in self.blk_groups():
            t0, tw = b0 * 128, nblk * 128
            h = self.rtile("q_h", 2, [128, KC, 512], F32)
            self.load(h, h[:, :, 0:tw], hsrc[:, :, t0:t0 + tw], self.dr((src, "all")))
            hn = self.rtile("q_hn", 2, [128, KC, 512], BF16)
            self.rmsnorm(h, tw, lambda c: self.cc("norm_mix_g", layer * KC + c), hn)
            jobs = [(wq, qT, 0.125, "QT")]
            if do_kv:
                kn = self.rtile("q_kn", 2, [128, KC, 512], BF16)
                self.rmsnorm(h, tw, lambda c: self.cc("kv_norm_g", c), kn)
                jobs.append((wk, kT, 1.0, "KT"))
            for (w, dst, scale, dname) in jobs:
                xin = hn if dname == "QT" else kn
                for n in range(KC):
                    ps = self.rtile("q_ps", 4, [128, 512], F32, psum=True)
                    for c in range(KC):
                        self.op("pe", lambda e: e.matmul(ps[:, 0:tw], lhsT=w[:, c, n * 128:(n + 1) * 128],
                                                         rhs=xin[:, c, 0:tw], start=(c == 0), stop=(c == KC - 1)),
                                [w, xin], [ps])
                    o = self.rtile("q_o", 4, [128, 512], BF16)
                    self.op("act", lambda e: e.activation(out=o[:, 0:tw], in_=ps[:, 0:tw], func=AF.Copy,
                                                          scale=scale), [ps], [o])
                    self.store(dst[:, n, t0:t0 + tw], self.dr((dname, "all")), o, o[:, 0:tw])
            if do_kv:
                for bi in range(nblk):
                    for hf in range(2):
                        ps = self.rtile("q_ps", 4, [128, 512], F32, psum=True)
                        for c in range(KC):
                            self.op("pe", lambda e: e.matmul(ps[:, :], lhsT=kn[:, c, bi * 128:(bi + 1) * 128],
                                                             rhs=wv[:, c, hf * 512:(hf + 1) * 512],
                                                             start=(c == 0), stop=(c == KC - 1)), [wv, kn], [ps])
                        o = self.rtile("q_o", 4, [128, 512], BF16)
                        self.op("dve", lambda e: e.tensor_copy(out=o[:, :], in_=ps[:, :]), [ps], [o])
                        r0 = (b0 + bi) * 128
                        self.store(vd[r0:r0 + 128, hf * 512:(hf + 1) * 512], self.dr(("VV", "all")), o, o[:, :])
        self.phase_end()

    def att_sweep(self, slot, kt, qt, vt, at, msk, ntri, nones, hh, b0, ng):
        pb = 64 * hh
        W = ng * 128
        q0 = b0 * 128
        hi = b0 + ng - 1
        outp = slot["out"]
        z = slot["z"]
        spsum = None
        for kb in range(hi, -1, -1):
            self.op("pe", lambda e: e.matmul(z[:, 0:W], lhsT=kt[pb:pb + 64, kb * 128:(kb + 1) * 128],
                                             rhs=qt[pb:pb + 64, q0:q0 + W], start=True, stop=False), [kt, qt], [z])
            yield
            ex = slot["ex"][kb % 2]
            sp = slot["sp"][kb % 2]
            self.op("act", lambda e: e.activation(out=ex[:, 0:W], in_=z[:, 0:W], func=AF.Exp), [z], [ex])
            self.op("act", lambda e: e.activation(out=sp[:, 0:W], in_=ex[:, 0:W], func=AF.Ln, bias=1.0, scale=1.0),
                    [ex], [sp])
            mi = None
            if kb >= b0:
                mi = 4 if kb == 0 else kb - b0
            elif kb == 0:
                mi = 5
            if mi is not None:
                self.op("pool", lambda e: e.tensor_tensor(out=sp[:, 0:W], in0=sp[:, 0:W], in1=msk[:, mi, 0:W],
                                                          op=ALU.mult), [sp, msk], [sp])
            yield
            self.op("pe", lambda e: e.matmul(z[:, 0:W], lhsT=ntri[:], rhs=sp[:, 0:W], start=False,
                                             stop=(spsum is None)), [ntri, sp], [z])
            if spsum is not None:
                self.op("pe", lambda e: e.matmul(z[:, 0:W], lhsT=nones[:], rhs=spsum[:, 0:W], start=False, stop=True),
                        [nones, spsum], [z])
            if kb > 0:
                nsum = slot["spsum"][kb % 2]
                if spsum is None:
                    self.op("dve", lambda e: e.tensor_copy(out=nsum[:, 0:W], in_=sp[:, 0:W]), [sp], [nsum])
                else:
                    self.op("dve", lambda e: e.tensor_tensor(out=nsum[:, 0:W], in0=spsum[:, 0:W], in1=sp[:, 0:W],
                                                              op=ALU.add), [spsum, sp], [nsum])
                spsum = nsum
            yield
            w = slot["w"][kb % 2]
            self.op("act", lambda e: e.activation(out=w[:, 0:W], in_=z[:, 0:W], func=AF.Exp), [z], [w])
            if mi is not None:
                self.op("pool", lambda e: e.tensor_tensor(out=w[:, 0:W], in0=w[:, 0:W], in1=msk[:, mi, 0:W],
                                                          op=ALU.mult), [w, msk], [w])
            yield
            self.op("pe", lambda e: e.matmul(outp[pb:pb + 64, 0:W], lhsT=vt[:, kb, pb:pb + 64], rhs=w[:, 0:W],
                                             start=(kb == hi), stop=(kb == 0)), [vt, w], [outp])
        yield
        self.op("dve", lambda e: e.tensor_copy(out=at[pb:pb + 64, q0:q0 + W], in_=outp[pb:pb + 64, 0:W]), [outp], [at])

    def att_phase(self, nslots=4):
        self.phase_begin()
        nb, LP = self.nb, self.LP
        msk = self.tile("amask", [128, 6, 512], BF16)
        self.load(msk, msk[:], self.dram["amask"][:, :, :], eng="pool")
        ntri = self.tile("ntri", [128, 128], BF16)
        self.load(ntri, ntri[:], self.dram["ntri"][:, :], eng="pool")
        nones = self.tile("nones1", [128, 128], BF16)
        self.op("pool", lambda e: e.memset(nones[:], -1.0), [], [nones])
        qT = self.dram["QT"]
        kT = self.dram["KT"]
        vd = self.dram["VV"].rearrange("(b s) n -> s b n", s=128)
        aT = self.dram["AT"]
        slots = []
        outs = [self.tile("a_out%d" % k, [128, 512], F32, psum=True) for k in range((nslots + 1) // 2)]
        for si in range(nslots):
            slots.append({
                "out": outs[si // 2],
                "z": self.tile("a_z%d" % si, [128, 512], F32, psum=True),
                "ex": [self.tile("a_e%d_%d" % (si, k), [128, 512], F32) for k in range(2)],
                "sp": [self.tile("a_sp%d_%d" % (si, k), [128, 512], BF16) for k in range(2)],
                "w": [self.tile("a_w%d_%d" % (si, k), [128, 512], BF16) for k in range(2)],
                "spsum": [self.tile("a_ss%d_%d" % (si, k), [128, 512], BF16) for k in range(2)],
            })
        groups = self.blk_groups()
        for c in range(KC):
            kt = self.rtile("a_k", 2, [128, LP], BF16)
            qt = self.rtile("a_q", 2, [128, LP], BF16)
            vt = self.rtile("a_v", 2, [128, nb, 128], BF16)
            at = self.rtile("a_o", 2, [128, LP], BF16)
            self.load(kt, kt[:], kT[c])
            self.load(qt, qt[:], qT[c])
            self.load(vt, vt[:], vd[:, :, c * 128:(c + 1) * 128])
            todo = [[(hh, b0, ng) for (b0, ng) in reversed(groups)] for hh in range(2)]
            active = [None] * nslots
            while todo[0] or todo[1] or any(a is not None for a in active):
                for si in range(nslots):
                    if active[si] is None:
                        lst = todo[si % 2]
                        if lst:
                            hh, b0, ng = lst.pop(0)
                            active[si] = self.att_sweep(slots[si], kt, qt, vt, at, msk, ntri, nones, hh, b0, ng)
                    if active[si] is not None:
                        try:
                            next(active[si])
                        except StopIteration:
                            active[si] = None
            self.store(aT[c], None, at, at[:])
        self.phase_end()

    def o_phase(self, j, src, dst):
        self.phase_begin()
        wo = self.load_w1024("wo", self.dram["sb_wo"][j])
        hsrc = self.dram[src].rearrange("c p t -> p c t")
        hdst = self.dram[dst].rearrange("c p t -> p c t")
        aT = self.dram["AT"].rearrange("c p t -> p c t")
        for (b0, nblk) in self.blk_groups():
            t0, tw = b0 * 128, nblk * 128
            v0 = PAD if b0 == 0 else 0
            h = self.rtile("o_h", 2, [128, KC, 512], F32)
            self.load(h, h[:, :, 0:tw], hsrc[:, :, t0:t0 + tw], self.dr((src, "all")))
            a = self.rtile("o_a", 2, [128, KC, 512], BF16)
            self.load(a, a[:, :, 0:tw], aT[:, :, t0:t0 + tw], self.dr(("AT", "all")))
            for n in range(KC):
                ps = self.rtile("o_ps", 4, [128, 512], F32, psum=True)
                for c in range(KC):
                    self.op("pe", lambda e: e.matmul(ps[:, 0:tw], lhsT=wo[:, c, n * 128:(n + 1) * 128],
                                                     rhs=a[:, c, 0:tw], start=(c == 0), stop=(c == KC - 1)),
                            [wo, a], [ps])
                ho = self.rtile("o_ho", 3, [128, 512], F32)
                self.op("dve", lambda e: e.tensor_tensor(out=ho[:, 0:tw], in0=ps[:, 0:tw], in1=h[:, n, 0:tw],
                                                         op=ALU.add), [ps, h], [ho])
                self.store(hdst[:, n, t0 + v0:t0 + tw], self.dr((dst, "all")), ho, ho[:, v0:tw])
        self.phase_end()


    def pool_init(self, n):
        self.tpool = [self.tile("tp%d" % i, [128, D], F32) for i in range(n)]

    def tget(self):
        return self.tpool.pop(0)

    def tfree(self, *ts):
        for t in ts:
            self.tpool.append(t)

    @staticmethod
    def fm(t, lo=0, hi=128):
        return t.t[:].rearrange("p (c t) -> p c t", c=KC)[:, :, lo:hi]

    def bc(self, nm, idx0):
        o = self.coff[nm] + idx0
        return self.tiles["consts"][:, o:o + KC].unsqueeze(2).broadcast_to([128, KC, 128])

    def rwkv_phase(self, i, src, dst):
        self.phase_begin()
        self.rn_w = 128
        nb, LP = self.nb, self.LP
        layer = i
        cst = self.tiles["consts"]
        dr = self.dram
        wr = self.load_w1024("wr", dr["rw_wr"][i])
        wk = self.load_w1024("wk", dr["rw_wk"][i])
        wv = self.load_w1024("wv", dr["rw_wv"][i])
        wo = self.load_w1024("wo", dr["rw_wo"][i])

        def ldw(name, shape, ap):
            t = self.tile(name, shape, BF16)
            self.load(t, t[:], ap, eng="pool")
            return t
        w1 = ldw("w1", [128, KC, 64], dr["rw_w1"][i])
        a1 = ldw("a1", [128, KC, 64], dr["rw_a1"][i])
        g1 = ldw("g1", [128, KC, 160], dr["rw_g1"][i])
        w2 = ldw("w2", [64, D], dr["rw_w2"][i])
        a2 = ldw("a2", [64, D], dr["rw_a2"][i])
        g2a = ldw("g2a", [128, D], dr["rw_g2"][i, 0:128, :])
        g2b = ldw("g2b", [32, D], dr["rw_g2"][i, 128:160, :])
        if i > 0:
            v1 = ldw("v1", [128, KC, 32], dr["rw_v1"][i - 1])
            v2 = ldw("v2", [32, D], dr["rw_v2"][i - 1])
        rc = self.tile("rwc", [128, 128 * 5 + 2 + 256], F32)
        self.load(rc, rc[:], dr["rw_const"][:, :])
        ident = rc[:, 0:128]
        bd = rc[:, 128:256]
        mstrict = rc[:, 256:384]
        mt2 = rc[:, 384:640]
        ind2 = rc[:, 640:642]
        onesf = rc[:, 642:770]
        zcol = rc[:, 770:771]
        tmb = self.tile("tmb", [128, 3 if i > 0 else 2, D], F32)
        self.load(tmb, tmb[:, 0, :], dr["rw_lnx_g"][i:i + 1, :].partition_broadcast(128))
        self.load(tmb, tmb[:, 1, :], dr["rw_lnx_b"][i:i + 1, :].partition_broadcast(128))
        if i > 0:
            self.load(tmb, tmb[:, 2, :], dr["rw_v0"][i - 1:i, :].partition_broadcast(128))
        self.pool_init(10)
        st2 = [self.tile("st2_%d" % c, [128, 128], F32) for c in range(KC)]
        for c in range(KC):
            self.op("pool", lambda e: e.memset(st2[c][:], 0.0), [], [st2[c]])
        hsrc = dr[src].rearrange("c p t -> p c t")
        hdst = dr[dst].rearrange("c p t -> p c t")
        vf = dr["VF"]
        hn_prev = None
        ART = self.tile("AR", [128, KC, 256], F32)
        btT = self.tile("bt", [128, KC, 128], F32)
        ktT = self.tile("kt", [128, KC, 128], F32)
        pcT = self.tile("pc", [128, KC], F32)
        ssb = self.tile("ssb", [128, NH], F32)
        mu = self.tile("mu", [128, NH], F32)
        var = self.tile("var", [128, NH], F32)
        midw = self.tile("midw", [64, 128], BF16)
        mida = self.tile("mida", [64, 128], BF16)
        midga = self.tile("midga", [128, 128], BF16)
        midgb = self.tile("midgb", [32, 128], BF16)
        midv = self.tile("midv", [32, 128], BF16)
        ygT = self.tile("yg", [128, KC, 128], BF16)
        C0 = 0.6065306597126334

        def pbig():
            return self.rtile("pbig", 2, [128, D], F32, psum=True)

        def psm():
            return self.rtile("psm", 3, [128, 512], F32, psum=True)

        for cb in range(nb):
            t0 = cb * 128
            h = self.rtile("r_h", 2, [128, KC, 128], F32)
            self.load(h, h[:], hsrc[:, :, t0:t0 + 128])
            hn = self.rtile("r_hn", 1, [128, KC, 129], F32)
            if hn_prev is None:
                self.op("pool", lambda e: e.memset(hn[:, :, 0:1], 0.0), [], [hn])
            else:
                self.op("pool", lambda e: e.tensor_copy(out=hn[:, :, 0:1], in_=hn[:, :, 128:129]), [hn], [hn])
            self.rmsnorm(h, 128, lambda c: self.cc("norm_mix_g", layer * KC + c), hn, out_off=1)
            hn_prev = hn
            xx = self.tget()
            self.op("dve", lambda e: e.tensor_tensor(out=self.fm(xx), in0=hn[:, :, 0:128], in1=hn[:, :, 1:129],
                                                     op=ALU.subtract), [hn], [xx])
            def mkx(q):
                x = self.rtile("xq", 3, [128, KC, 128], BF16)
                tm = self.tget()
                eng = "dve" if q % 2 == 0 else "pool"
                self.op(eng, lambda e: e.tensor_tensor(out=self.fm(tm), in0=self.fm(xx),
                                                       in1=self.bc("rw_mix", (i * 6 + q) * KC), op=ALU.mult),
                        [xx, cst], [tm])
                self.op(eng, lambda e: e.tensor_tensor(out=x[:], in0=self.fm(tm), in1=hn[:, :, 1:129],
                                                       op=ALU.add), [tm, hn], [x])
                self.tfree(tm)
                return x
            rT = self.tget()
            kT_ = self.tget()
            vS = self.tget()
            for (w, qi, dstt) in ((wr, 0, rT), (wk, 2, kT_)):
                x = mkx(qi)
                ps = pbig()
                for n in range(KC):
                    for c in range(KC):
                        self.op("pe", lambda e: e.matmul(ps[:, n * 128:(n + 1) * 128],
                                                         lhsT=w[:, c, n * 128:(n + 1) * 128], rhs=x[:, c, :],
                                                         start=(c == 0), stop=(c == KC - 1)), [w, x], [ps])
                self.op("act", lambda e: e.copy(out=dstt[:], in_=ps[:]), [ps], [dstt])
            xv = mkx(3)
            ps = pbig()
            for hf in range(2):
                for c in range(KC):
                    self.op("pe", lambda e: e.matmul(ps[:, hf * 512:(hf + 1) * 512], lhsT=xv[:, c, :],
                                                     rhs=wv[:, c, hf * 512:(hf + 1) * 512],
                                                     start=(c == 0), stop=(c == KC - 1)), [wv, xv], [ps])
            self.op("act", lambda e: e.copy(out=vS[:], in_=ps[:]), [ps], [vS])
            for (wt, qi, ncol, mid, fn) in (((v1, 3, 32, midv, AF.Copy),) if i > 0 else ()) + \
                    ((w1, 1, 64, midw, AF.Tanh), (a1, 4, 64, mida, AF.Copy),
                     (g1, 5, 128, midga, AF.Sigmoid), (g1, 5, 32, midgb, AF.Sigmoid)):
                x = xv if qi == 3 else (x if (mid is midgb) else mkx(qi))
                ps = psm()
                c0 = 128 if mid is midgb else 0
                for c in range(KC):
                    self.op("pe", lambda e: e.matmul(ps[0:ncol, 0:128], lhsT=wt[:, c, c0:c0 + ncol], rhs=x[:, c, :],
                                                     start=(c == 0), stop=(c == KC - 1)), [wt, x], [ps])
                self.op("act", lambda e: e.activation(out=mid[0:ncol, :], in_=ps[0:ncol, 0:128], func=fn), [ps], [mid])
            self.tfree(xx)
            sig = self.tget()
            ps = pbig()
            for n in range(KC):
                self.op("pe", lambda e: e.matmul(ps[:, n * 128:(n + 1) * 128], lhsT=w2[0:64, n * 128:(n + 1) * 128],
                                                 rhs=midw[0:64, :], start=True, stop=True), [w2, midw], [ps])
            self.op("dve", lambda e: e.tensor_tensor(out=self.fm(sig), in0=ps[:].rearrange("p (c t) -> p c t", c=KC),
                                                     in1=self.bc("rw_w0", i * KC), op=ALU.add), [ps, cst], [sig])
            self.op("act", lambda e: e.activation(out=sig[:], in_=sig[:], func=AF.Sigmoid), [sig], [sig])
            aT_ = self.tget()
            ps = pbig()
            for n in range(KC):
                self.op("pe", lambda e: e.matmul(ps[:, n * 128:(n + 1) * 128], lhsT=a2[0:64, n * 128:(n + 1) * 128],
                                                 rhs=mida[0:64, :], start=True, stop=True), [a2, mida], [ps])
            self.op("dve", lambda e: e.tensor_tensor(out=self.fm(aT_), in0=ps[:].rearrange("p (c t) -> p c t", c=KC),
                                                     in1=self.bc("rw_a0", i * KC), op=ALU.add), [ps, cst], [aT_])
            self.op("act", lambda e: e.activation(out=aT_[:], in_=aT_[:], func=AF.Sigmoid), [aT_], [aT_])
            if i > 0:
                vg = self.tget()
                vfT = self.tget()
                self.load(vfT, vfT[:], vf[t0:t0 + 128, :])
                ps = pbig()
                for hf in range(2):
                    self.op("pe", lambda e: e.matmul(ps[:, hf * 512:(hf + 1) * 512], lhsT=midv[0:32, :],
                                                     rhs=v2[0:32, hf * 512:(hf + 1) * 512], start=True, stop=True),
                            [v2, midv], [ps])
                self.op("dve", lambda e: e.tensor_tensor(out=vg[:], in0=ps[:], in1=tmb[:, 2, :], op=ALU.add),
                        [ps, tmb], [vg])
                self.op("act", lambda e: e.activation(out=vg[:], in_=vg[:], func=AF.Sigmoid), [vg], [vg])
                self.op("pool", lambda e: e.tensor_tensor(out=vfT[:], in0=vfT[:], in1=vS[:], op=ALU.subtract),
                        [vfT, vS], [vfT])
                self.op("dve", lambda e: e.tensor_tensor(out=vfT[:], in0=vfT[:], in1=vg[:], op=ALU.mult),
                        [vfT, vg], [vfT])
                self.op("dve", lambda e: e.tensor_tensor(out=vS[:], in0=vS[:], in1=vfT[:], op=ALU.add),
                        [vS, vfT], [vS])
                self.tfree(vg, vfT)
            else:
                self.store(vf[t0:t0 + 128, :], None, vS, vS[:])
            kk = self.tget()
            self.op("dve", lambda e: e.tensor_tensor(out=self.fm(kk), in0=self.fm(kT_), in1=self.bc("rw_kk", i * KC),
                                                     op=ALU.mult), [kT_, cst], [kk])
            ksq = self.tget()
            self.op("pool", lambda e: e.tensor_tensor(out=ksq[:], in0=kk[:], in1=kk[:], op=ALU.mult), [kk], [ksq])
            ps = pbig()
            for c in range(KC):
                self.op("pe", lambda e: e.matmul(ps[:, c * 128:(c + 1) * 128], lhsT=bd, rhs=ksq[:, c * 128:(c + 1) * 128],
                                                 start=True, stop=True), [rc, ksq], [ps])
            self.op("act", lambda e: e.activation(out=ksq[:], in_=ps[:], func=AF.Sqrt), [ps], [ksq])
            self.op("dve", lambda e: e.tensor_scalar(out=ksq[:], in0=ksq[:], scalar1=1e-12, scalar2=None, op0=ALU.max),
                    [ksq], [ksq])
            self.op("dve", lambda e: e.reciprocal(out=ksq[:], in_=ksq[:]), [ksq], [ksq])
            self.op("dve", lambda e: e.tensor_tensor(out=kk[:], in0=kk[:], in1=ksq[:], op=ALU.mult), [kk, ksq], [kk])
            self.tfree(ksq)
            km = self.tget()
            self.op("dve", lambda e: e.scalar_tensor_tensor(out=self.fm(km), in0=self.fm(aT_), scalar=-1.0,
                                                            in1=self.bc("rw_ka", i * KC), op0=ALU.add, op1=ALU.mult),
                    [aT_, cst], [km])
            self.op("dve", lambda e: e.scalar_tensor_tensor(out=km[:], in0=km[:], scalar=1.0, in1=kT_[:],
                                                            op0=ALU.add, op1=ALU.mult), [km, kT_], [km])
            self.tfree(kT_)
            cs = self.tget()
            for c in range(KC):
                self.op("dve", lambda e: e.tensor_tensor_scan(out=cs[:, c * 128:(c + 1) * 128], data0=onesf,
                                                              data1=sig[:, c * 128:(c + 1) * 128], initial=zcol,
                                                              op0=ALU.mult, op1=ALU.add), [sig, rc], [cs])
            ein = self.tget()
            eneg = self.tget()
            self.op("act", lambda e: e.activation(out=ein[:], in_=cs[:], func=AF.Exp, scale=-C0), [cs], [ein])
            self.op("act", lambda e: e.activation(out=eneg[:], in_=cs[:], func=AF.Exp, scale=C0), [cs], [eneg])
            self.op("pool", lambda e: e.tensor_tensor(out=cs[:], in0=cs[:], in1=sig[:], op=ALU.subtract), [cs, sig], [cs])
            self.op("act", lambda e: e.activation(out=cs[:], in_=cs[:], func=AF.Exp, scale=-C0), [cs], [cs])
            self.tfree(sig)
            self.op("dve", lambda e: e.scalar_tensor_tensor(out=ART[:, :, 0:128], in0=self.fm(kk), scalar=-1.0,
                                                            in1=self.fm(cs), op0=ALU.mult, op1=ALU.mult),
                    [kk, cs], [ART])
            self.op("pool", lambda e: e.tensor_tensor(out=ART[:, :, 128:256], in0=self.fm(rT), in1=self.fm(ein),
                                                      op=ALU.mult), [rT, ein], [ART])
            self.tfree(cs)
            self.op("dve", lambda e: e.tensor_tensor(out=btT[:], in0=self.fm(kk), in1=self.fm(aT_), op=ALU.mult),
                    [kk, aT_], [btT])
            self.op("dve", lambda e: e.tensor_tensor(out=btT[:], in0=btT[:], in1=self.fm(eneg), op=ALU.mult),
                    [btT, eneg], [btT])
            self.tfree(aT_, kk)
            self.op("pool", lambda e: e.tensor_tensor(out=ktT[:], in0=self.fm(km), in1=self.fm(eneg), op=ALU.mult),
                    [km, eneg], [ktT])
            self.tfree(eneg)
            self.op("act", lambda e: e.copy(out=pcT[:, :].unsqueeze(2), in_=self.fm(ein, 127, 128)), [ein], [pcT])
            self.tfree(ein)
            self.op("dve", lambda e: e.tensor_tensor(out=rT[:], in0=rT[:], in1=km[:], op=ALU.mult), [rT, km], [rT])
            self.op("dve", lambda e: e.tensor_tensor(out=self.fm(rT), in0=self.fm(rT), in1=self.bc("rw_rk", i * KC),
                                                     op=ALU.mult), [rT, cst], [rT])
            ps = psm()
            for c in range(KC):
                self.op("pe", lambda e: e.matmul(ps[:, 2 * c:2 * c + 2], lhsT=rT[:, c * 128:(c + 1) * 128], rhs=ind2,
                                                 start=True, stop=True), [rT, rc], [ps])
            self.op("act", lambda e: e.copy(out=ssb[:], in_=ps[:, 0:NH]), [ps], [ssb])
            self.tfree(rT, km)
            pcb = pcT[:, :].unsqueeze(2).broadcast_to([128, KC, 128])
            KH = self.tget()
            BH = self.tget()
            for (srcT, dstT) in ((ktT, KH), (btT, BH)):
                tmp = self.tget()
                self.op("dve", lambda e: e.tensor_tensor(out=self.fm(tmp), in0=srcT[:], in1=pcb, op=ALU.mult),
                        [srcT, pcT], [tmp])
                ps = pbig()
                for c in range(KC):
                    self.op("pe", lambda e: e.transpose(ps[:, c * 128:(c + 1) * 128], tmp[:, c * 128:(c + 1) * 128],
                                                        ident), [tmp, rc], [ps])
                self.op("act", lambda e: e.copy(out=dstT[:], in_=ps[:]), [ps], [dstT])
                self.tfree(tmp)
            yT = self.tget()
            for c in range(KC):
                sk = self.rtile("sk", 2, [128, 2, 256], F32)
                sbm = self.rtile("sbm", 2, [128, 2, 256], F32)
                tb0 = self.rtile("tb0", 2, [128, 2, 128], BF16)
                sx = self.rtile("sx", 3, [128, 2, 192], BF16)
                p1 = psm()
                p2 = psm()
                p3 = psm()
                for hh in range(2):
                    pb = 64 * hh
                    self.op("pe", lambda e: e.matmul(p1[:, hh * 256:(hh + 1) * 256], lhsT=ktT[pb:pb + 64, c, :],
                                                     rhs=ART[pb:pb + 64, c, :], start=True, stop=True), [ktT, ART], [p1])
                    self.op("pe", lambda e: e.matmul(p2[:, hh * 256:(hh + 1) * 256], lhsT=btT[pb:pb + 64, c, :],
                                                     rhs=ART[pb:pb + 64, c, :], start=True, stop=True), [btT, ART], [p2])
                    self.op("pe", lambda e: e.matmul(p3[:, hh * 128:(hh + 1) * 128], lhsT=ART[pb:pb + 64, c, 0:128],
                                                     rhs=btT[pb:pb + 64, c, :], start=True, stop=True), [btT, ART], [p3])
                mt2b = mt2.unsqueeze(1).broadcast_to([128, 2, 256])
                msb = mstrict.unsqueeze(1).broadcast_to([128, 2, 128])
                self.op("dve", lambda e: e.tensor_tensor(out=sk[:], in0=p1[:].rearrange("p (h x) -> p h x", h=2),
                                                         in1=mt2b, op=ALU.mult), [p1, rc], [sk])
                self.op("dve", lambda e: e.tensor_tensor(out=sbm[:], in0=p2[:].rearrange("p (h x) -> p h x", h=2),
                                                         in1=mt2b, op=ALU.mult), [p2, rc], [sbm])
                self.op("act", lambda e: e.copy(out=tb0[:], in_=sbm[:, :, 0:128]), [sbm], [tb0])
                self.op("dve", lambda e: e.tensor_tensor(out=sx[:, :, 0:128],
                                                         in0=p3[:, 0:256].rearrange("p (h x) -> p h x", h=2),
                                                         in1=msb, op=ALU.mult), [p3, rc], [sx])
                px = psm()
                self.op("pe", lambda e: e.matmul(px[:, 0:128], lhsT=ART[:, c, 0:128], rhs=st2[c][:], start=True,
                                                 stop=False), [ART, st2[c]], [px])
                for hh in range(2):
                    hd = 2 * c + hh
                    self.op("pe", lambda e: e.matmul(px[:, hh * 64:(hh + 1) * 64], lhsT=sk[:, hh, 0:128],
                                                     rhs=vS[:, hd * 64:(hd + 1) * 64], start=False, stop=(hh == 1)),
                            [sk, vS], [px])
                self.op("act", lambda e: e.copy(out=sx[:, :, 128:192],
                                                in_=px[:, 0:128].rearrange("p (h x) -> p h x", h=2)), [px], [sx])
                tcur = None
                for k in range(7):
                    pd = psm()
                    for hh in range(2):
                        tk = tb0[:, hh, :] if tcur is None else tcur[:, hh, :]
                        ncol = 192 if k < 6 else 64
                        c0 = 0 if k < 6 else 128
                        self.op("pe", lambda e: e.matmul(pd[:, hh * 192:hh * 192 + ncol], lhsT=tk,
                                                         rhs=sx[:, hh, c0:c0 + ncol], start=True, stop=True),
                                [tb0 if tcur is None else tcur, sx], [pd])
                    if k < 6:
                        pt = psm()
                        for hh in range(2):
                            tk = tb0[:, hh, :] if tcur is None else tcur[:, hh, :]
                            self.op("pe", lambda e: e.matmul(pt[:, hh * 128:(hh + 1) * 128], lhsT=sx[:, hh, 0:128],
                                                             rhs=tk, start=True, stop=True),
                                    [tb0 if tcur is None else tcur, sx], [pt])
                        sxn = self.rtile("sx", 3, [128, 2, 192], BF16)
                        pdv = pd[:, 0:384].rearrange("p (h x) -> p h x", h=2)
                        self.op("act", lambda e: e.copy(out=sxn[:, :, 0:128], in_=pdv[:, :, 0:128]), [pd], [sxn])
                        self.op("dve", lambda e: e.tensor_tensor(out=sxn[:, :, 128:192], in0=pdv[:, :, 128:192],
                                                                 in1=sx[:, :, 128:192], op=ALU.add), [pd, sx], [sxn])
                        tn = self.rtile("tt", 2, [128, 2, 128], BF16)
                        self.op("act", lambda e: e.copy(out=tn[:], in_=pt[:, 0:256].rearrange("p (h x) -> p h x", h=2)),
                                [pt], [tn])
                        sx = sxn
                        tcur = tn
                    else:
                        uf = self.rtile("uf", 2, [128, 2, 64], F32)
                        pdv = pd[:, 0:384].rearrange("p (h x) -> p h x", h=2)
                        self.op("dve", lambda e: e.tensor_tensor(out=uf[:], in0=pdv[:, :, 0:64],
                                                                 in1=sx[:, :, 128:192], op=ALU.add), [pd, sx], [uf])
                py = psm()
                self.op("pe", lambda e: e.matmul(py[:, 0:128], lhsT=ART[:, c, 128:256], rhs=st2[c][:], start=True,
                                                 stop=False), [ART, st2[c]], [py])
                for hh in range(2):
                    hd = 2 * c + hh
                    self.op("pe", lambda e: e.matmul(py[:, hh * 64:(hh + 1) * 64], lhsT=sbm[:, hh, 128:256],
                                                     rhs=uf[:, hh, :], start=False, stop=False), [sbm, uf], [py])
                    self.op("pe", lambda e: e.matmul(py[:, hh * 64:(hh + 1) * 64], lhsT=sk[:, hh, 128:256],
                                                     rhs=vS[:, hd * 64:(hd + 1) * 64], start=False, stop=(hh == 1)),
                            [sk, vS], [py])
                self.op("act", lambda e: e.copy(out=yT[:, c * 128:(c + 1) * 128], in_=py[:, 0:128]), [py], [yT])
                pst = psm()
                for hh in range(2):
                    hd = 2 * c + hh
                    pb = 64 * hh
                    self.op("pe", lambda e: e.matmul(pst[pb:pb + 64, hh * 64:(hh + 1) * 64],
                                                     lhsT=BH[:, hd * 64:(hd + 1) * 64], rhs=uf[:, hh, :],
                                                     start=True, stop=False), [BH, uf], [pst])
                    self.op("pe", lambda e: e.matmul(pst[pb:pb + 64, hh * 64:(hh + 1) * 64],
                                                     lhsT=KH[:, hd * 64:(hd + 1) * 64], rhs=vS[:, hd * 64:(hd + 1) * 64],
                                                     start=False, stop=True), [KH, vS], [pst])
                for hh in range(2):
                    pb = 64 * hh
                    self.op("dve", lambda e: e.scalar_tensor_tensor(
                        out=st2[c][pb:pb + 64, hh * 64:(hh + 1) * 64], in0=st2[c][pb:pb + 64, hh * 64:(hh + 1) * 64],
                        scalar=pcT[pb:pb + 64, c:c + 1], in1=pst[pb:pb + 64, hh * 64:(hh + 1) * 64],
                        op0=ALU.mult, op1=ALU.add), [st2[c], pcT, pst], [st2[c]])
            self.tfree(KH, BH)
            y3 = yT.t[:].rearrange("p (h x) -> p h x", h=NH)
            self.op("dve", lambda e: e.tensor_reduce(out=mu[:], in_=y3, axis=AX.X, op=ALU.add), [yT], [mu])
            mub = mu[:, :].unsqueeze(2).broadcast_to([128, NH, HD])
            self.op("dve", lambda e: e.scalar_tensor_tensor(out=y3, in0=mub, scalar=-1.0 / HD, in1=y3, op0=ALU.mult,
                                                            op1=ALU.add), [yT, mu], [yT])
            sq = self.tget()
            self.op("pool", lambda e: e.tensor_tensor(out=sq[:], in0=yT[:], in1=yT[:], op=ALU.mult), [yT], [sq])
            self.op("dve", lambda e: e.tensor_reduce(out=var[:], in_=sq.t[:].rearrange("p (h x) -> p h x", h=NH),
                                                     axis=AX.X, op=ALU.add), [sq], [var])
            self.op("act", lambda e: e.activation(out=var[:], in_=var[:], func=AF.Sqrt, bias=GN_EPS, scale=1.0 / HD),
                    [var], [var])
            self.op("dve", lambda e: e.reciprocal(out=var[:], in_=var[:]), [var], [var])
            varb = var[:, :].unsqueeze(2).broadcast_to([128, NH, HD])
            self.op("dve", lambda e: e.tensor_tensor(out=y3, in0=y3, in1=varb, op=ALU.mult), [yT, var], [yT])
            self.op("pool", lambda e: e.tensor_tensor(out=yT[:], in0=yT[:], in1=tmb[:, 0, :], op=ALU.mult), [yT, tmb], [yT])
            self.op("pool", lambda e: e.tensor_tensor(out=yT[:], in0=yT[:], in1=tmb[:, 1, :], op=ALU.add), [yT, tmb], [yT])
            ssbb = ssb[:, :].unsqueeze(2).broadcast_to([128, NH, HD])
            self.op("dve", lambda e: e.tensor_tensor(out=sq.t[:].rearrange("p (h x) -> p h x", h=NH),
                                                     in0=vS.t[:].rearrange("p (h x) -> p h x", h=NH), in1=ssbb,
                                                     op=ALU.mult), [vS, ssb], [sq])
            self.op("dve", lambda e: e.tensor_tensor(out=yT[:], in0=yT[:], in1=sq[:], op=ALU.add), [yT, sq], [yT])
            self.tfree(sq, vS)
            gS = self.tget()
            ps = pbig()
            for n in range(KC):
                self.op("pe", lambda e: e.matmul(ps[:, n * 128:(n + 1) * 128], lhsT=g2a[:, n * 128:(n + 1) * 128],
                                                 rhs=midga[:, :], start=True, stop=False), [g2a, midga], [ps])
                self.op("pe", lambda e: e.matmul(ps[:, n * 128:(n + 1) * 128], lhsT=g2b[0:32, n * 128:(n + 1) * 128],
                                                 rhs=midgb[0:32, :], start=False, stop=True), [g2b, midgb], [ps])
            self.op("act", lambda e: e.copy(out=gS[:], in_=ps[:]), [ps], [gS])
            ps = pbig()
            for c in range(KC):
                self.op("pe", lambda e: e.transpose(ps[:, c * 128:(c + 1) * 128], yT[:, c * 128:(c + 1) * 128], ident),
                        [yT, rc], [ps])
            self.op("dve", lambda e: e.tensor_tensor(out=ygT[:], in0=ps[:].rearrange("p (c t) -> p c t", c=KC),
                                                     in1=self.fm(gS), op=ALU.mult), [ps, gS], [ygT])
            self.tfree(yT, gS)
            ps = pbig()
            for n in range(KC):
                for c in range(KC):
                    self.op("pe", lambda e: e.matmul(ps[:, n * 128:(n + 1) * 128], lhsT=wo[:, c, n * 128:(n + 1) * 128],
                                                     rhs=ygT[:, c, :], start=(c == 0), stop=(c == KC - 1)),
                            [wo, ygT], [ps])
            ho = self.rtile("r_ho", 1, [128, KC, 128], F32)
            self.op("dve", lambda e: e.tensor_tensor(out=ho[:], in0=ps[:].rearrange("p (c t) -> p c t", c=KC),
                                                     in1=h[:], op=ALU.add), [ps, h], [ho])
            v0 = PAD if cb == 0 else 0
            self.store(hdst[:, :, t0 + v0:t0 + 128], None, ho, ho[:, :, v0:128])
        self.rn_w = 512
        self.phase_end()

    def zero_pads(self, names):
        zt = self.tiles["zeros"]
        for nm in names:
            for c in range(KC):
                self.store(self.dram[nm][c, :, 0:PAD], None, zt, zt[:], eng="sp")

    def finish(self, out_res=None):
        self.barrier()
        return self.nc


def build_ffn_test(nb):
    p = Prog(nb)
    p.din("hT0", [KC, 128, p.LP])
    p.din("ffn_up", [DEPTH, 128, KC, F2])
    p.din("ffn_down", [DEPTH, 128, NJ, D])
    p.dout("hT1", [KC, 128, p.LP])
    p.setup_consts()
    p.zero_pads(["hT1"])
    p.ffn_phase(0, "hT0", "hT1")
    return p.finish([p.dr(("hT1", "all"))])


def attn_masks():
    s = np.arange(128)[:, None]
    col = np.arange(512)[None, :]
    q, t = col // 128, col % 128
    m = np.zeros((128, 6, 512), np.float32)
    for r in range(4):
        m[:, r, :] = ((q > r) | ((q == r) & (t > s))).astype(np.float32)
    row = (s >= PAD).astype(np.float32)
    m[:, 4, :] = m[:, 0, :] * row
    m[:, 5, :] = row * np.ones((1, 512), np.float32)
    j = np.arange(128)[:, None]
    ss = np.arange(128)[None, :]
    ntri = -(j >= ss).astype(np.float32)
    return m, ntri


def build_att_test(nb):
    p = Prog(nb)
    p.din("hT0", [KC, 128, p.LP])
    p.din("sb_wq", [2, 128, KC, D]); p.din("sb_wk", [128, KC, D]); p.din("sb_wv", [128, KC, D])
    p.din("sb_wo", [2, 128, KC, D])
    p.din("amask", [128, 6, 512]); p.din("ntri", [128, 128])
    p.dscratch("QT", [KC, 128, p.LP], BF16); p.dscratch("KT", [KC, 128, p.LP], BF16)
    p.dscratch("VV", [p.LP, D], BF16); p.dscratch("AT", [KC, 128, p.LP], BF16)
    p.dout("hT1", [KC, 128, p.LP])
    p.setup_consts()
    p.zero_pads(["hT1"])
    p.qkv_phase(0, "hT0", True)
    p.att_phase()
    p.o_phase(0, "hT0", "hT1")
    return p.finish([p.dr(("hT1", "all"))])


def rw_consts():
    p = np.arange(128)[:, None]
    q = np.arange(128)[None, :]
    ident = (p == q).astype(np.float32)
    bd = ((p // 64) == (q // 64)).astype(np.float32)
    mstrict = (p > q).astype(np.float32)
    mt_strict = (q > p).astype(np.float32)
    mt_incl = (q >= p).astype(np.float32)
    ind2 = np.concatenate([(p // 64 == 0), (p // 64 == 1)], axis=1).astype(np.float32)
    onesf = np.ones((128, 128), np.float32)
    zcol = np.zeros((128, 1), np.float32)
    pad = np.zeros((128, 127), np.float32)
    return np.ascontiguousarray(np.concatenate([ident, bd, mstrict, mt_strict, mt_incl, ind2, onesf, zcol, pad],
                                               axis=1))


def declare_rw_inputs(p):
    p.din("rw_wr", [2, 128, KC, D]); p.din("rw_wk", [2, 128, KC, D]); p.din("rw_wv", [2, 128, KC, D])
    p.din("rw_wo", [2, 128, KC, D])
    p.din("rw_w1", [2, 128, KC, 64]); p.din("rw_a1", [2, 128, KC, 64]); p.din("rw_g1", [2, 128, KC, 160])
    p.din("rw_v1", [1, 128, KC, 32])
    p.din("rw_w2", [2, 64, D]); p.din("rw_a2", [2, 64, D]); p.din("rw_g2", [2, 160, D]); p.din("rw_v2", [1, 32, D])
    p.din("rw_lnx_g", [2, D]); p.din("rw_lnx_b", [2, D]); p.din("rw_v0", [1, D])
    p.din("rw_const", [128, 128 * 5 + 2 + 256])
    p.dscratch("VF", [p.LP, D], F32)


def build_rw_test(nb, nlayers=1):
    p = Prog(nb)
    p.din("hT0", [KC, 128, p.LP])
    declare_rw_inputs(p)
    p.dout("hT1", [KC, 128, p.LP])
    p.dscratch("hTa", [KC, 128, p.LP])
    p.setup_consts()
    p.zero_pads(["hT1", "hTa"])
    if nlayers == 1:
        p.rwkv_phase(0, "hT0", "hT1")
    else:
        p.rwkv_phase(0, "hT0", "hTa")
        p.rwkv_phase(1, "hTa", "hT1")
    return p.finish()


def final_phase(p, src):
    p.phase_begin()
    hsrc = p.dram[src].rearrange("c p t -> p c t")
    yo = p.dram["yT"].rearrange("c p t -> p c t")
    t = PAD + NMETA
    while t < p.LP:
        tw = min(512, p.LP - t)
        h = p.rtile("fn_h", 2, [128, KC, 512], F32)
        p.load(h, h[:, :, 0:tw], hsrc[:, :, t:t + tw])
        o = p.rtile("fn_o", 2, [128, KC, 512], F32)
        p.rmsnorm(h, tw, lambda c: p.cc("final_g", c), o)
        p.store(yo[:, :, t - PAD - NMETA:t - PAD - NMETA + tw], None, o, o[:, :, 0:tw])
        t += tw
    p.phase_end()


def build_full(nb):
    p = Prog(nb)
    p.din("hT0", [KC, 128, p.LP])
    p.din("ffn_up", [DEPTH, 128, KC, F2])
    p.din("ffn_down", [DEPTH, 128, NJ, D])
    declare_rw_inputs(p)
    p.din("sb_wq", [2, 128, KC, D]); p.din("sb_wk", [128, KC, D]); p.din("sb_wv", [128, KC, D])
    p.din("sb_wo", [2, 128, KC, D])
    p.din("amask", [128, 6, 512]); p.din("ntri", [128, 128])
    p.dscratch("QT", [KC, 128, p.LP], BF16); p.dscratch("KT", [KC, 128, p.LP], BF16)
    p.dscratch("VV", [p.LP, D], BF16); p.dscratch("AT", [KC, 128, p.LP], BF16)
    p.dscratch("hA", [KC, 128, p.LP]); p.dscratch("hB", [KC, 128, p.LP])
    p.dout("yT", [KC, 128, p.LP - PAD - NMETA])
    p.setup_consts()
    p.zero_pads(["hA", "hB"])
    p.barrier()
    p.rwkv_phase(0, "hT0", "hA")
    p.ffn_phase(0, "hA", "hB")
    p.rwkv_phase(1, "hB", "hA")
    p.ffn_phase(1, "hA", "hB")
    p.qkv_phase(0, "hB", True)
    p.att_phase()
    p.o_phase(0, "hB", "hA")
    p.ffn_phase(2, "hA", "hB")
    p.qkv_phase(1, "hB", False)
    p.att_phase()
    p.o_phase(1, "hB", "hA")
    p.ffn_phase(3, "hA", "hB")
    final_phase(p, "hB")
    return p.finish()


def _w1024(w):
    return np.ascontiguousarray(w.reshape(KC, 128, -1).transpose(1, 0, 2))


def _wst(w):
    return np.stack([_w1024(w[i]) for i in range(w.shape[0])])


def _cols(v):
    v = np.asarray(v, np.float32).reshape(-1, KC, 128)
    return np.ascontiguousarray(v.transpose(2, 0, 1).reshape(128, -1))


def host_inputs(inp, nb):
    f = {k: np.asarray(v, np.float32) for k, v in inp.items()}
    m, ntri = attn_masks()
    cw = f["ffn_conv_w"]
    shared = {
        "ffn_up": np.ascontiguousarray(f["ffn_up"].reshape(DEPTH, KC, 128, F2).transpose(0, 2, 1, 3)),
        "ffn_down": np.ascontiguousarray(f["ffn_down"].reshape(DEPTH, NJ, 128, D).transpose(0, 2, 1, 3)),
        "norm_ffn_g": _cols(f["norm_ffn_g"]), "norm_mix_g": _cols(f["norm_mix_g"]),
        "kv_norm_g": _cols(f["kv_norm_g"]), "final_g": _cols(f["final_norm_g"]),
        "conv_w": np.ascontiguousarray(cw.reshape(DEPTH, 3, 44, 128).transpose(3, 0, 2, 1).reshape(128, -1)),
        "conv_b": np.ascontiguousarray(f["ffn_conv_b"].reshape(DEPTH, 44, 128).transpose(2, 0, 1).reshape(128, -1)),
        "rw_mix": _cols(f["rw_mix"]), "rw_w0": _cols(f["rw_w0"]), "rw_a0": _cols(f["rw_a0"]),
        "rw_kk": _cols(f["rw_kk"]), "rw_ka": _cols(f["rw_ka"]), "rw_rk": _cols(f["rw_rk"].reshape(2, D)),
        "rw_wr": _wst(f["rw_wr"]), "rw_wk": _wst(f["rw_wk"]), "rw_wv": _wst(f["rw_wv"]), "rw_wo": _wst(f["rw_wo"]),
        "rw_w1": _wst(f["rw_w1"]), "rw_a1": _wst(f["rw_a1"]), "rw_g1": _wst(f["rw_g1"]), "rw_v1": _wst(f["rw_v1"]),
        "rw_w2": f["rw_w2"], "rw_a2": f["rw_a2"], "rw_g2": f["rw_g2"], "rw_v2": f["rw_v2"],
        "rw_lnx_g": f["rw_lnx_g"], "rw_lnx_b": f["rw_lnx_b"], "rw_v0": f["rw_v0"],
        "rw_const": rw_consts(),
        "sb_wq": _wst(f["sb_wq"]), "sb_wo": _wst(f["sb_wo"]), "sb_wk": _w1024(f["sb_wk"]), "sb_wv": _w1024(f["sb_wv"]),
        "amask": m, "ntri": ntri,
    }
    return shared


def host_h0(xb, meta, nb):
    LP = nb * 128
    hT = np.zeros((D, LP), np.float32)
    hT[:, PAD:PAD + NMETA] = meta.T
    hT[:, PAD + NMETA:] = xb.T
    return hT.reshape(KC, 128, LP)


_CACHE = {}


def kernel(**inputs):
    x = np.asarray(inputs["x"], np.float32)
    B, S, _ = x.shape
    nb = (PAD + NMETA + S) // 128
    if nb not in _CACHE:
        _CACHE[nb] = build_full(nb)
    nc = _CACHE[nb]
    shared = host_inputs({k: v for k, v in inputs.items() if k != "x"}, nb)
    meta = np.asarray(inputs["meta_tokens"], np.float32)
    ncores = 8
    in_maps = []
    for cidx in range(ncores):
        b = cidx % B
        m = dict(shared)
        m["hT0"] = host_h0(x[b], meta, nb)
        in_maps.append(m)
    res = run_bass_kernel_spmd(nc, in_maps, core_ids=list(range(ncores)))
    out = np.empty((B, S, D), np.float32)
    for b in range(B):
        out[b] = res.results[b]["yT"].reshape(D, S).T
    return out
```
